# Optimizing a Trainium2 kernel written in Bass

```python
import math
import jax, jax.numpy as jnp
from jax import lax
import numpy as np

D_MODEL = 1024
BATCH = 4
SEQ = 4096
DEPTH = 4

MEM_LEN = 256
NORM_EPS = 1e-6
GDN_HEADS = 8
GDN_DK = 128
GDN_DV = 128
GDN_CONV = 4
GDN_CHUNK = 64
DIFF_HEADS = 8
DIFF_DH = 64
DIFF_Q_BLOCK = 128
MEM_HEADS = 4
MEM_DH = 256
N_BRANCH = 3

GDN_W = GDN_HEADS * GDN_DV
GDN_QKV_W = 2 * GDN_HEADS * GDN_DK + GDN_W
DIFF_W = DIFF_HEADS * 2 * DIFF_DH
MEM_W = MEM_HEADS * MEM_DH
IN_SPLITS = (GDN_QKV_W, GDN_HEADS, GDN_HEADS, GDN_W, DIFF_W, DIFF_W, DIFF_W, DIFF_W, MEM_W, MEM_W, N_BRANCH * D_MODEL)
IN_COLS = sum(IN_SPLITS)

kernel_name = "hybrid_gdn_diffattn_memxattn_gated_merge"


def rms_norm(x, w):
    xf = x.astype(jnp.float32)
    y = xf * lax.rsqrt(jnp.mean(xf * xf, axis=-1, keepdims=True) + NORM_EPS)
    return (y * w.astype(jnp.float32)).astype(x.dtype)


def l2_normalize(t):
    t = t.astype(jnp.float32)
    return t * lax.rsqrt(jnp.sum(t * t, axis=-1, keepdims=True) + NORM_EPS)


def causal_depthwise_conv(x, w):
    c = x.shape[-1]
    k = w.shape[0]
    return lax.conv_general_dilated(x, w[:, None, :].astype(x.dtype), window_strides=(1,), padding=[(k - 1, 0)], dimension_numbers=("NWC", "WIO", "NWC"), feature_group_count=c)


def gated_delta_rule_chunked(q, k, v, g, beta):
    b, s, h, dk = q.shape
    dv = v.shape[-1]
    c = GDN_CHUNK
    n = s // c
    chunk4 = lambda t: t.reshape(b, n, c, h, t.shape[-1]).transpose(0, 1, 3, 2, 4)
    chunk3 = lambda t: t.reshape(b, n, c, h).transpose(0, 1, 3, 2)
    qc, kc, vc = chunk4(q), chunk4(k), chunk4(v)
    gc, bc = chunk3(g), chunk3(beta)
    G = jnp.cumsum(gc, axis=-1)
    idx = jnp.arange(c)
    incl = idx[:, None] >= idx[None, :]
    strict = idx[:, None] > idx[None, :]
    gamma = jnp.exp(jnp.where(incl, G[..., :, None] - G[..., None, :], -jnp.inf))
    kk = jnp.einsum("bnhcd,bnhed->bnhce", kc, kc)
    a_low = jnp.where(strict, bc[..., :, None] * kk * gamma, 0.0)
    lmat = a_low + jnp.eye(c, dtype=jnp.float32)
    rhs = jnp.concatenate([vc * bc[..., None], kc * (bc * jnp.exp(G))[..., None]], axis=-1)
    sol = lax.linalg.triangular_solve(lmat, rhs, left_side=True, lower=True, unit_diagonal=True)
    u, w = sol[..., :dv], sol[..., dv:]
    attn_intra = jnp.einsum("bnhcd,bnhed->bnhce", qc, kc) * gamma
    q_dec = qc * jnp.exp(G)[..., None]
    k_dec = kc * jnp.exp(G[..., -1:] - G)[..., None]
    g_last = jnp.exp(G[..., -1])
    xs = tuple(jnp.moveaxis(t, 1, 0) for t in (u, w, q_dec, k_dec, attn_intra, g_last))

    def step(state, inp):
        u_i, w_i, qd_i, kd_i, a_i, gl_i = inp
        v_new = u_i - jnp.einsum("bhcd,bhde->bhce", w_i, state)
        o_i = jnp.einsum("bhcd,bhde->bhce", qd_i, state) + jnp.einsum("bhce,bhef->bhcf", a_i, v_new)
        state = state * gl_i[..., None, None] + jnp.einsum("bhcd,bhce->bhde", kd_i, v_new)
        return state, o_i

    s0 = jnp.zeros((b, h, dk, dv), jnp.float32)
    _, o = lax.scan(step, s0, xs)
    return o.transpose(1, 0, 3, 2, 4).reshape(b, s, h, dv)


def alibi_slopes(n_heads):
    return jnp.asarray(2.0 ** (-8.0 * np.arange(1, n_heads + 1) / n_heads), jnp.float32)


def diff_attention_causal(q, k, v, lam, slopes):
    s_len = q.shape[1]
    scale = q.shape[-1] ** -0.5
    outs = []
    for start in range(0, s_len, DIFF_Q_BLOCK):
        end = start + DIFF_Q_BLOCK
        sc = jnp.einsum("bqhmd,bkhmd->bhmqk", q[:, start:end], k[:, :end], preferred_element_type=jnp.float32) * scale
        dist = jnp.arange(start, end, dtype=jnp.float32)[:, None] - jnp.arange(end, dtype=jnp.float32)[None, :]
        bias = jnp.where(dist >= 0, -slopes[:, None, None] * dist, -jnp.inf)
        p = jax.nn.softmax(sc + bias[None, :, None], axis=-1)
        pd = (p[:, :, 0] - lam * p[:, :, 1]).astype(v.dtype)
        outs.append(jnp.einsum("bhqk,bkhe->bqhe", pd, v[:, :end]))
    return jnp.concatenate(outs, axis=1)


def hybrid_layer(x, mem, layer_idx, pre_w, post_w, w_in, conv_w, a_log, dt_bias, gdn_norm_w, lam_vecs, diff_norm_w, mem_norm_w, w_mem_kv, w_br_gdn, w_br_diff, w_br_mem, w_out):
    b, s, d = x.shape
    f32 = jnp.float32
    h = rms_norm(x, pre_w)
    bounds = []
    off = 0
    for width in IN_SPLITS:
        bounds.append((off, off + width))
        off += width
    (gdn_qkv, gdn_a, gdn_b, gdn_z, dq, dk, dv, dz, mq, mz, gate_logits) = [h @ w_in[:, lo:hi] for lo, hi in bounds]

    qkv = jax.nn.silu(causal_depthwise_conv(gdn_qkv, conv_w))
    qa, ka, va = jnp.split(qkv, [GDN_HEADS * GDN_DK, 2 * GDN_HEADS * GDN_DK], axis=-1)
    qa = l2_normalize(qa.reshape(b, s, GDN_HEADS, GDN_DK)) * (GDN_DK ** -0.5)
    ka = l2_normalize(ka.reshape(b, s, GDN_HEADS, GDN_DK))
    va = va.reshape(b, s, GDN_HEADS, GDN_DV).astype(f32)
    beta = jax.nn.sigmoid(gdn_b.astype(f32))
    g = -jnp.exp(a_log.astype(f32)) * jax.nn.softplus(gdn_a.astype(f32) + dt_bias.astype(f32))
    oa = gated_delta_rule_chunked(qa, ka, va, g, beta)
    oa = rms_norm(oa, gdn_norm_w) * jax.nn.silu(gdn_z.reshape(b, s, GDN_HEADS, GDN_DV))
    y_gdn = oa.reshape(b, s, GDN_W).astype(x.dtype) @ w_br_gdn

    lam_init = 0.8 - 0.6 * math.exp(-0.3 * layer_idx)
    lv = lam_vecs.astype(f32)
    lam = jnp.exp(jnp.sum(lv[0] * lv[1])) - jnp.exp(jnp.sum(lv[2] * lv[3])) + lam_init
    qb = dq.reshape(b, s, DIFF_HEADS, 2, DIFF_DH)
    kb = dk.reshape(b, s, DIFF_HEADS, 2, DIFF_DH)
    vb = dv.reshape(b, s, DIFF_HEADS, 2 * DIFF_DH)
    ob = diff_attention_causal(qb, kb, vb, lam, alibi_slopes(DIFF_HEADS))
    ob = rms_norm(ob, diff_norm_w) * (1.0 - lam_init)
    ob = ob * jax.nn.silu(dz.reshape(b, s, DIFF_HEADS, 2 * DIFF_DH))
    y_diff = ob.reshape(b, s, DIFF_W) @ w_br_diff

    m = rms_norm(mem, mem_norm_w)
    mk, mv = jnp.split(m @ w_mem_kv, 2, axis=-1)
    mk = mk.reshape(b, -1, MEM_HEADS, MEM_DH)
    mv = mv.reshape(b, -1, MEM_HEADS, MEM_DH)
    qc = mq.reshape(b, s, MEM_HEADS, MEM_DH)
    sc = jnp.einsum("bshd,bmhd->bhsm", qc, mk, preferred_element_type=f32) * (MEM_DH ** -0.5)
    p = jax.nn.softmax(sc, axis=-1).astype(mv.dtype)
    oc = jnp.einsum("bhsm,bmhd->bshd", p, mv).reshape(b, s, MEM_W) * jax.nn.silu(mz)
    y_mem = oc @ w_br_mem

    gates = jax.nn.sigmoid(gate_logits).reshape(b, s, N_BRANCH, d)
    y = gates[:, :, 0] * y_gdn + gates[:, :, 1] * y_diff + gates[:, :, 2] * y_mem
    out = y @ w_out
    return x + rms_norm(out, post_w)


def setup_inputs(seed: int = 0) -> dict:
    key = jax.random.key(seed)
    ks = jax.random.split(key, 20)
    f32 = jnp.float32
    nrm = lambda k, shape, scale: jax.random.normal(k, shape, f32) * scale
    x = nrm(ks[0], (BATCH, SEQ, D_MODEL), 1.0)
    mem = nrm(ks[1], (BATCH, MEM_LEN, D_MODEL), 1.0)
    pre_norm_w = 1.0 + nrm(ks[2], (DEPTH, D_MODEL), 0.02)
    post_norm_w = 1.0 + nrm(ks[3], (DEPTH, D_MODEL), 0.02)
    w_in = nrm(ks[4], (DEPTH, D_MODEL, IN_COLS), D_MODEL ** -0.5)
    gdn_conv_w = nrm(ks[5], (DEPTH, GDN_CONV, GDN_QKV_W), GDN_CONV ** -0.5)
    gdn_a_log = jnp.log(jax.random.uniform(ks[6], (DEPTH, GDN_HEADS), f32, 1.0, 16.0))
    dt = jnp.exp(jax.random.uniform(ks[7], (DEPTH, GDN_HEADS), f32, math.log(1e-3), math.log(1e-1)))
    gdn_dt_bias = dt + jnp.log(-jnp.expm1(-dt))
    gdn_norm_w = 1.0 + nrm(ks[8], (DEPTH, GDN_DV), 0.02)
    diff_lambda = nrm(ks[9], (DEPTH, 4, DIFF_DH), 0.1)
    diff_norm_w = 1.0 + nrm(ks[10], (DEPTH, 2 * DIFF_DH), 0.02)
    mem_norm_w = 1.0 + nrm(ks[11], (DEPTH, D_MODEL), 0.02)
    w_mem_kv = nrm(ks[12], (DEPTH, D_MODEL, 2 * MEM_W), D_MODEL ** -0.5)
    w_br_gdn = nrm(ks[13], (DEPTH, GDN_W, D_MODEL), GDN_W ** -0.5)
    w_br_diff = nrm(ks[14], (DEPTH, DIFF_W, D_MODEL), DIFF_W ** -0.5)
    w_br_mem = nrm(ks[15], (DEPTH, MEM_W, D_MODEL), MEM_W ** -0.5)
    w_out = nrm(ks[16], (DEPTH, D_MODEL, D_MODEL), D_MODEL ** -0.5)
    return {"x": x, "mem": mem, "pre_norm_w": pre_norm_w, "post_norm_w": post_norm_w, "w_in": w_in, "gdn_conv_w": gdn_conv_w, "gdn_a_log": gdn_a_log, "gdn_dt_bias": gdn_dt_bias, "gdn_norm_w": gdn_norm_w, "diff_lambda": diff_lambda, "diff_norm_w": diff_norm_w, "mem_norm_w": mem_norm_w, "w_mem_kv": w_mem_kv, "w_br_gdn": w_br_gdn, "w_br_diff": w_br_diff, "w_br_mem": w_br_mem, "w_out": w_out}


def reference(x, mem, pre_norm_w, post_norm_w, w_in, gdn_conv_w, gdn_a_log, gdn_dt_bias, gdn_norm_w, diff_lambda, diff_norm_w, mem_norm_w, w_mem_kv, w_br_gdn, w_br_diff, w_br_mem, w_out):
    for l in range(DEPTH):
        x = hybrid_layer(x, mem, l, pre_norm_w[l], post_norm_w[l], w_in[l], gdn_conv_w[l], gdn_a_log[l], gdn_dt_bias[l], gdn_norm_w[l], diff_lambda[l], diff_norm_w[l], mem_norm_w[l], w_mem_kv[l], w_br_gdn[l], w_br_diff[l], w_br_mem[l], w_out[l])
    return x
```

```python
import contextlib
import math
import numpy as np
import concourse.bass as bass
import concourse.mybir as mybir
from concourse.bass_utils import run_bass_kernel_spmd

F32 = mybir.dt.float32
F32R = mybir.dt.float32r
BF16 = mybir.dt.bfloat16
ALU = mybir.AluOpType
AF = mybir.ActivationFunctionType

D_MODEL = 1024
BATCH = 4
SEQ = 4096
DEPTH = 4
MEM_LEN = 256
EPS = 1e-6
IN_COLS = 13328
NEG = -1.0e30

ENGS = ("pe", "act", "dve", "pool", "sp")


class Buf:
    __slots__ = ("name", "w", "r", "excl", "multi")

    def __init__(self, name="", excl=False, multi=False):
        self.name = name
        self.w = {}
        self.r = {}
        self.multi = multi
        self.excl = excl


class Prog:
    def __init__(self, nc, same_engine_sync=True):
        self.nc = nc
        self.q = {e: [] for e in ENGS}
        self.cnt = {e: 0 for e in ENGS}
        self.seen = {e: {} for e in ENGS}
        self.dma_sems = {}
        self.same = same_engine_sync
        self.sem_handles = {}
        self.gst = None

    def _need(self, eng, reads, writes):
        need = {}

        def add(d):
            for k, v in d.items():
                if need.get(k, 0) < v:
                    need[k] = v
        for b in reads:
            add(b.w)
            if b.excl:
                add({k: v for k, v in b.r.items() if k != ("e", eng)})
        for b in writes:
            if b.multi:
                continue
            add(b.w)
            add(b.r)
        out = []
        seen = self.seen[eng]
        for k, v in need.items():
            if not self.same and k == ("e", eng):
                continue
            if seen.get(k, 0) >= v:
                continue
            seen[k] = v
            out.append((k, v))
        return out

    def op(self, eng, fn, reads=(), writes=()):
        waits = self._need(eng, reads, writes)
        self.cnt[eng] += 1
        c = self.cnt[eng]
        key = ("e", eng)
        self.q[eng].append((waits, fn, key, 1))
        for b in writes:
            b.w = {key: c}
            b.r = {}
        for b in reads:
            if b.r.get(key, 0) < c:
                b.r[key] = c

    def dma(self, eng, out_ap, in_ap, reads=(), writes=(), semname=None, **kw):
        waits = self._need(eng, reads, writes)
        key = ("d", semname)
        self.dma_sems[semname] = self.dma_sems.get(semname, 0) + 16
        c = self.dma_sems[semname]

        def fn(e, out_ap=out_ap, in_ap=in_ap, kw=kw):
            return e.dma_start(out=out_ap, in_=in_ap, **kw)
        self.q[eng].append((waits, fn, key, 16))
        for b in writes:
            if b.multi:
                b.w[key] = c
                continue
            b.w = {key: c}
            b.r = {}
        for b in reads:
            if b.r.get(key, 0) < c:
                b.r[key] = c

    def coll(self, fn, reads=(), writes=(), semname=None):
        waits = self._need("pool", reads, writes)
        key = ("d", semname)
        self.dma_sems[semname] = self.dma_sems.get(semname, 0) + 1
        c = self.dma_sems[semname]
        self.q["pool"].append((waits, fn, key, 1))
        for b in writes:
            b.w = {key: c}
            b.r = {}
        for b in reads:
            if b.r.get(key, 0) < c:
                b.r[key] = c

    def final_waits(self, eng, bufs):
        waits = self._need(eng, bufs, ())
        self.q[eng].append((waits, None, None, 0))

    def barrier(self):
        allk = [(("e", e), c) for e, c in self.cnt.items() if c > 0]
        allk += [(("d", n), c) for n, c in self.dma_sems.items()]
        for eng in ENGS:
            seen = self.seen[eng]
            waits = []
            for k, v in allk:
                if seen.get(k, 0) >= v:
                    continue
                seen[k] = v
                waits.append((k, v))
            self.q[eng].append((waits, None, None, 0))

    def _sem(self, key):
        if key not in self.sem_handles:
            if self.gst is None:
                self.gst = contextlib.ExitStack()
            nm = ("se_" if key[0] == "e" else "sd_") + key[1]
            self.sem_handles[key] = self.gst.enter_context(self.nc.semaphore(nm))
        return self.sem_handles[key]

    def flush(self):
        nc = self.nc
        for e in ENGS:
            self._sem(("e", e))
        for lst in self.q.values():
            for waits, fn, key, inc in lst:
                for k, v in waits:
                    self._sem(k)
                if key is not None:
                    self._sem(key)
        H = self.sem_handles
        q = self.q
        self.q = {e: [] for e in ENGS}
        with nc.Block() as block:
            def run(engobj, lst):
                for waits, fn, key, inc in lst:
                    for k, v in waits:
                        engobj.wait_ge(H[k], v)
                    if fn is not None:
                        fn(engobj).then_inc(H[key], inc)

            @block.tensor
            def _(e):
                run(e, q["pe"])

            @block.scalar
            def _(e):
                run(e, q["act"])

            @block.vector
            def _(e):
                run(e, q["dve"])

            @block.gpsimd
            def _(e):
                run(e, q["pool"])

            @block.sync
            def _(e):
                run(e, q["sp"])

    def emit(self):
        self.flush()

    def finish(self):
        if self.gst is not None:
            self.gst.close()
            self.gst = None


class Ctx:
    def __init__(self, nc):
        self.nc = nc
        self.P = Prog(nc)
        self.st = contextlib.ExitStack()
        self.n = 0
        self.rot = {}
        self.pfx = ""
        self.K = None
        self.gst = None

    @contextlib.contextmanager
    def phase(self, name):
        self.pfx = name + "_"
        self.rot = {}
        self.st = contextlib.ExitStack()
        with self.st:
            yield
            self.P.barrier()
            self.P.flush()

    def sb(self, shape, dt=F32, name=None):
        self.n += 1
        nm = name or f"sb{self.n}"
        t = self.st.enter_context(self.nc.sbuf_tensor(self.pfx + nm, list(shape), dt))
        return t, Buf(nm)

    def ps(self, shape, dt=F32, name=None):
        self.n += 1
        nm = name or f"ps{self.n}"
        t = self.st.enter_context(self.nc.psum_tensor(self.pfx + nm, list(shape), dt))
        return t, Buf(nm, excl=True)

    def pool(self, tag, n, shape, dt=F32, psum=False):
        self.rot[tag] = [[(self.ps if psum else self.sb)(shape, dt, f"{tag}{i}") for i in range(n)], 0]

    def get(self, tag):
        r = self.rot[tag]
        t = r[0][r[1] % len(r[0])]
        r[1] += 1
        return t

    def dram_in(self, name, shape, dt=F32):
        return self.nc.dram_tensor(name, list(shape), dt, kind="ExternalInput").ap()

    def dram_out(self, name, shape, dt=F32):
        return self.nc.dram_tensor(name, list(shape), dt, kind="ExternalOutput").ap()


def _r(ap):
    return ap


def _consts(C):
    if C.K is not None:
        return C.K
    P = C.P
    K = {}
    ident, bi = C.sb([128, 128], F32, "ident")
    ones, bo = C.sb([128, 128], F32, "ones")
    triu, bt = C.sb([128, 128], F32, "triu")
    mincl, bm1 = C.sb([128, 128], F32, "mincl")
    mstr, bm2 = C.sb([128, 128], F32, "mstr")
    identb, bib = C.sb([128, 128], BF16, "identb")
    epst, be = C.sb([128, 1], F32, "epst")

    def mk0(e):
        e.memset(ident[:], 0.0)
        e.memset(ones[:], 1.0)
        e.memset(triu[:], 1.0)
        e.memset(mincl[:], 0.0)
        e.memset(mstr[:], 0.0)
        return e.memset(epst[:], EPS)
    P.op("pool", mk0, writes=[bi, bo, bt, bm1, bm2, be])

    def mk(e):
        e.affine_select(out=ident[:], in_=ident[:], pattern=[[-1, 128]], compare_op=ALU.not_equal, fill=1.0,
                        base=0, channel_multiplier=1)
        e.affine_select(out=triu[:], in_=triu[:], pattern=[[1, 128]], compare_op=ALU.is_ge, fill=0.0,
                        base=0, channel_multiplier=-1)
        e.affine_select(out=mincl[:], in_=mincl[:], pattern=[[1, 128]], compare_op=ALU.is_ge, fill=NEG,
                        base=0, channel_multiplier=-1)
        return e.affine_select(out=mstr[:], in_=mstr[:], pattern=[[1, 128]], compare_op=ALU.is_gt, fill=NEG,
                               base=0, channel_multiplier=-1)
    P.op("pool", mk, reads=[bi, bt, bm1, bm2], writes=[bi, bt, bm1, bm2])
    P.op("pool", lambda e: e.tensor_copy(out=identb[:], in_=ident[:]), reads=[bi], writes=[bib])
    K.update(ident=(ident, bi), ones=(ones, bo), triu=(triu, bt), mincl=(mincl, bm1), mstr=(mstr, bm2),
             identb=(identb, bib), eps=(epst, be))
    return K


def _bcast_load(C, dram_vec, n, name):
    t, b = C.sb([128, n], F32, name + "_bc")
    src = dram_vec.partition_broadcast(128)
    C.P.dma("sp", t[:], src, writes=[b], semname="ld_" + name)
    return t, b


def _load_w(C, wt, wb, wdram, ncols, semname):
    src = wdram.rearrange("(kc p) n -> p kc n", p=128)
    for kc in range(8):
        C.P.dma("pool", wt[:, kc, :], src[:, kc, :], writes=[wb] if kc == 0 else [], reads=[], semname=semname)
    wb.w = {("d", semname): C.P.dma_sems[semname]}


def _make_hT(C, K, x_dram, st, prew, hT, hTb, n_tiles=4):
    P = C.P
    identb, bib = K["identb"]
    epst, be = K["eps"]
    for t in range(n_tiles):
        xt, bx = C.get("xt")
        r0 = (st * n_tiles + t) * 128
        P.dma("sp", xt[:], x_dram[r0:r0 + 128, :], writes=[bx], semname="ld_" + bx.name)
        sq, bsq = C.get("junk")
        ss, bss = C.get("col")
        P.op("act", lambda e, xt=xt, sq=sq, ss=ss: e.activation(out=sq[:, 0:1024], in_=xt[:], func=AF.Square,
                                                                 accum_out=ss[:, 0:1]),
             reads=[bx], writes=[bsq, bss])
        P.op("act", lambda e, ss=ss: e.activation(out=ss[:, 1:2], in_=ss[:, 0:1], func=AF.Sqrt, bias=epst[:, 0:1],
                                                  scale=1.0 / D_MODEL), reads=[bss, be], writes=[bss])
        P.op("dve", lambda e, ss=ss: e.reciprocal(out=ss[:, 2:3], in_=ss[:, 1:2]), reads=[bss], writes=[bss])
        hb, bhb = C.get("hb")
        P.op("dve", lambda e, hb=hb, xt=xt, ss=ss: e.scalar_tensor_tensor(
            out=hb[:], in0=xt[:], scalar=ss[:, 2:3], in1=prew[0][:], op0=ALU.mult, op1=ALU.mult),
            reads=[bx, bss, prew[1]], writes=[bhb])
        pt, bpt = C.get("pst")

        def tr(e, hb=hb, pt=pt):
            for kc in range(8):
                ins = e.transpose(pt[:, kc * 128:(kc + 1) * 128], hb[:, kc * 128:(kc + 1) * 128], identb[:])
            return ins
        P.op("pe", tr, reads=[bhb, bib], writes=[bpt])
        P.op("act", lambda e, pt=pt, t=t: e.copy(out=hT[:, :, t * 128:(t + 1) * 128],
                                                 in_=pt[:, :].rearrange("p (k n) -> p k n", k=8)),
             reads=[bpt], writes=[hTb])


def build_gdn(S, stage=99, C=None, io=None, tag="a1"):
    standalone = C is None
    if standalone:
        nc = bass.Bass("TRN2", target_bir_lowering=False)
        C = Ctx(nc)
        io = dict(x=C.dram_in("x", [S, D_MODEL]), wg=C.dram_in("wg", [D_MODEL, 2056]),
                  convw=C.dram_in("convw", [1536, 4]), prew=C.dram_in("prew", [D_MODEL]),
                  alog=C.dram_in("alog", [4]), dtb=C.dram_in("dtb", [4]), gnw=C.dram_in("gnw", [128]),
                  o_gdn=C.dram_out("o_gdn", [S, 512]), mem=C.dram_in("mem", [MEM_LEN, D_MODEL]),
                  mnw=C.dram_in("mnw", [D_MODEL]), wkv=C.dram_in("wkv", [D_MODEL, 1024]),
                  wm=C.dram_in("wm", [D_MODEL, 1024]), o_mem=C.dram_out("o_mem", [S, 512]))
    nc = C.nc
    P = C.P
    NT = S // 128
    NST = S // 512
    x_d, wg_d, convw_d, prew_d = io["x"], io["wg"], io["convw"], io["prew"]
    alog_d, dtb_d, gnw_d, o_d = io["alog"], io["dtb"], io["gnw"], io["o_gdn"]
    mem_d, mnw_d, wkv_d, wm_d, om_d = io["mem"], io["mnw"], io["wkv"], io["wm"], io["o_mem"]
    bo_d = Buf("o_d", multi=True)
    bom_d = Buf("om_d", multi=True)
    with C.phase(tag):
        K = _consts(C)
        ident, bi = K["ident"]; ones, bon = K["ones"]; triu, btr = K["triu"]
        mincl, bmi = K["mincl"]; mstr, bms = K["mstr"]; epst, be = K["eps"]
        prew = _bcast_load(C, prew_d, D_MODEL, "prew")
        gnw = _bcast_load(C, gnw_d, 128, "gnw")
        alog = _bcast_load(C, alog_d, 4, "alog")
        dtb = _bcast_load(C, dtb_d, 4, "dtb")
        cw, bcw = C.sb([128, 12, 4], F32, "cw")
        P.dma("sp", cw[:], convw_d.rearrange("(c p) j -> p c j", p=128), writes=[bcw], semname="ld_cw")
        negA, bnA = C.sb([128, 4], F32, "negA")
        P.op("act", lambda e: e.activation(out=negA[:], in_=alog[0][:], func=AF.Exp), reads=[alog[1]], writes=[bnA])
        P.op("dve", lambda e: e.tensor_scalar(out=negA[:], in0=negA[:], scalar1=-1.0, scalar2=None, op0=ALU.mult),
             reads=[bnA], writes=[bnA])
        wg, bwg = C.sb([128, 8, 2056], BF16, "wg_sb")
        _load_w(C, wg, bwg, wg_d, 2056, "ld_wg")
        hT, bhT = C.sb([128, 8, 512], BF16, "hT")
        cin, _ = C.sb([128, 12, 515], F32, "cin")
        qkv, _ = C.sb([128, 12, 512], F32, "qkvT")
        bcins = [Buf(f"cin{c}") for c in range(12)]
        bqkvs = [Buf(f"qkv{c}") for c in range(12)]
        S_t = [C.sb([128, 128], F32, f"S{h}") for h in range(4)]
        C.pool("xt", 2, [128, 1024], F32)
        C.pool("junk", 2, [128, 1024], F32)
        C.pool("col", 10, [128, 8], F32)
        C.pool("hb", 2, [128, 1024], BF16)
        C.pool("pst", 1, [128, 1024], BF16, psum=True)
        C.pool("ps", 4, [128, 512], F32, psum=True)
        C.pool("ps2", 3, [128, 512], F32, psum=True)
        C.pool("cacc", 2, [128, 512], F32)
        C.pool("sz", 2, [128, 512], F32)
        C.pool("sm", 4, [128, 32], F32)
        hm = [[C.sb([128, 128], F32, f"hm{h}_{i}") for i in range(13)] for h in range(4)]
        hpb = [[C.sb([128, 256], F32, f"hpb{h}_{i}") for i in range(2)] for h in range(4)]
        C.pool("ocat", 2, [128, 512], F32)
        P.op("pool", lambda e: e.memset(cin[:, :, 0:3], 0.0), writes=bcins)
        mnw = _bcast_load(C, mnw_d, D_MODEL, "mnw")
        wm, bwm = C.sb([128, 8, 1024], BF16, "wm_sb")
        wkv, bwkv = wm, bwm
        _load_w(C, wkv, bwkv, wkv_d, 1024, "ld_wkv")
        mT, bmT = C.sb([128, 8, 256], BF16, "mT")
        mkT, bmk = C.sb([128, 4, 256], BF16, "mkT")
        mva, bmv = C.sb([128, 2, 2, 257], BF16, "mva")
        mq, bmq = C.sb([128, 4, 512], BF16, "mq")
        C.pool("pTm", 2, [128, 128], BF16)
        C.pool("smz", 2, [128, 512], F32)
        C.pool("omem", 2, [128, 512], F32)
        _make_hT(C, K, mem_d, 0, mnw, mT, bmT, n_tiles=2)
        P.op("pool", lambda e: e.memset(mva[:, :, :, 256:257], 1.0), writes=[bmv])
        for c in range(4):
            pp, bpp = C.get("ps")

            def mmk_(e, pp=pp, c=c):
                for kc in range(8):
                    ins = e.matmul(pp[:, 0:256], lhsT=wkv[:, kc, c * 128:(c + 1) * 128], rhs=mT[:, kc, :],
                                   start=(kc == 0), stop=(kc == 7))
                return ins
            P.op("pe", mmk_, reads=[bwkv, bmT], writes=[bpp])
            P.op("act", lambda e, pp=pp, c=c: e.copy(out=mkT[:, c, :], in_=pp[:, 0:256]), reads=[bpp], writes=[bmk])
        for mt in range(2):
            pp, bpp = C.get("ps")

            def mmv_(e, pp=pp, mt=mt):
                for kc in range(8):
                    ins = e.matmul(pp[:, :], lhsT=mT[:, kc, mt * 128:(mt + 1) * 128], rhs=wkv[:, kc, 512:1024],
                                   start=(kc == 0), stop=(kc == 7))
                return ins
            P.op("pe", mmv_, reads=[bwkv, bmT], writes=[bpp])
            P.op("act", lambda e, pp=pp, mt=mt: e.copy(out=mva[:, mt, :, 0:256],
                                                       in_=pp[:, :].rearrange("p (h e) -> p h e", h=2)),
                 reads=[bpp], writes=[bmv])
        _load_w(C, wm, bwm, wm_d, 1024, "ld_wm")
        for h in range(4):
            P.op("pool", lambda e, h=h: e.tensor_tensor(out=_r(S_t[h][0][:]), in0=ident[:], in1=ident[:], op=ALU.subtract),
                 reads=[bi], writes=[S_t[h][1]])

        for st in range(NST):
            _make_hT(C, K, x_d, st, prew, hT, bhT)
            for c in range(12):
                pp, bpp = C.get("ps")
                bcin = bcins[c]
                bqkv = bqkvs[c]

                def mm(e, pp=pp, c=c):
                    for kc in range(8):
                        ins = e.matmul(pp[:, :], lhsT=wg[:, kc, c * 128:(c + 1) * 128], rhs=hT[:, kc, :],
                                       start=(kc == 0), stop=(kc == 7))
                    return ins
                P.op("pe", mm, reads=[bwg, bhT], writes=[bpp])
                P.op("act", lambda e, pp=pp, c=c: e.copy(out=cin[:, c, 3:515], in_=pp[:, :]), reads=[bpp], writes=[bcin])
                acc, bacc = C.get("cacc")

                P.op("dve", lambda e, acc=acc, c=c: e.tensor_scalar(
                    out=acc[:], in0=cin[:, c, 0:512], scalar1=cw[:, c, 0:1], scalar2=None, op0=ALU.mult),
                    reads=[bcin, bcw], writes=[bacc])
                for j in range(1, 4):
                    P.op("dve", lambda e, acc=acc, c=c, j=j: e.scalar_tensor_tensor(
                        out=acc[:], in0=cin[:, c, j:j + 512], scalar=cw[:, c, j:j + 1], in1=acc[:], op0=ALU.mult,
                        op1=ALU.add), reads=[bcin, bcw, bacc], writes=[bacc])
                P.op("pool", lambda e, c=c: e.tensor_copy(out=cin[:, c, 0:3], in_=cin[:, c, 512:515]),
                     reads=[bcin, bacc], writes=[bcin])
                P.op("act", lambda e, acc=acc, c=c: e.activation(out=_r(qkv[:, c, :]), in_=acc[:], func=AF.Silu),
                     reads=[bacc], writes=[bqkv])
            for c in range(8 if stage >= 2 else 0):
                bqkv = bqkvs[c]
                sq, bsq = C.get("cacc")
                P.op("pool", lambda e, sq=sq, c=c: e.tensor_tensor(out=sq[:], in0=qkv[:, c, :], in1=qkv[:, c, :],
                                                                   op=ALU.mult), reads=[bqkv], writes=[bsq])
                pp, bpp = C.get("ps")
                P.op("pe", lambda e, pp=pp, sq=sq: e.matmul(pp[:, :], lhsT=ones[:], rhs=sq[:], start=True, stop=True),
                     reads=[bsq, bon], writes=[bpp])
                rn, brn = C.get("cacc")
                P.op("act", lambda e, rn=rn, pp=pp: e.activation(out=rn[:], in_=pp[:, :], func=AF.Sqrt,
                                                                  bias=epst[:, 0:1], scale=1.0),
                     reads=[bpp, be], writes=[brn])
                P.op("dve", lambda e, rn=rn: e.reciprocal(out=rn[:], in_=rn[:]), reads=[brn], writes=[brn])
                sc = (128.0 ** -0.5) if c < 4 else 1.0
                P.op("dve", lambda e, rn=rn, c=c, sc=sc: e.scalar_tensor_tensor(
                    out=_r(qkv[:, c, :]), in0=qkv[:, c, :], scalar=sc, in1=rn[:], op0=ALU.mult, op1=ALU.mult),
                    reads=[bqkv, brn], writes=[bqkv])
            for c in range(4):
                pp, bpp = C.get("ps")

                def mmq_(e, pp=pp, c=c):
                    for kc in range(8):
                        ins = e.matmul(pp[:, :], lhsT=wm[:, kc, c * 128:(c + 1) * 128], rhs=hT[:, kc, :],
                                       start=(kc == 0), stop=(kc == 7))
                    return ins
                P.op("pe", mmq_, reads=[bwm, bhT], writes=[bpp])
                P.op("act", lambda e, pp=pp, c=c: e.copy(out=mq[:, c, :], in_=pp[:, :]), reads=[bpp], writes=[bmq])
            tile_res = {}

            def pro_gen(t):
                tsl = slice(t * 128, (t + 1) * 128)
                r0 = (st * 4 + t) * 128
                pmz, bpmz = C.get("ps2")

                def mmmz(e, pmz=pmz, tsl=tsl):
                    for kc in range(8):
                        ins = e.matmul(pmz[:, :], lhsT=hT[:, kc, tsl], rhs=wm[:, kc, 512:1024], start=(kc == 0),
                                       stop=(kc == 7))
                    return ins
                P.op("pe", mmmz, reads=[bwm, bhT], writes=[bpmz])
                smz, bsmz = C.get("smz")
                P.op("act", lambda e, smz=smz, pmz=pmz: e.activation(out=smz[:], in_=pmz[:, :], func=AF.Silu),
                     reads=[bpmz], writes=[bsmz])
                yield
                omem, bomem = C.get("omem")
                for hd in range(2):
                    accM, baM = C.get("ps2")
                    for mt in range(2):
                        scm, bscm = C.get("ps2")

                        def mmsc(e, scm=scm, hd=hd, mt=mt, tsl=tsl):
                            for dc in range(2):
                                ins = e.matmul(scm[:, 0:128], lhsT=mkT[:, hd * 2 + dc, mt * 128:(mt + 1) * 128],
                                               rhs=mq[:, hd * 2 + dc, tsl], start=(dc == 0), stop=(dc == 1))
                            return ins
                        P.op("pe", mmsc, reads=[bmk, bmq], writes=[bscm])
                        pTm, bpTm = C.get("pTm")
                        P.op("act", lambda e, pTm=pTm, scm=scm: e.activation(out=pTm[:], in_=scm[:, 0:128], func=AF.Exp,
                                                                             scale=1.0 / 16.0), reads=[bscm], writes=[bpTm])
                        P.op("pe", lambda e, accM=accM, pTm=pTm, mt=mt, hd=hd: e.matmul(
                            accM[:, 0:257], lhsT=pTm[:], rhs=mva[:, mt, hd, :], start=(mt == 0), stop=(mt == 1)),
                            reads=[bpTm, bmv], writes=[baM])
                        yield
                    cl, bcl = C.get("col")
                    P.op("dve", lambda e, cl=cl, accM=accM: e.reciprocal(out=cl[:, 0:1], in_=accM[:, 256:257]),
                         reads=[baM], writes=[bcl])
                    P.op("dve", lambda e, omem=omem, accM=accM, cl=cl, smz=smz, hd=hd: e.scalar_tensor_tensor(
                        out=omem[:, hd * 256:(hd + 1) * 256], in0=accM[:, 0:256], scalar=cl[:, 0:1],
                        in1=smz[:, hd * 256:(hd + 1) * 256], op0=ALU.mult, op1=ALU.mult),
                        reads=[baM, bcl, bsmz, bomem], writes=[bomem])
                    yield
                P.dma("sp", om_d[r0:r0 + 128, :], omem[:], reads=[bomem], writes=[bom_d], semname="st_" + bomem.name)
                pab, bpab = C.get("ps2")

                def mmab(e, pab=pab, tsl=tsl):
                    for kc in range(8):
                        ins = e.matmul(pab[:, 0:8], lhsT=hT[:, kc, tsl], rhs=wg[:, kc, 1536:1544],
                                       start=(kc == 0), stop=(kc == 7))
                    return ins
                P.op("pe", mmab, reads=[bwg, bhT], writes=[bpab])
                pz, bpz = C.get("ps2")

                def mmz(e, pz=pz, tsl=tsl):
                    for kc in range(8):
                        ins = e.matmul(pz[:, :], lhsT=hT[:, kc, tsl], rhs=wg[:, kc, 1544:2056],
                                       start=(kc == 0), stop=(kc == 7))
                    return ins
                P.op("pe", mmz, reads=[bwg, bhT], writes=[bpz])
                sz, bsz = C.get("sz")
                P.op("act", lambda e, sz=sz, pz=pz: e.activation(out=sz[:], in_=pz[:, :], func=AF.Silu),
                     reads=[bpz], writes=[bsz])
                yield
                sm, bsm = C.get("sm")

                def small1(e, sm=sm, pab=pab):
                    e.tensor_tensor(out=sm[:, 0:4], in0=pab[:, 0:4], in1=dtb[0][:], op=ALU.add)
                    return e.tensor_copy(out=sm[:, 4:8], in_=pab[:, 4:8])
                P.op("dve", small1, reads=[bpab, dtb[1]], writes=[bsm])
                yield

                def small2(e, sm=sm):
                    e.activation(out=sm[:, 0:4], in_=sm[:, 0:4], func=AF.Exp)
                    return e.activation(out=sm[:, 4:8], in_=sm[:, 4:8], func=AF.Exp, scale=-1.0)
                P.op("act", small2, reads=[bsm], writes=[bsm])
                P.op("act", lambda e, sm=sm: e.activation(out=sm[:, 0:4], in_=sm[:, 0:4], func=AF.Ln,
                                                          bias=ones[:, 0:1], scale=1.0), reads=[bsm, bon],
                     writes=[bsm])
                yield

                def small3(e, sm=sm):
                    e.tensor_tensor(out=sm[:, 0:4], in0=sm[:, 0:4], in1=negA[:], op=ALU.mult)
                    return e.tensor_scalar(out=sm[:, 4:8], in0=sm[:, 4:8], scalar1=1.0, scalar2=None, op0=ALU.add)
                P.op("dve", small3, reads=[bsm, bnA], writes=[bsm])
                P.op("dve", lambda e, sm=sm: e.reciprocal(out=sm[:, 4:8], in_=sm[:, 4:8]), reads=[bsm], writes=[bsm])
                yield
                P.op("act", lambda e, sm=sm: e.activation(out=sm[:, 8:12], in_=sm[:, 4:8], func=AF.Ln),
                     reads=[bsm], writes=[bsm])
                pg, bpg = C.get("ps2")

                def mmg(e, pg=pg, sm=sm):
                    e.matmul(pg[:, 0:4], lhsT=triu[:], rhs=sm[:, 0:4], start=True, stop=True)
                    return e.matmul(pg[:, 4:8], lhsT=ones[:], rhs=sm[:, 0:4], start=True, stop=True)
                P.op("pe", mmg, reads=[bsm, btr, bon], writes=[bpg])
                yield

                def small4(e, sm=sm, pg=pg):
                    e.tensor_copy(out=sm[:, 12:16], in_=pg[:, 0:4])
                    return e.tensor_scalar(out=sm[:, 16:20], in0=pg[:, 0:4], scalar1=-1.0, scalar2=None, op0=ALU.mult)
                P.op("dve", small4, reads=[bpg], writes=[bsm])
                P.op("dve", lambda e, sm=sm, pg=pg: e.tensor_tensor(out=sm[:, 28:32], in0=pg[:, 4:8], in1=sm[:, 12:16],
                                                                    op=ALU.subtract), reads=[bpg, bsm], writes=[bsm])

                def small5(e, sm=sm, pg=pg):
                    e.activation(out=sm[:, 20:24], in_=pg[:, 4:8], func=AF.Exp)
                    e.activation(out=sm[:, 24:28], in_=sm[:, 12:16], func=AF.Exp)
                    return e.activation(out=sm[:, 28:32], in_=sm[:, 28:32], func=AF.Exp)
                P.op("act", small5, reads=[bsm, bpg], writes=[bsm])
                yield
                P.op("dve", lambda e, sm=sm: e.tensor_tensor(out=sm[:, 24:28], in0=sm[:, 24:28], in1=sm[:, 4:8],
                                                             op=ALU.mult), reads=[bsm], writes=[bsm])
                ocat, boc = C.get("ocat")
                tile_res[t] = dict(sm=sm, bsm=bsm, sz=sz, bsz=bsz, ocat=ocat, boc=boc, tsl=tsl, r0=r0)

            for _ in pro_gen(0):
                pass
            for t in range(4):
                tr_ = tile_res[t]
                sm, bsm, sz, bsz = tr_['sm'], tr_['bsm'], tr_['sz'], tr_['bsz']
                ocat, boc, tsl, r0 = tr_['ocat'], tr_['boc'], tr_['tsl'], tr_['r0']

                def head_gen(h, sm=sm, bsm=bsm, sz=sz, bsz=bsz, ocat=ocat, boc=boc, tsl=tsl):
                    qT = qkv[:, h, tsl]
                    kT = qkv[:, 4 + h, tsl]
                    vT = qkv[:, 8 + h, tsl]
                    St, bS = S_t[h]
                    bqkv_h = [bqkvs[h], bqkvs[4 + h], bqkvs[8 + h]]
                    M = hm[h]
                    (gtri, bgt), (gtri2, bgt2), (E3, bE3), (E1, bE1), (E2, bE2) = M[0], M[1], M[2], M[3], M[4]
                    (Bm, bB), (attT, bat), (kb, bkb), (kd, bkd), (vb, bvb), (qd, bqd) = M[5], M[6], M[7], M[8], M[9], M[10]
                    P.op("pool", lambda e: e.tensor_scalar(
                        out=gtri[:], in0=triu[:], scalar1=sm[:, h:h + 1], scalar2=None, op0=ALU.mult),
                        reads=[bsm, btr], writes=[bgt])
                    P.op("dve", lambda e: e.scalar_tensor_tensor(
                        out=gtri2[:], in0=ident[:], scalar=sm[:, 8 + h:9 + h], in1=gtri[:], op0=ALU.mult, op1=ALU.add),
                        reads=[bsm, bi, bgt], writes=[bgt2])
                    yield
                    pX, bpX = C.get("ps")

                    def mmx(e):
                        e.matmul(pX[:, 0:128], lhsT=ones[:], rhs=gtri[:], start=True, stop=True)
                        e.matmul(pX[:, 128:256], lhsT=ones[:], rhs=gtri[:], start=True, stop=False)
                        e.matmul(pX[:, 128:256], lhsT=ident[:], rhs=mincl[:], start=False, stop=True)
                        e.matmul(pX[:, 256:384], lhsT=ones[:], rhs=gtri2[:], start=True, stop=False)
                        return e.matmul(pX[:, 256:384], lhsT=ident[:], rhs=mstr[:], start=False, stop=True)
                    P.op("pe", mmx, reads=[bgt, bgt2, bon, bi, bmi, bms], writes=[bpX])

                    def exps(e):
                        e.activation(out=_r(E3[:]), in_=pX[:, 0:128], func=AF.Exp)
                        e.activation(out=_r(E1[:]), in_=pX[:, 128:256], func=AF.Exp, bias=sm[:, 16 + h:17 + h], scale=1.0)
                        return e.activation(out=_r(E2[:]), in_=pX[:, 256:384], func=AF.Exp, bias=sm[:, 16 + h:17 + h],
                                            scale=1.0)
                    P.op("act", exps, reads=[bpX, bsm], writes=[bE3, bE1, bE2])
                    yield
                    pK, bpK = C.get("ps")

                    def mmk(e):
                        e.matmul(pK[:, 0:128], lhsT=_r(kT), rhs=_r(kT), start=True, stop=True)
                        e.matmul(pK[:, 128:256], lhsT=_r(kT), rhs=_r(qT), start=True, stop=True)
                        e.transpose(pK[:, 256:384], kT, ident[:])
                        return e.transpose(pK[:, 384:512], vT, ident[:])
                    P.op("pe", mmk, reads=bqkv_h + [bi], writes=[bpK])

                    def ev1(e):
                        e.tensor_tensor(out=_r(Bm[:]), in0=pK[:, 0:128], in1=E2[:], op=ALU.mult)
                        e.tensor_tensor(out=_r(attT[:]), in0=pK[:, 128:256], in1=E1[:], op=ALU.mult)
                        e.tensor_scalar(out=_r(kb[:]), in0=pK[:, 256:384], scalar1=sm[:, 24 + h:25 + h], scalar2=None,
                                        op0=ALU.mult)
                        e.tensor_scalar(out=_r(kd[:]), in0=pK[:, 256:384], scalar1=sm[:, 28 + h:29 + h], scalar2=None,
                                        op0=ALU.mult)
                        return e.tensor_scalar(out=_r(vb[:]), in0=pK[:, 384:512], scalar1=sm[:, 4 + h:5 + h], scalar2=None,
                                               op0=ALU.mult)
                    P.op("dve", ev1, reads=[bpK, bE1, bE2, bsm], writes=[bB, bat, bkb, bkd, bvb])
                    P.op("pool", lambda e: e.tensor_tensor(out=_r(qd[:]), in0=qT, in1=E3[:], op=ALU.mult),
                         reads=[bqkvs[h], bE3], writes=[bqd])
                    yield
                    pA, bpA = C.get("ps")
                    P.op("pe", lambda e: e.transpose(pA[:, 0:128], Bm[:], ident[:]), reads=[bB, bi], writes=[bpA])
                    PT, bPT = M[11]
                    P.op("act", lambda e: e.copy(out=_r(PT[:]), in_=pA[:, 0:128]), reads=[bpA], writes=[bPT])
                    PB, bPB = hpb[h][0]
                    P.op("pool", lambda e: e.tensor_tensor(out=_r(PB[:, 128:256]), in0=ident[:], in1=Bm[:], op=ALU.subtract),
                         reads=[bB, bi], writes=[bPB])
                    yield
                    pN, bpN = C.get("ps")

                    def n0(e, pN=pN, PT=PT):
                        e.matmul(pN[:, 0:128], lhsT=_r(PT[:]), rhs=_r(Bm[:]), start=True, stop=True)
                        return e.matmul(pN[:, 256:384], lhsT=_r(Bm[:]), rhs=_r(PT[:]), start=True, stop=True)
                    P.op("pe", n0, reads=[bPT, bB], writes=[bpN])
                    PT2, bPT2 = M[12]
                    P.op("act", lambda e, pN=pN: e.copy(out=_r(PT2[:]), in_=pN[:, 256:384]), reads=[bpN], writes=[bPT2])
                    P.op("dve", lambda e, pN=pN, PB=PB: e.tensor_copy(out=_r(PB[:, 0:128]), in_=pN[:, 0:128]), reads=[bpN],
                         writes=[bPB])
                    PT, bPT = PT2, bPT2
                    cur = 1
                    yield
                    for j in range(1, 7):
                        last = (j == 6)
                        pN, bpN = C.get("ps")

                        def nj(e, pN=pN, PT=PT, PB=PB, last=last):
                            if last:
                                return e.matmul(pN[:, 128:256], lhsT=_r(PT[:]), rhs=_r(PB[:, 128:256]), start=True, stop=True)
                            e.matmul(pN[:, 0:256], lhsT=_r(PT[:]), rhs=_r(PB[:, 0:256]), start=True, stop=True)
                            return e.matmul(pN[:, 256:384], lhsT=_r(PB[:, 0:128]), rhs=_r(PT[:]), start=True, stop=True)
                        P.op("pe", nj, reads=[bPT, bPB], writes=[bpN])
                        PBn, bPBn = hpb[h][j % 2]
                        if not last:
                            PTn, bPTn = M[11 + (1 - cur)]
                            P.op("act", lambda e, PTn=PTn, pN=pN, PBn=PBn: (
                                e.copy(out=_r(PTn[:]), in_=pN[:, 256:384]),
                                e.copy(out=_r(PBn[:, 0:128]), in_=pN[:, 0:128]))[1], reads=[bpN], writes=[bPTn, bPBn])
                        P.op("dve", lambda e, PBn=PBn, PB=PB, pN=pN: e.tensor_tensor(
                            out=_r(PBn[:, 128:256]), in0=PB[:, 128:256], in1=pN[:, 128:256], op=ALU.add),
                            reads=[bpN, bPB], writes=[bPBn])
                        PB, bPB = PBn, bPBn
                        if not last:
                            PT, bPT = PTn, bPTn
                            cur = 1 - cur
                        yield
                    TT = PB[:, 128:256]
                    pW, bpW = C.get("ps")
                    P.op("pe", lambda e: e.matmul(pW[:, 0:128], lhsT=_r(kb[:]), rhs=_r(TT), start=True, stop=True),
                         reads=[bkb, bPB], writes=[bpW])
                    nwT, bnw = M[2]
                    P.op("act", lambda e: e.activation(out=_r(nwT[:]), in_=pW[:, 0:128], func=AF.Copy, scale=-1.0),
                         reads=[bpW], writes=[bnw])
                    yield
                    pV, bpV = C.get("ps")

                    def mv(e):
                        e.matmul(pV[:, 0:128], lhsT=_r(TT), rhs=_r(vb[:]), start=True, stop=False)
                        return e.matmul(pV[:, 0:128], lhsT=_r(nwT[:]), rhs=_r(St[:]), start=False, stop=True)
                    P.op("pe", mv, reads=[bPB, bvb, bnw, bS], writes=[bpV])
                    vn, bvn = M[3]
                    P.op("act", lambda e: e.copy(out=_r(vn[:]), in_=pV[:, 0:128]), reads=[bpV], writes=[bvn])
                    yield
                    pO, bpO = C.get("ps")

                    def mo(e):
                        e.matmul(pO[:, 0:128], lhsT=_r(qd[:]), rhs=_r(St[:]), start=True, stop=False)
                        e.matmul(pO[:, 0:128], lhsT=_r(attT[:]), rhs=_r(vn[:]), start=False, stop=True)
                        return e.matmul(pO[:, 128:256], lhsT=_r(kd[:]), rhs=_r(vn[:]), start=True, stop=True)
                    P.op("pe", mo, reads=[bqd, bS, bat, bvn, bkd], writes=[bpO])
                    P.op("dve", lambda e: e.scalar_tensor_tensor(
                        out=_r(St[:]), in0=St[:], scalar=sm[:, 20 + h:21 + h], in1=pO[:, 128:256], op0=ALU.mult, op1=ALU.add),
                        reads=[bpO, bsm, bS], writes=[bS])
                    jk, bjk = M[4]
                    ss, bss = C.get("col")
                    P.op("act", lambda e: e.activation(out=_r(jk[:]), in_=pO[:, 0:128], func=AF.Square, accum_out=ss[:, 0:1]),
                         reads=[bpO], writes=[bjk, bss])
                    P.op("act", lambda e: e.activation(out=ss[:, 1:2], in_=ss[:, 0:1], func=AF.Sqrt, bias=epst[:, 0:1],
                                                       scale=1.0 / 128.0), reads=[bss, be], writes=[bss])
                    P.op("dve", lambda e: e.reciprocal(out=ss[:, 2:3], in_=ss[:, 1:2]), reads=[bss], writes=[bss])
                    P.op("dve", lambda e: e.scalar_tensor_tensor(
                        out=_r(jk[:]), in0=pO[:, 0:128], scalar=ss[:, 2:3], in1=gnw[0][:], op0=ALU.mult, op1=ALU.mult),
                        reads=[bpO, bss, gnw[1], bjk], writes=[bjk])
                    P.op("dve", lambda e: e.tensor_tensor(
                        out=ocat[:, h * 128:(h + 1) * 128], in0=jk[:], in1=sz[:, h * 128:(h + 1) * 128], op=ALU.mult),
                        reads=[bjk, bsz, boc], writes=[boc])

                gens = [head_gen(h) for h in range(4)]
                if t + 1 < 4:
                    gens.append(pro_gen(t + 1))
                while gens:
                    for g in list(gens):
                        try:
                            next(g)
                        except StopIteration:
                            gens.remove(g)
                P.dma("sp", o_d[r0:r0 + 128, :], ocat[:], reads=[boc], writes=[bo_d], semname="st_" + boc.name)
        P.final_waits("sp", [bo_d, bom_d])
    if standalone:
        P.finish()
    return nc


def build_diff(S, C=None, io=None, tag="a2"):
    standalone = C is None
    if standalone:
        nc = bass.Bass("TRN2", target_bir_lowering=False)
        C = Ctx(nc)
        io = dict(x=C.dram_in("x", [S, D_MODEL]), wd=C.dram_in("wd", [D_MODEL, 2048]),
                  prew=C.dram_in("prew", [D_MODEL]), lamv=C.dram_in("lamv", [256]), dnw=C.dram_in("dnw", [128]),
                  qx=C.dram_in("qx", [8, 512]), alb=C.dram_in("alb", [128, 128]), li=C.dram_in("li", [2]),
                  o_diff=C.dram_out("o_diff", [S, 512]))
    nc = C.nc
    P = C.P
    NT = S // 128
    NST = S // 512
    x_d, wd_d, prew_d, lamv_d, dnw_d = io["x"], io["wd"], io["prew"], io["lamv"], io["dnw"]
    qx_d, alb_d, li_d, o_d = io["qx"], io["alb"], io["li"], io["o_diff"]
    bo_d = Buf("o_d", multi=True)
    with C.phase(tag):
        K = _consts(C)
        ident, bi = K["ident"]; ones, bon = K["ones"]; mincl, bmi = K["mincl"]; epst, be = K["eps"]
        identb, bib = K["identb"]
        minclb, bmb = C.sb([128, 128], BF16, "minclb")
        P.op("pool", lambda e: e.tensor_copy(out=minclb[:], in_=mincl[:]), reads=[bmi], writes=[bmb])
        prew = _bcast_load(C, prew_d, D_MODEL, "prew")
        dnw = _bcast_load(C, dnw_d, 128, "dnw")
        lamv = _bcast_load(C, lamv_d, 256, "lamv")
        li = _bcast_load(C, li_d, 2, "li")
        alb, balb = C.sb([128, 128], F32, "alb_sb")
        P.dma("sp", alb[:], alb_d[:, :], writes=[balb], semname="ld_alb")
        lm, blm = C.sb([128, 8], F32, "lm")
        lp, blp = C.sb([128, 128], F32, "lp")

        def lam1(e):
            e.tensor_tensor(out=lp[:, 0:64], in0=lamv[0][:, 0:64], in1=lamv[0][:, 64:128], op=ALU.mult)
            return e.tensor_tensor(out=lp[:, 64:128], in0=lamv[0][:, 128:192], in1=lamv[0][:, 192:256], op=ALU.mult)
        P.op("dve", lam1, reads=[lamv[1]], writes=[blp])

        def lam2(e):
            e.reduce_sum(out=lm[:, 0:1], in_=lp[:, 0:64], axis=mybir.AxisListType.X)
            return e.reduce_sum(out=lm[:, 1:2], in_=lp[:, 64:128], axis=mybir.AxisListType.X)
        P.op("dve", lam2, reads=[blp], writes=[blm])
        P.op("act", lambda e: e.activation(out=lm[:, 2:4], in_=lm[:, 0:2], func=AF.Exp), reads=[blm], writes=[blm])
        P.op("dve", lambda e: e.tensor_tensor(out=lm[:, 4:5], in0=lm[:, 3:4], in1=lm[:, 2:3], op=ALU.subtract),
             reads=[blm], writes=[blm])
        P.op("dve", lambda e: e.tensor_scalar(out=lm[:, 5:6], in0=lm[:, 4:5], scalar1=li[0][:, 0:1], scalar2=None,
                                              op0=ALU.add), reads=[blm, li[1]], writes=[blm])
        P.op("dve", lambda e: e.tensor_scalar(out=dnw[0][:], in0=dnw[0][:], scalar1=li[0][:, 1:2], scalar2=None,
                                              op0=ALU.mult), reads=[dnw[1], li[1]], writes=[dnw[1]])
        wd, bwd = C.sb([128, 8, 2048], BF16, "wd_sb")
        _load_w(C, wd, bwd, wd_d, 2048, "ld_wd")
        hT, bhT = C.sb([128, 8, 512], BF16, "hT")
        ka = [[C.sb([66, S], BF16, f"ka{h}{m}") for m in range(2)] for h in range(4)]
        qa = [[C.sb([66, 512], BF16, f"qa{h}{m}") for m in range(2)] for h in range(4)]
        vaug, _ = C.sb([128, NT, 4, 129], BF16, "vaug")
        bv = [Buf(f"v{t}") for t in range(NT)]
        bka = [[[Buf(f"ka{h}{m}_{s}") for s in range(NST)] for m in range(2)] for h in range(4)]
        for h in range(4):
            for m in range(2):
                P.op("pool", lambda e, h=h, m=m: e.memset(ka[h][m][0][64:66, :], 1.0), writes=bka[h][m])
                P.dma("pool", qa[h][m][0][64:66, :], qx_d[2 * h:2 * h + 2, :], writes=[qa[h][m][1]], semname=f"ld_qx{h}{m}")
        P.op("pool", lambda e: e.memset(vaug[:, :, :, 128:129], 1.0), writes=bv)
        C.pool("xt", 2, [128, 1024], F32)
        C.pool("junk", 1, [128, 1024], F32)
        C.pool("col", 12, [128, 8], F32)
        C.pool("hb", 2, [128, 1024], BF16)
        C.pool("pst", 1, [128, 1024], BF16, psum=True)
        C.pool("ps", 3, [128, 512], F32, psum=True)
        C.pool("acc", 4, [128, 512], F32, psum=True)
        C.pool("pT", 4, [128, 512], BF16)
        C.pool("szd", 1, [128, 4, 512], F32)
        C.pool("o0", 2, [128, 4, 128], F32)
        C.pool("ob", 8, [128, 128], F32)
        C.pool("ocat", 1, [128, 4, 512], F32)

        for I in range(NST):
            _make_hT(C, K, x_d, I, prew, hT, bhT)
            for h in range(4):
                for which, col0 in (("q", h * 128), ("k", 512 + h * 128)):
                    pp, bpp = C.get("ps")

                    def mm(e, pp=pp, col0=col0):
                        for kc in range(8):
                            ins = e.matmul(pp[:, :], lhsT=wd[:, kc, col0:col0 + 128], rhs=hT[:, kc, :],
                                           start=(kc == 0), stop=(kc == 7))
                        return ins
                    P.op("pe", mm, reads=[bwd, bhT], writes=[bpp])
                    for m in range(2):
                        if which == "q":
                            dst, bd = qa[h][m][0][0:64, :], qa[h][m][1]
                        else:
                            dst, bd = ka[h][m][0][0:64, I * 512:(I + 1) * 512], bka[h][m][I]
                        eng = "act" if m == 0 else "dve"
                        if eng == "act":
                            P.op("act", lambda e, dst=dst, pp=pp, m=m: e.copy(out=dst, in_=pp[64 * m:64 * m + 64, :]),
                                 reads=[bpp], writes=[bd])
                        else:
                            P.op("dve", lambda e, dst=dst, pp=pp, m=m: e.tensor_copy(out=dst, in_=pp[64 * m:64 * m + 64, :]),
                                 reads=[bpp], writes=[bd])
            szd, bszd = C.get("szd")
            for t in range(4):
                tsl = slice(t * 128, (t + 1) * 128)
                tile = I * 4 + t
                pv_, bpv = C.get("ps")

                def mmv(e, pv_=pv_, tsl=tsl):
                    for kc in range(8):
                        ins = e.matmul(pv_[:, :], lhsT=hT[:, kc, tsl], rhs=wd[:, kc, 1024:1536], start=(kc == 0),
                                       stop=(kc == 7))
                    return ins
                P.op("pe", mmv, reads=[bwd, bhT], writes=[bpv])
                P.op("dve", lambda e, pv_=pv_, tile=tile: e.tensor_copy(
                    out=vaug[:, tile, :, 0:128], in_=pv_[:, :].rearrange("p (h e) -> p h e", h=4)),
                    reads=[bpv], writes=[bv[tile]])
                pz, bpz = C.get("ps")

                def mmz(e, pz=pz, tsl=tsl):
                    for kc in range(8):
                        ins = e.matmul(pz[:, :], lhsT=hT[:, kc, tsl], rhs=wd[:, kc, 1536:2048], start=(kc == 0),
                                       stop=(kc == 7))
                    return ins
                P.op("pe", mmz, reads=[bwd, bhT], writes=[bpz])
                P.op("act", lambda e, szd=szd, pz=pz, t=t: e.activation(out=szd[:, t, :], in_=pz[:, :], func=AF.Silu),
                     reads=[bpz], writes=[bszd])
            ocat, boc = C.get("ocat")
            nj = 4 * I + 4
            items = []
            for h in range(4):
                o0, bo0 = C.get("o0")
                for m in range(2):
                    accA, baA = C.get("acc")
                    accB, baB = C.get("acc")
                    for j in range(nj):
                        items.append(dict(h=h, m=m, j=j, o0=o0, bo0=bo0, accA=accA, baA=baA, accB=accB, baB=baB))

            def issue_sc(it):
                h, m, j = it["h"], it["m"], it["j"]
                jj = j - 4 * I
                q0 = 128 * jj if jj > 0 else 0
                sc, bsc = C.get("ps")

                def mms(e):
                    ins = e.matmul(sc[:, q0:512], lhsT=ka[h][m][0][0:66, j * 128:(j + 1) * 128],
                                   rhs=qa[h][m][0][0:66, q0:512], start=True, stop=(jj < 0))
                    if jj >= 0:
                        ins = e.matmul(sc[:, q0:q0 + 128], lhsT=identb[:], rhs=minclb[:], start=False, stop=True,
                                       skip_group_check=True)
                    return ins
                P.op("pe", mms, reads=[bka[h][m][j // 4], qa[h][m][1], bib, bmb], writes=[bsc])
                pT, bpT = C.get("pT")
                bcol = h * 32 + (jj + 28)
                P.op("act", lambda e: e.activation(out=pT[:, q0:512], in_=sc[:, q0:512], func=AF.Exp,
                                                   bias=alb[:, bcol:bcol + 1], scale=0.125),
                     reads=[bsc, balb], writes=[bpT])
                it["pT"], it["bpT"], it["jj"] = pT, bpT, jj

            def acc_of(it, ss):
                return (it["accA"], ss * 129, it["baA"]) if ss < 3 else (it["accB"], 0, it["baB"])

            def issue_pv(it):
                h, j, jj, pT = it["h"], it["j"], it["jj"], it["pT"]

                def mmpv(e):
                    ins = None
                    for ss in range(max(jj, 0), 4):
                        a, off, _ = acc_of(it, ss)
                        first = (j == 0) and (ss == 0 or ss == 3)
                        ins = e.matmul(a[:, off:off + 129], lhsT=pT[:, ss * 128:(ss + 1) * 128], rhs=vaug[:, j, h, :],
                                       start=first, stop=(j == 4 * I + ss), skip_group_check=True)
                    return ins
                P.op("pe", mmpv, reads=[it["bpT"], bv[j]], writes=[it["baA"], it["baB"]])

            def finalize(it):
                h, m, o0, bo0 = it["h"], it["m"], it["o0"], it["bo0"]
                for ss in range(4):
                    a, off, ba = acc_of(it, ss)
                    cl, bcl = C.get("col")
                    P.op("dve", lambda e, cl=cl, a=a, off=off: e.reciprocal(out=cl[:, 0:1], in_=a[:, off + 128:off + 129]),
                         reads=[ba], writes=[bcl])
                    if m == 0:
                        P.op("dve", lambda e, ss=ss, a=a, off=off, cl=cl: e.tensor_scalar(
                            out=o0[:, ss, :], in0=a[:, off:off + 128], scalar1=cl[:, 0:1], scalar2=None, op0=ALU.mult),
                            reads=[ba, bcl], writes=[bo0])
                        continue
                    ob, bob = C.get("ob")
                    P.op("dve", lambda e, ob=ob, a=a, off=off, cl=cl: e.tensor_scalar(
                        out=ob[:], in0=a[:, off:off + 128], scalar1=cl[:, 0:1], scalar2=None, op0=ALU.mult),
                        reads=[ba, bcl], writes=[bob])
                    P.op("dve", lambda e, ob=ob, ss=ss: e.scalar_tensor_tensor(
                        out=ob[:], in0=ob[:], scalar=lm[:, 5:6], in1=o0[:, ss, :], op0=ALU.mult, op1=ALU.add),
                        reads=[bob, bo0, blm], writes=[bob])
                    jk, bjk = C.get("ob")
                    P.op("act", lambda e, jk=jk, ob=ob, cl=cl: e.activation(out=jk[:], in_=ob[:], func=AF.Square,
                                                                             accum_out=cl[:, 1:2]),
                         reads=[bob, bcl], writes=[bjk, bcl])
                    P.op("act", lambda e, cl=cl: e.activation(out=cl[:, 2:3], in_=cl[:, 1:2], func=AF.Sqrt,
                                                              bias=epst[:, 0:1], scale=1.0 / 128.0),
                         reads=[bcl, be], writes=[bcl])
                    P.op("dve", lambda e, cl=cl: e.reciprocal(out=cl[:, 3:4], in_=cl[:, 2:3]), reads=[bcl], writes=[bcl])
                    P.op("dve", lambda e, ob=ob, cl=cl: e.scalar_tensor_tensor(
                        out=ob[:], in0=ob[:], scalar=cl[:, 3:4], in1=dnw[0][:], op0=ALU.mult, op1=ALU.mult),
                        reads=[bob, bcl, dnw[1]], writes=[bob])
                    P.op("dve", lambda e, ob=ob, ss=ss: e.tensor_tensor(
                        out=ocat[:, ss, h * 128:(h + 1) * 128], in0=ob[:], in1=szd[:, ss, h * 128:(h + 1) * 128],
                        op=ALU.mult), reads=[bob, bszd, boc], writes=[boc])

            pending = []
            for it in items:
                issue_sc(it)
                pending.append(it)
                if len(pending) > 2:
                    p_ = pending.pop(0)
                    issue_pv(p_)
                    if p_["j"] == nj - 1:
                        finalize(p_)
            for p_ in pending:
                issue_pv(p_)
                if p_["j"] == nj - 1:
                    finalize(p_)
            for ss in range(4):
                r0 = (I * 4 + ss) * 128
                P.dma("sp", o_d[r0:r0 + 128, :], ocat[:, ss, :], reads=[boc], writes=[bo_d], semname="st_" + boc.name)
        P.final_waits("sp", [bo_d])
    if standalone:
        P.finish()
    return nc


def diff_consts(hh):
    slopes = [2.0 ** (-(4 * hh + h + 1)) for h in range(4)]
    q = np.arange(512)
    qx = np.zeros((8, 512), np.float32)
    alb = np.zeros((128, 128), np.float32)
    k = np.arange(128, dtype=np.float64)
    for h in range(4):
        qx[2 * h] = -8.0 * slopes[h] * (q % 128)
        qx[2 * h + 1] = -8.0 * slopes[h] * 128.0 * (q // 128)
        for d in range(32):
            alb[:, h * 32 + d] = slopes[h] * (k + 128.0 * (d - 28))
    return qx, alb


def build_merge(TB):
    nc = bass.Bass("TRN2", target_bir_lowering=False)
    C = Ctx(nc)
    P = C.P
    NT = TB // 128
    x_d = C.dram_in("x", [TB, D_MODEL])
    oc_d = C.dram_in("oc", [TB, 3072])
    wgt_d = C.dram_in("wgate", [D_MODEL, 3072])
    wbr_d = C.dram_in("wbr", [3072, D_MODEL])
    wout_d = C.dram_in("wout", [D_MODEL, D_MODEL])
    prew_d = C.dram_in("prew", [D_MODEL])
    postw_d = C.dram_in("postw", [D_MODEL])
    y_d = C.dram_out("xo", [TB, D_MODEL])
    by_d = Buf("y_d", multi=True)
    with C.phase("b"):
        K = _consts(C)
        identb, bib = K["identb"]; epst, be = K["eps"]
        prew = _bcast_load(C, prew_d, D_MODEL, "prew")
        postw = _bcast_load(C, postw_d, D_MODEL, "postw")
        wgt, bwgt = C.sb([128, 8, 3072], BF16, "wgt_sb")
        _load_w(C, wgt, bwgt, wgt_d, 3072, "ld_wgt")
        wbr, bwbr = C.sb([128, 24, 1024], BF16, "wbr_sb")
        src = wbr_d.rearrange("(kc p) n -> p kc n", p=128)
        for kc in range(24):
            P.dma("pool", wbr[:, kc, :], src[:, kc, :], writes=[bwbr] if kc == 0 else [], semname="ld_wbr")
        bwbr.w = {("d", "ld_wbr"): P.dma_sems["ld_wbr"]}
        wout, bwout = C.sb([128, 8, 1024], BF16, "wout_sb")
        _load_w(C, wout, bwout, wout_d, 1024, "ld_wout")
        C.pool("xt", 2, [128, 1024], F32)
        C.pool("junk", 1, [128, 1024], F32)
        C.pool("col", 4, [128, 8], F32)
        C.pool("hb", 2, [128, 1024], BF16)
        C.pool("pst", 2, [128, 1024], BF16, psum=True)
        C.pool("ps", 5, [128, 512], F32, psum=True)
        C.pool("hT", 2, [128, 8, 128], BF16)
        C.pool("ob", 2, [128, 3072], BF16)
        C.pool("oT", 2, [128, 24, 128], BF16)
        C.pool("sig", 1, [128, 3, 1024], F32)
        C.pool("y", 2, [128, 1024], F32)
        C.pool("yb", 2, [128, 1024], BF16)
        C.pool("yT", 2, [128, 8, 128], BF16)
        C.pool("tmp", 2, [128, 512], F32)
        C.pool("xo", 2, [128, 1024], F32)
        for t in range(NT):
            r0 = t * 128
            hT, bhT = C.get("hT")
            xt, bx = C.get("xt")
            P.dma("sp", xt[:], x_d[r0:r0 + 128, :], writes=[bx], semname="ld_" + bx.name)
            sq, bsq = C.get("junk")
            ss, bss = C.get("col")
            P.op("act", lambda e, xt=xt, sq=sq, ss=ss: e.activation(out=sq[:], in_=xt[:], func=AF.Square,
                                                                     accum_out=ss[:, 0:1]), reads=[bx], writes=[bsq, bss])
            P.op("act", lambda e, ss=ss: e.activation(out=ss[:, 1:2], in_=ss[:, 0:1], func=AF.Sqrt, bias=epst[:, 0:1],
                                                      scale=1.0 / D_MODEL), reads=[bss, be], writes=[bss])
            P.op("dve", lambda e, ss=ss: e.reciprocal(out=ss[:, 2:3], in_=ss[:, 1:2]), reads=[bss], writes=[bss])
            hb, bhb = C.get("hb")
            P.op("dve", lambda e, hb=hb, xt=xt, ss=ss: e.scalar_tensor_tensor(
                out=hb[:], in0=xt[:], scalar=ss[:, 2:3], in1=prew[0][:], op0=ALU.mult, op1=ALU.mult),
                reads=[bx, bss, prew[1]], writes=[bhb])
            pt, bpt = C.get("pst")

            def tr(e, hb=hb, pt=pt):
                for kc in range(8):
                    ins = e.transpose(pt[:, kc * 128:(kc + 1) * 128], hb[:, kc * 128:(kc + 1) * 128], identb[:])
                return ins
            P.op("pe", tr, reads=[bhb, bib], writes=[bpt])
            P.op("act", lambda e, pt=pt, hT=hT: e.copy(out=hT[:, :, :], in_=pt[:, :].rearrange("p (k n) -> p k n", k=8)),
                 reads=[bpt], writes=[bhT])
            ob, bob = C.get("ob")
            P.dma("pool", ob[:], oc_d[r0:r0 + 128, :], writes=[bob], semname="ld_" + bob.name)
            oT, boT = C.get("oT")
            for g in range(3):
                pt, bpt = C.get("pst")

                def tr2(e, ob=ob, pt=pt, g=g):
                    for kc in range(8):
                        c0 = (g * 8 + kc) * 128
                        ins = e.transpose(pt[:, kc * 128:(kc + 1) * 128], ob[:, c0:c0 + 128], identb[:])
                    return ins
                P.op("pe", tr2, reads=[bob, bib], writes=[bpt])
                P.op("dve" if g == 1 else "act", (lambda e, pt=pt, oT=oT, g=g: e.tensor_copy(
                    out=oT[:, g * 8:(g + 1) * 8, :], in_=pt[:, :].rearrange("p (k n) -> p k n", k=8))) if g == 1 else
                    (lambda e, pt=pt, oT=oT, g=g: e.copy(out=oT[:, g * 8:(g + 1) * 8, :],
                                                        in_=pt[:, :].rearrange("p (k n) -> p k n", k=8))),
                    reads=[bpt], writes=[boT])
            sig, bsig = C.get("sig")
            for br in range(3):
                for half in range(2):
                    pg, bpg = C.get("ps")
                    c0 = br * 1024 + half * 512

                    def mmg(e, pg=pg, c0=c0, hT=hT):
                        for kc in range(8):
                            ins = e.matmul(pg[:, :], lhsT=hT[:, kc, :], rhs=wgt[:, kc, c0:c0 + 512], start=(kc == 0),
                                           stop=(kc == 7))
                        return ins
                    P.op("pe", mmg, reads=[bhT, bwgt], writes=[bpg])
                    P.op("act", lambda e, pg=pg, sig=sig, br=br, half=half: e.activation(
                        out=sig[:, br, half * 512:(half + 1) * 512], in_=pg[:, :], func=AF.Sigmoid),
                        reads=[bpg], writes=[bsig])
            y, by = C.get("y")
            for half in range(2):
                hs = slice(half * 512, (half + 1) * 512)
                for br in range(3):
                    pb, bpb = C.get("ps")

                    def mmb(e, pb=pb, br=br, half=half, oT=oT):
                        for kc in range(8):
                            ins = e.matmul(pb[:, :], lhsT=oT[:, br * 8 + kc, :],
                                           rhs=wbr[:, br * 8 + kc, half * 512:(half + 1) * 512], start=(kc == 0),
                                           stop=(kc == 7))
                        return ins
                    P.op("pe", mmb, reads=[boT, bwbr], writes=[bpb])
                    if br == 0:
                        P.op("dve", lambda e, y=y, pb=pb, sig=sig, hs=hs: e.tensor_tensor(
                            out=y[:, hs], in0=pb[:, :], in1=sig[:, 0, hs], op=ALU.mult), reads=[bpb, bsig, by], writes=[by])
                    else:
                        tmp, btmp = C.get("tmp")
                        P.op("dve", lambda e, tmp=tmp, pb=pb, sig=sig, hs=hs, br=br: e.tensor_tensor(
                            out=tmp[:], in0=pb[:, :], in1=sig[:, br, hs], op=ALU.mult), reads=[bpb, bsig], writes=[btmp])
                        P.op("pool", lambda e, y=y, tmp=tmp, hs=hs: e.tensor_tensor(
                            out=y[:, hs], in0=y[:, hs], in1=tmp[:], op=ALU.add), reads=[btmp, by], writes=[by])
            yb, byb = C.get("yb")
            P.op("act", lambda e, yb=yb, y=y: e.copy(out=yb[:], in_=y[:]), reads=[by], writes=[byb])
            pt, bpt = C.get("pst")

            def tr3(e, yb=yb, pt=pt):
                for kc in range(8):
                    ins = e.transpose(pt[:, kc * 128:(kc + 1) * 128], yb[:, kc * 128:(kc + 1) * 128], identb[:])
                return ins
            P.op("pe", tr3, reads=[byb, bib], writes=[bpt])
            yT, byT = C.get("yT")
            P.op("act", lambda e, pt=pt, yT=yT: e.copy(out=yT[:, :, :], in_=pt[:, :].rearrange("p (k n) -> p k n", k=8)),
                 reads=[bpt], writes=[byT])
            xo, bxo = C.get("xo")
            cl, bcl = C.get("col")
            for half in range(2):
                po, bpo = C.get("ps")

                def mmo(e, po=po, half=half, yT=yT):
                    for kc in range(8):
                        ins = e.matmul(po[:, :], lhsT=yT[:, kc, :], rhs=wout[:, kc, half * 512:(half + 1) * 512],
                                       start=(kc == 0), stop=(kc == 7))
                    return ins
                P.op("pe", mmo, reads=[byT, bwout], writes=[bpo])
                P.op("act", lambda e, po=po, xo=xo, half=half: e.copy(out=xo[:, half * 512:(half + 1) * 512], in_=po[:, :]),
                     reads=[bpo], writes=[bxo])
            sq, bsq = C.get("junk")
            P.op("act", lambda e, sq=sq, xo=xo, cl=cl: e.activation(out=sq[:], in_=xo[:], func=AF.Square,
                                                                    accum_out=cl[:, 0:1]), reads=[bxo], writes=[bsq, bcl])
            P.op("act", lambda e, cl=cl: e.activation(out=cl[:, 1:2], in_=cl[:, 0:1], func=AF.Sqrt, bias=epst[:, 0:1],
                                                      scale=1.0 / D_MODEL), reads=[bcl, be], writes=[bcl])
            P.op("dve", lambda e, cl=cl: e.reciprocal(out=cl[:, 2:3], in_=cl[:, 1:2]), reads=[bcl], writes=[bcl])
            P.op("dve", lambda e, xo=xo, cl=cl: e.scalar_tensor_tensor(
                out=xo[:], in0=xo[:], scalar=cl[:, 2:3], in1=postw[0][:], op0=ALU.mult, op1=ALU.mult),
                reads=[bxo, bcl, postw[1]], writes=[bxo])
            P.op("pool", lambda e, xo=xo, xt=xt: e.tensor_tensor(out=xo[:], in0=xo[:], in1=xt[:], op=ALU.add),
                 reads=[bxo, bx], writes=[bxo])
            P.dma("sp", y_d[r0:r0 + 128, :], xo[:], reads=[bxo], writes=[by_d], semname="st_" + bxo.name)
        P.final_waits("sp", [by_d])
    P.finish()
    return nc


def phase_b1(C, S, io, tag):
    P = C.P
    NT = S // 128
    x_d, oc_d, wgt_d, wbr_d, prew_d, yp_d = io["x"], io["oc"], io["wgate"], io["wbrm"], io["prew"], io["yp"]
    byp = Buf("yp_d", multi=True)
    with C.phase(tag):
        K = _consts(C)
        identb, bib = K["identb"]; epst, be = K["eps"]
        prew = _bcast_load(C, prew_d, D_MODEL, "prew")
        wgt, bwgt = C.sb([128, 8, 3072], BF16, "wgt_sb")
        _load_w(C, wgt, bwgt, wgt_d, 3072, "ld_wgt")
        wbr, bwbr = C.sb([128, 12, 1024], BF16, "wbr_sb")
        src = wbr_d.rearrange("(kc p) n -> p kc n", p=128)
        for kc in range(12):
            P.dma("pool", wbr[:, kc, :], src[:, kc, :], writes=[bwbr] if kc == 0 else [], semname="ld_wbr")
        bwbr.w = {("d", "ld_wbr"): P.dma_sems["ld_wbr"]}
        C.pool("xt", 2, [128, 1024], F32)
        C.pool("junk", 1, [128, 1024], F32)
        C.pool("col", 4, [128, 8], F32)
        C.pool("hb", 2, [128, 1024], BF16)
        C.pool("pst", 2, [128, 1024], BF16, psum=True)
        C.pool("ps", 5, [128, 512], F32, psum=True)
        C.pool("hT1", 3, [128, 8, 128], BF16)
        C.pool("ob", 2, [128, 1536], BF16)
        C.pool("oT", 3, [128, 12, 128], BF16)
        C.pool("sig", 2, [128, 3, 1024], F32)
        C.pool("y", 2, [128, 1024], F32)
        C.pool("ybf", 2, [128, 1024], BF16)
        C.pool("tmp", 4, [128, 512], F32)

        def front(t):
            r0 = t * 128
            hT, bhT = C.get("hT1")
            _make_hT(C, K, x_d, t, prew, hT, bhT, n_tiles=1)
            ob, bob = C.get("ob")
            P.dma("pool", ob[:], oc_d[r0:r0 + 128, :], writes=[bob], semname="ld_" + bob.name)
            oT, boT = C.get("oT")
            for g in range(2):
                pt, bpt = C.get("pst")
                nk = 8 if g == 0 else 4

                def tr2(e, ob=ob, pt=pt, g=g, nk=nk):
                    for kc in range(nk):
                        c0 = (g * 8 + kc) * 128
                        ins = e.transpose(pt[:, kc * 128:(kc + 1) * 128], ob[:, c0:c0 + 128], identb[:])
                    return ins
                P.op("pe", tr2, reads=[bob, bib], writes=[bpt])
                if g == 0:
                    P.op("act", lambda e, pt=pt, oT=oT: e.copy(
                        out=oT[:, 0:8, :], in_=pt[:, :].rearrange("p (k n) -> p k n", k=8)), reads=[bpt], writes=[boT])
                else:
                    P.op("dve", lambda e, pt=pt, oT=oT: e.tensor_copy(
                        out=oT[:, 8:12, :], in_=pt[:, 0:512].rearrange("p (k n) -> p k n", k=4)), reads=[bpt],
                        writes=[boT])
            return dict(hT=hT, bhT=bhT, oT=oT, boT=boT, r0=r0)

        def back(d):
            hT, bhT, oT, boT, r0 = d["hT"], d["bhT"], d["oT"], d["boT"], d["r0"]
            sig, bsig = C.get("sig")
            for br in range(3):
                for half in range(2):
                    pg, bpg = C.get("ps")
                    c0 = br * 1024 + half * 512

                    def mmg(e, pg=pg, c0=c0, hT=hT):
                        for kc in range(8):
                            ins = e.matmul(pg[:, :], lhsT=hT[:, kc, :], rhs=wgt[:, kc, c0:c0 + 512], start=(kc == 0),
                                           stop=(kc == 7))
                        return ins
                    P.op("pe", mmg, reads=[bhT, bwgt], writes=[bpg])
                    P.op("act", lambda e, pg=pg, sig=sig, br=br, half=half: e.activation(
                        out=sig[:, br, half * 512:(half + 1) * 512], in_=pg[:, :], func=AF.Sigmoid),
                        reads=[bpg], writes=[bsig])
            y, by = C.get("y")
            for half in range(2):
                hs = slice(half * 512, (half + 1) * 512)
                for br in range(3):
                    pb, bpb = C.get("ps")

                    def mmb(e, pb=pb, br=br, half=half, oT=oT):
                        for kc in range(4):
                            ins = e.matmul(pb[:, :], lhsT=oT[:, br * 4 + kc, :],
                                           rhs=wbr[:, br * 4 + kc, half * 512:(half + 1) * 512], start=(kc == 0),
                                           stop=(kc == 3))
                        return ins
                    P.op("pe", mmb, reads=[boT, bwbr], writes=[bpb])
                    if br == 0:
                        P.op("dve", lambda e, y=y, pb=pb, sig=sig, hs=hs: e.tensor_tensor(
                            out=y[:, hs], in0=pb[:, :], in1=sig[:, 0, hs], op=ALU.mult), reads=[bpb, bsig, by], writes=[by])
                    else:
                        tmp, btmp = C.get("tmp")
                        P.op("dve", lambda e, tmp=tmp, pb=pb, sig=sig, hs=hs, br=br: e.tensor_tensor(
                            out=tmp[:], in0=pb[:, :], in1=sig[:, br, hs], op=ALU.mult), reads=[bpb, bsig], writes=[btmp])
                        P.op("pool", lambda e, y=y, tmp=tmp, hs=hs: e.tensor_tensor(
                            out=y[:, hs], in0=y[:, hs], in1=tmp[:], op=ALU.add), reads=[btmp, by], writes=[by])
            yb, byb = C.get("ybf")
            P.op("act", lambda e: e.copy(out=yb[:], in_=y[:]), reads=[by], writes=[byb])
            P.dma("sp", yp_d[r0:r0 + 128, :], yb[:], reads=[byb], writes=[byp], semname="st_" + byb.name)

        nxt = front(0)
        for t in range(NT):
            cur_ = nxt
            if t + 1 < NT:
                nxt = front(t + 1)
            back(cur_)
        P.final_waits("sp", [byp])


def phase_b2(C, TB, io, tag):
    P = C.P
    NT = TB // 128
    ys_d, xr_d, wout_d, postw_d, xo_d = io["ysum"], io["xres"], io["wout"], io["postw"], io["xout"]
    bxo_d = Buf("xo_d", multi=True)
    with C.phase(tag):
        K = _consts(C)
        identb, bib = K["identb"]; epst, be = K["eps"]
        postw = _bcast_load(C, postw_d, D_MODEL, "postw")
        wout, bwout = C.sb([128, 8, 1024], BF16, "wout_sb")
        _load_w(C, wout, bwout, wout_d, 1024, "ld_wout")
        C.pool("xt", 2, [128, 1024], F32)
        C.pool("junk", 1, [128, 1024], F32)
        C.pool("col", 4, [128, 8], F32)
        C.pool("pst", 2, [128, 1024], BF16, psum=True)
        C.pool("ps", 4, [128, 512], F32, psum=True)
        C.pool("yb", 2, [128, 1024], BF16)
        C.pool("yT", 2, [128, 8, 128], BF16)
        C.pool("xo", 2, [128, 1024], F32)
        for t in range(NT):
            r0 = t * 128
            xt, bx = C.get("xt")
            P.dma("sp", xt[:], xr_d[r0:r0 + 128, :], writes=[bx], semname="ld_" + bx.name)
            yb, byb = C.get("yb")
            P.dma("sp", yb[:], ys_d[r0:r0 + 128, :], writes=[byb], semname="ld_" + byb.name)
            pt, bpt = C.get("pst")

            def tr3(e, yb=yb, pt=pt):
                for kc in range(8):
                    ins = e.transpose(pt[:, kc * 128:(kc + 1) * 128], yb[:, kc * 128:(kc + 1) * 128], identb[:])
                return ins
            P.op("pe", tr3, reads=[byb, bib], writes=[bpt])
            yT, byT = C.get("yT")
            P.op("act", lambda e, pt=pt, yT=yT: e.copy(out=yT[:, :, :], in_=pt[:, :].rearrange("p (k n) -> p k n", k=8)),
                 reads=[bpt], writes=[byT])
            xo, bxo = C.get("xo")
            cl, bcl = C.get("col")
            for half in range(2):
                po, bpo = C.get("ps")

                def mmo(e, po=po, half=half, yT=yT):
                    for kc in range(8):
                        ins = e.matmul(po[:, :], lhsT=yT[:, kc, :], rhs=wout[:, kc, half * 512:(half + 1) * 512],
                                       start=(kc == 0), stop=(kc == 7))
                    return ins
                P.op("pe", mmo, reads=[byT, bwout], writes=[bpo])
                P.op("act", lambda e, po=po, xo=xo, half=half: e.copy(out=xo[:, half * 512:(half + 1) * 512], in_=po[:, :]),
                     reads=[bpo], writes=[bxo])
            sq, bsq = C.get("junk")
            P.op("act", lambda e, sq=sq, xo=xo, cl=cl: e.activation(out=sq[:], in_=xo[:], func=AF.Square,
                                                                    accum_out=cl[:, 0:1]), reads=[bxo], writes=[bsq, bcl])
            P.op("act", lambda e, cl=cl: e.activation(out=cl[:, 1:2], in_=cl[:, 0:1], func=AF.Sqrt, bias=epst[:, 0:1],
                                                      scale=1.0 / D_MODEL), reads=[bcl, be], writes=[bcl])
            P.op("dve", lambda e, cl=cl: e.reciprocal(out=cl[:, 2:3], in_=cl[:, 1:2]), reads=[bcl], writes=[bcl])
            P.op("dve", lambda e, xo=xo, cl=cl: e.scalar_tensor_tensor(
                out=xo[:], in0=xo[:], scalar=cl[:, 2:3], in1=postw[0][:], op0=ALU.mult, op1=ALU.mult),
                reads=[bxo, bcl, postw[1]], writes=[bxo])
            P.op("pool", lambda e, xo=xo, xt=xt: e.tensor_tensor(out=xo[:], in0=xo[:], in1=xt[:], op=ALU.add),
                 reads=[bxo, bx], writes=[bxo])
            P.dma("sp", xo_d[r0:r0 + 128, :], xo[:], reads=[bxo], writes=[bxo_d], semname="st_" + bxo.name)
        P.final_waits("sp", [bxo_d])


PAIRS = [[0, 1], [2, 3], [4, 5], [6, 7]]


def build_fused(S, L):
    nc = bass.Bass("TRN2", target_bir_lowering=False, num_devices=8)
    C = Ctx(nc)
    P = C.P
    TB = S // 2
    di = C.dram_in
    x_d = di("x", [S, D_MODEL]); xh_d = di("xhalf", [TB, D_MODEL]); mem_d = di("mem", [MEM_LEN, D_MODEL])
    wg_d = di("wg", [L, D_MODEL, 2056]); cw_d = di("convw", [L, 1536, 4]); prew_d = di("prew", [L, D_MODEL])
    alog_d = di("alog", [L, 4]); dtb_d = di("dtb", [L, 4]); gnw_d = di("gnw", [L, 128]); mnw_d = di("mnw", [L, D_MODEL])
    wkv_d = di("wkv", [L, D_MODEL, 1024]); wm_d = di("wm", [L, D_MODEL, 1024]); wd_d = di("wd", [L, D_MODEL, 2048])
    lamv_d = di("lamv", [L, 256]); dnw_d = di("dnw", [L, 128]); qx_d = di("qx", [8, 512]); alb_d = di("alb", [128, 128])
    li_d = di("li", [L, 2]); wgt_d = di("wgate", [L, D_MODEL, 3072]); wbr_d = di("wbrm", [L, 1536, D_MODEL])
    wout_d = di("wout", [L, D_MODEL, D_MODEL]); postw_d = di("postw", [L, D_MODEL])
    xo_d = C.dram_out("xo", [TB, D_MODEL])
    it = lambda name, shape: nc.dram_tensor(name, list(shape), F32, addr_space="Local", kind="Internal").ap()
    oc_i = it("oc_i", [S, 1536])
    itb = lambda name, shape: nc.dram_tensor(name, list(shape), BF16, addr_space="Local", kind="Internal").ap()
    yp_i = itb("yp_i", [S, D_MODEL])
    ys_i = itb("ys_i", [TB, D_MODEL])
    xh_i = it("xh_i", [TB, D_MODEL])
    xf_i = it("xf_i", [S, D_MODEL])
    C.gst = contextlib.ExitStack()
    C.st = C.gst
    C.pfx = "g_"
    C.K = _consts(C)
    for l in range(L):
        xs = x_d if l == 0 else xf_i
        build_gdn(S, 99, C, dict(x=xs, wg=wg_d[l], convw=cw_d[l], prew=prew_d[l], alog=alog_d[l], dtb=dtb_d[l],
                                 gnw=gnw_d[l], o_gdn=oc_i[:, 0:512], mem=mem_d, mnw=mnw_d[l], wkv=wkv_d[l], wm=wm_d[l],
                                 o_mem=oc_i[:, 1024:1536]), tag=f"L{l}a1")
        build_diff(S, C, dict(x=xs, wd=wd_d[l], prew=prew_d[l], lamv=lamv_d[l], dnw=dnw_d[l], qx=qx_d, alb=alb_d,
                              li=li_d[l], o_diff=oc_i[:, 512:1024]), tag=f"L{l}a2")
        phase_b1(C, S, dict(x=xs, oc=oc_i, wgate=wgt_d[l], wbrm=wbr_d[l], prew=prew_d[l], yp=yp_i), tag=f"L{l}b1")
        with C.phase(f"L{l}rs"):
            P.coll(lambda e: e.collective_compute("ReduceScatter", ALU.add, replica_groups=PAIRS, ins=[yp_i],
                                                  outs=[ys_i]), semname="cc_rs")
        last = (l == L - 1)
        phase_b2(C, TB, dict(ysum=ys_i, xres=(xh_d if l == 0 else xh_i), wout=wout_d[l], postw=postw_d[l],
                             xout=(xo_d if last else xh_i)), tag=f"L{l}b2")
        if not last:
            with C.phase(f"L{l}ag"):
                P.coll(lambda e: e.collective_compute("AllGather", ALU.bypass, replica_groups=PAIRS, ins=[xh_i],
                                                      outs=[xf_i]), semname="cc_ag")
    P.finish()
    C.gst.close()
    return nc


_PROGS = {}


def _c(a):
    return np.ascontiguousarray(a, dtype=np.float32)


def _core_inputs(r, L, w_in, gdn_conv_w, gdn_a_log, gdn_dt_bias, w_mem_kv, w_br_gdn, w_br_diff, w_br_mem):
    sl = lambda base: slice(base + r * 512, base + r * 512 + 512)
    wg, wd, wm, cw, wkv, wbrm, wgate = [], [], [], [], [], [], []
    for l in range(L):
        wl = np.asarray(w_in[l], np.float32)
        wg.append(np.concatenate([wl[:, sl(0)], wl[:, sl(1024)], wl[:, sl(2048)], wl[:, 3072 + r * 4:3076 + r * 4],
                                  wl[:, 3080 + r * 4:3084 + r * 4], wl[:, sl(3088)]], axis=1))
        wd.append(np.concatenate([wl[:, sl(4112)], wl[:, sl(5136)], wl[:, sl(6160)], wl[:, sl(7184)]], axis=1))
        wm.append(np.concatenate([wl[:, sl(8208)], wl[:, sl(9232)]], axis=1))
        wgate.append(wl[:, 10256:13328])
        cwl = np.asarray(gdn_conv_w[l], np.float32)
        cw.append(np.concatenate([cwl[:, sl(0)], cwl[:, sl(1024)], cwl[:, sl(2048)]], axis=1).T)
        kvl = np.asarray(w_mem_kv[l], np.float32)
        wkv.append(np.concatenate([kvl[:, sl(0)], kvl[:, sl(1024)]], axis=1))
        wbrm.append(np.concatenate([np.asarray(w_br_gdn[l])[sl(0)], np.asarray(w_br_diff[l])[sl(0)],
                                    np.asarray(w_br_mem[l])[sl(0)]], axis=0))
    st = lambda xs: _c(np.stack(xs))
    return dict(wg=st(wg), wd=st(wd), wm=st(wm), convw=st(cw), wkv=st(wkv), wbrm=st(wbrm), wgate=st(wgate),
                alog=_c(np.asarray(gdn_a_log)[:L, r * 4:r * 4 + 4]), dtb=_c(np.asarray(gdn_dt_bias)[:L, r * 4:r * 4 + 4]))


LAYERS_PER_LAUNCH = 1


def kernel(x, mem, pre_norm_w, post_norm_w, w_in, gdn_conv_w, gdn_a_log, gdn_dt_bias, gdn_norm_w, diff_lambda,
           diff_norm_w, mem_norm_w, w_mem_kv, w_br_gdn, w_br_diff, w_br_mem, w_out):
    x = np.asarray(x, np.float32)
    B, S, D = x.shape
    L = np.asarray(w_in).shape[0]
    TB = S // 2
    G = LAYERS_PER_LAUNCH
    key = (S, G)
    if key not in _PROGS:
        _PROGS[key] = build_fused(S, G)
    nc = _PROGS[key]
    li_all = np.array([[-(0.8 - 0.6 * math.exp(-0.3 * l)), 1.0 - (0.8 - 0.6 * math.exp(-0.3 * l))] for l in range(L)],
                      np.float32)
    consts = [diff_consts(r) for r in range(2)]
    for l0 in range(0, L, G):
        ls = slice(l0, l0 + G)
        shared = dict(prew=_c(np.asarray(pre_norm_w)[ls]), postw=_c(np.asarray(post_norm_w)[ls]),
                      gnw=_c(np.asarray(gdn_norm_w)[ls]), mnw=_c(np.asarray(mem_norm_w)[ls]),
                      lamv=_c(np.asarray(diff_lambda)[ls].reshape(G, 256)), dnw=_c(np.asarray(diff_norm_w)[ls]),
                      wout=_c(np.asarray(w_out)[ls]), li=_c(li_all[ls]))
        per_r = []
        for r in range(2):
            d = _core_inputs(r, G, np.asarray(w_in)[ls], np.asarray(gdn_conv_w)[ls], np.asarray(gdn_a_log)[ls],
                             np.asarray(gdn_dt_bias)[ls], np.asarray(w_mem_kv)[ls], np.asarray(w_br_gdn)[ls],
                             np.asarray(w_br_diff)[ls], np.asarray(w_br_mem)[ls])
            d.update(qx=consts[r][0], alb=consts[r][1])
            d.update(shared)
            per_r.append(d)
        in_maps = []
        for c in range(8):
            b, r = c // 2, c % 2
            m = dict(per_r[r])
            m.update(x=_c(x[b]), xhalf=_c(x[b, r * TB:(r + 1) * TB]), mem=_c(np.asarray(mem)[b]))
            in_maps.append(m)
        res = run_bass_kernel_spmd(nc, in_maps, core_ids=list(range(8))).results
        xn = np.empty((B, S, D), np.float32)
        for c in range(8):
            b, r = c // 2, c % 2
            xn[b, r * TB:(r + 1) * TB] = res[c]["xo"]
        x = xn
    return x
```

```python
import contextlib
import math
import numpy as np
import concourse.bass as bass
import concourse.mybir as mybir
from concourse.bass_utils import run_bass_kernel_spmd

F32 = mybir.dt.float32
F32R = mybir.dt.float32r
BF16 = mybir.dt.bfloat16
ALU = mybir.AluOpType
AF = mybir.ActivationFunctionType

D_MODEL = 1024
BATCH = 4
SEQ = 4096
DEPTH = 4
MEM_LEN = 256
EPS = 1e-6
IN_COLS = 13328
NEG = -1.0e30

ENGS = ("pe", "act", "dve", "pool", "sp")


class Buf:
    __slots__ = ("name", "w", "r", "excl", "multi")

    def __init__(self, name="", excl=False, multi=False):
        self.name = name
        self.w = {}
        self.r = {}
        self.multi = multi
        self.excl = excl


class Prog:
    def __init__(self, nc, same_engine_sync=True):
        self.nc = nc
        self.q = {e: [] for e in ENGS}
        self.cnt = {e: 0 for e in ENGS}
        self.seen = {e: {} for e in ENGS}
        self.dma_sems = {}
        self.same = same_engine_sync
        self.sem_handles = {}
        self.gst = None

    def _need(self, eng, reads, writes):
        need = {}

        def add(d):
            for k, v in d.items():
                if need.get(k, 0) < v:
                    need[k] = v
        for b in reads:
            add(b.w)
            if b.excl:
                add({k: v for k, v in b.r.items() if k != ("e", eng)})
        for b in writes:
            if b.multi:
                continue
            add(b.w)
            add(b.r)
        out = []
        seen = self.seen[eng]
        for k, v in need.items():
            if not self.same and k == ("e", eng):
                continue
            if seen.get(k, 0) >= v:
                continue
            seen[k] = v
            out.append((k, v))
        return out

    def op(self, eng, fn, reads=(), writes=()):
        waits = self._need(eng, reads, writes)
        self.cnt[eng] += 1
        c = self.cnt[eng]
        key = ("e", eng)
        self.q[eng].append((waits, fn, key, 1))
        for b in writes:
            b.w = {key: c}
            b.r = {}
        for b in reads:
            if b.r.get(key, 0) < c:
                b.r[key] = c

    def dma(self, eng, out_ap, in_ap, reads=(), writes=(), semname=None, **kw):
        waits = self._need(eng, reads, writes)
        key = ("d", semname)
        self.dma_sems[semname] = self.dma_sems.get(semname, 0) + 16
        c = self.dma_sems[semname]

        def fn(e, out_ap=out_ap, in_ap=in_ap, kw=kw):
            return e.dma_start(out=out_ap, in_=in_ap, **kw)
        self.q[eng].append((waits, fn, key, 16))
        for b in writes:
            if b.multi:
                b.w[key] = c
                continue
            b.w = {key: c}
            b.r = {}
        for b in reads:
            if b.r.get(key, 0) < c:
                b.r[key] = c

    def coll(self, fn, reads=(), writes=(), semname=None):
        waits = self._need("pool", reads, writes)
        key = ("d", semname)
        self.dma_sems[semname] = self.dma_sems.get(semname, 0) + 1
        c = self.dma_sems[semname]
        self.q["pool"].append((waits, fn, key, 1))
        for b in writes:
            b.w = {key: c}
            b.r = {}
        for b in reads:
            if b.r.get(key, 0) < c:
                b.r[key] = c

    def final_waits(self, eng, bufs):
        waits = self._need(eng, bufs, ())
        self.q[eng].append((waits, None, None, 0))

    def barrier(self):
        allk = [(("e", e), c) for e, c in self.cnt.items() if c > 0]
        allk += [(("d", n), c) for n, c in self.dma_sems.items()]
        for eng in ENGS:
            seen = self.seen[eng]
            waits = []
            for k, v in allk:
                if seen.get(k, 0) >= v:
                    continue
                seen[k] = v
                waits.append((k, v))
            self.q[eng].append((waits, None, None, 0))

    def _sem(self, key):
        if key not in self.sem_handles:
            if self.gst is None:
                self.gst = contextlib.ExitStack()
            nm = ("se_" if key[0] == "e" else "sd_") + key[1]
            self.sem_handles[key] = self.gst.enter_context(self.nc.semaphore(nm))
        return self.sem_handles[key]

    def flush(self):
        nc = self.nc
        for e in ENGS:
            self._sem(("e", e))
        for lst in self.q.values():
            for waits, fn, key, inc in lst:
                for k, v in waits:
                    self._sem(k)
                if key is not None:
                    self._sem(key)
        H = self.sem_handles
        q = self.q
        self.q = {e: [] for e in ENGS}
        with nc.Block() as block:
            def run(engobj, lst):
                for waits, fn, key, inc in lst:
                    for k, v in waits:
                        engobj.wait_ge(H[k], v)
                    if fn is not None:
                        fn(engobj).then_inc(H[key], inc)

            @block.tensor
            def _(e):
                run(e, q["pe"])

            @block.scalar
            def _(e):
                run(e, q["act"])

            @block.vector
            def _(e):
                run(e, q["dve"])

            @block.gpsimd
            def _(e):
                run(e, q["pool"])

            @block.sync
            def _(e):
                run(e, q["sp"])

    def emit(self):
        self.flush()

    def finish(self):
        if self.gst is not None:
            self.gst.close()
            self.gst = None


class Ctx:
    def __init__(self, nc):
        self.nc = nc
        self.P = Prog(nc)
        self.st = contextlib.ExitStack()
        self.n = 0
        self.rot = {}
        self.pfx = ""
        self.K = None
        self.gst = None

    @contextlib.contextmanager
    def phase(self, name):
        self.pfx = name + "_"
        self.rot = {}
        self.st = contextlib.ExitStack()
        with self.st:
            yield
            self.P.barrier()
            self.P.flush()

    def sb(self, shape, dt=F32, name=None):
        self.n += 1
        nm = name or f"sb{self.n}"
        t = self.st.enter_context(self.nc.sbuf_tensor(self.pfx + nm, list(shape), dt))
        return t, Buf(nm)

    def ps(self, shape, dt=F32, name=None):
        self.n += 1
        nm = name or f"ps{self.n}"
        t = self.st.enter_context(self.nc.psum_tensor(self.pfx + nm, list(shape), dt))
        return t, Buf(nm, excl=True)

    def pool(self, tag, n, shape, dt=F32, psum=False):
        self.rot[tag] = [[(self.ps if psum else self.sb)(shape, dt, f"{tag}{i}") for i in range(n)], 0]

    def get(self, tag):
        r = self.rot[tag]
        t = r[0][r[1] % len(r[0])]
        r[1] += 1
        return t

    def dram_in(self, name, shape, dt=F32):
        return self.nc.dram_tensor(name, list(shape), dt, kind="ExternalInput").ap()

    def dram_out(self, name, shape, dt=F32):
        return self.nc.dram_tensor(name, list(shape), dt, kind="ExternalOutput").ap()


def _r(ap):
    return ap


def _consts(C):
    if C.K is not None:
        return C.K
    P = C.P
    K = {}
    ident, bi = C.sb([128, 128], F32, "ident")
    ones, bo = C.sb([128, 128], F32, "ones")
    triu, bt = C.sb([128, 128], F32, "triu")
    mincl, bm1 = C.sb([128, 128], F32, "mincl")
    mstr, bm2 = C.sb([128, 128], F32, "mstr")
    identb, bib = C.sb([128, 128], BF16, "identb")
    epst, be = C.sb([128, 1], F32, "epst")

    def mk0(e):
        e.memset(ident[:], 0.0)
        e.memset(ones[:], 1.0)
        e.memset(triu[:], 1.0)
        e.memset(mincl[:], 0.0)
        e.memset(mstr[:], 0.0)
        return e.memset(epst[:], EPS)
    P.op("pool", mk0, writes=[bi, bo, bt, bm1, bm2, be])

    def mk(e):
        e.affine_select(out=ident[:], in_=ident[:], pattern=[[-1, 128]], compare_op=ALU.not_equal, fill=1.0,
                        base=0, channel_multiplier=1)
        e.affine_select(out=triu[:], in_=triu[:], pattern=[[1, 128]], compare_op=ALU.is_ge, fill=0.0,
                        base=0, channel_multiplier=-1)
        e.affine_select(out=mincl[:], in_=mincl[:], pattern=[[1, 128]], compare_op=ALU.is_ge, fill=NEG,
                        base=0, channel_multiplier=-1)
        return e.affine_select(out=mstr[:], in_=mstr[:], pattern=[[1, 128]], compare_op=ALU.is_gt, fill=NEG,
                               base=0, channel_multiplier=-1)
    P.op("pool", mk, reads=[bi, bt, bm1, bm2], writes=[bi, bt, bm1, bm2])
    P.op("pool", lambda e: e.tensor_copy(out=identb[:], in_=ident[:]), reads=[bi], writes=[bib])
    K.update(ident=(ident, bi), ones=(ones, bo), triu=(triu, bt), mincl=(mincl, bm1), mstr=(mstr, bm2),
             identb=(identb, bib), eps=(epst, be))
    return K


def _bcast_load(C, dram_vec, n, name):
    t, b = C.sb([128, n], F32, name + "_bc")
    src = dram_vec.partition_broadcast(128)
    C.P.dma("sp", t[:], src, writes=[b], semname="ld_" + name)
    return t, b


def _load_w(C, wt, wb, wdram, ncols, semname):
    src = wdram.rearrange("(kc p) n -> p kc n", p=128)
    for kc in range(8):
        C.P.dma("pool", wt[:, kc, :], src[:, kc, :], writes=[wb] if kc == 0 else [], reads=[], semname=semname)
    wb.w = {("d", semname): C.P.dma_sems[semname]}


def _make_hT(C, K, x_dram, st, prew, hT, hTb, n_tiles=4):
    P = C.P
    identb, bib = K["identb"]
    epst, be = K["eps"]
    for t in range(n_tiles):
        xt, bx = C.get("xt")
        r0 = (st * n_tiles + t) * 128
        P.dma("sp", xt[:], x_dram[r0:r0 + 128, :], writes=[bx], semname="ld_" + bx.name)
        sq, bsq = C.get("junk")
        ss, bss = C.get("col")
        P.op("act", lambda e, xt=xt, sq=sq, ss=ss: e.activation(out=sq[:, 0:1024], in_=xt[:], func=AF.Square,
                                                                 accum_out=ss[:, 0:1]),
             reads=[bx], writes=[bsq, bss])
        P.op("act", lambda e, ss=ss: e.activation(out=ss[:, 1:2], in_=ss[:, 0:1], func=AF.Sqrt, bias=epst[:, 0:1],
                                                  scale=1.0 / D_MODEL), reads=[bss, be], writes=[bss])
        P.op("dve", lambda e, ss=ss: e.reciprocal(out=ss[:, 2:3], in_=ss[:, 1:2]), reads=[bss], writes=[bss])
        hb, bhb = C.get("hb")
        P.op("dve", lambda e, hb=hb, xt=xt, ss=ss: e.scalar_tensor_tensor(
            out=hb[:], in0=xt[:], scalar=ss[:, 2:3], in1=prew[0][:], op0=ALU.mult, op1=ALU.mult),
            reads=[bx, bss, prew[1]], writes=[bhb])
        pt, bpt = C.get("pst")

        def tr(e, hb=hb, pt=pt):
            for kc in range(8):
                ins = e.transpose(pt[:, kc * 128:(kc + 1) * 128], hb[:, kc * 128:(kc + 1) * 128], identb[:])
            return ins
        P.op("pe", tr, reads=[bhb, bib], writes=[bpt])
        P.op("act", lambda e, pt=pt, t=t: e.copy(out=hT[:, :, t * 128:(t + 1) * 128],
                                                 in_=pt[:, :].rearrange("p (k n) -> p k n", k=8)),
             reads=[bpt], writes=[hTb])


def build_gdn(S, stage=99, C=None, io=None, tag="a1"):
    standalone = C is None
    if standalone:
        nc = bass.Bass("TRN2", target_bir_lowering=False)
        C = Ctx(nc)
        io = dict(x=C.dram_in("x", [S, D_MODEL]), wg=C.dram_in("wg", [D_MODEL, 2056]),
                  convw=C.dram_in("convw", [1536, 4]), prew=C.dram_in("prew", [D_MODEL]),
                  alog=C.dram_in("alog", [4]), dtb=C.dram_in("dtb", [4]), gnw=C.dram_in("gnw", [128]),
                  o_gdn=C.dram_out("o_gdn", [S, 512]), mem=C.dram_in("mem", [MEM_LEN, D_MODEL]),
                  mnw=C.dram_in("mnw", [D_MODEL]), wkv=C.dram_in("wkv", [D_MODEL, 1024]),
                  wm=C.dram_in("wm", [D_MODEL, 1024]), o_mem=C.dram_out("o_mem", [S, 512]))
    nc = C.nc
    P = C.P
    NT = S // 128
    NST = S // 512
    x_d, wg_d, convw_d, prew_d = io["x"], io["wg"], io["convw"], io["prew"]
    alog_d, dtb_d, gnw_d, o_d = io["alog"], io["dtb"], io["gnw"], io["o_gdn"]
    mem_d, mnw_d, wkv_d, wm_d, om_d = io["mem"], io["mnw"], io["wkv"], io["wm"], io["o_mem"]
    bo_d = Buf("o_d", multi=True)
    bom_d = Buf("om_d", multi=True)
    with C.phase(tag):
        K = _consts(C)
        ident, bi = K["ident"]; ones, bon = K["ones"]; triu, btr = K["triu"]
        mincl, bmi = K["mincl"]; mstr, bms = K["mstr"]; epst, be = K["eps"]
        prew = _bcast_load(C, prew_d, D_MODEL, "prew")
        gnw = _bcast_load(C, gnw_d, 128, "gnw")
        alog = _bcast_load(C, alog_d, 4, "alog")
        dtb = _bcast_load(C, dtb_d, 4, "dtb")
        cw, bcw = C.sb([128, 12, 4], F32, "cw")
        P.dma("sp", cw[:], convw_d.rearrange("(c p) j -> p c j", p=128), writes=[bcw], semname="ld_cw")
        negA, bnA = C.sb([128, 4], F32, "negA")
        P.op("act", lambda e: e.activation(out=negA[:], in_=alog[0][:], func=AF.Exp), reads=[alog[1]], writes=[bnA])
        P.op("dve", lambda e: e.tensor_scalar(out=negA[:], in0=negA[:], scalar1=-1.0, scalar2=None, op0=ALU.mult),
             reads=[bnA], writes=[bnA])
        wg, bwg = C.sb([128, 8, 2056], BF16, "wg_sb")
        _load_w(C, wg, bwg, wg_d, 2056, "ld_wg")
        hT, bhT = C.sb([128, 8, 512], BF16, "hT")
        cin, _ = C.sb([128, 12, 515], F32, "cin")
        qkv, _ = C.sb([128, 12, 512], F32, "qkvT")
        bcins = [Buf(f"cin{c}") for c in range(12)]
        bqkvs = [Buf(f"qkv{c}") for c in range(12)]
        S_t = [C.sb([128, 128], F32, f"S{h}") for h in range(4)]
        C.pool("xt", 2, [128, 1024], F32)
        C.pool("junk", 2, [128, 1024], F32)
        C.pool("col", 10, [128, 8], F32)
        C.pool("hb", 2, [128, 1024], BF16)
        C.pool("pst", 1, [128, 1024], BF16, psum=True)
        C.pool("ps", 4, [128, 512], F32, psum=True)
        C.pool("ps2", 3, [128, 512], F32, psum=True)
        C.pool("cacc", 2, [128, 512], F32)
        C.pool("sz", 2, [128, 512], F32)
        C.pool("sm", 4, [128, 32], F32)
        hm = [[C.sb([128, 128], F32, f"hm{h}_{i}") for i in range(13)] for h in range(4)]
        hpb = [[C.sb([128, 256], F32, f"hpb{h}_{i}") for i in range(2)] for h in range(4)]
        C.pool("ocat", 2, [128, 512], F32)
        P.op("pool", lambda e: e.memset(cin[:, :, 0:3], 0.0), writes=bcins)
        mnw = _bcast_load(C, mnw_d, D_MODEL, "mnw")
        wm, bwm = C.sb([128, 8, 1024], BF16, "wm_sb")
        wkv, bwkv = wm, bwm
        _load_w(C, wkv, bwkv, wkv_d, 1024, "ld_wkv")
        mT, bmT = C.sb([128, 8, 256], BF16, "mT")
        mkT, bmk = C.sb([128, 4, 256], BF16, "mkT")
        mva, bmv = C.sb([128, 2, 2, 257], BF16, "mva")
        mq, bmq = C.sb([128, 4, 512], BF16, "mq")
        C.pool("pTm", 2, [128, 128], BF16)
        C.pool("smz", 2, [128, 512], F32)
        C.pool("omem", 2, [128, 512], F32)
        _make_hT(C, K, mem_d, 0, mnw, mT, bmT, n_tiles=2)
        P.op("pool", lambda e: e.memset(mva[:, :, :, 256:257], 1.0), writes=[bmv])
        for c in range(4):
            pp, bpp = C.get("ps")

            def mmk_(e, pp=pp, c=c):
                for kc in range(8):
                    ins = e.matmul(pp[:, 0:256], lhsT=wkv[:, kc, c * 128:(c + 1) * 128], rhs=mT[:, kc, :],
                                   start=(kc == 0), stop=(kc == 7))
                return ins
            P.op("pe", mmk_, reads=[bwkv, bmT], writes=[bpp])
            P.op("act", lambda e, pp=pp, c=c: e.copy(out=mkT[:, c, :], in_=pp[:, 0:256]), reads=[bpp], writes=[bmk])
        for mt in range(2):
            pp, bpp = C.get("ps")

            def mmv_(e, pp=pp, mt=mt):
                for kc in range(8):
                    ins = e.matmul(pp[:, :], lhsT=mT[:, kc, mt * 128:(mt + 1) * 128], rhs=wkv[:, kc, 512:1024],
                                   start=(kc == 0), stop=(kc == 7))
                return ins
            P.op("pe", mmv_, reads=[bwkv, bmT], writes=[bpp])
            P.op("act", lambda e, pp=pp, mt=mt: e.copy(out=mva[:, mt, :, 0:256],
                                                       in_=pp[:, :].rearrange("p (h e) -> p h e", h=2)),
                 reads=[bpp], writes=[bmv])
        _load_w(C, wm, bwm, wm_d, 1024, "ld_wm")
        for h in range(4):
            P.op("pool", lambda e, h=h: e.tensor_tensor(out=_r(S_t[h][0][:]), in0=ident[:], in1=ident[:], op=ALU.subtract),
                 reads=[bi], writes=[S_t[h][1]])

        for st in range(NST):
            _make_hT(C, K, x_d, st, prew, hT, bhT)
            for c in range(12):
                pp, bpp = C.get("ps")
                bcin = bcins[c]
                bqkv = bqkvs[c]

                def mm(e, pp=pp, c=c):
                    for kc in range(8):
                        ins = e.matmul(pp[:, :], lhsT=wg[:, kc, c * 128:(c + 1) * 128], rhs=hT[:, kc, :],
                                       start=(kc == 0), stop=(kc == 7))
                    return ins
                P.op("pe", mm, reads=[bwg, bhT], writes=[bpp])
                P.op("act", lambda e, pp=pp, c=c: e.copy(out=cin[:, c, 3:515], in_=pp[:, :]), reads=[bpp], writes=[bcin])
                acc, bacc = C.get("cacc")

                P.op("dve", lambda e, acc=acc, c=c: e.tensor_scalar(
                    out=acc[:], in0=cin[:, c, 0:512], scalar1=cw[:, c, 0:1], scalar2=None, op0=ALU.mult),
                    reads=[bcin, bcw], writes=[bacc])
                for j in range(1, 4):
                    P.op("dve", lambda e, acc=acc, c=c, j=j: e.scalar_tensor_tensor(
                        out=acc[:], in0=cin[:, c, j:j + 512], scalar=cw[:, c, j:j + 1], in1=acc[:], op0=ALU.mult,
                        op1=ALU.add), reads=[bcin, bcw, bacc], writes=[bacc])
                P.op("pool", lambda e, c=c: e.tensor_copy(out=cin[:, c, 0:3], in_=cin[:, c, 512:515]),
                     reads=[bcin, bacc], writes=[bcin])
                P.op("act", lambda e, acc=acc, c=c: e.activation(out=_r(qkv[:, c, :]), in_=acc[:], func=AF.Silu),
                     reads=[bacc], writes=[bqkv])
            for c in range(8 if stage >= 2 else 0):
                bqkv = bqkvs[c]
                sq, bsq = C.get("cacc")
                P.op("pool", lambda e, sq=sq, c=c: e.tensor_tensor(out=sq[:], in0=qkv[:, c, :], in1=qkv[:, c, :],
                                                                   op=ALU.mult), reads=[bqkv], writes=[bsq])
                pp, bpp = C.get("ps")
                P.op("pe", lambda e, pp=pp, sq=sq: e.matmul(pp[:, :], lhsT=ones[:], rhs=sq[:], start=True, stop=True),
                     reads=[bsq, bon], writes=[bpp])
                rn, brn = C.get("cacc")
                P.op("act", lambda e, rn=rn, pp=pp: e.activation(out=rn[:], in_=pp[:, :], func=AF.Sqrt,
                                                                  bias=epst[:, 0:1], scale=1.0),
                     reads=[bpp, be], writes=[brn])
                P.op("dve", lambda e, rn=rn: e.reciprocal(out=rn[:], in_=rn[:]), reads=[brn], writes=[brn])
                sc = (128.0 ** -0.5) if c < 4 else 1.0
                P.op("dve", lambda e, rn=rn, c=c, sc=sc: e.scalar_tensor_tensor(
                    out=_r(qkv[:, c, :]), in0=qkv[:, c, :], scalar=sc, in1=rn[:], op0=ALU.mult, op1=ALU.mult),
                    reads=[bqkv, brn], writes=[bqkv])
            for c in range(4):
                pp, bpp = C.get("ps")

                def mmq_(e, pp=pp, c=c):
                    for kc in range(8):
                        ins = e.matmul(pp[:, :], lhsT=wm[:, kc, c * 128:(c + 1) * 128], rhs=hT[:, kc, :],
                                       start=(kc == 0), stop=(kc == 7))
                    return ins
                P.op("pe", mmq_, reads=[bwm, bhT], writes=[bpp])
                P.op("act", lambda e, pp=pp, c=c: e.copy(out=mq[:, c, :], in_=pp[:, :]), reads=[bpp], writes=[bmq])
            tile_res = {}

            def pro_gen(t):
                tsl = slice(t * 128, (t + 1) * 128)
                r0 = (st * 4 + t) * 128
                pmz, bpmz = C.get("ps2")

                def mmmz(e, pmz=pmz, tsl=tsl):
                    for kc in range(8):
                        ins = e.matmul(pmz[:, :], lhsT=hT[:, kc, tsl], rhs=wm[:, kc, 512:1024], start=(kc == 0),
                                       stop=(kc == 7))
                    return ins
                P.op("pe", mmmz, reads=[bwm, bhT], writes=[bpmz])
                smz, bsmz = C.get("smz")
                P.op("act", lambda e, smz=smz, pmz=pmz: e.activation(out=smz[:], in_=pmz[:, :], func=AF.Silu),
                     reads=[bpmz], writes=[bsmz])
                yield
                omem, bomem = C.get("omem")
                for hd in range(2):
                    accM, baM = C.get("ps2")
                    for mt in range(2):
                        scm, bscm = C.get("ps2")

                        def mmsc(e, scm=scm, hd=hd, mt=mt, tsl=tsl):
                            for dc in range(2):
                                ins = e.matmul(scm[:, 0:128], lhsT=mkT[:, hd * 2 + dc, mt * 128:(mt + 1) * 128],
                                               rhs=mq[:, hd * 2 + dc, tsl], start=(dc == 0), stop=(dc == 1))
                            return ins
                        P.op("pe", mmsc, reads=[bmk, bmq], writes=[bscm])
                        pTm, bpTm = C.get("pTm")
                        P.op("act", lambda e, pTm=pTm, scm=scm: e.activation(out=pTm[:], in_=scm[:, 0:128], func=AF.Exp,
                                                                             scale=1.0 / 16.0), reads=[bscm], writes=[bpTm])
                        P.op("pe", lambda e, accM=accM, pTm=pTm, mt=mt, hd=hd: e.matmul(
                            accM[:, 0:257], lhsT=pTm[:], rhs=mva[:, mt, hd, :], start=(mt == 0), stop=(mt == 1)),
                            reads=[bpTm, bmv], writes=[baM])
                        yield
                    cl, bcl = C.get("col")
                    P.op("dve", lambda e, cl=cl, accM=accM: e.reciprocal(out=cl[:, 0:1], in_=accM[:, 256:257]),
                         reads=[baM], writes=[bcl])
                    P.op("dve", lambda e, omem=omem, accM=accM, cl=cl, smz=smz, hd=hd: e.scalar_tensor_tensor(
                        out=omem[:, hd * 256:(hd + 1) * 256], in0=accM[:, 0:256], scalar=cl[:, 0:1],
                        in1=smz[:, hd * 256:(hd + 1) * 256], op0=ALU.mult, op1=ALU.mult),
                        reads=[baM, bcl, bsmz, bomem], writes=[bomem])
                    yield
                P.dma("sp", om_d[r0:r0 + 128, :], omem[:], reads=[bomem], writes=[bom_d], semname="st_" + bomem.name)
                pab, bpab = C.get("ps2")

                def mmab(e, pab=pab, tsl=tsl):
                    for kc in range(8):
                        ins = e.matmul(pab[:, 0:8], lhsT=hT[:, kc, tsl], rhs=wg[:, kc, 1536:1544],
                                       start=(kc == 0), stop=(kc == 7))
                    return ins
                P.op("pe", mmab, reads=[bwg, bhT], writes=[bpab])
                pz, bpz = C.get("ps2")

                def mmz(e, pz=pz, tsl=tsl):
                    for kc in range(8):
                        ins = e.matmul(pz[:, :], lhsT=hT[:, kc, tsl], rhs=wg[:, kc, 1544:2056],
                                       start=(kc == 0), stop=(kc == 7))
                    return ins
                P.op("pe", mmz, reads=[bwg, bhT], writes=[bpz])
                sz, bsz = C.get("sz")
                P.op("act", lambda e, sz=sz, pz=pz: e.activation(out=sz[:], in_=pz[:, :], func=AF.Silu),
                     reads=[bpz], writes=[bsz])
                yield
                sm, bsm = C.get("sm")

                def small1(e, sm=sm, pab=pab):
                    e.tensor_tensor(out=sm[:, 0:4], in0=pab[:, 0:4], in1=dtb[0][:], op=ALU.add)
                    return e.tensor_copy(out=sm[:, 4:8], in_=pab[:, 4:8])
                P.op("dve", small1, reads=[bpab, dtb[1]], writes=[bsm])
                yield

                def small2(e, sm=sm):
                    e.activation(out=sm[:, 0:4], in_=sm[:, 0:4], func=AF.Exp)
                    return e.activation(out=sm[:, 4:8], in_=sm[:, 4:8], func=AF.Exp, scale=-1.0)
                P.op("act", small2, reads=[bsm], writes=[bsm])
                P.op("act", lambda e, sm=sm: e.activation(out=sm[:, 0:4], in_=sm[:, 0:4], func=AF.Ln,
                                                          bias=ones[:, 0:1], scale=1.0), reads=[bsm, bon],
                     writes=[bsm])
                yield

                def small3(e, sm=sm):
                    e.tensor_tensor(out=sm[:, 0:4], in0=sm[:, 0:4], in1=negA[:], op=ALU.mult)
                    return e.tensor_scalar(out=sm[:, 4:8], in0=sm[:, 4:8], scalar1=1.0, scalar2=None, op0=ALU.add)
                P.op("dve", small3, reads=[bsm, bnA], writes=[bsm])
                P.op("dve", lambda e, sm=sm: e.reciprocal(out=sm[:, 4:8], in_=sm[:, 4:8]), reads=[bsm], writes=[bsm])
                yield
                P.op("act", lambda e, sm=sm: e.activation(out=sm[:, 8:12], in_=sm[:, 4:8], func=AF.Ln),
                     reads=[bsm], writes=[bsm])
                pg, bpg = C.get("ps2")

                def mmg(e, pg=pg, sm=sm):
                    e.matmul(pg[:, 0:4], lhsT=triu[:], rhs=sm[:, 0:4], start=True, stop=True)
                    return e.matmul(pg[:, 4:8], lhsT=ones[:], rhs=sm[:, 0:4], start=True, stop=True)
                P.op("pe", mmg, reads=[bsm, btr, bon], writes=[bpg])
                yield

                def small4(e, sm=sm, pg=pg):
                    e.tensor_copy(out=sm[:, 12:16], in_=pg[:, 0:4])
                    return e.tensor_scalar(out=sm[:, 16:20], in0=pg[:, 0:4], scalar1=-1.0, scalar2=None, op0=ALU.mult)
                P.op("dve", small4, reads=[bpg], writes=[bsm])
                P.op("dve", lambda e, sm=sm, pg=pg: e.tensor_tensor(out=sm[:, 28:32], in0=pg[:, 4:8], in1=sm[:, 12:16],
                                                                    op=ALU.subtract), reads=[bpg, bsm], writes=[bsm])

                def small5(e, sm=sm, pg=pg):
                    e.activation(out=sm[:, 20:24], in_=pg[:, 4:8], func=AF.Exp)
                    e.activation(out=sm[:, 24:28], in_=sm[:, 12:16], func=AF.Exp)
                    return e.activation(out=sm[:, 28:32], in_=sm[:, 28:32], func=AF.Exp)
                P.op("act", small5, reads=[bsm, bpg], writes=[bsm])
                yield
                P.op("dve", lambda e, sm=sm: e.tensor_tensor(out=sm[:, 24:28], in0=sm[:, 24:28], in1=sm[:, 4:8],
                                                             op=ALU.mult), reads=[bsm], writes=[bsm])
                ocat, boc = C.get("ocat")
                tile_res[t] = dict(sm=sm, bsm=bsm, sz=sz, bsz=bsz, ocat=ocat, boc=boc, tsl=tsl, r0=r0)

            for _ in pro_gen(0):
                pass
            for t in range(4):
                tr_ = tile_res[t]
                sm, bsm, sz, bsz = tr_['sm'], tr_['bsm'], tr_['sz'], tr_['bsz']
                ocat, boc, tsl, r0 = tr_['ocat'], tr_['boc'], tr_['tsl'], tr_['r0']

                def head_gen(h, sm=sm, bsm=bsm, sz=sz, bsz=bsz, ocat=ocat, boc=boc, tsl=tsl):
                    qT = qkv[:, h, tsl]
                    kT = qkv[:, 4 + h, tsl]
                    vT = qkv[:, 8 + h, tsl]
                    St, bS = S_t[h]
                    bqkv_h = [bqkvs[h], bqkvs[4 + h], bqkvs[8 + h]]
                    M = hm[h]
                    (gtri, bgt), (gtri2, bgt2), (E3, bE3), (E1, bE1), (E2, bE2) = M[0], M[1], M[2], M[3], M[4]
                    (Bm, bB), (attT, bat), (kb, bkb), (kd, bkd), (vb, bvb), (qd, bqd) = M[5], M[6], M[7], M[8], M[9], M[10]
                    P.op("pool", lambda e: e.tensor_scalar(
                        out=gtri[:], in0=triu[:], scalar1=sm[:, h:h + 1], scalar2=None, op0=ALU.mult),
                        reads=[bsm, btr], writes=[bgt])
                    P.op("dve", lambda e: e.scalar_tensor_tensor(
                        out=gtri2[:], in0=ident[:], scalar=sm[:, 8 + h:9 + h], in1=gtri[:], op0=ALU.mult, op1=ALU.add),
                        reads=[bsm, bi, bgt], writes=[bgt2])
                    yield
                    pX, bpX = C.get("ps")

                    def mmx(e):
                        e.matmul(pX[:, 0:128], lhsT=ones[:], rhs=gtri[:], start=True, stop=True)
                        e.matmul(pX[:, 128:256], lhsT=ones[:], rhs=gtri[:], start=True, stop=False)
                        e.matmul(pX[:, 128:256], lhsT=ident[:], rhs=mincl[:], start=False, stop=True)
                        e.matmul(pX[:, 256:384], lhsT=ones[:], rhs=gtri2[:], start=True, stop=False)
                        return e.matmul(pX[:, 256:384], lhsT=ident[:], rhs=mstr[:], start=False, stop=True)
                    P.op("pe", mmx, reads=[bgt, bgt2, bon, bi, bmi, bms], writes=[bpX])

                    def exps(e):
                        e.activation(out=_r(E3[:]), in_=pX[:, 0:128], func=AF.Exp)
                        e.activation(out=_r(E1[:]), in_=pX[:, 128:256], func=AF.Exp, bias=sm[:, 16 + h:17 + h], scale=1.0)
                        return e.activation(out=_r(E2[:]), in_=pX[:, 256:384], func=AF.Exp, bias=sm[:, 16 + h:17 + h],
                                            scale=1.0)
                    P.op("act", exps, reads=[bpX, bsm], writes=[bE3, bE1, bE2])
                    yield
                    pK, bpK = C.get("ps")

                    def mmk(e):
                        e.matmul(pK[:, 0:128], lhsT=_r(kT), rhs=_r(kT), start=True, stop=True)
                        e.matmul(pK[:, 128:256], lhsT=_r(kT), rhs=_r(qT), start=True, stop=True)
                        e.transpose(pK[:, 256:384], kT, ident[:])
                        return e.transpose(pK[:, 384:512], vT, ident[:])
                    P.op("pe", mmk, reads=bqkv_h + [bi], writes=[bpK])

                    def ev1(e):
                        e.tensor_tensor(out=_r(Bm[:]), in0=pK[:, 0:128], in1=E2[:], op=ALU.mult)
                        e.tensor_tensor(out=_r(attT[:]), in0=pK[:, 128:256], in1=E1[:], op=ALU.mult)
                        e.tensor_scalar(out=_r(kb[:]), in0=pK[:, 256:384], scalar1=sm[:, 24 + h:25 + h], scalar2=None,
                                        op0=ALU.mult)
                        e.tensor_scalar(out=_r(kd[:]), in0=pK[:, 256:384], scalar1=sm[:, 28 + h:29 + h], scalar2=None,
                                        op0=ALU.mult)
                        return e.tensor_scalar(out=_r(vb[:]), in0=pK[:, 384:512], scalar1=sm[:, 4 + h:5 + h], scalar2=None,
                                               op0=ALU.mult)
                    P.op("dve", ev1, reads=[bpK, bE1, bE2, bsm], writes=[bB, bat, bkb, bkd, bvb])
                    P.op("pool", lambda e: e.tensor_tensor(out=_r(qd[:]), in0=qT, in1=E3[:], op=ALU.mult),
                         reads=[bqkvs[h], bE3], writes=[bqd])
                    yield
                    pA, bpA = C.get("ps")
                    P.op("pe", lambda e: e.transpose(pA[:, 0:128], Bm[:], ident[:]), reads=[bB, bi], writes=[bpA])
                    PT, bPT = M[11]
                    P.op("act", lambda e: e.copy(out=_r(PT[:]), in_=pA[:, 0:128]), reads=[bpA], writes=[bPT])
                    PB, bPB = hpb[h][0]
                    P.op("pool", lambda e: e.tensor_tensor(out=_r(PB[:, 128:256]), in0=ident[:], in1=Bm[:], op=ALU.subtract),
                         reads=[bB, bi], writes=[bPB])
                    yield
                    pN, bpN = C.get("ps")

                    def n0(e, pN=pN, PT=PT):
                        e.matmul(pN[:, 0:128], lhsT=_r(PT[:]), rhs=_r(Bm[:]), start=True, stop=True)
                        return e.matmul(pN[:, 256:384], lhsT=_r(Bm[:]), rhs=_r(PT[:]), start=True, stop=True)
                    P.op("pe", n0, reads=[bPT, bB], writes=[bpN])
                    PT2, bPT2 = M[12]
                    P.op("act", lambda e, pN=pN: e.copy(out=_r(PT2[:]), in_=pN[:, 256:384]), reads=[bpN], writes=[bPT2])
                    P.op("dve", lambda e, pN=pN, PB=PB: e.tensor_copy(out=_r(PB[:, 0:128]), in_=pN[:, 0:128]), reads=[bpN],
                         writes=[bPB])
                    PT, bPT = PT2, bPT2
                    cur = 1
                    yield
                    for j in range(1, 7):
                        last = (j == 6)
                        pN, bpN = C.get("ps")

                        def nj(e, pN=pN, PT=PT, PB=PB, last=last):
                            if last:
                                return e.matmul(pN[:, 128:256], lhsT=_r(PT[:]), rhs=_r(PB[:, 128:256]), start=True, stop=True)
                            e.matmul(pN[:, 0:256], lhsT=_r(PT[:]), rhs=_r(PB[:, 0:256]), start=True, stop=True)
                            return e.matmul(pN[:, 256:384], lhsT=_r(PB[:, 0:128]), rhs=_r(PT[:]), start=True, stop=True)
                        P.op("pe", nj, reads=[bPT, bPB], writes=[bpN])
                        PBn, bPBn = hpb[h][j % 2]
                        if not last:
                            PTn, bPTn = M[11 + (1 - cur)]
                            P.op("act", lambda e, PTn=PTn, pN=pN, PBn=PBn: (
                                e.copy(out=_r(PTn[:]), in_=pN[:, 256:384]),
                                e.copy(out=_r(PBn[:, 0:128]), in_=pN[:, 0:128]))[1], reads=[bpN], writes=[bPTn, bPBn])
                        P.op("dve", lambda e, PBn=PBn, PB=PB, pN=pN: e.tensor_tensor(
                            out=_r(PBn[:, 128:256]), in0=PB[:, 128:256], in1=pN[:, 128:256], op=ALU.add),
                            reads=[bpN, bPB], writes=[bPBn])
                        PB, bPB = PBn, bPBn
                        if not last:
                            PT, bPT = PTn, bPTn
                            cur = 1 - cur
                        yield
                    TT = PB[:, 128:256]
                    pW, bpW = C.get("ps")
                    P.op("pe", lambda e: e.matmul(pW[:, 0:128], lhsT=_r(kb[:]), rhs=_r(TT), start=True, stop=True),
                         reads=[bkb, bPB], writes=[bpW])
                    nwT, bnw = M[2]
                    P.op("act", lambda e: e.activation(out=_r(nwT[:]), in_=pW[:, 0:128], func=AF.Copy, scale=-1.0),
                         reads=[bpW], writes=[bnw])
                    yield
                    pV, bpV = C.get("ps")

                    def mv(e):
                        e.matmul(pV[:, 0:128], lhsT=_r(TT), rhs=_r(vb[:]), start=True, stop=False)
                        return e.matmul(pV[:, 0:128], lhsT=_r(nwT[:]), rhs=_r(St[:]), start=False, stop=True)
                    P.op("pe", mv, reads=[bPB, bvb, bnw, bS], writes=[bpV])
                    vn, bvn = M[3]
                    P.op("act", lambda e: e.copy(out=_r(vn[:]), in_=pV[:, 0:128]), reads=[bpV], writes=[bvn])
                    yield
                    pO, bpO = C.get("ps")

                    def mo(e):
                        e.matmul(pO[:, 0:128], lhsT=_r(qd[:]), rhs=_r(St[:]), start=True, stop=False)
                        e.matmul(pO[:, 0:128], lhsT=_r(attT[:]), rhs=_r(vn[:]), start=False, stop=True)
                        return e.matmul(pO[:, 128:256], lhsT=_r(kd[:]), rhs=_r(vn[:]), start=True, stop=True)
                    P.op("pe", mo, reads=[bqd, bS, bat, bvn, bkd], writes=[bpO])
                    P.op("dve", lambda e: e.scalar_tensor_tensor(
                        out=_r(St[:]), in0=St[:], scalar=sm[:, 20 + h:21 + h], in1=pO[:, 128:256], op0=ALU.mult, op1=ALU.add),
                        reads=[bpO, bsm, bS], writes=[bS])
                    jk, bjk = M[4]
                    ss, bss = C.get("col")
                    P.op("act", lambda e: e.activation(out=_r(jk[:]), in_=pO[:, 0:128], func=AF.Square, accum_out=ss[:, 0:1]),
                         reads=[bpO], writes=[bjk, bss])
                    P.op("act", lambda e: e.activation(out=ss[:, 1:2], in_=ss[:, 0:1], func=AF.Sqrt, bias=epst[:, 0:1],
                                                       scale=1.0 / 128.0), reads=[bss, be], writes=[bss])
                    P.op("dve", lambda e: e.reciprocal(out=ss[:, 2:3], in_=ss[:, 1:2]), reads=[bss], writes=[bss])
                    P.op("dve", lambda e: e.scalar_tensor_tensor(
                        out=_r(jk[:]), in0=pO[:, 0:128], scalar=ss[:, 2:3], in1=gnw[0][:], op0=ALU.mult, op1=ALU.mult),
                        reads=[bpO, bss, gnw[1], bjk], writes=[bjk])
                    P.op("dve", lambda e: e.tensor_tensor(
                        out=ocat[:, h * 128:(h + 1) * 128], in0=jk[:], in1=sz[:, h * 128:(h + 1) * 128], op=ALU.mult),
                        reads=[bjk, bsz, boc], writes=[boc])

                gens = [head_gen(h) for h in range(4)]
                if t + 1 < 4:
                    gens.append(pro_gen(t + 1))
                while gens:
                    for g in list(gens):
                        try:
                            next(g)
                        except StopIteration:
                            gens.remove(g)
                P.dma("sp", o_d[r0:r0 + 128, :], ocat[:], reads=[boc], writes=[bo_d], semname="st_" + boc.name)
        P.final_waits("sp", [bo_d, bom_d])
    if standalone:
        P.finish()
    return nc


def build_diff(S, C=None, io=None, tag="a2"):
    standalone = C is None
    if standalone:
        nc = bass.Bass("TRN2", target_bir_lowering=False)
        C = Ctx(nc)
        io = dict(x=C.dram_in("x", [S, D_MODEL]), wd=C.dram_in("wd", [D_MODEL, 2048]),
                  prew=C.dram_in("prew", [D_MODEL]), lamv=C.dram_in("lamv", [256]), dnw=C.dram_in("dnw", [128]),
                  qx=C.dram_in("qx", [8, 512]), alb=C.dram_in("alb", [128, 128]), li=C.dram_in("li", [2]),
                  o_diff=C.dram_out("o_diff", [S, 512]))
    nc = C.nc
    P = C.P
    NT = S // 128
    NST = S // 512
    x_d, wd_d, prew_d, lamv_d, dnw_d = io["x"], io["wd"], io["prew"], io["lamv"], io["dnw"]
    qx_d, alb_d, li_d, o_d = io["qx"], io["alb"], io["li"], io["o_diff"]
    bo_d = Buf("o_d", multi=True)
    with C.phase(tag):
        K = _consts(C)
        ident, bi = K["ident"]; ones, bon = K["ones"]; mincl, bmi = K["mincl"]; epst, be = K["eps"]
        identb, bib = K["identb"]
        minclb, bmb = C.sb([128, 128], BF16, "minclb")
        P.op("pool", lambda e: e.tensor_copy(out=minclb[:], in_=mincl[:]), reads=[bmi], writes=[bmb])
        prew = _bcast_load(C, prew_d, D_MODEL, "prew")
        dnw = _bcast_load(C, dnw_d, 128, "dnw")
        lamv = _bcast_load(C, lamv_d, 256, "lamv")
        li = _bcast_load(C, li_d, 2, "li")
        alb, balb = C.sb([128, 128], F32, "alb_sb")
        P.dma("sp", alb[:], alb_d[:, :], writes=[balb], semname="ld_alb")
        lm, blm = C.sb([128, 8], F32, "lm")
        lp, blp = C.sb([128, 128], F32, "lp")

        def lam1(e):
            e.tensor_tensor(out=lp[:, 0:64], in0=lamv[0][:, 0:64], in1=lamv[0][:, 64:128], op=ALU.mult)
            return e.tensor_tensor(out=lp[:, 64:128], in0=lamv[0][:, 128:192], in1=lamv[0][:, 192:256], op=ALU.mult)
        P.op("dve", lam1, reads=[lamv[1]], writes=[blp])

        def lam2(e):
            e.reduce_sum(out=lm[:, 0:1], in_=lp[:, 0:64], axis=mybir.AxisListType.X)
            return e.reduce_sum(out=lm[:, 1:2], in_=lp[:, 64:128], axis=mybir.AxisListType.X)
        P.op("dve", lam2, reads=[blp], writes=[blm])
        P.op("act", lambda e: e.activation(out=lm[:, 2:4], in_=lm[:, 0:2], func=AF.Exp), reads=[blm], writes=[blm])
        P.op("dve", lambda e: e.tensor_tensor(out=lm[:, 4:5], in0=lm[:, 3:4], in1=lm[:, 2:3], op=ALU.subtract),
             reads=[blm], writes=[blm])
        P.op("dve", lambda e: e.tensor_scalar(out=lm[:, 5:6], in0=lm[:, 4:5], scalar1=li[0][:, 0:1], scalar2=None,
                                              op0=ALU.add), reads=[blm, li[1]], writes=[blm])
        P.op("dve", lambda e: e.tensor_scalar(out=dnw[0][:], in0=dnw[0][:], scalar1=li[0][:, 1:2], scalar2=None,
                                              op0=ALU.mult), reads=[dnw[1], li[1]], writes=[dnw[1]])
        wd, bwd = C.sb([128, 8, 2048], BF16, "wd_sb")
        _load_w(C, wd, bwd, wd_d, 2048, "ld_wd")
        hT, bhT = C.sb([128, 8, 512], BF16, "hT")
        ka = [[C.sb([66, S], BF16, f"ka{h}{m}") for m in range(2)] for h in range(4)]
        qa = [[C.sb([66, 512], BF16, f"qa{h}{m}") for m in range(2)] for h in range(4)]
        vaug, _ = C.sb([128, NT, 4, 129], BF16, "vaug")
        bv = [Buf(f"v{t}") for t in range(NT)]
        bka = [[[Buf(f"ka{h}{m}_{s}") for s in range(NST)] for m in range(2)] for h in range(4)]
        for h in range(4):
            for m in range(2):
                P.op("pool", lambda e, h=h, m=m: e.memset(ka[h][m][0][64:66, :], 1.0), writes=bka[h][m])
                P.dma("pool", qa[h][m][0][64:66, :], qx_d[2 * h:2 * h + 2, :], writes=[qa[h][m][1]], semname=f"ld_qx{h}{m}")
        P.op("pool", lambda e: e.memset(vaug[:, :, :, 128:129], 1.0), writes=bv)
        C.pool("xt", 2, [128, 1024], F32)
        C.pool("junk", 1, [128, 1024], F32)
        C.pool("col", 12, [128, 8], F32)
        C.pool("hb", 2, [128, 1024], BF16)
        C.pool("pst", 1, [128, 1024], BF16, psum=True)
        C.pool("ps", 3, [128, 512], F32, psum=True)
        C.pool("acc", 4, [128, 512], F32, psum=True)
        C.pool("pT", 4, [128, 512], BF16)
        C.pool("szd", 1, [128, 4, 512], F32)
        C.pool("o0", 2, [128, 4, 128], F32)
        C.pool("ob", 8, [128, 128], F32)
        C.pool("ocat", 1, [128, 4, 512], F32)

        for I in range(NST):
            _make_hT(C, K, x_d, I, prew, hT, bhT)
            for h in range(4):
                for which, col0 in (("q", h * 128), ("k", 512 + h * 128)):
                    pp, bpp = C.get("ps")

                    def mm(e, pp=pp, col0=col0):
                        for kc in range(8):
                            ins = e.matmul(pp[:, :], lhsT=wd[:, kc, col0:col0 + 128], rhs=hT[:, kc, :],
                                           start=(kc == 0), stop=(kc == 7))
                        return ins
                    P.op("pe", mm, reads=[bwd, bhT], writes=[bpp])
                    for m in range(2):
                        if which == "q":
                            dst, bd = qa[h][m][0][0:64, :], qa[h][m][1]
                        else:
                            dst, bd = ka[h][m][0][0:64, I * 512:(I + 1) * 512], bka[h][m][I]
                        eng = "act" if m == 0 else "dve"
                        if eng == "act":
                            P.op("act", lambda e, dst=dst, pp=pp, m=m: e.copy(out=dst, in_=pp[64 * m:64 * m + 64, :]),
                                 reads=[bpp], writes=[bd])
                        else:
                            P.op("dve", lambda e, dst=dst, pp=pp, m=m: e.tensor_copy(out=dst, in_=pp[64 * m:64 * m + 64, :]),
                                 reads=[bpp], writes=[bd])
            szd, bszd = C.get("szd")
            for t in range(4):
                tsl = slice(t * 128, (t + 1) * 128)
                tile = I * 4 + t
                pv_, bpv = C.get("ps")

                def mmv(e, pv_=pv_, tsl=tsl):
                    for kc in range(8):
                        ins = e.matmul(pv_[:, :], lhsT=hT[:, kc, tsl], rhs=wd[:, kc, 1024:1536], start=(kc == 0),
                                       stop=(kc == 7))
                    return ins
                P.op("pe", mmv, reads=[bwd, bhT], writes=[bpv])
                P.op("dve", lambda e, pv_=pv_, tile=tile: e.tensor_copy(
                    out=vaug[:, tile, :, 0:128], in_=pv_[:, :].rearrange("p (h e) -> p h e", h=4)),
                    reads=[bpv], writes=[bv[tile]])
                pz, bpz = C.get("ps")

                def mmz(e, pz=pz, tsl=tsl):
                    for kc in range(8):
                        ins = e.matmul(pz[:, :], lhsT=hT[:, kc, tsl], rhs=wd[:, kc, 1536:2048], start=(kc == 0),
                                       stop=(kc == 7))
                    return ins
                P.op("pe", mmz, reads=[bwd, bhT], writes=[bpz])
                P.op("act", lambda e, szd=szd, pz=pz, t=t: e.activation(out=szd[:, t, :], in_=pz[:, :], func=AF.Silu),
                     reads=[bpz], writes=[bszd])
            ocat, boc = C.get("ocat")
            nj = 4 * I + 4
            items = []
            for h in range(4):
                o0, bo0 = C.get("o0")
                for m in range(2):
                    accA, baA = C.get("acc")
                    accB, baB = C.get("acc")
                    for j in range(nj):
                        items.append(dict(h=h, m=m, j=j, o0=o0, bo0=bo0, accA=accA, baA=baA, accB=accB, baB=baB))

            def issue_sc(it):
                h, m, j = it["h"], it["m"], it["j"]
                jj = j - 4 * I
                q0 = 128 * jj if jj > 0 else 0
                sc, bsc = C.get("ps")

                def mms(e):
                    ins = e.matmul(sc[:, q0:512], lhsT=ka[h][m][0][0:66, j * 128:(j + 1) * 128],
                                   rhs=qa[h][m][0][0:66, q0:512], start=True, stop=(jj < 0))
                    if jj >= 0:
                        ins = e.matmul(sc[:, q0:q0 + 128], lhsT=identb[:], rhs=minclb[:], start=False, stop=True,
                                       skip_group_check=True)
                    return ins
                P.op("pe", mms, reads=[bka[h][m][j // 4], qa[h][m][1], bib, bmb], writes=[bsc])
                pT, bpT = C.get("pT")
                bcol = h * 32 + (jj + 28)
                P.op("act", lambda e: e.activation(out=pT[:, q0:512], in_=sc[:, q0:512], func=AF.Exp,
                                                   bias=alb[:, bcol:bcol + 1], scale=0.125),
                     reads=[bsc, balb], writes=[bpT])
                it["pT"], it["bpT"], it["jj"] = pT, bpT, jj

            def acc_of(it, ss):
                return (it["accA"], ss * 129, it["baA"]) if ss < 3 else (it["accB"], 0, it["baB"])

            def issue_pv(it):
                h, j, jj, pT = it["h"], it["j"], it["jj"], it["pT"]

                def mmpv(e):
                    ins = None
                    for ss in range(max(jj, 0), 4):
                        a, off, _ = acc_of(it, ss)
                        first = (j == 0) and (ss == 0 or ss == 3)
                        ins = e.matmul(a[:, off:off + 129], lhsT=pT[:, ss * 128:(ss + 1) * 128], rhs=vaug[:, j, h, :],
                                       start=first, stop=(j == 4 * I + ss), skip_group_check=True)
                    return ins
                P.op("pe", mmpv, reads=[it["bpT"], bv[j]], writes=[it["baA"], it["baB"]])

            def finalize(it):
                h, m, o0, bo0 = it["h"], it["m"], it["o0"], it["bo0"]
                for ss in range(4):
                    a, off, ba = acc_of(it, ss)
                    cl, bcl = C.get("col")
                    P.op("dve", lambda e, cl=cl, a=a, off=off: e.reciprocal(out=cl[:, 0:1], in_=a[:, off + 128:off + 129]),
                         reads=[ba], writes=[bcl])
                    if m == 0:
                        P.op("dve", lambda e, ss=ss, a=a, off=off, cl=cl: e.tensor_scalar(
                            out=o0[:, ss, :], in0=a[:, off:off + 128], scalar1=cl[:, 0:1], scalar2=None, op0=ALU.mult),
                            reads=[ba, bcl], writes=[bo0])
                        continue
                    ob, bob = C.get("ob")
                    P.op("dve", lambda e, ob=ob, a=a, off=off, cl=cl: e.tensor_scalar(
                        out=ob[:], in0=a[:, off:off + 128], scalar1=cl[:, 0:1], scalar2=None, op0=ALU.mult),
                        reads=[ba, bcl], writes=[bob])
                    P.op("dve", lambda e, ob=ob, ss=ss: e.scalar_tensor_tensor(
                        out=ob[:], in0=ob[:], scalar=lm[:, 5:6], in1=o0[:, ss, :], op0=ALU.mult, op1=ALU.add),
                        reads=[bob, bo0, blm], writes=[bob])
                    jk, bjk = C.get("ob")
                    P.op("act", lambda e, jk=jk, ob=ob, cl=cl: e.activation(out=jk[:], in_=ob[:], func=AF.Square,
                                                                             accum_out=cl[:, 1:2]),
                         reads=[bob, bcl], writes=[bjk, bcl])
                    P.op("act", lambda e, cl=cl: e.activation(out=cl[:, 2:3], in_=cl[:, 1:2], func=AF.Sqrt,
                                                              bias=epst[:, 0:1], scale=1.0 / 128.0),
                         reads=[bcl, be], writes=[bcl])
                    P.op("dve", lambda e, cl=cl: e.reciprocal(out=cl[:, 3:4], in_=cl[:, 2:3]), reads=[bcl], writes=[bcl])
                    P.op("dve", lambda e, ob=ob, cl=cl: e.scalar_tensor_tensor(
                        out=ob[:], in0=ob[:], scalar=cl[:, 3:4], in1=dnw[0][:], op0=ALU.mult, op1=ALU.mult),
                        reads=[bob, bcl, dnw[1]], writes=[bob])
                    P.op("dve", lambda e, ob=ob, ss=ss: e.tensor_tensor(
                        out=ocat[:, ss, h * 128:(h + 1) * 128], in0=ob[:], in1=szd[:, ss, h * 128:(h + 1) * 128],
                        op=ALU.mult), reads=[bob, bszd, boc], writes=[boc])

            pending = []
            for it in items:
                issue_sc(it)
                pending.append(it)
                if len(pending) > 2:
                    p_ = pending.pop(0)
                    issue_pv(p_)
                    if p_["j"] == nj - 1:
                        finalize(p_)
            for p_ in pending:
                issue_pv(p_)
                if p_["j"] == nj - 1:
                    finalize(p_)
            for ss in range(4):
                r0 = (I * 4 + ss) * 128
                P.dma("sp", o_d[r0:r0 + 128, :], ocat[:, ss, :], reads=[boc], writes=[bo_d], semname="st_" + boc.name)
        P.final_waits("sp", [bo_d])
    if standalone:
        P.finish()
    return nc


def diff_consts(hh):
    slopes = [2.0 ** (-(4 * hh + h + 1)) for h in range(4)]
    q = np.arange(512)
    qx = np.zeros((8, 512), np.float32)
    alb = np.zeros((128, 128), np.float32)
    k = np.arange(128, dtype=np.float64)
    for h in range(4):
        qx[2 * h] = -8.0 * slopes[h] * (q % 128)
        qx[2 * h + 1] = -8.0 * slopes[h] * 128.0 * (q // 128)
        for d in range(32):
            alb[:, h * 32 + d] = slopes[h] * (k + 128.0 * (d - 28))
    return qx, alb


def build_merge(TB):
    nc = bass.Bass("TRN2", target_bir_lowering=False)
    C = Ctx(nc)
    P = C.P
    NT = TB // 128
    x_d = C.dram_in("x", [TB, D_MODEL])
    oc_d = C.dram_in("oc", [TB, 3072])
    wgt_d = C.dram_in("wgate", [D_MODEL, 3072])
    wbr_d = C.dram_in("wbr", [3072, D_MODEL])
    wout_d = C.dram_in("wout", [D_MODEL, D_MODEL])
    prew_d = C.dram_in("prew", [D_MODEL])
    postw_d = C.dram_in("postw", [D_MODEL])
    y_d = C.dram_out("xo", [TB, D_MODEL])
    by_d = Buf("y_d", multi=True)
    with C.phase("b"):
        K = _consts(C)
        identb, bib = K["identb"]; epst, be = K["eps"]
        prew = _bcast_load(C, prew_d, D_MODEL, "prew")
        postw = _bcast_load(C, postw_d, D_MODEL, "postw")
        wgt, bwgt = C.sb([128, 8, 3072], BF16, "wgt_sb")
        _load_w(C, wgt, bwgt, wgt_d, 3072, "ld_wgt")
        wbr, bwbr = C.sb([128, 24, 1024], BF16, "wbr_sb")
        src = wbr_d.rearrange("(kc p) n -> p kc n", p=128)
        for kc in range(24):
            P.dma("pool", wbr[:, kc, :], src[:, kc, :], writes=[bwbr] if kc == 0 else [], semname="ld_wbr")
        bwbr.w = {("d", "ld_wbr"): P.dma_sems["ld_wbr"]}
        wout, bwout = C.sb([128, 8, 1024], BF16, "wout_sb")
        _load_w(C, wout, bwout, wout_d, 1024, "ld_wout")
        C.pool("xt", 2, [128, 1024], F32)
        C.pool("junk", 1, [128, 1024], F32)
        C.pool("col", 4, [128, 8], F32)
        C.pool("hb", 2, [128, 1024], BF16)
        C.pool("pst", 2, [128, 1024], BF16, psum=True)
        C.pool("ps", 5, [128, 512], F32, psum=True)
        C.pool("hT", 2, [128, 8, 128], BF16)
        C.pool("ob", 2, [128, 3072], BF16)
        C.pool("oT", 2, [128, 24, 128], BF16)
        C.pool("sig", 1, [128, 3, 1024], F32)
        C.pool("y", 2, [128, 1024], F32)
        C.pool("yb", 2, [128, 1024], BF16)
        C.pool("yT", 2, [128, 8, 128], BF16)
        C.pool("tmp", 2, [128, 512], F32)
        C.pool("xo", 2, [128, 1024], F32)
        for t in range(NT):
            r0 = t * 128
            hT, bhT = C.get("hT")
            xt, bx = C.get("xt")
            P.dma("sp", xt[:], x_d[r0:r0 + 128, :], writes=[bx], semname="ld_" + bx.name)
            sq, bsq = C.get("junk")
            ss, bss = C.get("col")
            P.op("act", lambda e, xt=xt, sq=sq, ss=ss: e.activation(out=sq[:], in_=xt[:], func=AF.Square,
                                                                     accum_out=ss[:, 0:1]), reads=[bx], writes=[bsq, bss])
            P.op("act", lambda e, ss=ss: e.activation(out=ss[:, 1:2], in_=ss[:, 0:1], func=AF.Sqrt, bias=epst[:, 0:1],
                                                      scale=1.0 / D_MODEL), reads=[bss, be], writes=[bss])
            P.op("dve", lambda e, ss=ss: e.reciprocal(out=ss[:, 2:3], in_=ss[:, 1:2]), reads=[bss], writes=[bss])
            hb, bhb = C.get("hb")
            P.op("dve", lambda e, hb=hb, xt=xt, ss=ss: e.scalar_tensor_tensor(
                out=hb[:], in0=xt[:], scalar=ss[:, 2:3], in1=prew[0][:], op0=ALU.mult, op1=ALU.mult),
                reads=[bx, bss, prew[1]], writes=[bhb])
            pt, bpt = C.get("pst")

            def tr(e, hb=hb, pt=pt):
                for kc in range(8):
                    ins = e.transpose(pt[:, kc * 128:(kc + 1) * 128], hb[:, kc * 128:(kc + 1) * 128], identb[:])
                return ins
            P.op("pe", tr, reads=[bhb, bib], writes=[bpt])
            P.op("act", lambda e, pt=pt, hT=hT: e.copy(out=hT[:, :, :], in_=pt[:, :].rearrange("p (k n) -> p k n", k=8)),
                 reads=[bpt], writes=[bhT])
            ob, bob = C.get("ob")
            P.dma("pool", ob[:], oc_d[r0:r0 + 128, :], writes=[bob], semname="ld_" + bob.name)
            oT, boT = C.get("oT")
            for g in range(3):
                pt, bpt = C.get("pst")

                def tr2(e, ob=ob, pt=pt, g=g):
                    for kc in range(8):
                        c0 = (g * 8 + kc) * 128
                        ins = e.transpose(pt[:, kc * 128:(kc + 1) * 128], ob[:, c0:c0 + 128], identb[:])
                    return ins
                P.op("pe", tr2, reads=[bob, bib], writes=[bpt])
                P.op("dve" if g == 1 else "act", (lambda e, pt=pt, oT=oT, g=g: e.tensor_copy(
                    out=oT[:, g * 8:(g + 1) * 8, :], in_=pt[:, :].rearrange("p (k n) -> p k n", k=8))) if g == 1 else
                    (lambda e, pt=pt, oT=oT, g=g: e.copy(out=oT[:, g * 8:(g + 1) * 8, :],
                                                        in_=pt[:, :].rearrange("p (k n) -> p k n", k=8))),
                    reads=[bpt], writes=[boT])
            sig, bsig = C.get("sig")
            for br in range(3):
                for half in range(2):
                    pg, bpg = C.get("ps")
                    c0 = br * 1024 + half * 512

                    def mmg(e, pg=pg, c0=c0, hT=hT):
                        for kc in range(8):
                            ins = e.matmul(pg[:, :], lhsT=hT[:, kc, :], rhs=wgt[:, kc, c0:c0 + 512], start=(kc == 0),
                                           stop=(kc == 7))
                        return ins
                    P.op("pe", mmg, reads=[bhT, bwgt], writes=[bpg])
                    P.op("act", lambda e, pg=pg, sig=sig, br=br, half=half: e.activation(
                        out=sig[:, br, half * 512:(half + 1) * 512], in_=pg[:, :], func=AF.Sigmoid),
                        reads=[bpg], writes=[bsig])
            y, by = C.get("y")
            for half in range(2):
                hs = slice(half * 512, (half + 1) * 512)
                for br in range(3):
                    pb, bpb = C.get("ps")

                    def mmb(e, pb=pb, br=br, half=half, oT=oT):
                        for kc in range(8):
                            ins = e.matmul(pb[:, :], lhsT=oT[:, br * 8 + kc, :],
                                           rhs=wbr[:, br * 8 + kc, half * 512:(half + 1) * 512], start=(kc == 0),
                                           stop=(kc == 7))
                        return ins
                    P.op("pe", mmb, reads=[boT, bwbr], writes=[bpb])
                    if br == 0:
                        P.op("dve", lambda e, y=y, pb=pb, sig=sig, hs=hs: e.tensor_tensor(
                            out=y[:, hs], in0=pb[:, :], in1=sig[:, 0, hs], op=ALU.mult), reads=[bpb, bsig, by], writes=[by])
                    else:
                        tmp, btmp = C.get("tmp")
                        P.op("dve", lambda e, tmp=tmp, pb=pb, sig=sig, hs=hs, br=br: e.tensor_tensor(
                            out=tmp[:], in0=pb[:, :], in1=sig[:, br, hs], op=ALU.mult), reads=[bpb, bsig], writes=[btmp])
                        P.op("pool", lambda e, y=y, tmp=tmp, hs=hs: e.tensor_tensor(
                            out=y[:, hs], in0=y[:, hs], in1=tmp[:], op=ALU.add), reads=[btmp, by], writes=[by])
            yb, byb = C.get("yb")
            P.op("act", lambda e, yb=yb, y=y: e.copy(out=yb[:], in_=y[:]), reads=[by], writes=[byb])
            pt, bpt = C.get("pst")

            def tr3(e, yb=yb, pt=pt):
                for kc in range(8):
                    ins = e.transpose(pt[:, kc * 128:(kc + 1) * 128], yb[:, kc * 128:(kc + 1) * 128], identb[:])
                return ins
            P.op("pe", tr3, reads=[byb, bib], writes=[bpt])
            yT, byT = C.get("yT")
            P.op("act", lambda e, pt=pt, yT=yT: e.copy(out=yT[:, :, :], in_=pt[:, :].rearrange("p (k n) -> p k n", k=8)),
                 reads=[bpt], writes=[byT])
            xo, bxo = C.get("xo")
            cl, bcl = C.get("col")
            for half in range(2):
                po, bpo = C.get("ps")

                def mmo(e, po=po, half=half, yT=yT):
                    for kc in range(8):
                        ins = e.matmul(po[:, :], lhsT=yT[:, kc, :], rhs=wout[:, kc, half * 512:(half + 1) * 512],
                                       start=(kc == 0), stop=(kc == 7))
                    return ins
                P.op("pe", mmo, reads=[byT, bwout], writes=[bpo])
                P.op("act", lambda e, po=po, xo=xo, half=half: e.copy(out=xo[:, half * 512:(half + 1) * 512], in_=po[:, :]),
                     reads=[bpo], writes=[bxo])
            sq, bsq = C.get("junk")
            P.op("act", lambda e, sq=sq, xo=xo, cl=cl: e.activation(out=sq[:], in_=xo[:], func=AF.Square,
                                                                    accum_out=cl[:, 0:1]), reads=[bxo], writes=[bsq, bcl])
            P.op("act", lambda e, cl=cl: e.activation(out=cl[:, 1:2], in_=cl[:, 0:1], func=AF.Sqrt, bias=epst[:, 0:1],
                                                      scale=1.0 / D_MODEL), reads=[bcl, be], writes=[bcl])
            P.op("dve", lambda e, cl=cl: e.reciprocal(out=cl[:, 2:3], in_=cl[:, 1:2]), reads=[bcl], writes=[bcl])
            P.op("dve", lambda e, xo=xo, cl=cl: e.scalar_tensor_tensor(
                out=xo[:], in0=xo[:], scalar=cl[:, 2:3], in1=postw[0][:], op0=ALU.mult, op1=ALU.mult),
                reads=[bxo, bcl, postw[1]], writes=[bxo])
            P.op("pool", lambda e, xo=xo, xt=xt: e.tensor_tensor(out=xo[:], in0=xo[:], in1=xt[:], op=ALU.add),
                 reads=[bxo, bx], writes=[bxo])
            P.dma("sp", y_d[r0:r0 + 128, :], xo[:], reads=[bxo], writes=[by_d], semname="st_" + bxo.name)
        P.final_waits("sp", [by_d])
    P.finish()
    return nc


def phase_b1(C, S, io, tag):
    P = C.P
    NT = S // 128
    x_d, oc_d, wgt_d, wbr_d, prew_d, yp_d = io["x"], io["oc"], io["wgate"], io["wbrm"], io["prew"], io["yp"]
    byp = Buf("yp_d", multi=True)
    with C.phase(tag):
        K = _consts(C)
        identb, bib = K["identb"]; epst, be = K["eps"]
        prew = _bcast_load(C, prew_d, D_MODEL, "prew")
        wgt, bwgt = C.sb([128, 8, 3072], BF16, "wgt_sb")
        _load_w(C, wgt, bwgt, wgt_d, 3072, "ld_wgt")
        wbr, bwbr = C.sb([128, 12, 1024], BF16, "wbr_sb")
        src = wbr_d.rearrange("(kc p) n -> p kc n", p=128)
        for kc in range(12):
            P.dma("pool", wbr[:, kc, :], src[:, kc, :], writes=[bwbr] if kc == 0 else [], semname="ld_wbr")
        bwbr.w = {("d", "ld_wbr"): P.dma_sems["ld_wbr"]}
        C.pool("xt", 3, [128, 1024], F32)
        C.pool("junk", 1, [128, 1024], F32)
        C.pool("col", 6, [128, 8], F32)
        C.pool("hb", 3, [128, 1024], BF16)
        C.pool("pst", 2, [128, 1024], BF16, psum=True)
        C.pool("ps", 5, [128, 512], F32, psum=True)
        C.pool("hT1", 3, [128, 8, 128], BF16)
        C.pool("ob", 3, [128, 1536], BF16)
        C.pool("oT", 3, [128, 12, 128], BF16)
        C.pool("sig", 2, [128, 3, 1024], F32)
        C.pool("y", 2, [128, 1024], F32)
        C.pool("ybf", 2, [128, 1024], BF16)
        C.pool("tmp", 4, [128, 512], F32)

        def front_a(t):
            r0 = t * 128
            xt, bx = C.get("xt")
            P.dma("sp", xt[:], x_d[r0:r0 + 128, :], writes=[bx], semname="ld_" + bx.name)
            sq, bsq = C.get("junk")
            ss, bss = C.get("col")
            P.op("act", lambda e: e.activation(out=sq[:, 0:1024], in_=xt[:], func=AF.Square, accum_out=ss[:, 0:1]),
                 reads=[bx], writes=[bsq, bss])
            P.op("act", lambda e: e.activation(out=ss[:, 1:2], in_=ss[:, 0:1], func=AF.Sqrt, bias=epst[:, 0:1],
                                               scale=1.0 / D_MODEL), reads=[bss, be], writes=[bss])
            P.op("dve", lambda e: e.reciprocal(out=ss[:, 2:3], in_=ss[:, 1:2]), reads=[bss], writes=[bss])
            hb, bhb = C.get("hb")
            P.op("dve", lambda e: e.scalar_tensor_tensor(out=hb[:], in0=xt[:], scalar=ss[:, 2:3], in1=prew[0][:],
                                                         op0=ALU.mult, op1=ALU.mult),
                 reads=[bx, bss, prew[1]], writes=[bhb])
            ob, bob = C.get("ob")
            P.dma("pool", ob[:], oc_d[r0:r0 + 128, :], writes=[bob], semname="ld_" + bob.name)
            return dict(r0=r0, hb=hb, bhb=bhb, ob=ob, bob=bob)

        def front(d0):
            r0, hb, bhb, ob, bob = d0["r0"], d0["hb"], d0["bhb"], d0["ob"], d0["bob"]
            hT, bhT = C.get("hT1")
            pt, bpt = C.get("pst")

            def tr(e):
                for kc in range(8):
                    ins = e.transpose(pt[:, kc * 128:(kc + 1) * 128], hb[:, kc * 128:(kc + 1) * 128], identb[:])
                return ins
            P.op("pe", tr, reads=[bhb, bib], writes=[bpt])
            P.op("act", lambda e: e.copy(out=hT[:, :, :], in_=pt[:, :].rearrange("p (k n) -> p k n", k=8)),
                 reads=[bpt], writes=[bhT])
            oT, boT = C.get("oT")
            for g in range(2):
                pt, bpt = C.get("pst")
                nk = 8 if g == 0 else 4

                def tr2(e, ob=ob, pt=pt, g=g, nk=nk):
                    for kc in range(nk):
                        c0 = (g * 8 + kc) * 128
                        ins = e.transpose(pt[:, kc * 128:(kc + 1) * 128], ob[:, c0:c0 + 128], identb[:])
                    return ins
                P.op("pe", tr2, reads=[bob, bib], writes=[bpt])
                if g == 0:
                    P.op("act", lambda e, pt=pt, oT=oT: e.copy(
                        out=oT[:, 0:8, :], in_=pt[:, :].rearrange("p (k n) -> p k n", k=8)), reads=[bpt], writes=[boT])
                else:
                    P.op("dve", lambda e, pt=pt, oT=oT: e.tensor_copy(
                        out=oT[:, 8:12, :], in_=pt[:, 0:512].rearrange("p (k n) -> p k n", k=4)), reads=[bpt],
                        writes=[boT])
            return dict(hT=hT, bhT=bhT, oT=oT, boT=boT, r0=r0)

        def back(d):
            hT, bhT, oT, boT, r0 = d["hT"], d["bhT"], d["oT"], d["boT"], d["r0"]
            sig, bsig = C.get("sig")
            for br in range(3):
                for half in range(2):
                    pg, bpg = C.get("ps")
                    c0 = br * 1024 + half * 512

                    def mmg(e, pg=pg, c0=c0, hT=hT):
                        for kc in range(8):
                            ins = e.matmul(pg[:, :], lhsT=hT[:, kc, :], rhs=wgt[:, kc, c0:c0 + 512], start=(kc == 0),
                                           stop=(kc == 7))
                        return ins
                    P.op("pe", mmg, reads=[bhT, bwgt], writes=[bpg])
                    P.op("act", lambda e, pg=pg, sig=sig, br=br, half=half: e.activation(
                        out=sig[:, br, half * 512:(half + 1) * 512], in_=pg[:, :], func=AF.Sigmoid),
                        reads=[bpg], writes=[bsig])
            y, by = C.get("y")
            for half in range(2):
                hs = slice(half * 512, (half + 1) * 512)
                for br in range(3):
                    pb, bpb = C.get("ps")

                    def mmb(e, pb=pb, br=br, half=half, oT=oT):
                        for kc in range(4):
                            ins = e.matmul(pb[:, :], lhsT=oT[:, br * 4 + kc, :],
                                           rhs=wbr[:, br * 4 + kc, half * 512:(half + 1) * 512], start=(kc == 0),
                                           stop=(kc == 3))
                        return ins
                    P.op("pe", mmb, reads=[boT, bwbr], writes=[bpb])
                    if br == 0:
                        P.op("dve", lambda e, y=y, pb=pb, sig=sig, hs=hs: e.tensor_tensor(
                            out=y[:, hs], in0=pb[:, :], in1=sig[:, 0, hs], op=ALU.mult), reads=[bpb, bsig, by], writes=[by])
                    else:
                        tmp, btmp = C.get("tmp")
                        P.op("dve", lambda e, tmp=tmp, pb=pb, sig=sig, hs=hs, br=br: e.tensor_tensor(
                            out=tmp[:], in0=pb[:, :], in1=sig[:, br, hs], op=ALU.mult), reads=[bpb, bsig], writes=[btmp])
                        P.op("pool", lambda e, y=y, tmp=tmp, hs=hs: e.tensor_tensor(
                            out=y[:, hs], in0=y[:, hs], in1=tmp[:], op=ALU.add), reads=[btmp, by], writes=[by])
            yb, byb = C.get("ybf")
            P.op("act", lambda e: e.copy(out=yb[:], in_=y[:]), reads=[by], writes=[byb])
            P.dma("sp", yp_d[r0:r0 + 128, :], yb[:], reads=[byb], writes=[byp], semname="st_" + byb.name)

        fa = front_a(0)
        nxt = front(fa)
        for t in range(NT):
            cur_ = nxt
            if t + 1 < NT:
                fa = front_a(t + 1)
            back(cur_)
            if t + 1 < NT:
                nxt = front(fa)
        P.final_waits("sp", [byp])


def phase_b2(C, TB, io, tag):
    P = C.P
    NT = TB // 128
    ys_d, xr_d, wout_d, postw_d, xo_d = io["ysum"], io["xres"], io["wout"], io["postw"], io["xout"]
    bxo_d = Buf("xo_d", multi=True)
    with C.phase(tag):
        K = _consts(C)
        identb, bib = K["identb"]; epst, be = K["eps"]
        postw = _bcast_load(C, postw_d, D_MODEL, "postw")
        wout, bwout = C.sb([128, 8, 1024], BF16, "wout_sb")
        _load_w(C, wout, bwout, wout_d, 1024, "ld_wout")
        C.pool("xt", 2, [128, 1024], F32)
        C.pool("junk", 1, [128, 1024], F32)
        C.pool("col", 4, [128, 8], F32)
        C.pool("pst", 2, [128, 1024], BF16, psum=True)
        C.pool("ps", 4, [128, 512], F32, psum=True)
        C.pool("yb", 2, [128, 1024], BF16)
        C.pool("yT", 2, [128, 8, 128], BF16)
        C.pool("xo", 2, [128, 1024], F32)
        for t in range(NT):
            r0 = t * 128
            xt, bx = C.get("xt")
            P.dma("sp", xt[:], xr_d[r0:r0 + 128, :], writes=[bx], semname="ld_" + bx.name)
            yb, byb = C.get("yb")
            P.dma("sp", yb[:], ys_d[r0:r0 + 128, :], writes=[byb], semname="ld_" + byb.name)
            pt, bpt = C.get("pst")

            def tr3(e, yb=yb, pt=pt):
                for kc in range(8):
                    ins = e.transpose(pt[:, kc * 128:(kc + 1) * 128], yb[:, kc * 128:(kc + 1) * 128], identb[:])
                return ins
            P.op("pe", tr3, reads=[byb, bib], writes=[bpt])
            yT, byT = C.get("yT")
            P.op("act", lambda e, pt=pt, yT=yT: e.copy(out=yT[:, :, :], in_=pt[:, :].rearrange("p (k n) -> p k n", k=8)),
                 reads=[bpt], writes=[byT])
            xo, bxo = C.get("xo")
            cl, bcl = C.get("col")
            for half in range(2):
                po, bpo = C.get("ps")

                def mmo(e, po=po, half=half, yT=yT):
                    for kc in range(8):
                        ins = e.matmul(po[:, :], lhsT=yT[:, kc, :], rhs=wout[:, kc, half * 512:(half + 1) * 512],
                                       start=(kc == 0), stop=(kc == 7))
                    return ins
                P.op("pe", mmo, reads=[byT, bwout], writes=[bpo])
                P.op("act", lambda e, po=po, xo=xo, half=half: e.copy(out=xo[:, half * 512:(half + 1) * 512], in_=po[:, :]),
                     reads=[bpo], writes=[bxo])
            sq, bsq = C.get("junk")
            P.op("act", lambda e, sq=sq, xo=xo, cl=cl: e.activation(out=sq[:], in_=xo[:], func=AF.Square,
                                                                    accum_out=cl[:, 0:1]), reads=[bxo], writes=[bsq, bcl])
            P.op("act", lambda e, cl=cl: e.activation(out=cl[:, 1:2], in_=cl[:, 0:1], func=AF.Sqrt, bias=epst[:, 0:1],
                                                      scale=1.0 / D_MODEL), reads=[bcl, be], writes=[bcl])
            P.op("dve", lambda e, cl=cl: e.reciprocal(out=cl[:, 2:3], in_=cl[:, 1:2]), reads=[bcl], writes=[bcl])
            P.op("dve", lambda e, xo=xo, cl=cl: e.scalar_tensor_tensor(
                out=xo[:], in0=xo[:], scalar=cl[:, 2:3], in1=postw[0][:], op0=ALU.mult, op1=ALU.mult),
                reads=[bxo, bcl, postw[1]], writes=[bxo])
            P.op("pool", lambda e, xo=xo, xt=xt: e.tensor_tensor(out=xo[:], in0=xo[:], in1=xt[:], op=ALU.add),
                 reads=[bxo, bx], writes=[bxo])
            P.dma("sp", xo_d[r0:r0 + 128, :], xo[:], reads=[bxo], writes=[bxo_d], semname="st_" + bxo.name)
        P.final_waits("sp", [bxo_d])


PAIRS = [[0, 1], [2, 3], [4, 5], [6, 7]]


def build_fused(S, L):
    nc = bass.Bass("TRN2", target_bir_lowering=False, num_devices=8)
    C = Ctx(nc)
    P = C.P
    TB = S // 2
    di = C.dram_in
    x_d = di("x", [S, D_MODEL]); xh_d = di("xhalf", [TB, D_MODEL]); mem_d = di("mem", [MEM_LEN, D_MODEL])
    wg_d = di("wg", [L, D_MODEL, 2056]); cw_d = di("convw", [L, 1536, 4]); prew_d = di("prew", [L, D_MODEL])
    alog_d = di("alog", [L, 4]); dtb_d = di("dtb", [L, 4]); gnw_d = di("gnw", [L, 128]); mnw_d = di("mnw", [L, D_MODEL])
    wkv_d = di("wkv", [L, D_MODEL, 1024]); wm_d = di("wm", [L, D_MODEL, 1024]); wd_d = di("wd", [L, D_MODEL, 2048])
    lamv_d = di("lamv", [L, 256]); dnw_d = di("dnw", [L, 128]); qx_d = di("qx", [8, 512]); alb_d = di("alb", [128, 128])
    li_d = di("li", [L, 2]); wgt_d = di("wgate", [L, D_MODEL, 3072]); wbr_d = di("wbrm", [L, 1536, D_MODEL])
    wout_d = di("wout", [L, D_MODEL, D_MODEL]); postw_d = di("postw", [L, D_MODEL])
    xo_d = C.dram_out("xo", [TB, D_MODEL])
    it = lambda name, shape: nc.dram_tensor(name, list(shape), F32, addr_space="Local", kind="Internal").ap()
    oc_i = it("oc_i", [S, 1536])
    itb = lambda name, shape: nc.dram_tensor(name, list(shape), BF16, addr_space="Local", kind="Internal").ap()
    yp_i = itb("yp_i", [S, D_MODEL])
    ys_i = itb("ys_i", [TB, D_MODEL])
    xh_i = it("xh_i", [TB, D_MODEL])
    xf_i = it("xf_i", [S, D_MODEL])
    C.gst = contextlib.ExitStack()
    C.st = C.gst
    C.pfx = "g_"
    C.K = _consts(C)
    for l in range(L):
        xs = x_d if l == 0 else xf_i
        build_gdn(S, 99, C, dict(x=xs, wg=wg_d[l], convw=cw_d[l], prew=prew_d[l], alog=alog_d[l], dtb=dtb_d[l],
                                 gnw=gnw_d[l], o_gdn=oc_i[:, 0:512], mem=mem_d, mnw=mnw_d[l], wkv=wkv_d[l], wm=wm_d[l],
                                 o_mem=oc_i[:, 1024:1536]), tag=f"L{l}a1")
        build_diff(S, C, dict(x=xs, wd=wd_d[l], prew=prew_d[l], lamv=lamv_d[l], dnw=dnw_d[l], qx=qx_d, alb=alb_d,
                              li=li_d[l], o_diff=oc_i[:, 512:1024]), tag=f"L{l}a2")
        phase_b1(C, S, dict(x=xs, oc=oc_i, wgate=wgt_d[l], wbrm=wbr_d[l], prew=prew_d[l], yp=yp_i), tag=f"L{l}b1")
        with C.phase(f"L{l}rs"):
            P.coll(lambda e: e.collective_compute("ReduceScatter", ALU.add, replica_groups=PAIRS, ins=[yp_i],
                                                  outs=[ys_i]), semname="cc_rs")
        last = (l == L - 1)
        phase_b2(C, TB, dict(ysum=ys_i, xres=(xh_d if l == 0 else xh_i), wout=wout_d[l], postw=postw_d[l],
                             xout=(xo_d if last else xh_i)), tag=f"L{l}b2")
        if not last:
            with C.phase(f"L{l}ag"):
                P.coll(lambda e: e.collective_compute("AllGather", ALU.bypass, replica_groups=PAIRS, ins=[xh_i],
                                                      outs=[xf_i]), semname="cc_ag")
    P.finish()
    C.gst.close()
    return nc


_PROGS = {}


def _c(a):
    return np.ascontiguousarray(a, dtype=np.float32)


def _core_inputs(r, L, w_in, gdn_conv_w, gdn_a_log, gdn_dt_bias, w_mem_kv, w_br_gdn, w_br_diff, w_br_mem):
    sl = lambda base: slice(base + r * 512, base + r * 512 + 512)
    wg, wd, wm, cw, wkv, wbrm, wgate = [], [], [], [], [], [], []
    for l in range(L):
        wl = np.asarray(w_in[l], np.float32)
        wg.append(np.concatenate([wl[:, sl(0)], wl[:, sl(1024)], wl[:, sl(2048)], wl[:, 3072 + r * 4:3076 + r * 4],
                                  wl[:, 3080 + r * 4:3084 + r * 4], wl[:, sl(3088)]], axis=1))
        wd.append(np.concatenate([wl[:, sl(4112)], wl[:, sl(5136)], wl[:, sl(6160)], wl[:, sl(7184)]], axis=1))
        wm.append(np.concatenate([wl[:, sl(8208)], wl[:, sl(9232)]], axis=1))
        wgate.append(wl[:, 10256:13328])
        cwl = np.asarray(gdn_conv_w[l], np.float32)
        cw.append(np.concatenate([cwl[:, sl(0)], cwl[:, sl(1024)], cwl[:, sl(2048)]], axis=1).T)
        kvl = np.asarray(w_mem_kv[l], np.float32)
        wkv.append(np.concatenate([kvl[:, sl(0)], kvl[:, sl(1024)]], axis=1))
        wbrm.append(np.concatenate([np.asarray(w_br_gdn[l])[sl(0)], np.asarray(w_br_diff[l])[sl(0)],
                                    np.asarray(w_br_mem[l])[sl(0)]], axis=0))
    st = lambda xs: _c(np.stack(xs))
    return dict(wg=st(wg), wd=st(wd), wm=st(wm), convw=st(cw), wkv=st(wkv), wbrm=st(wbrm), wgate=st(wgate),
                alog=_c(np.asarray(gdn_a_log)[:L, r * 4:r * 4 + 4]), dtb=_c(np.asarray(gdn_dt_bias)[:L, r * 4:r * 4 + 4]))


LAYERS_PER_LAUNCH = 1


def kernel(x, mem, pre_norm_w, post_norm_w, w_in, gdn_conv_w, gdn_a_log, gdn_dt_bias, gdn_norm_w, diff_lambda,
           diff_norm_w, mem_norm_w, w_mem_kv, w_br_gdn, w_br_diff, w_br_mem, w_out):
    x = np.asarray(x, np.float32)
    B, S, D = x.shape
    L = np.asarray(w_in).shape[0]
    TB = S // 2
    G = LAYERS_PER_LAUNCH
    key = (S, G)
    if key not in _PROGS:
        _PROGS[key] = build_fused(S, G)
    nc = _PROGS[key]
    li_all = np.array([[-(0.8 - 0.6 * math.exp(-0.3 * l)), 1.0 - (0.8 - 0.6 * math.exp(-0.3 * l))] for l in range(L)],
                      np.float32)
    consts = [diff_consts(r) for r in range(2)]
    for l0 in range(0, L, G):
        ls = slice(l0, l0 + G)
        shared = dict(prew=_c(np.asarray(pre_norm_w)[ls]), postw=_c(np.asarray(post_norm_w)[ls]),
                      gnw=_c(np.asarray(gdn_norm_w)[ls]), mnw=_c(np.asarray(mem_norm_w)[ls]),
                      lamv=_c(np.asarray(diff_lambda)[ls].reshape(G, 256)), dnw=_c(np.asarray(diff_norm_w)[ls]),
                      wout=_c(np.asarray(w_out)[ls]), li=_c(li_all[ls]))
        per_r = []
        for r in range(2):
            d = _core_inputs(r, G, np.asarray(w_in)[ls], np.asarray(gdn_conv_w)[ls], np.asarray(gdn_a_log)[ls],
                             np.asarray(gdn_dt_bias)[ls], np.asarray(w_mem_kv)[ls], np.asarray(w_br_gdn)[ls],
                             np.asarray(w_br_diff)[ls], np.asarray(w_br_mem)[ls])
            d.update(qx=consts[r][0], alb=consts[r][1])
            d.update(shared)
            per_r.append(d)
        in_maps = []
        for c in range(8):
            b, r = c // 2, c % 2
            m = dict(per_r[r])
            m.update(x=_c(x[b]), xhalf=_c(x[b, r * TB:(r + 1) * TB]), mem=_c(np.asarray(mem)[b]))
            in_maps.append(m)
        res = run_bass_kernel_spmd(nc, in_maps, core_ids=list(range(8))).results
        xn = np.empty((B, S, D), np.float32)
        for c in range(8):
            b, r = c // 2, c % 2
            xn[b, r * TB:(r + 1) * TB] = res[c]["xo"]
        x = xn
    return x
```

```python
import contextlib
import math
import numpy as np
import concourse.bass as bass
import concourse.mybir as mybir
from concourse.bass_utils import run_bass_kernel_spmd

F32 = mybir.dt.float32
F32R = mybir.dt.float32r
BF16 = mybir.dt.bfloat16
ALU = mybir.AluOpType
AF = mybir.ActivationFunctionType

D_MODEL = 1024
BATCH = 4
SEQ = 4096
DEPTH = 4
MEM_LEN = 256
EPS = 1e-6
IN_COLS = 13328
NEG = -1.0e30

ENGS = ("pe", "act", "dve", "pool", "sp")


class Buf:
    __slots__ = ("name", "w", "r", "excl", "multi")

    def __init__(self, name="", excl=False, multi=False):
        self.name = name
        self.w = {}
        self.r = {}
        self.multi = multi
        self.excl = excl


class Prog:
    def __init__(self, nc, same_engine_sync=True):
        self.nc = nc
        self.q = {e: [] for e in ENGS}
        self.cnt = {e: 0 for e in ENGS}
        self.seen = {e: {} for e in ENGS}
        self.dma_sems = {}
        self.same = same_engine_sync
        self.sem_handles = {}
        self.gst = None

    def _need(self, eng, reads, writes):
        need = {}

        def add(d):
            for k, v in d.items():
                if need.get(k, 0) < v:
                    need[k] = v
        for b in reads:
            add(b.w)
            if b.excl:
                add({k: v for k, v in b.r.items() if k != ("e", eng)})
        for b in writes:
            if b.multi:
                continue
            add(b.w)
            add(b.r)
        out = []
        seen = self.seen[eng]
        for k, v in need.items():
            if not self.same and k == ("e", eng):
                continue
            if seen.get(k, 0) >= v:
                continue
            seen[k] = v
            out.append((k, v))
        return out

    def op(self, eng, fn, reads=(), writes=()):
        waits = self._need(eng, reads, writes)
        self.cnt[eng] += 1
        c = self.cnt[eng]
        key = ("e", eng)
        self.q[eng].append((waits, fn, key, 1))
        for b in writes:
            b.w = {key: c}
            b.r = {}
        for b in reads:
            if b.r.get(key, 0) < c:
                b.r[key] = c

    def dma(self, eng, out_ap, in_ap, reads=(), writes=(), semname=None, **kw):
        waits = self._need(eng, reads, writes)
        key = ("d", semname)
        self.dma_sems[semname] = self.dma_sems.get(semname, 0) + 16
        c = self.dma_sems[semname]

        def fn(e, out_ap=out_ap, in_ap=in_ap, kw=kw):
            return e.dma_start(out=out_ap, in_=in_ap, **kw)
        self.q[eng].append((waits, fn, key, 16))
        for b in writes:
            if b.multi:
                b.w[key] = c
                continue
            b.w = {key: c}
            b.r = {}
        for b in reads:
            if b.r.get(key, 0) < c:
                b.r[key] = c

    def coll(self, fn, reads=(), writes=(), semname=None):
        waits = self._need("pool", reads, writes)
        key = ("d", semname)
        self.dma_sems[semname] = self.dma_sems.get(semname, 0) + 1
        c = self.dma_sems[semname]
        self.q["pool"].append((waits, fn, key, 1))
        for b in writes:
            b.w = {key: c}
            b.r = {}
        for b in reads:
            if b.r.get(key, 0) < c:
                b.r[key] = c

    def final_waits(self, eng, bufs):
        waits = self._need(eng, bufs, ())
        self.q[eng].append((waits, None, None, 0))

    def barrier(self):
        allk = [(("e", e), c) for e, c in self.cnt.items() if c > 0]
        allk += [(("d", n), c) for n, c in self.dma_sems.items()]
        for eng in ENGS:
            seen = self.seen[eng]
            waits = []
            for k, v in allk:
                if seen.get(k, 0) >= v:
                    continue
                seen[k] = v
                waits.append((k, v))
            self.q[eng].append((waits, None, None, 0))

    def _sem(self, key):
        if key not in self.sem_handles:
            if self.gst is None:
                self.gst = contextlib.ExitStack()
            nm = ("se_" if key[0] == "e" else "sd_") + key[1]
            self.sem_handles[key] = self.gst.enter_context(self.nc.semaphore(nm))
        return self.sem_handles[key]

    def flush(self):
        nc = self.nc
        for e in ENGS:
            self._sem(("e", e))
        for lst in self.q.values():
            for waits, fn, key, inc in lst:
                for k, v in waits:
                    self._sem(k)
                if key is not None:
                    self._sem(key)
        H = self.sem_handles
        q = self.q
        self.q = {e: [] for e in ENGS}
        with nc.Block() as block:
            def run(engobj, lst):
                for waits, fn, key, inc in lst:
                    for k, v in waits:
                        engobj.wait_ge(H[k], v)
                    if fn is not None:
                        fn(engobj).then_inc(H[key], inc)

            @block.tensor
            def _(e):
                run(e, q["pe"])

            @block.scalar
            def _(e):
                run(e, q["act"])

            @block.vector
            def _(e):
                run(e, q["dve"])

            @block.gpsimd
            def _(e):
                run(e, q["pool"])

            @block.sync
            def _(e):
                run(e, q["sp"])

    def emit(self):
        self.flush()

    def finish(self):
        if self.gst is not None:
            self.gst.close()
            self.gst = None


class Ctx:
    def __init__(self, nc):
        self.nc = nc
        self.P = Prog(nc)
        self.st = contextlib.ExitStack()
        self.n = 0
        self.rot = {}
        self.pfx = ""
        self.K = None
        self.gst = None

    @contextlib.contextmanager
    def phase(self, name):
        self.pfx = name + "_"
        self.rot = {}
        self.st = contextlib.ExitStack()
        with self.st:
            yield
            self.P.barrier()
            self.P.flush()

    def sb(self, shape, dt=F32, name=None):
        self.n += 1
        nm = name or f"sb{self.n}"
        t = self.st.enter_context(self.nc.sbuf_tensor(self.pfx + nm, list(shape), dt))
        return t, Buf(nm)

    def ps(self, shape, dt=F32, name=None):
        self.n += 1
        nm = name or f"ps{self.n}"
        t = self.st.enter_context(self.nc.psum_tensor(self.pfx + nm, list(shape), dt))
        return t, Buf(nm, excl=True)

    def pool(self, tag, n, shape, dt=F32, psum=False):
        self.rot[tag] = [[(self.ps if psum else self.sb)(shape, dt, f"{tag}{i}") for i in range(n)], 0]

    def get(self, tag):
        r = self.rot[tag]
        t = r[0][r[1] % len(r[0])]
        r[1] += 1
        return t

    def dram_in(self, name, shape, dt=F32):
        return self.nc.dram_tensor(name, list(shape), dt, kind="ExternalInput").ap()

    def dram_out(self, name, shape, dt=F32):
        return self.nc.dram_tensor(name, list(shape), dt, kind="ExternalOutput").ap()


def _r(ap):
    return ap


def _consts(C):
    if C.K is not None:
        return C.K
    P = C.P
    K = {}
    ident, bi = C.sb([128, 128], F32, "ident")
    ones, bo = C.sb([128, 128], F32, "ones")
    triu, bt = C.sb([128, 128], F32, "triu")
    mincl, bm1 = C.sb([128, 128], F32, "mincl")
    mstr, bm2 = C.sb([128, 128], F32, "mstr")
    identb, bib = C.sb([128, 128], BF16, "identb")
    epst, be = C.sb([128, 1], F32, "epst")

    def mk0(e):
        e.memset(ident[:], 0.0)
        e.memset(ones[:], 1.0)
        e.memset(triu[:], 1.0)
        e.memset(mincl[:], 0.0)
        e.memset(mstr[:], 0.0)
        return e.memset(epst[:], EPS)
    P.op("pool", mk0, writes=[bi, bo, bt, bm1, bm2, be])

    def mk(e):
        e.affine_select(out=ident[:], in_=ident[:], pattern=[[-1, 128]], compare_op=ALU.not_equal, fill=1.0,
                        base=0, channel_multiplier=1)
        e.affine_select(out=triu[:], in_=triu[:], pattern=[[1, 128]], compare_op=ALU.is_ge, fill=0.0,
                        base=0, channel_multiplier=-1)
        e.affine_select(out=mincl[:], in_=mincl[:], pattern=[[1, 128]], compare_op=ALU.is_ge, fill=NEG,
                        base=0, channel_multiplier=-1)
        return e.affine_select(out=mstr[:], in_=mstr[:], pattern=[[1, 128]], compare_op=ALU.is_gt, fill=NEG,
                               base=0, channel_multiplier=-1)
    P.op("pool", mk, reads=[bi, bt, bm1, bm2], writes=[bi, bt, bm1, bm2])
    P.op("pool", lambda e: e.tensor_copy(out=identb[:], in_=ident[:]), reads=[bi], writes=[bib])
    K.update(ident=(ident, bi), ones=(ones, bo), triu=(triu, bt), mincl=(mincl, bm1), mstr=(mstr, bm2),
             identb=(identb, bib), eps=(epst, be))
    return K


def _bcast_load(C, dram_vec, n, name):
    t, b = C.sb([128, n], F32, name + "_bc")
    src = dram_vec.partition_broadcast(128)
    C.P.dma("sp", t[:], src, writes=[b], semname="ld_" + name)
    return t, b


def _load_w(C, wt, wb, wdram, ncols, semname):
    src = wdram.rearrange("(kc p) n -> p kc n", p=128)
    for kc in range(8):
        C.P.dma("pool", wt[:, kc, :], src[:, kc, :], writes=[wb] if kc == 0 else [], reads=[], semname=semname)
    wb.w = {("d", semname): C.P.dma_sems[semname]}


def _norm_part(C, K, x_dram, st, prew, n_tiles=4):
    P = C.P
    epst, be = K["eps"]
    hbs = []
    for t in range(n_tiles):
        xt, bx = C.get("xt")
        r0 = (st * n_tiles + t) * 128
        P.dma("sp", xt[:], x_dram[r0:r0 + 128, :], writes=[bx], semname="ld_" + bx.name)
        sq, bsq = C.get("junk")
        ss, bss = C.get("col")
        P.op("act", lambda e, xt=xt, sq=sq, ss=ss: e.activation(out=sq[:, 0:1024], in_=xt[:], func=AF.Square,
                                                                 accum_out=ss[:, 0:1]),
             reads=[bx], writes=[bsq, bss])
        P.op("act", lambda e, ss=ss: e.activation(out=ss[:, 1:2], in_=ss[:, 0:1], func=AF.Sqrt, bias=epst[:, 0:1],
                                                  scale=1.0 / D_MODEL), reads=[bss, be], writes=[bss])
        P.op("dve", lambda e, ss=ss: e.reciprocal(out=ss[:, 2:3], in_=ss[:, 1:2]), reads=[bss], writes=[bss])
        hb, bhb = C.get("hb")
        P.op("dve", lambda e, hb=hb, xt=xt, ss=ss: e.scalar_tensor_tensor(
            out=hb[:], in0=xt[:], scalar=ss[:, 2:3], in1=prew[0][:], op0=ALU.mult, op1=ALU.mult),
            reads=[bx, bss, prew[1]], writes=[bhb])
        hbs.append((hb, bhb))
    return hbs


def _tr_part(C, K, hbs, hT, hTb):
    P = C.P
    identb, bib = K["identb"]
    for t, (hb, bhb) in enumerate(hbs):
        pt, bpt = C.get("pst")

        def tr(e, hb=hb, pt=pt):
            for kc in range(8):
                ins = e.transpose(pt[:, kc * 128:(kc + 1) * 128], hb[:, kc * 128:(kc + 1) * 128], identb[:])
            return ins
        P.op("pe", tr, reads=[bhb, bib], writes=[bpt])
        P.op("act", lambda e, pt=pt, t=t: e.copy(out=hT[:, :, t * 128:(t + 1) * 128],
                                                 in_=pt[:, :].rearrange("p (k n) -> p k n", k=8)),
             reads=[bpt], writes=[hTb])


def _make_hT(C, K, x_dram, st, prew, hT, hTb, n_tiles=4):
    _tr_part(C, K, _norm_part(C, K, x_dram, st, prew, n_tiles), hT, hTb)


def build_gdn(S, stage=99, C=None, io=None, tag="a1"):
    standalone = C is None
    if standalone:
        nc = bass.Bass("TRN2", target_bir_lowering=False)
        C = Ctx(nc)
        io = dict(x=C.dram_in("x", [S, D_MODEL]), wg=C.dram_in("wg", [D_MODEL, 2056]),
                  convw=C.dram_in("convw", [1536, 4]), prew=C.dram_in("prew", [D_MODEL]),
                  alog=C.dram_in("alog", [4]), dtb=C.dram_in("dtb", [4]), gnw=C.dram_in("gnw", [128]),
                  o_gdn=C.dram_out("o_gdn", [S, 512]), mem=C.dram_in("mem", [MEM_LEN, D_MODEL]),
                  mnw=C.dram_in("mnw", [D_MODEL]), wkv=C.dram_in("wkv", [D_MODEL, 1024]),
                  wm=C.dram_in("wm", [D_MODEL, 1024]), o_mem=C.dram_out("o_mem", [S, 512]))
    nc = C.nc
    P = C.P
    NT = S // 128
    NST = S // 512
    x_d, wg_d, convw_d, prew_d = io["x"], io["wg"], io["convw"], io["prew"]
    alog_d, dtb_d, gnw_d, o_d = io["alog"], io["dtb"], io["gnw"], io["o_gdn"]
    mem_d, mnw_d, wkv_d, wm_d, om_d = io["mem"], io["mnw"], io["wkv"], io["wm"], io["o_mem"]
    bo_d = Buf("o_d", multi=True)
    bom_d = Buf("om_d", multi=True)
    with C.phase(tag):
        K = _consts(C)
        ident, bi = K["ident"]; ones, bon = K["ones"]; triu, btr = K["triu"]
        mincl, bmi = K["mincl"]; mstr, bms = K["mstr"]; epst, be = K["eps"]
        prew = _bcast_load(C, prew_d, D_MODEL, "prew")
        gnw = _bcast_load(C, gnw_d, 128, "gnw")
        alog = _bcast_load(C, alog_d, 4, "alog")
        dtb = _bcast_load(C, dtb_d, 4, "dtb")
        cw, bcw = C.sb([128, 12, 4], F32, "cw")
        P.dma("sp", cw[:], convw_d.rearrange("(c p) j -> p c j", p=128), writes=[bcw], semname="ld_cw")
        negA, bnA = C.sb([128, 4], F32, "negA")
        P.op("act", lambda e: e.activation(out=negA[:], in_=alog[0][:], func=AF.Exp), reads=[alog[1]], writes=[bnA])
        P.op("dve", lambda e: e.tensor_scalar(out=negA[:], in0=negA[:], scalar1=-1.0, scalar2=None, op0=ALU.mult),
             reads=[bnA], writes=[bnA])
        wg, bwg = C.sb([128, 8, 2056], BF16, "wg_sb")
        _load_w(C, wg, bwg, wg_d, 2056, "ld_wg")
        hT, bhT = C.sb([128, 8, 512], BF16, "hT")
        cin, _ = C.sb([128, 12, 515], F32, "cin")
        qkv, _ = C.sb([128, 12, 512], F32, "qkvT")
        bcins = [Buf(f"cin{c}") for c in range(12)]
        bqkvs = [Buf(f"qkv{c}") for c in range(12)]
        S_t = [C.sb([128, 128], F32, f"S{h}") for h in range(4)]
        C.pool("xt", 2, [128, 1024], F32)
        C.pool("junk", 1, [128, 1024], F32)
        C.pool("col", 10, [128, 8], F32)
        C.pool("hb", 4, [128, 1024], BF16)
        C.pool("pst", 1, [128, 1024], BF16, psum=True)
        C.pool("ps", 4, [128, 512], F32, psum=True)
        C.pool("ps2", 3, [128, 512], F32, psum=True)
        C.pool("cacc", 2, [128, 512], F32)
        C.pool("sz", 2, [128, 512], F32)
        C.pool("sm", 4, [128, 32], F32)
        hm = [[C.sb([128, 128], F32, f"hm{h}_{i}") for i in range(13)] for h in range(4)]
        hpb = [[C.sb([128, 256], F32, f"hpb{h}_{i}") for i in range(2)] for h in range(4)]
        C.pool("ocat", 2, [128, 512], F32)
        P.op("pool", lambda e: e.memset(cin[:, :, 0:3], 0.0), writes=bcins)
        mnw = _bcast_load(C, mnw_d, D_MODEL, "mnw")
        wm, bwm = C.sb([128, 8, 1024], BF16, "wm_sb")
        wkv, bwkv = wm, bwm
        _load_w(C, wkv, bwkv, wkv_d, 1024, "ld_wkv")
        mT, bmT = C.sb([128, 8, 256], BF16, "mT")
        mkT, bmk = C.sb([128, 4, 256], BF16, "mkT")
        mva, bmv = C.sb([128, 2, 2, 257], BF16, "mva")
        mq, bmq = C.sb([128, 4, 512], BF16, "mq")
        C.pool("pTm", 2, [128, 128], BF16)
        C.pool("smz", 2, [128, 512], F32)
        C.pool("omem", 2, [128, 512], F32)
        _make_hT(C, K, mem_d, 0, mnw, mT, bmT, n_tiles=2)
        P.op("pool", lambda e: e.memset(mva[:, :, :, 256:257], 1.0), writes=[bmv])
        for c in range(4):
            pp, bpp = C.get("ps")

            def mmk_(e, pp=pp, c=c):
                for kc in range(8):
                    ins = e.matmul(pp[:, 0:256], lhsT=wkv[:, kc, c * 128:(c + 1) * 128], rhs=mT[:, kc, :],
                                   start=(kc == 0), stop=(kc == 7))
                return ins
            P.op("pe", mmk_, reads=[bwkv, bmT], writes=[bpp])
            P.op("act", lambda e, pp=pp, c=c: e.copy(out=mkT[:, c, :], in_=pp[:, 0:256]), reads=[bpp], writes=[bmk])
        for mt in range(2):
            pp, bpp = C.get("ps")

            def mmv_(e, pp=pp, mt=mt):
                for kc in range(8):
                    ins = e.matmul(pp[:, :], lhsT=mT[:, kc, mt * 128:(mt + 1) * 128], rhs=wkv[:, kc, 512:1024],
                                   start=(kc == 0), stop=(kc == 7))
                return ins
            P.op("pe", mmv_, reads=[bwkv, bmT], writes=[bpp])
            P.op("act", lambda e, pp=pp, mt=mt: e.copy(out=mva[:, mt, :, 0:256],
                                                       in_=pp[:, :].rearrange("p (h e) -> p h e", h=2)),
                 reads=[bpp], writes=[bmv])
        _load_w(C, wm, bwm, wm_d, 1024, "ld_wm")
        for h in range(4):
            P.op("pool", lambda e, h=h: e.tensor_tensor(out=_r(S_t[h][0][:]), in0=ident[:], in1=ident[:], op=ALU.subtract),
                 reads=[bi], writes=[S_t[h][1]])

        hbs_next = _norm_part(C, K, x_d, 0, prew)
        for st in range(NST):
            _tr_part(C, K, hbs_next, hT, bhT)
            for c in range(12):
                pp, bpp = C.get("ps")
                bcin = bcins[c]
                bqkv = bqkvs[c]

                def mm(e, pp=pp, c=c):
                    for kc in range(8):
                        ins = e.matmul(pp[:, :], lhsT=wg[:, kc, c * 128:(c + 1) * 128], rhs=hT[:, kc, :],
                                       start=(kc == 0), stop=(kc == 7))
                    return ins
                P.op("pe", mm, reads=[bwg, bhT], writes=[bpp])
                P.op("act", lambda e, pp=pp, c=c: e.copy(out=cin[:, c, 3:515], in_=pp[:, :]), reads=[bpp], writes=[bcin])
                acc, bacc = C.get("cacc")

                P.op("dve", lambda e, acc=acc, c=c: e.tensor_scalar(
                    out=acc[:], in0=cin[:, c, 0:512], scalar1=cw[:, c, 0:1], scalar2=None, op0=ALU.mult),
                    reads=[bcin, bcw], writes=[bacc])
                for j in range(1, 4):
                    P.op("dve", lambda e, acc=acc, c=c, j=j: e.scalar_tensor_tensor(
                        out=acc[:], in0=cin[:, c, j:j + 512], scalar=cw[:, c, j:j + 1], in1=acc[:], op0=ALU.mult,
                        op1=ALU.add), reads=[bcin, bcw, bacc], writes=[bacc])
                P.op("pool", lambda e, c=c: e.tensor_copy(out=cin[:, c, 0:3], in_=cin[:, c, 512:515]),
                     reads=[bcin, bacc], writes=[bcin])
                P.op("act", lambda e, acc=acc, c=c: e.activation(out=_r(qkv[:, c, :]), in_=acc[:], func=AF.Silu),
                     reads=[bacc], writes=[bqkv])
            for c in range(4):
                pp, bpp = C.get("ps")

                def mmq_(e, pp=pp, c=c):
                    for kc in range(8):
                        ins = e.matmul(pp[:, :], lhsT=wm[:, kc, c * 128:(c + 1) * 128], rhs=hT[:, kc, :],
                                       start=(kc == 0), stop=(kc == 7))
                    return ins
                P.op("pe", mmq_, reads=[bwm, bhT], writes=[bpp])
                P.op("act", lambda e, pp=pp, c=c: e.copy(out=mq[:, c, :], in_=pp[:, :]), reads=[bpp], writes=[bmq])
            if st + 1 < NST:
                hbs_next = _norm_part(C, K, x_d, st + 1, prew)
            tile_res = {}

            def pro_gen(t):
                tsl = slice(t * 128, (t + 1) * 128)
                r0 = (st * 4 + t) * 128
                pmz, bpmz = C.get("ps2")

                def mmmz(e, pmz=pmz, tsl=tsl):
                    for kc in range(8):
                        ins = e.matmul(pmz[:, :], lhsT=hT[:, kc, tsl], rhs=wm[:, kc, 512:1024], start=(kc == 0),
                                       stop=(kc == 7))
                    return ins
                P.op("pe", mmmz, reads=[bwm, bhT], writes=[bpmz])
                smz, bsmz = C.get("smz")
                P.op("act", lambda e, smz=smz, pmz=pmz: e.activation(out=smz[:], in_=pmz[:, :], func=AF.Silu),
                     reads=[bpmz], writes=[bsmz])
                yield
                omem, bomem = C.get("omem")
                for hd in range(2):
                    accM, baM = C.get("ps2")
                    for mt in range(2):
                        scm, bscm = C.get("ps2")

                        def mmsc(e, scm=scm, hd=hd, mt=mt, tsl=tsl):
                            for dc in range(2):
                                ins = e.matmul(scm[:, 0:128], lhsT=mkT[:, hd * 2 + dc, mt * 128:(mt + 1) * 128],
                                               rhs=mq[:, hd * 2 + dc, tsl], start=(dc == 0), stop=(dc == 1))
                            return ins
                        P.op("pe", mmsc, reads=[bmk, bmq], writes=[bscm])
                        pTm, bpTm = C.get("pTm")
                        P.op("act", lambda e, pTm=pTm, scm=scm: e.activation(out=pTm[:], in_=scm[:, 0:128], func=AF.Exp,
                                                                             scale=1.0 / 16.0), reads=[bscm], writes=[bpTm])
                        P.op("pe", lambda e, accM=accM, pTm=pTm, mt=mt, hd=hd: e.matmul(
                            accM[:, 0:257], lhsT=pTm[:], rhs=mva[:, mt, hd, :], start=(mt == 0), stop=(mt == 1)),
                            reads=[bpTm, bmv], writes=[baM])
                        yield
                    cl, bcl = C.get("col")
                    P.op("dve", lambda e, cl=cl, accM=accM: e.reciprocal(out=cl[:, 0:1], in_=accM[:, 256:257]),
                         reads=[baM], writes=[bcl])
                    P.op("dve", lambda e, omem=omem, accM=accM, cl=cl, smz=smz, hd=hd: e.scalar_tensor_tensor(
                        out=omem[:, hd * 256:(hd + 1) * 256], in0=accM[:, 0:256], scalar=cl[:, 0:1],
                        in1=smz[:, hd * 256:(hd + 1) * 256], op0=ALU.mult, op1=ALU.mult),
                        reads=[baM, bcl, bsmz, bomem], writes=[bomem])
                    yield
                P.dma("sp", om_d[r0:r0 + 128, :], omem[:], reads=[bomem], writes=[bom_d], semname="st_" + bomem.name)
                pab, bpab = C.get("ps2")

                def mmab(e, pab=pab, tsl=tsl):
                    for kc in range(8):
                        ins = e.matmul(pab[:, 0:8], lhsT=hT[:, kc, tsl], rhs=wg[:, kc, 1536:1544],
                                       start=(kc == 0), stop=(kc == 7))
                    return ins
                P.op("pe", mmab, reads=[bwg, bhT], writes=[bpab])
                pz, bpz = C.get("ps2")

                def mmz(e, pz=pz, tsl=tsl):
                    for kc in range(8):
                        ins = e.matmul(pz[:, :], lhsT=hT[:, kc, tsl], rhs=wg[:, kc, 1544:2056],
                                       start=(kc == 0), stop=(kc == 7))
                    return ins
                P.op("pe", mmz, reads=[bwg, bhT], writes=[bpz])
                sz, bsz = C.get("sz")
                P.op("act", lambda e, sz=sz, pz=pz: e.activation(out=sz[:], in_=pz[:, :], func=AF.Silu),
                     reads=[bpz], writes=[bsz])
                yield
                sm, bsm = C.get("sm")

                def small1(e, sm=sm, pab=pab):
                    e.tensor_tensor(out=sm[:, 0:4], in0=pab[:, 0:4], in1=dtb[0][:], op=ALU.add)
                    return e.tensor_copy(out=sm[:, 4:8], in_=pab[:, 4:8])
                P.op("dve", small1, reads=[bpab, dtb[1]], writes=[bsm])
                yield

                def small2(e, sm=sm):
                    e.activation(out=sm[:, 0:4], in_=sm[:, 0:4], func=AF.Exp)
                    return e.activation(out=sm[:, 4:8], in_=sm[:, 4:8], func=AF.Exp, scale=-1.0)
                P.op("act", small2, reads=[bsm], writes=[bsm])
                P.op("act", lambda e, sm=sm: e.activation(out=sm[:, 0:4], in_=sm[:, 0:4], func=AF.Ln,
                                                          bias=ones[:, 0:1], scale=1.0), reads=[bsm, bon],
                     writes=[bsm])
                yield

                def small3(e, sm=sm):
                    e.tensor_tensor(out=sm[:, 0:4], in0=sm[:, 0:4], in1=negA[:], op=ALU.mult)
                    return e.tensor_scalar(out=sm[:, 4:8], in0=sm[:, 4:8], scalar1=1.0, scalar2=None, op0=ALU.add)
                P.op("dve", small3, reads=[bsm, bnA], writes=[bsm])
                P.op("dve", lambda e, sm=sm: e.reciprocal(out=sm[:, 4:8], in_=sm[:, 4:8]), reads=[bsm], writes=[bsm])
                yield
                P.op("act", lambda e, sm=sm: e.activation(out=sm[:, 8:12], in_=sm[:, 4:8], func=AF.Ln),
                     reads=[bsm], writes=[bsm])
                pg, bpg = C.get("ps2")

                def mmg(e, pg=pg, sm=sm):
                    e.matmul(pg[:, 0:4], lhsT=triu[:], rhs=sm[:, 0:4], start=True, stop=True)
                    return e.matmul(pg[:, 4:8], lhsT=ones[:], rhs=sm[:, 0:4], start=True, stop=True)
                P.op("pe", mmg, reads=[bsm, btr, bon], writes=[bpg])
                yield

                def small4(e, sm=sm, pg=pg):
                    e.tensor_copy(out=sm[:, 12:16], in_=pg[:, 0:4])
                    return e.tensor_scalar(out=sm[:, 16:20], in0=pg[:, 0:4], scalar1=-1.0, scalar2=None, op0=ALU.mult)
                P.op("dve", small4, reads=[bpg], writes=[bsm])
                P.op("dve", lambda e, sm=sm, pg=pg: e.tensor_tensor(out=sm[:, 28:32], in0=pg[:, 4:8], in1=sm[:, 12:16],
                                                                    op=ALU.subtract), reads=[bpg, bsm], writes=[bsm])

                def small5(e, sm=sm, pg=pg):
                    e.activation(out=sm[:, 20:24], in_=pg[:, 4:8], func=AF.Exp)
                    e.activation(out=sm[:, 24:28], in_=sm[:, 12:16], func=AF.Exp)
                    return e.activation(out=sm[:, 28:32], in_=sm[:, 28:32], func=AF.Exp)
                P.op("act", small5, reads=[bsm, bpg], writes=[bsm])
                yield
                P.op("dve", lambda e, sm=sm: e.tensor_tensor(out=sm[:, 24:28], in0=sm[:, 24:28], in1=sm[:, 4:8],
                                                             op=ALU.mult), reads=[bsm], writes=[bsm])
                ocat, boc = C.get("ocat")
                tile_res[t] = dict(sm=sm, bsm=bsm, sz=sz, bsz=bsz, ocat=ocat, boc=boc, tsl=tsl, r0=r0)

            for _ in pro_gen(0):
                pass
            for c in range(8 if stage >= 2 else 0):
                bqkv = bqkvs[c]
                sq, bsq = C.get("cacc")
                P.op("pool", lambda e, sq=sq, c=c: e.tensor_tensor(out=sq[:], in0=qkv[:, c, :], in1=qkv[:, c, :],
                                                                   op=ALU.mult), reads=[bqkv], writes=[bsq])
                pp, bpp = C.get("ps")
                P.op("pe", lambda e, pp=pp, sq=sq: e.matmul(pp[:, :], lhsT=ones[:], rhs=sq[:], start=True, stop=True),
                     reads=[bsq, bon], writes=[bpp])
                rn, brn = C.get("cacc")
                P.op("act", lambda e, rn=rn, pp=pp: e.activation(out=rn[:], in_=pp[:, :], func=AF.Sqrt,
                                                                  bias=epst[:, 0:1], scale=1.0),
                     reads=[bpp, be], writes=[brn])
                P.op("dve", lambda e, rn=rn: e.reciprocal(out=rn[:], in_=rn[:]), reads=[brn], writes=[brn])
                sc = (128.0 ** -0.5) if c < 4 else 1.0
                P.op("dve", lambda e, rn=rn, c=c, sc=sc: e.scalar_tensor_tensor(
                    out=_r(qkv[:, c, :]), in0=qkv[:, c, :], scalar=sc, in1=rn[:], op0=ALU.mult, op1=ALU.mult),
                    reads=[bqkv, brn], writes=[bqkv])
            for t in range(4):
                tr_ = tile_res[t]
                sm, bsm, sz, bsz = tr_['sm'], tr_['bsm'], tr_['sz'], tr_['bsz']
                ocat, boc, tsl, r0 = tr_['ocat'], tr_['boc'], tr_['tsl'], tr_['r0']

                def head_gen(h, sm=sm, bsm=bsm, sz=sz, bsz=bsz, ocat=ocat, boc=boc, tsl=tsl):
                    qT = qkv[:, h, tsl]
                    kT = qkv[:, 4 + h, tsl]
                    vT = qkv[:, 8 + h, tsl]
                    St, bS = S_t[h]
                    bqkv_h = [bqkvs[h], bqkvs[4 + h], bqkvs[8 + h]]
                    M = hm[h]
                    (gtri, bgt), (gtri2, bgt2), (E3, bE3), (E1, bE1), (E2, bE2) = M[0], M[1], M[2], M[3], M[4]
                    (Bm, bB), (attT, bat), (kb, bkb), (kd, bkd), (vb, bvb), (qd, bqd) = M[5], M[6], M[7], M[8], M[9], M[10]
                    P.op("pool", lambda e: e.tensor_scalar(
                        out=gtri[:], in0=triu[:], scalar1=sm[:, h:h + 1], scalar2=None, op0=ALU.mult),
                        reads=[bsm, btr], writes=[bgt])
                    P.op("dve", lambda e: e.scalar_tensor_tensor(
                        out=gtri2[:], in0=ident[:], scalar=sm[:, 8 + h:9 + h], in1=gtri[:], op0=ALU.mult, op1=ALU.add),
                        reads=[bsm, bi, bgt], writes=[bgt2])
                    yield
                    pX, bpX = C.get("ps")

                    def mmx(e):
                        e.matmul(pX[:, 0:128], lhsT=ones[:], rhs=gtri[:], start=True, stop=True)
                        e.matmul(pX[:, 128:256], lhsT=ones[:], rhs=gtri[:], start=True, stop=False)
                        e.matmul(pX[:, 128:256], lhsT=ident[:], rhs=mincl[:], start=False, stop=True)
                        e.matmul(pX[:, 256:384], lhsT=ones[:], rhs=gtri2[:], start=True, stop=False)
                        return e.matmul(pX[:, 256:384], lhsT=ident[:], rhs=mstr[:], start=False, stop=True)
                    P.op("pe", mmx, reads=[bgt, bgt2, bon, bi, bmi, bms], writes=[bpX])

                    def exps(e):
                        e.activation(out=_r(E3[:]), in_=pX[:, 0:128], func=AF.Exp)
                        e.activation(out=_r(E1[:]), in_=pX[:, 128:256], func=AF.Exp, bias=sm[:, 16 + h:17 + h], scale=1.0)
                        return e.activation(out=_r(E2[:]), in_=pX[:, 256:384], func=AF.Exp, bias=sm[:, 16 + h:17 + h],
                                            scale=1.0)
                    P.op("act", exps, reads=[bpX, bsm], writes=[bE3, bE1, bE2])
                    yield
                    pK, bpK = C.get("ps")

                    def mmk(e):
                        e.matmul(pK[:, 0:128], lhsT=_r(kT), rhs=_r(kT), start=True, stop=True)
                        e.matmul(pK[:, 128:256], lhsT=_r(kT), rhs=_r(qT), start=True, stop=True)
                        e.transpose(pK[:, 256:384], kT, ident[:])
                        return e.transpose(pK[:, 384:512], vT, ident[:])
                    P.op("pe", mmk, reads=bqkv_h + [bi], writes=[bpK])

                    def ev1(e):
                        e.tensor_tensor(out=_r(Bm[:]), in0=pK[:, 0:128], in1=E2[:], op=ALU.mult)
                        e.tensor_tensor(out=_r(attT[:]), in0=pK[:, 128:256], in1=E1[:], op=ALU.mult)
                        e.tensor_scalar(out=_r(kb[:]), in0=pK[:, 256:384], scalar1=sm[:, 24 + h:25 + h], scalar2=None,
                                        op0=ALU.mult)
                        e.tensor_scalar(out=_r(kd[:]), in0=pK[:, 256:384], scalar1=sm[:, 28 + h:29 + h], scalar2=None,
                                        op0=ALU.mult)
                        return e.tensor_scalar(out=_r(vb[:]), in0=pK[:, 384:512], scalar1=sm[:, 4 + h:5 + h], scalar2=None,
                                               op0=ALU.mult)
                    P.op("dve", ev1, reads=[bpK, bE1, bE2, bsm], writes=[bB, bat, bkb, bkd, bvb])
                    P.op("pool", lambda e: e.tensor_tensor(out=_r(qd[:]), in0=qT, in1=E3[:], op=ALU.mult),
                         reads=[bqkvs[h], bE3], writes=[bqd])
                    yield
                    pA, bpA = C.get("ps")
                    P.op("pe", lambda e: e.transpose(pA[:, 0:128], Bm[:], ident[:]), reads=[bB, bi], writes=[bpA])
                    PT, bPT = M[11]
                    P.op("act", lambda e: e.copy(out=_r(PT[:]), in_=pA[:, 0:128]), reads=[bpA], writes=[bPT])
                    PB, bPB = hpb[h][0]
                    P.op("pool", lambda e: e.tensor_tensor(out=_r(PB[:, 128:256]), in0=ident[:], in1=Bm[:], op=ALU.subtract),
                         reads=[bB, bi], writes=[bPB])
                    yield
                    pN, bpN = C.get("ps")

                    def n0(e, pN=pN, PT=PT):
                        e.matmul(pN[:, 0:128], lhsT=_r(PT[:]), rhs=_r(Bm[:]), start=True, stop=True)
                        return e.matmul(pN[:, 256:384], lhsT=_r(Bm[:]), rhs=_r(PT[:]), start=True, stop=True)
                    P.op("pe", n0, reads=[bPT, bB], writes=[bpN])
                    PT2, bPT2 = M[12]
                    P.op("act", lambda e, pN=pN: e.copy(out=_r(PT2[:]), in_=pN[:, 256:384]), reads=[bpN], writes=[bPT2])
                    P.op("dve", lambda e, pN=pN, PB=PB: e.tensor_copy(out=_r(PB[:, 0:128]), in_=pN[:, 0:128]), reads=[bpN],
                         writes=[bPB])
                    PT, bPT = PT2, bPT2
                    cur = 1
                    yield
                    for j in range(1, 7):
                        last = (j == 6)
                        pN, bpN = C.get("ps")

                        def nj(e, pN=pN, PT=PT, PB=PB, last=last):
                            if last:
                                return e.matmul(pN[:, 128:256], lhsT=_r(PT[:]), rhs=_r(PB[:, 128:256]), start=True, stop=True)
                            e.matmul(pN[:, 0:256], lhsT=_r(PT[:]), rhs=_r(PB[:, 0:256]), start=True, stop=True)
                            return e.matmul(pN[:, 256:384], lhsT=_r(PB[:, 0:128]), rhs=_r(PT[:]), start=True, stop=True)
                        P.op("pe", nj, reads=[bPT, bPB], writes=[bpN])
                        PBn, bPBn = hpb[h][j % 2]
                        if not last:
                            PTn, bPTn = M[11 + (1 - cur)]
                            P.op("act", lambda e, PTn=PTn, pN=pN, PBn=PBn: (
                                e.copy(out=_r(PTn[:]), in_=pN[:, 256:384]),
                                e.copy(out=_r(PBn[:, 0:128]), in_=pN[:, 0:128]))[1], reads=[bpN], writes=[bPTn, bPBn])
                        P.op("dve", lambda e, PBn=PBn, PB=PB, pN=pN: e.tensor_tensor(
                            out=_r(PBn[:, 128:256]), in0=PB[:, 128:256], in1=pN[:, 128:256], op=ALU.add),
                            reads=[bpN, bPB], writes=[bPBn])
                        PB, bPB = PBn, bPBn
                        if not last:
                            PT, bPT = PTn, bPTn
                            cur = 1 - cur
                        yield
                    TT = PB[:, 128:256]
                    pW, bpW = C.get("ps")
                    P.op("pe", lambda e: e.matmul(pW[:, 0:128], lhsT=_r(kb[:]), rhs=_r(TT), start=True, stop=True),
                         reads=[bkb, bPB], writes=[bpW])
                    nwT, bnw = M[2]
                    P.op("act", lambda e: e.activation(out=_r(nwT[:]), in_=pW[:, 0:128], func=AF.Copy, scale=-1.0),
                         reads=[bpW], writes=[bnw])
                    yield
                    pV, bpV = C.get("ps")

                    def mv(e):
                        e.matmul(pV[:, 0:128], lhsT=_r(TT), rhs=_r(vb[:]), start=True, stop=False)
                        return e.matmul(pV[:, 0:128], lhsT=_r(nwT[:]), rhs=_r(St[:]), start=False, stop=True)
                    P.op("pe", mv, reads=[bPB, bvb, bnw, bS], writes=[bpV])
                    vn, bvn = M[3]
                    P.op("act", lambda e: e.copy(out=_r(vn[:]), in_=pV[:, 0:128]), reads=[bpV], writes=[bvn])
                    yield
                    pO, bpO = C.get("ps")

                    def mo(e):
                        e.matmul(pO[:, 0:128], lhsT=_r(qd[:]), rhs=_r(St[:]), start=True, stop=False)
                        e.matmul(pO[:, 0:128], lhsT=_r(attT[:]), rhs=_r(vn[:]), start=False, stop=True)
                        return e.matmul(pO[:, 128:256], lhsT=_r(kd[:]), rhs=_r(vn[:]), start=True, stop=True)
                    P.op("pe", mo, reads=[bqd, bS, bat, bvn, bkd], writes=[bpO])
                    P.op("dve", lambda e: e.scalar_tensor_tensor(
                        out=_r(St[:]), in0=St[:], scalar=sm[:, 20 + h:21 + h], in1=pO[:, 128:256], op0=ALU.mult, op1=ALU.add),
                        reads=[bpO, bsm, bS], writes=[bS])
                    jk, bjk = M[4]
                    ss, bss = C.get("col")
                    P.op("act", lambda e: e.activation(out=_r(jk[:]), in_=pO[:, 0:128], func=AF.Square, accum_out=ss[:, 0:1]),
                         reads=[bpO], writes=[bjk, bss])
                    P.op("act", lambda e: e.activation(out=ss[:, 1:2], in_=ss[:, 0:1], func=AF.Sqrt, bias=epst[:, 0:1],
                                                       scale=1.0 / 128.0), reads=[bss, be], writes=[bss])
                    P.op("dve", lambda e: e.reciprocal(out=ss[:, 2:3], in_=ss[:, 1:2]), reads=[bss], writes=[bss])
                    P.op("dve", lambda e: e.scalar_tensor_tensor(
                        out=_r(jk[:]), in0=pO[:, 0:128], scalar=ss[:, 2:3], in1=gnw[0][:], op0=ALU.mult, op1=ALU.mult),
                        reads=[bpO, bss, gnw[1], bjk], writes=[bjk])
                    P.op("dve", lambda e: e.tensor_tensor(
                        out=ocat[:, h * 128:(h + 1) * 128], in0=jk[:], in1=sz[:, h * 128:(h + 1) * 128], op=ALU.mult),
                        reads=[bjk, bsz, boc], writes=[boc])

                gens = [head_gen(h) for h in range(4)]
                if t + 1 < 4:
                    gens.append(pro_gen(t + 1))
                while gens:
                    for g in list(gens):
                        try:
                            next(g)
                        except StopIteration:
                            gens.remove(g)
                P.dma("sp", o_d[r0:r0 + 128, :], ocat[:], reads=[boc], writes=[bo_d], semname="st_" + boc.name)
        P.final_waits("sp", [bo_d, bom_d])
    if standalone:
        P.finish()
    return nc


def build_diff(S, C=None, io=None, tag="a2"):
    standalone = C is None
    if standalone:
        nc = bass.Bass("TRN2", target_bir_lowering=False)
        C = Ctx(nc)
        io = dict(x=C.dram_in("x", [S, D_MODEL]), wd=C.dram_in("wd", [D_MODEL, 2048]),
                  prew=C.dram_in("prew", [D_MODEL]), lamv=C.dram_in("lamv", [256]), dnw=C.dram_in("dnw", [128]),
                  qx=C.dram_in("qx", [8, 512]), alb=C.dram_in("alb", [128, 128]), li=C.dram_in("li", [2]),
                  o_diff=C.dram_out("o_diff", [S, 512]))
    nc = C.nc
    P = C.P
    NT = S // 128
    NST = S // 512
    x_d, wd_d, prew_d, lamv_d, dnw_d = io["x"], io["wd"], io["prew"], io["lamv"], io["dnw"]
    qx_d, alb_d, li_d, o_d = io["qx"], io["alb"], io["li"], io["o_diff"]
    bo_d = Buf("o_d", multi=True)
    with C.phase(tag):
        K = _consts(C)
        ident, bi = K["ident"]; ones, bon = K["ones"]; mincl, bmi = K["mincl"]; epst, be = K["eps"]
        identb, bib = K["identb"]
        minclb, bmb = C.sb([128, 128], BF16, "minclb")
        P.op("pool", lambda e: e.tensor_copy(out=minclb[:], in_=mincl[:]), reads=[bmi], writes=[bmb])
        prew = _bcast_load(C, prew_d, D_MODEL, "prew")
        dnw = _bcast_load(C, dnw_d, 128, "dnw")
        lamv = _bcast_load(C, lamv_d, 256, "lamv")
        li = _bcast_load(C, li_d, 2, "li")
        alb, balb = C.sb([128, 128], F32, "alb_sb")
        P.dma("sp", alb[:], alb_d[:, :], writes=[balb], semname="ld_alb")
        lm, blm = C.sb([128, 8], F32, "lm")
        lp, blp = C.sb([128, 128], F32, "lp")

        def lam1(e):
            e.tensor_tensor(out=lp[:, 0:64], in0=lamv[0][:, 0:64], in1=lamv[0][:, 64:128], op=ALU.mult)
            return e.tensor_tensor(out=lp[:, 64:128], in0=lamv[0][:, 128:192], in1=lamv[0][:, 192:256], op=ALU.mult)
        P.op("dve", lam1, reads=[lamv[1]], writes=[blp])

        def lam2(e):
            e.reduce_sum(out=lm[:, 0:1], in_=lp[:, 0:64], axis=mybir.AxisListType.X)
            return e.reduce_sum(out=lm[:, 1:2], in_=lp[:, 64:128], axis=mybir.AxisListType.X)
        P.op("dve", lam2, reads=[blp], writes=[blm])
        P.op("act", lambda e: e.activation(out=lm[:, 2:4], in_=lm[:, 0:2], func=AF.Exp), reads=[blm], writes=[blm])
        P.op("dve", lambda e: e.tensor_tensor(out=lm[:, 4:5], in0=lm[:, 3:4], in1=lm[:, 2:3], op=ALU.subtract),
             reads=[blm], writes=[blm])
        P.op("dve", lambda e: e.tensor_scalar(out=lm[:, 5:6], in0=lm[:, 4:5], scalar1=li[0][:, 0:1], scalar2=None,
                                              op0=ALU.add), reads=[blm, li[1]], writes=[blm])
        P.op("dve", lambda e: e.tensor_scalar(out=dnw[0][:], in0=dnw[0][:], scalar1=li[0][:, 1:2], scalar2=None,
                                              op0=ALU.mult), reads=[dnw[1], li[1]], writes=[dnw[1]])
        wd, bwd = C.sb([128, 8, 2048], BF16, "wd_sb")
        _load_w(C, wd, bwd, wd_d, 2048, "ld_wd")
        hT, bhT = C.sb([128, 8, 512], BF16, "hT")
        ka = [[C.sb([66, S], BF16, f"ka{h}{m}") for m in range(2)] for h in range(4)]
        qa = [[C.sb([66, 512], BF16, f"qa{h}{m}") for m in range(2)] for h in range(4)]
        vaug, _ = C.sb([128, NT, 4, 129], BF16, "vaug")
        bv = [Buf(f"v{t}") for t in range(NT)]
        bka = [[[Buf(f"ka{h}{m}_{s}") for s in range(NST)] for m in range(2)] for h in range(4)]
        for h in range(4):
            for m in range(2):
                P.op("pool", lambda e, h=h, m=m: e.memset(ka[h][m][0][64:66, :], 1.0), writes=bka[h][m])
                P.dma("pool", qa[h][m][0][64:66, :], qx_d[2 * h:2 * h + 2, :], writes=[qa[h][m][1]], semname=f"ld_qx{h}{m}")
        P.op("pool", lambda e: e.memset(vaug[:, :, :, 128:129], 1.0), writes=bv)
        C.pool("xt", 2, [128, 1024], F32)
        C.pool("junk", 1, [128, 1024], F32)
        C.pool("col", 12, [128, 8], F32)
        C.pool("hb", 4, [128, 1024], BF16)
        C.pool("pst", 1, [128, 1024], BF16, psum=True)
        C.pool("ps", 3, [128, 512], F32, psum=True)
        C.pool("acc", 4, [128, 512], F32, psum=True)
        C.pool("pT", 4, [128, 512], BF16)
        C.pool("szd", 1, [128, 4, 512], F32)
        C.pool("o0", 2, [128, 4, 128], F32)
        C.pool("ob", 8, [128, 128], F32)
        C.pool("ocat", 1, [128, 4, 512], F32)

        hbs_next = _norm_part(C, K, x_d, 0, prew)
        for I in range(NST):
            _tr_part(C, K, hbs_next, hT, bhT)
            for h in range(4):
                for which, col0 in (("q", h * 128), ("k", 512 + h * 128)):
                    pp, bpp = C.get("ps")

                    def mm(e, pp=pp, col0=col0):
                        for kc in range(8):
                            ins = e.matmul(pp[:, :], lhsT=wd[:, kc, col0:col0 + 128], rhs=hT[:, kc, :],
                                           start=(kc == 0), stop=(kc == 7))
                        return ins
                    P.op("pe", mm, reads=[bwd, bhT], writes=[bpp])
                    for m in range(2):
                        if which == "q":
                            dst, bd = qa[h][m][0][0:64, :], qa[h][m][1]
                        else:
                            dst, bd = ka[h][m][0][0:64, I * 512:(I + 1) * 512], bka[h][m][I]
                        eng = "act" if m == 0 else "dve"
                        if eng == "act":
                            P.op("act", lambda e, dst=dst, pp=pp, m=m: e.copy(out=dst, in_=pp[64 * m:64 * m + 64, :]),
                                 reads=[bpp], writes=[bd])
                        else:
                            P.op("dve", lambda e, dst=dst, pp=pp, m=m: e.tensor_copy(out=dst, in_=pp[64 * m:64 * m + 64, :]),
                                 reads=[bpp], writes=[bd])
            szd, bszd = C.get("szd")
            for t in range(4):
                tsl = slice(t * 128, (t + 1) * 128)
                tile = I * 4 + t
                pv_, bpv = C.get("ps")

                def mmv(e, pv_=pv_, tsl=tsl):
                    for kc in range(8):
                        ins = e.matmul(pv_[:, :], lhsT=hT[:, kc, tsl], rhs=wd[:, kc, 1024:1536], start=(kc == 0),
                                       stop=(kc == 7))
                    return ins
                P.op("pe", mmv, reads=[bwd, bhT], writes=[bpv])
                P.op("dve", lambda e, pv_=pv_, tile=tile: e.tensor_copy(
                    out=vaug[:, tile, :, 0:128], in_=pv_[:, :].rearrange("p (h e) -> p h e", h=4)),
                    reads=[bpv], writes=[bv[tile]])
                pz, bpz = C.get("ps")

                def mmz(e, pz=pz, tsl=tsl):
                    for kc in range(8):
                        ins = e.matmul(pz[:, :], lhsT=hT[:, kc, tsl], rhs=wd[:, kc, 1536:2048], start=(kc == 0),
                                       stop=(kc == 7))
                    return ins
                P.op("pe", mmz, reads=[bwd, bhT], writes=[bpz])
                P.op("act", lambda e, szd=szd, pz=pz, t=t: e.activation(out=szd[:, t, :], in_=pz[:, :], func=AF.Silu),
                     reads=[bpz], writes=[bszd])
            if I + 1 < NST:
                hbs_next = _norm_part(C, K, x_d, I + 1, prew)
            ocat, boc = C.get("ocat")
            nj = 4 * I + 4
            items = []
            for h in range(4):
                o0, bo0 = C.get("o0")
                for m in range(2):
                    accA, baA = C.get("acc")
                    accB, baB = C.get("acc")
                    for j in range(nj):
                        items.append(dict(h=h, m=m, j=j, o0=o0, bo0=bo0, accA=accA, baA=baA, accB=accB, baB=baB))

            def issue_sc(it):
                h, m, j = it["h"], it["m"], it["j"]
                jj = j - 4 * I
                q0 = 128 * jj if jj > 0 else 0
                sc, bsc = C.get("ps")

                def mms(e):
                    ins = e.matmul(sc[:, q0:512], lhsT=ka[h][m][0][0:66, j * 128:(j + 1) * 128],
                                   rhs=qa[h][m][0][0:66, q0:512], start=True, stop=(jj < 0))
                    if jj >= 0:
                        ins = e.matmul(sc[:, q0:q0 + 128], lhsT=identb[:], rhs=minclb[:], start=False, stop=True,
                                       skip_group_check=True)
                    return ins
                P.op("pe", mms, reads=[bka[h][m][j // 4], qa[h][m][1], bib, bmb], writes=[bsc])
                pT, bpT = C.get("pT")
                bcol = h * 32 + (jj + 28)
                P.op("act", lambda e: e.activation(out=pT[:, q0:512], in_=sc[:, q0:512], func=AF.Exp,
                                                   bias=alb[:, bcol:bcol + 1], scale=0.125),
                     reads=[bsc, balb], writes=[bpT])
                it["pT"], it["bpT"], it["jj"] = pT, bpT, jj

            def acc_of(it, ss):
                return (it["accA"], ss * 129, it["baA"]) if ss < 3 else (it["accB"], 0, it["baB"])

            def issue_pv(it):
                h, j, jj, pT = it["h"], it["j"], it["jj"], it["pT"]

                def mmpv(e):
                    ins = None
                    for ss in range(max(jj, 0), 4):
                        a, off, _ = acc_of(it, ss)
                        first = (j == 0) and (ss == 0 or ss == 3)
                        ins = e.matmul(a[:, off:off + 129], lhsT=pT[:, ss * 128:(ss + 1) * 128], rhs=vaug[:, j, h, :],
                                       start=first, stop=(j == 4 * I + ss), skip_group_check=True)
                    return ins
                P.op("pe", mmpv, reads=[it["bpT"], bv[j]], writes=[it["baA"], it["baB"]])

            def finalize(it):
                h, m, o0, bo0 = it["h"], it["m"], it["o0"], it["bo0"]
                for ss in range(4):
                    a, off, ba = acc_of(it, ss)
                    cl, bcl = C.get("col")
                    P.op("dve", lambda e, cl=cl, a=a, off=off: e.reciprocal(out=cl[:, 0:1], in_=a[:, off + 128:off + 129]),
                         reads=[ba], writes=[bcl])
                    if m == 0:
                        P.op("dve", lambda e, ss=ss, a=a, off=off, cl=cl: e.tensor_scalar(
                            out=o0[:, ss, :], in0=a[:, off:off + 128], scalar1=cl[:, 0:1], scalar2=None, op0=ALU.mult),
                            reads=[ba, bcl], writes=[bo0])
                        continue
                    ob, bob = C.get("ob")
                    P.op("dve", lambda e, ob=ob, a=a, off=off, cl=cl: e.tensor_scalar(
                        out=ob[:], in0=a[:, off:off + 128], scalar1=cl[:, 0:1], scalar2=None, op0=ALU.mult),
                        reads=[ba, bcl], writes=[bob])
                    P.op("dve", lambda e, ob=ob, ss=ss: e.scalar_tensor_tensor(
                        out=ob[:], in0=ob[:], scalar=lm[:, 5:6], in1=o0[:, ss, :], op0=ALU.mult, op1=ALU.add),
                        reads=[bob, bo0, blm], writes=[bob])
                    jk, bjk = C.get("ob")
                    P.op("act", lambda e, jk=jk, ob=ob, cl=cl: e.activation(out=jk[:], in_=ob[:], func=AF.Square,
                                                                             accum_out=cl[:, 1:2]),
                         reads=[bob, bcl], writes=[bjk, bcl])
                    P.op("act", lambda e, cl=cl: e.activation(out=cl[:, 2:3], in_=cl[:, 1:2], func=AF.Sqrt,
                                                              bias=epst[:, 0:1], scale=1.0 / 128.0),
                         reads=[bcl, be], writes=[bcl])
                    P.op("dve", lambda e, cl=cl: e.reciprocal(out=cl[:, 3:4], in_=cl[:, 2:3]), reads=[bcl], writes=[bcl])
                    P.op("dve", lambda e, ob=ob, cl=cl: e.scalar_tensor_tensor(
                        out=ob[:], in0=ob[:], scalar=cl[:, 3:4], in1=dnw[0][:], op0=ALU.mult, op1=ALU.mult),
                        reads=[bob, bcl, dnw[1]], writes=[bob])
                    P.op("dve", lambda e, ob=ob, ss=ss: e.tensor_tensor(
                        out=ocat[:, ss, h * 128:(h + 1) * 128], in0=ob[:], in1=szd[:, ss, h * 128:(h + 1) * 128],
                        op=ALU.mult), reads=[bob, bszd, boc], writes=[boc])

            pending = []
            for it in items:
                issue_sc(it)
                pending.append(it)
                if len(pending) > 2:
                    p_ = pending.pop(0)
                    issue_pv(p_)
                    if p_["j"] == nj - 1:
                        finalize(p_)
            for p_ in pending:
                issue_pv(p_)
                if p_["j"] == nj - 1:
                    finalize(p_)
            for ss in range(4):
                r0 = (I * 4 + ss) * 128
                P.dma("sp", o_d[r0:r0 + 128, :], ocat[:, ss, :], reads=[boc], writes=[bo_d], semname="st_" + boc.name)
        P.final_waits("sp", [bo_d])
    if standalone:
        P.finish()
    return nc


def diff_consts(hh):
    slopes = [2.0 ** (-(4 * hh + h + 1)) for h in range(4)]
    q = np.arange(512)
    qx = np.zeros((8, 512), np.float32)
    alb = np.zeros((128, 128), np.float32)
    k = np.arange(128, dtype=np.float64)
    for h in range(4):
        qx[2 * h] = -8.0 * slopes[h] * (q % 128)
        qx[2 * h + 1] = -8.0 * slopes[h] * 128.0 * (q // 128)
        for d in range(32):
            alb[:, h * 32 + d] = slopes[h] * (k + 128.0 * (d - 28))
    return qx, alb


def build_merge(TB):
    nc = bass.Bass("TRN2", target_bir_lowering=False)
    C = Ctx(nc)
    P = C.P
    NT = TB // 128
    x_d = C.dram_in("x", [TB, D_MODEL])
    oc_d = C.dram_in("oc", [TB, 3072])
    wgt_d = C.dram_in("wgate", [D_MODEL, 3072])
    wbr_d = C.dram_in("wbr", [3072, D_MODEL])
    wout_d = C.dram_in("wout", [D_MODEL, D_MODEL])
    prew_d = C.dram_in("prew", [D_MODEL])
    postw_d = C.dram_in("postw", [D_MODEL])
    y_d = C.dram_out("xo", [TB, D_MODEL])
    by_d = Buf("y_d", multi=True)
    with C.phase("b"):
        K = _consts(C)
        identb, bib = K["identb"]; epst, be = K["eps"]
        prew = _bcast_load(C, prew_d, D_MODEL, "prew")
        postw = _bcast_load(C, postw_d, D_MODEL, "postw")
        wgt, bwgt = C.sb([128, 8, 3072], BF16, "wgt_sb")
        _load_w(C, wgt, bwgt, wgt_d, 3072, "ld_wgt")
        wbr, bwbr = C.sb([128, 24, 1024], BF16, "wbr_sb")
        src = wbr_d.rearrange("(kc p) n -> p kc n", p=128)
        for kc in range(24):
            P.dma("pool", wbr[:, kc, :], src[:, kc, :], writes=[bwbr] if kc == 0 else [], semname="ld_wbr")
        bwbr.w = {("d", "ld_wbr"): P.dma_sems["ld_wbr"]}
        wout, bwout = C.sb([128, 8, 1024], BF16, "wout_sb")
        _load_w(C, wout, bwout, wout_d, 1024, "ld_wout")
        C.pool("xt", 2, [128, 1024], F32)
        C.pool("junk", 1, [128, 1024], F32)
        C.pool("col", 4, [128, 8], F32)
        C.pool("hb", 2, [128, 1024], BF16)
        C.pool("pst", 2, [128, 1024], BF16, psum=True)
        C.pool("ps", 5, [128, 512], F32, psum=True)
        C.pool("hT", 2, [128, 8, 128], BF16)
        C.pool("ob", 2, [128, 3072], BF16)
        C.pool("oT", 2, [128, 24, 128], BF16)
        C.pool("sig", 1, [128, 3, 1024], F32)
        C.pool("y", 2, [128, 1024], F32)
        C.pool("yb", 2, [128, 1024], BF16)
        C.pool("yT", 2, [128, 8, 128], BF16)
        C.pool("tmp", 2, [128, 512], F32)
        C.pool("xo", 2, [128, 1024], F32)
        for t in range(NT):
            r0 = t * 128
            hT, bhT = C.get("hT")
            xt, bx = C.get("xt")
            P.dma("sp", xt[:], x_d[r0:r0 + 128, :], writes=[bx], semname="ld_" + bx.name)
            sq, bsq = C.get("junk")
            ss, bss = C.get("col")
            P.op("act", lambda e, xt=xt, sq=sq, ss=ss: e.activation(out=sq[:], in_=xt[:], func=AF.Square,
                                                                     accum_out=ss[:, 0:1]), reads=[bx], writes=[bsq, bss])
            P.op("act", lambda e, ss=ss: e.activation(out=ss[:, 1:2], in_=ss[:, 0:1], func=AF.Sqrt, bias=epst[:, 0:1],
                                                      scale=1.0 / D_MODEL), reads=[bss, be], writes=[bss])
            P.op("dve", lambda e, ss=ss: e.reciprocal(out=ss[:, 2:3], in_=ss[:, 1:2]), reads=[bss], writes=[bss])
            hb, bhb = C.get("hb")
            P.op("dve", lambda e, hb=hb, xt=xt, ss=ss: e.scalar_tensor_tensor(
                out=hb[:], in0=xt[:], scalar=ss[:, 2:3], in1=prew[0][:], op0=ALU.mult, op1=ALU.mult),
                reads=[bx, bss, prew[1]], writes=[bhb])
            pt, bpt = C.get("pst")

            def tr(e, hb=hb, pt=pt):
                for kc in range(8):
                    ins = e.transpose(pt[:, kc * 128:(kc + 1) * 128], hb[:, kc * 128:(kc + 1) * 128], identb[:])
                return ins
            P.op("pe", tr, reads=[bhb, bib], writes=[bpt])
            P.op("act", lambda e, pt=pt, hT=hT: e.copy(out=hT[:, :, :], in_=pt[:, :].rearrange("p (k n) -> p k n", k=8)),
                 reads=[bpt], writes=[bhT])
            ob, bob = C.get("ob")
            P.dma("pool", ob[:], oc_d[r0:r0 + 128, :], writes=[bob], semname="ld_" + bob.name)
            oT, boT = C.get("oT")
            for g in range(3):
                pt, bpt = C.get("pst")

                def tr2(e, ob=ob, pt=pt, g=g):
                    for kc in range(8):
                        c0 = (g * 8 + kc) * 128
                        ins = e.transpose(pt[:, kc * 128:(kc + 1) * 128], ob[:, c0:c0 + 128], identb[:])
                    return ins
                P.op("pe", tr2, reads=[bob, bib], writes=[bpt])
                P.op("dve" if g == 1 else "act", (lambda e, pt=pt, oT=oT, g=g: e.tensor_copy(
                    out=oT[:, g * 8:(g + 1) * 8, :], in_=pt[:, :].rearrange("p (k n) -> p k n", k=8))) if g == 1 else
                    (lambda e, pt=pt, oT=oT, g=g: e.copy(out=oT[:, g * 8:(g + 1) * 8, :],
                                                        in_=pt[:, :].rearrange("p (k n) -> p k n", k=8))),
                    reads=[bpt], writes=[boT])
            sig, bsig = C.get("sig")
            for br in range(3):
                for half in range(2):
                    pg, bpg = C.get("ps")
                    c0 = br * 1024 + half * 512

                    def mmg(e, pg=pg, c0=c0, hT=hT):
                        for kc in range(8):
                            ins = e.matmul(pg[:, :], lhsT=hT[:, kc, :], rhs=wgt[:, kc, c0:c0 + 512], start=(kc == 0),
                                           stop=(kc == 7))
                        return ins
                    P.op("pe", mmg, reads=[bhT, bwgt], writes=[bpg])
                    P.op("act", lambda e, pg=pg, sig=sig, br=br, half=half: e.activation(
                        out=sig[:, br, half * 512:(half + 1) * 512], in_=pg[:, :], func=AF.Sigmoid),
                        reads=[bpg], writes=[bsig])
            y, by = C.get("y")
            for half in range(2):
                hs = slice(half * 512, (half + 1) * 512)
                for br in range(3):
                    pb, bpb = C.get("ps")

                    def mmb(e, pb=pb, br=br, half=half, oT=oT):
                        for kc in range(8):
                            ins = e.matmul(pb[:, :], lhsT=oT[:, br * 8 + kc, :],
                                           rhs=wbr[:, br * 8 + kc, half * 512:(half + 1) * 512], start=(kc == 0),
                                           stop=(kc == 7))
                        return ins
                    P.op("pe", mmb, reads=[boT, bwbr], writes=[bpb])
                    if br == 0:
                        P.op("dve", lambda e, y=y, pb=pb, sig=sig, hs=hs: e.tensor_tensor(
                            out=y[:, hs], in0=pb[:, :], in1=sig[:, 0, hs], op=ALU.mult), reads=[bpb, bsig, by], writes=[by])
                    else:
                        tmp, btmp = C.get("tmp")
                        P.op("dve", lambda e, tmp=tmp, pb=pb, sig=sig, hs=hs, br=br: e.tensor_tensor(
                            out=tmp[:], in0=pb[:, :], in1=sig[:, br, hs], op=ALU.mult), reads=[bpb, bsig], writes=[btmp])
                        P.op("pool", lambda e, y=y, tmp=tmp, hs=hs: e.tensor_tensor(
                            out=y[:, hs], in0=y[:, hs], in1=tmp[:], op=ALU.add), reads=[btmp, by], writes=[by])
            yb, byb = C.get("yb")
            P.op("act", lambda e, yb=yb, y=y: e.copy(out=yb[:], in_=y[:]), reads=[by], writes=[byb])
            pt, bpt = C.get("pst")

            def tr3(e, yb=yb, pt=pt):
                for kc in range(8):
                    ins = e.transpose(pt[:, kc * 128:(kc + 1) * 128], yb[:, kc * 128:(kc + 1) * 128], identb[:])
                return ins
            P.op("pe", tr3, reads=[byb, bib], writes=[bpt])
            yT, byT = C.get("yT")
            P.op("act", lambda e, pt=pt, yT=yT: e.copy(out=yT[:, :, :], in_=pt[:, :].rearrange("p (k n) -> p k n", k=8)),
                 reads=[bpt], writes=[byT])
            xo, bxo = C.get("xo")
            cl, bcl = C.get("col")
            for half in range(2):
                po, bpo = C.get("ps")

                def mmo(e, po=po, half=half, yT=yT):
                    for kc in range(8):
                        ins = e.matmul(po[:, :], lhsT=yT[:, kc, :], rhs=wout[:, kc, half * 512:(half + 1) * 512],
                                       start=(kc == 0), stop=(kc == 7))
                    return ins
                P.op("pe", mmo, reads=[byT, bwout], writes=[bpo])
                P.op("act", lambda e, po=po, xo=xo, half=half: e.copy(out=xo[:, half * 512:(half + 1) * 512], in_=po[:, :]),
                     reads=[bpo], writes=[bxo])
            sq, bsq = C.get("junk")
            P.op("act", lambda e, sq=sq, xo=xo, cl=cl: e.activation(out=sq[:], in_=xo[:], func=AF.Square,
                                                                    accum_out=cl[:, 0:1]), reads=[bxo], writes=[bsq, bcl])
            P.op("act", lambda e, cl=cl: e.activation(out=cl[:, 1:2], in_=cl[:, 0:1], func=AF.Sqrt, bias=epst[:, 0:1],
                                                      scale=1.0 / D_MODEL), reads=[bcl, be], writes=[bcl])
            P.op("dve", lambda e, cl=cl: e.reciprocal(out=cl[:, 2:3], in_=cl[:, 1:2]), reads=[bcl], writes=[bcl])
            P.op("dve", lambda e, xo=xo, cl=cl: e.scalar_tensor_tensor(
                out=xo[:], in0=xo[:], scalar=cl[:, 2:3], in1=postw[0][:], op0=ALU.mult, op1=ALU.mult),
                reads=[bxo, bcl, postw[1]], writes=[bxo])
            P.op("pool", lambda e, xo=xo, xt=xt: e.tensor_tensor(out=xo[:], in0=xo[:], in1=xt[:], op=ALU.add),
                 reads=[bxo, bx], writes=[bxo])
            P.dma("sp", y_d[r0:r0 + 128, :], xo[:], reads=[bxo], writes=[by_d], semname="st_" + bxo.name)
        P.final_waits("sp", [by_d])
    P.finish()
    return nc


def phase_b1(C, S, io, tag):
    P = C.P
    NT = S // 128
    x_d, oc_d, wgt_d, wbr_d, prew_d, yp_d = io["x"], io["oc"], io["wgate"], io["wbrm"], io["prew"], io["yp"]
    byp = Buf("yp_d", multi=True)
    with C.phase(tag):
        K = _consts(C)
        identb, bib = K["identb"]; epst, be = K["eps"]
        prew = _bcast_load(C, prew_d, D_MODEL, "prew")
        wgt, bwgt = C.sb([128, 8, 3072], BF16, "wgt_sb")
        _load_w(C, wgt, bwgt, wgt_d, 3072, "ld_wgt")
        wbr, bwbr = C.sb([128, 12, 1024], BF16, "wbr_sb")
        src = wbr_d.rearrange("(kc p) n -> p kc n", p=128)
        for kc in range(12):
            P.dma("pool", wbr[:, kc, :], src[:, kc, :], writes=[bwbr] if kc == 0 else [], semname="ld_wbr")
        bwbr.w = {("d", "ld_wbr"): P.dma_sems["ld_wbr"]}
        C.pool("xt", 3, [128, 1024], F32)
        C.pool("junk", 1, [128, 1024], F32)
        C.pool("col", 6, [128, 8], F32)
        C.pool("hb", 3, [128, 1024], BF16)
        C.pool("pst", 2, [128, 1024], BF16, psum=True)
        C.pool("ps", 5, [128, 512], F32, psum=True)
        C.pool("hT1", 3, [128, 8, 128], BF16)
        C.pool("ob", 3, [128, 1536], BF16)
        C.pool("oT", 3, [128, 12, 128], BF16)
        C.pool("sig", 2, [128, 3, 1024], F32)
        C.pool("y", 2, [128, 1024], F32)
        C.pool("ybf", 2, [128, 1024], BF16)
        C.pool("tmp", 4, [128, 512], F32)

        def front_a(t):
            r0 = t * 128
            xt, bx = C.get("xt")
            P.dma("sp", xt[:], x_d[r0:r0 + 128, :], writes=[bx], semname="ld_" + bx.name)
            sq, bsq = C.get("junk")
            ss, bss = C.get("col")
            P.op("act", lambda e: e.activation(out=sq[:, 0:1024], in_=xt[:], func=AF.Square, accum_out=ss[:, 0:1]),
                 reads=[bx], writes=[bsq, bss])
            P.op("act", lambda e: e.activation(out=ss[:, 1:2], in_=ss[:, 0:1], func=AF.Sqrt, bias=epst[:, 0:1],
                                               scale=1.0 / D_MODEL), reads=[bss, be], writes=[bss])
            P.op("dve", lambda e: e.reciprocal(out=ss[:, 2:3], in_=ss[:, 1:2]), reads=[bss], writes=[bss])
            hb, bhb = C.get("hb")
            P.op("dve", lambda e: e.scalar_tensor_tensor(out=hb[:], in0=xt[:], scalar=ss[:, 2:3], in1=prew[0][:],
                                                         op0=ALU.mult, op1=ALU.mult),
                 reads=[bx, bss, prew[1]], writes=[bhb])
            ob, bob = C.get("ob")
            P.dma("pool", ob[:], oc_d[r0:r0 + 128, :], writes=[bob], semname="ld_" + bob.name)
            return dict(r0=r0, hb=hb, bhb=bhb, ob=ob, bob=bob)

        def front(d0):
            r0, hb, bhb, ob, bob = d0["r0"], d0["hb"], d0["bhb"], d0["ob"], d0["bob"]
            hT, bhT = C.get("hT1")
            pt, bpt = C.get("pst")

            def tr(e):
                for kc in range(8):
                    ins = e.transpose(pt[:, kc * 128:(kc + 1) * 128], hb[:, kc * 128:(kc + 1) * 128], identb[:])
                return ins
            P.op("pe", tr, reads=[bhb, bib], writes=[bpt])
            P.op("act", lambda e: e.copy(out=hT[:, :, :], in_=pt[:, :].rearrange("p (k n) -> p k n", k=8)),
                 reads=[bpt], writes=[bhT])
            oT, boT = C.get("oT")
            for g in range(2):
                pt, bpt = C.get("pst")
                nk = 8 if g == 0 else 4

                def tr2(e, ob=ob, pt=pt, g=g, nk=nk):
                    for kc in range(nk):
                        c0 = (g * 8 + kc) * 128
                        ins = e.transpose(pt[:, kc * 128:(kc + 1) * 128], ob[:, c0:c0 + 128], identb[:])
                    return ins
                P.op("pe", tr2, reads=[bob, bib], writes=[bpt])
                if g == 0:
                    P.op("act", lambda e, pt=pt, oT=oT: e.copy(
                        out=oT[:, 0:8, :], in_=pt[:, :].rearrange("p (k n) -> p k n", k=8)), reads=[bpt], writes=[boT])
                else:
                    P.op("dve", lambda e, pt=pt, oT=oT: e.tensor_copy(
                        out=oT[:, 8:12, :], in_=pt[:, 0:512].rearrange("p (k n) -> p k n", k=4)), reads=[bpt],
                        writes=[boT])
            return dict(hT=hT, bhT=bhT, oT=oT, boT=boT, r0=r0)

        def back(d):
            hT, bhT, oT, boT, r0 = d["hT"], d["bhT"], d["oT"], d["boT"], d["r0"]
            sig, bsig = C.get("sig")
            for br in range(3):
                for half in range(2):
                    pg, bpg = C.get("ps")
                    c0 = br * 1024 + half * 512

                    def mmg(e, pg=pg, c0=c0, hT=hT):
                        for kc in range(8):
                            ins = e.matmul(pg[:, :], lhsT=hT[:, kc, :], rhs=wgt[:, kc, c0:c0 + 512], start=(kc == 0),
                                           stop=(kc == 7))
                        return ins
                    P.op("pe", mmg, reads=[bhT, bwgt], writes=[bpg])
                    P.op("act", lambda e, pg=pg, sig=sig, br=br, half=half: e.activation(
                        out=sig[:, br, half * 512:(half + 1) * 512], in_=pg[:, :], func=AF.Sigmoid),
                        reads=[bpg], writes=[bsig])
            y, by = C.get("y")
            for half in range(2):
                hs = slice(half * 512, (half + 1) * 512)
                for br in range(3):
                    pb, bpb = C.get("ps")

                    def mmb(e, pb=pb, br=br, half=half, oT=oT):
                        for kc in range(4):
                            ins = e.matmul(pb[:, :], lhsT=oT[:, br * 4 + kc, :],
                                           rhs=wbr[:, br * 4 + kc, half * 512:(half + 1) * 512], start=(kc == 0),
                                           stop=(kc == 3))
                        return ins
                    P.op("pe", mmb, reads=[boT, bwbr], writes=[bpb])
                    if br == 0:
                        P.op("dve", lambda e, y=y, pb=pb, sig=sig, hs=hs: e.tensor_tensor(
                            out=y[:, hs], in0=pb[:, :], in1=sig[:, 0, hs], op=ALU.mult), reads=[bpb, bsig, by], writes=[by])
                    else:
                        tmp, btmp = C.get("tmp")
                        P.op("dve", lambda e, tmp=tmp, pb=pb, sig=sig, hs=hs, br=br: e.tensor_tensor(
                            out=tmp[:], in0=pb[:, :], in1=sig[:, br, hs], op=ALU.mult), reads=[bpb, bsig], writes=[btmp])
                        P.op("pool", lambda e, y=y, tmp=tmp, hs=hs: e.tensor_tensor(
                            out=y[:, hs], in0=y[:, hs], in1=tmp[:], op=ALU.add), reads=[btmp, by], writes=[by])
            yb, byb = C.get("ybf")
            P.op("act", lambda e: e.copy(out=yb[:], in_=y[:]), reads=[by], writes=[byb])
            P.dma("sp", yp_d[r0:r0 + 128, :], yb[:], reads=[byb], writes=[byp], semname="st_" + byb.name)

        fa = front_a(0)
        nxt = front(fa)
        for t in range(NT):
            cur_ = nxt
            if t + 1 < NT:
                fa = front_a(t + 1)
            back(cur_)
            if t + 1 < NT:
                nxt = front(fa)
        P.final_waits("sp", [byp])


def phase_b2(C, TB, io, tag):
    P = C.P
    NT = TB // 128
    ys_d, xr_d, wout_d, postw_d, xo_d = io["ysum"], io["xres"], io["wout"], io["postw"], io["xout"]
    bxo_d = Buf("xo_d", multi=True)
    with C.phase(tag):
        K = _consts(C)
        identb, bib = K["identb"]; epst, be = K["eps"]
        postw = _bcast_load(C, postw_d, D_MODEL, "postw")
        wout, bwout = C.sb([128, 8, 1024], BF16, "wout_sb")
        _load_w(C, wout, bwout, wout_d, 1024, "ld_wout")
        C.pool("xt", 2, [128, 1024], F32)
        C.pool("junk", 1, [128, 1024], F32)
        C.pool("col", 4, [128, 8], F32)
        C.pool("pst", 2, [128, 1024], BF16, psum=True)
        C.pool("ps", 4, [128, 512], F32, psum=True)
        C.pool("yb", 2, [128, 1024], BF16)
        C.pool("yT", 2, [128, 8, 128], BF16)
        C.pool("xo", 2, [128, 1024], F32)
        for t in range(NT):
            r0 = t * 128
            xt, bx = C.get("xt")
            P.dma("sp", xt[:], xr_d[r0:r0 + 128, :], writes=[bx], semname="ld_" + bx.name)
            yb, byb = C.get("yb")
            P.dma("sp", yb[:], ys_d[r0:r0 + 128, :], writes=[byb], semname="ld_" + byb.name)
            pt, bpt = C.get("pst")

            def tr3(e, yb=yb, pt=pt):
                for kc in range(8):
                    ins = e.transpose(pt[:, kc * 128:(kc + 1) * 128], yb[:, kc * 128:(kc + 1) * 128], identb[:])
                return ins
            P.op("pe", tr3, reads=[byb, bib], writes=[bpt])
            yT, byT = C.get("yT")
            P.op("act", lambda e, pt=pt, yT=yT: e.copy(out=yT[:, :, :], in_=pt[:, :].rearrange("p (k n) -> p k n", k=8)),
                 reads=[bpt], writes=[byT])
            xo, bxo = C.get("xo")
            cl, bcl = C.get("col")
            for half in range(2):
                po, bpo = C.get("ps")

                def mmo(e, po=po, half=half, yT=yT):
                    for kc in range(8):
                        ins = e.matmul(po[:, :], lhsT=yT[:, kc, :], rhs=wout[:, kc, half * 512:(half + 1) * 512],
                                       start=(kc == 0), stop=(kc == 7))
                    return ins
                P.op("pe", mmo, reads=[byT, bwout], writes=[bpo])
                P.op("act", lambda e, po=po, xo=xo, half=half: e.copy(out=xo[:, half * 512:(half + 1) * 512], in_=po[:, :]),
                     reads=[bpo], writes=[bxo])
            sq, bsq = C.get("junk")
            P.op("act", lambda e, sq=sq, xo=xo, cl=cl: e.activation(out=sq[:], in_=xo[:], func=AF.Square,
                                                                    accum_out=cl[:, 0:1]), reads=[bxo], writes=[bsq, bcl])
            P.op("act", lambda e, cl=cl: e.activation(out=cl[:, 1:2], in_=cl[:, 0:1], func=AF.Sqrt, bias=epst[:, 0:1],
                                                      scale=1.0 / D_MODEL), reads=[bcl, be], writes=[bcl])
            P.op("dve", lambda e, cl=cl: e.reciprocal(out=cl[:, 2:3], in_=cl[:, 1:2]), reads=[bcl], writes=[bcl])
            P.op("dve", lambda e, xo=xo, cl=cl: e.scalar_tensor_tensor(
                out=xo[:], in0=xo[:], scalar=cl[:, 2:3], in1=postw[0][:], op0=ALU.mult, op1=ALU.mult),
                reads=[bxo, bcl, postw[1]], writes=[bxo])
            P.op("pool", lambda e, xo=xo, xt=xt: e.tensor_tensor(out=xo[:], in0=xo[:], in1=xt[:], op=ALU.add),
                 reads=[bxo, bx], writes=[bxo])
            P.dma("sp", xo_d[r0:r0 + 128, :], xo[:], reads=[bxo], writes=[bxo_d], semname="st_" + bxo.name)
        P.final_waits("sp", [bxo_d])


PAIRS = [[0, 1], [2, 3], [4, 5], [6, 7]]


def build_fused(S, L):
    nc = bass.Bass("TRN2", target_bir_lowering=False, num_devices=8)
    C = Ctx(nc)
    P = C.P
    TB = S // 2
    di = C.dram_in
    x_d = di("x", [S, D_MODEL]); xh_d = di("xhalf", [TB, D_MODEL]); mem_d = di("mem", [MEM_LEN, D_MODEL])
    wg_d = di("wg", [L, D_MODEL, 2056]); cw_d = di("convw", [L, 1536, 4]); prew_d = di("prew", [L, D_MODEL])
    alog_d = di("alog", [L, 4]); dtb_d = di("dtb", [L, 4]); gnw_d = di("gnw", [L, 128]); mnw_d = di("mnw", [L, D_MODEL])
    wkv_d = di("wkv", [L, D_MODEL, 1024]); wm_d = di("wm", [L, D_MODEL, 1024]); wd_d = di("wd", [L, D_MODEL, 2048])
    lamv_d = di("lamv", [L, 256]); dnw_d = di("dnw", [L, 128]); qx_d = di("qx", [8, 512]); alb_d = di("alb", [128, 128])
    li_d = di("li", [L, 2]); wgt_d = di("wgate", [L, D_MODEL, 3072]); wbr_d = di("wbrm", [L, 1536, D_MODEL])
    wout_d = di("wout", [L, D_MODEL, D_MODEL]); postw_d = di("postw", [L, D_MODEL])
    xo_d = C.dram_out("xo", [TB, D_MODEL])
    it = lambda name, shape: nc.dram_tensor(name, list(shape), F32, addr_space="Local", kind="Internal").ap()
    oc_i = it("oc_i", [S, 1536])
    itb = lambda name, shape: nc.dram_tensor(name, list(shape), BF16, addr_space="Local", kind="Internal").ap()
    yp_i = itb("yp_i", [S, D_MODEL])
    ys_i = itb("ys_i", [TB, D_MODEL])
    xh_i = it("xh_i", [TB, D_MODEL])
    xf_i = it("xf_i", [S, D_MODEL])
    C.gst = contextlib.ExitStack()
    C.st = C.gst
    C.pfx = "g_"
    C.K = _consts(C)
    for l in range(L):
        xs = x_d if l == 0 else xf_i
        build_gdn(S, 99, C, dict(x=xs, wg=wg_d[l], convw=cw_d[l], prew=prew_d[l], alog=alog_d[l], dtb=dtb_d[l],
                                 gnw=gnw_d[l], o_gdn=oc_i[:, 0:512], mem=mem_d, mnw=mnw_d[l], wkv=wkv_d[l], wm=wm_d[l],
                                 o_mem=oc_i[:, 1024:1536]), tag=f"L{l}a1")
        build_diff(S, C, dict(x=xs, wd=wd_d[l], prew=prew_d[l], lamv=lamv_d[l], dnw=dnw_d[l], qx=qx_d, alb=alb_d,
                              li=li_d[l], o_diff=oc_i[:, 512:1024]), tag=f"L{l}a2")
        phase_b1(C, S, dict(x=xs, oc=oc_i, wgate=wgt_d[l], wbrm=wbr_d[l], prew=prew_d[l], yp=yp_i), tag=f"L{l}b1")
        with C.phase(f"L{l}rs"):
            P.coll(lambda e: e.collective_compute("ReduceScatter", ALU.add, replica_groups=PAIRS, ins=[yp_i],
                                                  outs=[ys_i]), semname="cc_rs")
        last = (l == L - 1)
        phase_b2(C, TB, dict(ysum=ys_i, xres=(xh_d if l == 0 else xh_i), wout=wout_d[l], postw=postw_d[l],
                             xout=(xo_d if last else xh_i)), tag=f"L{l}b2")
        if not last:
            with C.phase(f"L{l}ag"):
                P.coll(lambda e: e.collective_compute("AllGather", ALU.bypass, replica_groups=PAIRS, ins=[xh_i],
                                                      outs=[xf_i]), semname="cc_ag")
    P.finish()
    C.gst.close()
    return nc


_PROGS = {}


def _c(a):
    return np.ascontiguousarray(a, dtype=np.float32)


def _core_inputs(r, L, w_in, gdn_conv_w, gdn_a_log, gdn_dt_bias, w_mem_kv, w_br_gdn, w_br_diff, w_br_mem):
    sl = lambda base: slice(base + r * 512, base + r * 512 + 512)
    wg, wd, wm, cw, wkv, wbrm, wgate = [], [], [], [], [], [], []
    for l in range(L):
        wl = np.asarray(w_in[l], np.float32)
        wg.append(np.concatenate([wl[:, sl(0)], wl[:, sl(1024)], wl[:, sl(2048)], wl[:, 3072 + r * 4:3076 + r * 4],
                                  wl[:, 3080 + r * 4:3084 + r * 4], wl[:, sl(3088)]], axis=1))
        wd.append(np.concatenate([wl[:, sl(4112)], wl[:, sl(5136)], wl[:, sl(6160)], wl[:, sl(7184)]], axis=1))
        wm.append(np.concatenate([wl[:, sl(8208)], wl[:, sl(9232)]], axis=1))
        wgate.append(wl[:, 10256:13328])
        cwl = np.asarray(gdn_conv_w[l], np.float32)
        cw.append(np.concatenate([cwl[:, sl(0)], cwl[:, sl(1024)], cwl[:, sl(2048)]], axis=1).T)
        kvl = np.asarray(w_mem_kv[l], np.float32)
        wkv.append(np.concatenate([kvl[:, sl(0)], kvl[:, sl(1024)]], axis=1))
        wbrm.append(np.concatenate([np.asarray(w_br_gdn[l])[sl(0)], np.asarray(w_br_diff[l])[sl(0)],
                                    np.asarray(w_br_mem[l])[sl(0)]], axis=0))
    st = lambda xs: _c(np.stack(xs))
    return dict(wg=st(wg), wd=st(wd), wm=st(wm), convw=st(cw), wkv=st(wkv), wbrm=st(wbrm), wgate=st(wgate),
                alog=_c(np.asarray(gdn_a_log)[:L, r * 4:r * 4 + 4]), dtb=_c(np.asarray(gdn_dt_bias)[:L, r * 4:r * 4 + 4]))


LAYERS_PER_LAUNCH = 1


def kernel(x, mem, pre_norm_w, post_norm_w, w_in, gdn_conv_w, gdn_a_log, gdn_dt_bias, gdn_norm_w, diff_lambda,
           diff_norm_w, mem_norm_w, w_mem_kv, w_br_gdn, w_br_diff, w_br_mem, w_out):
    x = np.asarray(x, np.float32)
    B, S, D = x.shape
    L = np.asarray(w_in).shape[0]
    TB = S // 2
    G = LAYERS_PER_LAUNCH
    key = (S, G)
    if key not in _PROGS:
        _PROGS[key] = build_fused(S, G)
    nc = _PROGS[key]
    li_all = np.array([[-(0.8 - 0.6 * math.exp(-0.3 * l)), 1.0 - (0.8 - 0.6 * math.exp(-0.3 * l))] for l in range(L)],
                      np.float32)
    consts = [diff_consts(r) for r in range(2)]
    for l0 in range(0, L, G):
        ls = slice(l0, l0 + G)
        shared = dict(prew=_c(np.asarray(pre_norm_w)[ls]), postw=_c(np.asarray(post_norm_w)[ls]),
                      gnw=_c(np.asarray(gdn_norm_w)[ls]), mnw=_c(np.asarray(mem_norm_w)[ls]),
                      lamv=_c(np.asarray(diff_lambda)[ls].reshape(G, 256)), dnw=_c(np.asarray(diff_norm_w)[ls]),
                      wout=_c(np.asarray(w_out)[ls]), li=_c(li_all[ls]))
        per_r = []
        for r in range(2):
            d = _core_inputs(r, G, np.asarray(w_in)[ls], np.asarray(gdn_conv_w)[ls], np.asarray(gdn_a_log)[ls],
                             np.asarray(gdn_dt_bias)[ls], np.asarray(w_mem_kv)[ls], np.asarray(w_br_gdn)[ls],
                             np.asarray(w_br_diff)[ls], np.asarray(w_br_mem)[ls])
            d.update(qx=consts[r][0], alb=consts[r][1])
            d.update(shared)
            per_r.append(d)
        in_maps = []
        for c in range(8):
            b, r = c // 2, c % 2
            m = dict(per_r[r])
            m.update(x=_c(x[b]), xhalf=_c(x[b, r * TB:(r + 1) * TB]), mem=_c(np.asarray(mem)[b]))
            in_maps.append(m)
        res = run_bass_kernel_spmd(nc, in_maps, core_ids=list(range(8))).results
        xn = np.empty((B, S, D), np.float32)
        for c in range(8):
            b, r = c // 2, c % 2
            xn[b, r * TB:(r + 1) * TB] = res[c]["xo"]
        x = xn
    return x
```

```python
import contextlib
import math
import numpy as np
import concourse.bass as bass
import concourse.mybir as mybir
from concourse.bass_utils import run_bass_kernel_spmd

F32 = mybir.dt.float32
F32R = mybir.dt.float32r
BF16 = mybir.dt.bfloat16
ALU = mybir.AluOpType
AF = mybir.ActivationFunctionType

D_MODEL = 1024
BATCH = 4
SEQ = 4096
DEPTH = 4
MEM_LEN = 256
EPS = 1e-6
IN_COLS = 13328
NEG = -1.0e30

ENGS = ("pe", "act", "dve", "pool", "sp")


class Buf:
    __slots__ = ("name", "w", "r", "excl", "multi")

    def __init__(self, name="", excl=False, multi=False):
        self.name = name
        self.w = {}
        self.r = {}
        self.multi = multi
        self.excl = excl


class Prog:
    def __init__(self, nc, same_engine_sync=True):
        self.nc = nc
        self.q = {e: [] for e in ENGS}
        self.cnt = {e: 0 for e in ENGS}
        self.seen = {e: {} for e in ENGS}
        self.dma_sems = {}
        self.same = same_engine_sync
        self.sem_handles = {}
        self.gst = None

    def _need(self, eng, reads, writes):
        need = {}

        def add(d):
            for k, v in d.items():
                if need.get(k, 0) < v:
                    need[k] = v
        for b in reads:
            add(b.w)
            if b.excl:
                add({k: v for k, v in b.r.items() if k != ("e", eng)})
        for b in writes:
            if b.multi:
                continue
            add(b.w)
            add(b.r)
        out = []
        seen = self.seen[eng]
        for k, v in need.items():
            if not self.same and k == ("e", eng):
                continue
            if seen.get(k, 0) >= v:
                continue
            seen[k] = v
            out.append((k, v))
        return out

    def op(self, eng, fn, reads=(), writes=()):
        waits = self._need(eng, reads, writes)
        self.cnt[eng] += 1
        c = self.cnt[eng]
        key = ("e", eng)
        self.q[eng].append((waits, fn, key, 1))
        for b in writes:
            b.w = {key: c}
            b.r = {}
        for b in reads:
            if b.r.get(key, 0) < c:
                b.r[key] = c

    def dma(self, eng, out_ap, in_ap, reads=(), writes=(), semname=None, **kw):
        waits = self._need(eng, reads, writes)
        key = ("d", semname)
        self.dma_sems[semname] = self.dma_sems.get(semname, 0) + 16
        c = self.dma_sems[semname]

        def fn(e, out_ap=out_ap, in_ap=in_ap, kw=kw):
            return e.dma_start(out=out_ap, in_=in_ap, **kw)
        self.q[eng].append((waits, fn, key, 16))
        for b in writes:
            if b.multi:
                b.w[key] = c
                continue
            b.w = {key: c}
            b.r = {}
        for b in reads:
            if b.r.get(key, 0) < c:
                b.r[key] = c

    def coll(self, fn, reads=(), writes=(), semname=None):
        waits = self._need("pool", reads, writes)
        key = ("d", semname)
        self.dma_sems[semname] = self.dma_sems.get(semname, 0) + 1
        c = self.dma_sems[semname]
        self.q["pool"].append((waits, fn, key, 1))
        for b in writes:
            b.w = {key: c}
            b.r = {}
        for b in reads:
            if b.r.get(key, 0) < c:
                b.r[key] = c

    def final_waits(self, eng, bufs):
        waits = self._need(eng, bufs, ())
        self.q[eng].append((waits, None, None, 0))

    def barrier(self):
        allk = [(("e", e), c) for e, c in self.cnt.items() if c > 0]
        allk += [(("d", n), c) for n, c in self.dma_sems.items()]
        for eng in ENGS:
            seen = self.seen[eng]
            waits = []
            for k, v in allk:
                if seen.get(k, 0) >= v:
                    continue
                seen[k] = v
                waits.append((k, v))
            self.q[eng].append((waits, None, None, 0))

    def _sem(self, key):
        if key not in self.sem_handles:
            if self.gst is None:
                self.gst = contextlib.ExitStack()
            nm = ("se_" if key[0] == "e" else "sd_") + key[1]
            self.sem_handles[key] = self.gst.enter_context(self.nc.semaphore(nm))
        return self.sem_handles[key]

    def flush(self):
        nc = self.nc
        for e in ENGS:
            self._sem(("e", e))
        for lst in self.q.values():
            for waits, fn, key, inc in lst:
                for k, v in waits:
                    self._sem(k)
                if key is not None:
                    self._sem(key)
        H = self.sem_handles
        q = self.q
        self.q = {e: [] for e in ENGS}
        with nc.Block() as block:
            def run(engobj, lst):
                for waits, fn, key, inc in lst:
                    for k, v in waits:
                        engobj.wait_ge(H[k], v)
                    if fn is not None:
                        fn(engobj).then_inc(H[key], inc)

            @block.tensor
            def _(e):
                run(e, q["pe"])

            @block.scalar
            def _(e):
                run(e, q["act"])

            @block.vector
            def _(e):
                run(e, q["dve"])

            @block.gpsimd
            def _(e):
                run(e, q["pool"])

            @block.sync
            def _(e):
                run(e, q["sp"])

    def emit(self):
        self.flush()

    def finish(self):
        if self.gst is not None:
            self.gst.close()
            self.gst = None


class Ctx:
    def __init__(self, nc):
        self.nc = nc
        self.P = Prog(nc)
        self.st = contextlib.ExitStack()
        self.n = 0
        self.rot = {}
        self.pfx = ""
        self.K = None
        self.gst = None

    @contextlib.contextmanager
    def phase(self, name):
        self.pfx = name + "_"
        self.rot = {}
        self.st = contextlib.ExitStack()
        with self.st:
            yield
            self.P.barrier()
            self.P.flush()

    def sb(self, shape, dt=F32, name=None):
        self.n += 1
        nm = name or f"sb{self.n}"
        t = self.st.enter_context(self.nc.sbuf_tensor(self.pfx + nm, list(shape), dt))
        return t, Buf(nm)

    def ps(self, shape, dt=F32, name=None):
        self.n += 1
        nm = name or f"ps{self.n}"
        t = self.st.enter_context(self.nc.psum_tensor(self.pfx + nm, list(shape), dt))
        return t, Buf(nm, excl=True)

    def pool(self, tag, n, shape, dt=F32, psum=False):
        self.rot[tag] = [[(self.ps if psum else self.sb)(shape, dt, f"{tag}{i}") for i in range(n)], 0]

    def get(self, tag):
        r = self.rot[tag]
        t = r[0][r[1] % len(r[0])]
        r[1] += 1
        return t

    def dram_in(self, name, shape, dt=F32):
        return self.nc.dram_tensor(name, list(shape), dt, kind="ExternalInput").ap()

    def dram_out(self, name, shape, dt=F32):
        return self.nc.dram_tensor(name, list(shape), dt, kind="ExternalOutput").ap()


def _r(ap):
    return ap


def _consts(C):
    if C.K is not None:
        return C.K
    P = C.P
    K = {}
    ident, bi = C.sb([128, 128], F32, "ident")
    ones, bo = C.sb([128, 128], F32, "ones")
    triu, bt = C.sb([128, 128], F32, "triu")
    mincl, bm1 = C.sb([128, 128], F32, "mincl")
    mstr, bm2 = C.sb([128, 128], F32, "mstr")
    identb, bib = C.sb([128, 128], BF16, "identb")
    epst, be = C.sb([128, 1], F32, "epst")

    def mk0(e):
        e.memset(ident[:], 0.0)
        e.memset(ones[:], 1.0)
        e.memset(triu[:], 1.0)
        e.memset(mincl[:], 0.0)
        e.memset(mstr[:], 0.0)
        return e.memset(epst[:], EPS)
    P.op("pool", mk0, writes=[bi, bo, bt, bm1, bm2, be])

    def mk(e):
        e.affine_select(out=ident[:], in_=ident[:], pattern=[[-1, 128]], compare_op=ALU.not_equal, fill=1.0,
                        base=0, channel_multiplier=1)
        e.affine_select(out=triu[:], in_=triu[:], pattern=[[1, 128]], compare_op=ALU.is_ge, fill=0.0,
                        base=0, channel_multiplier=-1)
        e.affine_select(out=mincl[:], in_=mincl[:], pattern=[[1, 128]], compare_op=ALU.is_ge, fill=NEG,
                        base=0, channel_multiplier=-1)
        return e.affine_select(out=mstr[:], in_=mstr[:], pattern=[[1, 128]], compare_op=ALU.is_gt, fill=NEG,
                               base=0, channel_multiplier=-1)
    P.op("pool", mk, reads=[bi, bt, bm1, bm2], writes=[bi, bt, bm1, bm2])
    P.op("pool", lambda e: e.tensor_copy(out=identb[:], in_=ident[:]), reads=[bi], writes=[bib])
    K.update(ident=(ident, bi), ones=(ones, bo), triu=(triu, bt), mincl=(mincl, bm1), mstr=(mstr, bm2),
             identb=(identb, bib), eps=(epst, be))
    return K


def _bcast_load(C, dram_vec, n, name):
    t, b = C.sb([128, n], F32, name + "_bc")
    src = dram_vec.partition_broadcast(128)
    C.P.dma("sp", t[:], src, writes=[b], semname="ld_" + name)
    return t, b


def _load_w(C, wt, wb, wdram, ncols, semname):
    src = wdram.rearrange("(kc p) n -> p kc n", p=128)
    for kc in range(8):
        C.P.dma("pool", wt[:, kc, :], src[:, kc, :], writes=[wb] if kc == 0 else [], reads=[], semname=semname)
    wb.w = {("d", semname): C.P.dma_sems[semname]}


def _norm_part(C, K, x_dram, st, prew, n_tiles=4):
    P = C.P
    epst, be = K["eps"]
    hbs = []
    for t in range(n_tiles):
        xt, bx = C.get("xt")
        r0 = (st * n_tiles + t) * 128
        P.dma("sp", xt[:], x_dram[r0:r0 + 128, :], writes=[bx], semname="ld_" + bx.name)
        sq, bsq = C.get("junk")
        ss, bss = C.get("col")
        P.op("act", lambda e, xt=xt, sq=sq, ss=ss: e.activation(out=sq[:, 0:1024], in_=xt[:], func=AF.Square,
                                                                 accum_out=ss[:, 0:1]),
             reads=[bx], writes=[bsq, bss])
        P.op("act", lambda e, ss=ss: e.activation(out=ss[:, 1:2], in_=ss[:, 0:1], func=AF.Sqrt, bias=epst[:, 0:1],
                                                  scale=1.0 / D_MODEL), reads=[bss, be], writes=[bss])
        P.op("dve", lambda e, ss=ss: e.reciprocal(out=ss[:, 2:3], in_=ss[:, 1:2]), reads=[bss], writes=[bss])
        hb, bhb = C.get("hb")
        P.op("dve", lambda e, hb=hb, xt=xt, ss=ss: e.scalar_tensor_tensor(
            out=hb[:], in0=xt[:], scalar=ss[:, 2:3], in1=prew[0][:], op0=ALU.mult, op1=ALU.mult),
            reads=[bx, bss, prew[1]], writes=[bhb])
        hbs.append((hb, bhb))
    return hbs


def _tr_part(C, K, hbs, hT, hTb):
    P = C.P
    identb, bib = K["identb"]
    for t, (hb, bhb) in enumerate(hbs):
        pt, bpt = C.get("pst")

        def tr(e, hb=hb, pt=pt):
            for kc in range(8):
                ins = e.transpose(pt[:, kc * 128:(kc + 1) * 128], hb[:, kc * 128:(kc + 1) * 128], identb[:])
            return ins
        P.op("pe", tr, reads=[bhb, bib], writes=[bpt])
        P.op("act", lambda e, pt=pt, t=t: e.copy(out=hT[:, :, t * 128:(t + 1) * 128],
                                                 in_=pt[:, :].rearrange("p (k n) -> p k n", k=8)),
             reads=[bpt], writes=[hTb])


def _make_hT(C, K, x_dram, st, prew, hT, hTb, n_tiles=4):
    _tr_part(C, K, _norm_part(C, K, x_dram, st, prew, n_tiles), hT, hTb)


def build_gdn(S, stage=99, C=None, io=None, tag="a1"):
    standalone = C is None
    if standalone:
        nc = bass.Bass("TRN2", target_bir_lowering=False)
        C = Ctx(nc)
        io = dict(x=C.dram_in("x", [S, D_MODEL]), wg=C.dram_in("wg", [D_MODEL, 2056]),
                  convw=C.dram_in("convw", [1536, 4]), prew=C.dram_in("prew", [D_MODEL]),
                  alog=C.dram_in("alog", [4]), dtb=C.dram_in("dtb", [4]), gnw=C.dram_in("gnw", [128]),
                  o_gdn=C.dram_out("o_gdn", [S, 512]), mem=C.dram_in("mem", [MEM_LEN, D_MODEL]),
                  mnw=C.dram_in("mnw", [D_MODEL]), wkv=C.dram_in("wkv", [D_MODEL, 1024]),
                  wm=C.dram_in("wm", [D_MODEL, 1024]), o_mem=C.dram_out("o_mem", [S, 512]))
    nc = C.nc
    P = C.P
    NT = S // 128
    NST = S // 512
    x_d, wg_d, convw_d, prew_d = io["x"], io["wg"], io["convw"], io["prew"]
    alog_d, dtb_d, gnw_d, o_d = io["alog"], io["dtb"], io["gnw"], io["o_gdn"]
    mem_d, mnw_d, wkv_d, wm_d, om_d = io["mem"], io["mnw"], io["wkv"], io["wm"], io["o_mem"]
    bo_d = Buf("o_d", multi=True)
    bom_d = Buf("om_d", multi=True)
    with C.phase(tag):
        K = _consts(C)
        ident, bi = K["ident"]; ones, bon = K["ones"]; triu, btr = K["triu"]
        mincl, bmi = K["mincl"]; mstr, bms = K["mstr"]; epst, be = K["eps"]
        prew = _bcast_load(C, prew_d, D_MODEL, "prew")
        gnw = _bcast_load(C, gnw_d, 128, "gnw")
        alog = _bcast_load(C, alog_d, 4, "alog")
        dtb = _bcast_load(C, dtb_d, 4, "dtb")
        cw, bcw = C.sb([128, 12, 4], F32, "cw")
        P.dma("sp", cw[:], convw_d.rearrange("(c p) j -> p c j", p=128), writes=[bcw], semname="ld_cw")
        negA, bnA = C.sb([128, 4], F32, "negA")
        P.op("act", lambda e: e.activation(out=negA[:], in_=alog[0][:], func=AF.Exp), reads=[alog[1]], writes=[bnA])
        P.op("dve", lambda e: e.tensor_scalar(out=negA[:], in0=negA[:], scalar1=-1.0, scalar2=None, op0=ALU.mult),
             reads=[bnA], writes=[bnA])
        wg, bwg = C.sb([128, 8, 2056], BF16, "wg_sb")
        _load_w(C, wg, bwg, wg_d, 2056, "ld_wg")
        hT, bhT = C.sb([128, 8, 512], BF16, "hT")
        cin, _ = C.sb([128, 12, 515], F32, "cin")
        qkv, _ = C.sb([128, 12, 512], F32, "qkvT")
        bcins = [Buf(f"cin{c}") for c in range(12)]
        bqkvs = [Buf(f"qkv{c}") for c in range(12)]
        S_t = [C.sb([128, 128], F32, f"S{h}") for h in range(4)]
        C.pool("xt", 2, [128, 1024], F32)
        C.pool("junk", 1, [128, 1024], F32)
        C.pool("col", 10, [128, 8], F32)
        C.pool("hb", 4, [128, 1024], BF16)
        C.pool("pst", 1, [128, 1024], BF16, psum=True)
        C.pool("ps", 4, [128, 512], F32, psum=True)
        C.pool("ps2", 3, [128, 512], F32, psum=True)
        C.pool("cacc", 2, [128, 512], F32)
        C.pool("sz", 2, [128, 512], F32)
        C.pool("sm", 4, [128, 32], F32)
        hm = [[C.sb([128, 128], F32, f"hm{h}_{i}") for i in range(13)] for h in range(4)]
        hpb = [[C.sb([128, 256], F32, f"hpb{h}_{i}") for i in range(2)] for h in range(4)]
        C.pool("ocat", 2, [128, 512], F32)
        P.op("pool", lambda e: e.memset(cin[:, :, 0:3], 0.0), writes=bcins)
        mnw = _bcast_load(C, mnw_d, D_MODEL, "mnw")
        wm, bwm = C.sb([128, 8, 1024], BF16, "wm_sb")
        wkv, bwkv = wm, bwm
        _load_w(C, wkv, bwkv, wkv_d, 1024, "ld_wkv")
        mT, bmT = C.sb([128, 8, 256], BF16, "mT")
        mkT, bmk = C.sb([128, 4, 256], BF16, "mkT")
        mva, bmv = C.sb([128, 2, 2, 257], BF16, "mva")
        mq, bmq = C.sb([128, 4, 512], BF16, "mq")
        C.pool("pTm", 2, [128, 128], BF16)
        C.pool("smz", 2, [128, 512], F32)
        C.pool("omem", 2, [128, 512], F32)
        _make_hT(C, K, mem_d, 0, mnw, mT, bmT, n_tiles=2)
        P.op("pool", lambda e: e.memset(mva[:, :, :, 256:257], 1.0), writes=[bmv])
        for c in range(4):
            pp, bpp = C.get("ps")

            def mmk_(e, pp=pp, c=c):
                for kc in range(8):
                    ins = e.matmul(pp[:, 0:256], lhsT=wkv[:, kc, c * 128:(c + 1) * 128], rhs=mT[:, kc, :],
                                   start=(kc == 0), stop=(kc == 7))
                return ins
            P.op("pe", mmk_, reads=[bwkv, bmT], writes=[bpp])
            P.op("act", lambda e, pp=pp, c=c: e.copy(out=mkT[:, c, :], in_=pp[:, 0:256]), reads=[bpp], writes=[bmk])
        for mt in range(2):
            pp, bpp = C.get("ps")

            def mmv_(e, pp=pp, mt=mt):
                for kc in range(8):
                    ins = e.matmul(pp[:, :], lhsT=mT[:, kc, mt * 128:(mt + 1) * 128], rhs=wkv[:, kc, 512:1024],
                                   start=(kc == 0), stop=(kc == 7))
                return ins
            P.op("pe", mmv_, reads=[bwkv, bmT], writes=[bpp])
            P.op("act", lambda e, pp=pp, mt=mt: e.copy(out=mva[:, mt, :, 0:256],
                                                       in_=pp[:, :].rearrange("p (h e) -> p h e", h=2)),
                 reads=[bpp], writes=[bmv])
        _load_w(C, wm, bwm, wm_d, 1024, "ld_wm")
        for h in range(4):
            P.op("pool", lambda e, h=h: e.tensor_tensor(out=_r(S_t[h][0][:]), in0=ident[:], in1=ident[:], op=ALU.subtract),
                 reads=[bi], writes=[S_t[h][1]])

        hbs_next = _norm_part(C, K, x_d, 0, prew)
        for st in range(NST):
            _tr_part(C, K, hbs_next, hT, bhT)
            for c in range(12):
                pp, bpp = C.get("ps")
                bcin = bcins[c]
                bqkv = bqkvs[c]

                def mm(e, pp=pp, c=c):
                    for kc in range(8):
                        ins = e.matmul(pp[:, :], lhsT=wg[:, kc, c * 128:(c + 1) * 128], rhs=hT[:, kc, :],
                                       start=(kc == 0), stop=(kc == 7))
                    return ins
                P.op("pe", mm, reads=[bwg, bhT], writes=[bpp])
                P.op("act", lambda e, pp=pp, c=c: e.copy(out=cin[:, c, 3:515], in_=pp[:, :]), reads=[bpp], writes=[bcin])
                acc, bacc = C.get("cacc")

                P.op("dve", lambda e, acc=acc, c=c: e.tensor_scalar(
                    out=acc[:], in0=cin[:, c, 0:512], scalar1=cw[:, c, 0:1], scalar2=None, op0=ALU.mult),
                    reads=[bcin, bcw], writes=[bacc])
                for j in range(1, 4):
                    P.op("dve", lambda e, acc=acc, c=c, j=j: e.scalar_tensor_tensor(
                        out=acc[:], in0=cin[:, c, j:j + 512], scalar=cw[:, c, j:j + 1], in1=acc[:], op0=ALU.mult,
                        op1=ALU.add), reads=[bcin, bcw, bacc], writes=[bacc])
                P.op("pool", lambda e, c=c: e.tensor_copy(out=cin[:, c, 0:3], in_=cin[:, c, 512:515]),
                     reads=[bcin, bacc], writes=[bcin])
                P.op("act", lambda e, acc=acc, c=c: e.activation(out=_r(qkv[:, c, :]), in_=acc[:], func=AF.Silu),
                     reads=[bacc], writes=[bqkv])
            for c in range(4):
                pp, bpp = C.get("ps")

                def mmq_(e, pp=pp, c=c):
                    for kc in range(8):
                        ins = e.matmul(pp[:, :], lhsT=wm[:, kc, c * 128:(c + 1) * 128], rhs=hT[:, kc, :],
                                       start=(kc == 0), stop=(kc == 7))
                    return ins
                P.op("pe", mmq_, reads=[bwm, bhT], writes=[bpp])
                P.op("act", lambda e, pp=pp, c=c: e.copy(out=mq[:, c, :], in_=pp[:, :]), reads=[bpp], writes=[bmq])
            if st + 1 < NST:
                hbs_next = _norm_part(C, K, x_d, st + 1, prew)
            tile_res = {}

            def pro_gen(t):
                tsl = slice(t * 128, (t + 1) * 128)
                r0 = (st * 4 + t) * 128
                pmz, bpmz = C.get("ps2")

                def mmmz(e, pmz=pmz, tsl=tsl):
                    for kc in range(8):
                        ins = e.matmul(pmz[:, :], lhsT=hT[:, kc, tsl], rhs=wm[:, kc, 512:1024], start=(kc == 0),
                                       stop=(kc == 7))
                    return ins
                P.op("pe", mmmz, reads=[bwm, bhT], writes=[bpmz])
                smz, bsmz = C.get("smz")
                P.op("act", lambda e, smz=smz, pmz=pmz: e.activation(out=smz[:], in_=pmz[:, :], func=AF.Silu),
                     reads=[bpmz], writes=[bsmz])
                yield
                omem, bomem = C.get("omem")
                for hd in range(2):
                    accM, baM = C.get("ps2")
                    for mt in range(2):
                        scm, bscm = C.get("ps2")

                        def mmsc(e, scm=scm, hd=hd, mt=mt, tsl=tsl):
                            for dc in range(2):
                                ins = e.matmul(scm[:, 0:128], lhsT=mkT[:, hd * 2 + dc, mt * 128:(mt + 1) * 128],
                                               rhs=mq[:, hd * 2 + dc, tsl], start=(dc == 0), stop=(dc == 1))
                            return ins
                        P.op("pe", mmsc, reads=[bmk, bmq], writes=[bscm])
                        pTm, bpTm = C.get("pTm")
                        P.op("act", lambda e, pTm=pTm, scm=scm: e.activation(out=pTm[:], in_=scm[:, 0:128], func=AF.Exp,
                                                                             scale=1.0 / 16.0), reads=[bscm], writes=[bpTm])
                        P.op("pe", lambda e, accM=accM, pTm=pTm, mt=mt, hd=hd: e.matmul(
                            accM[:, 0:257], lhsT=pTm[:], rhs=mva[:, mt, hd, :], start=(mt == 0), stop=(mt == 1)),
                            reads=[bpTm, bmv], writes=[baM])
                        yield
                    cl, bcl = C.get("col")
                    P.op("dve", lambda e, cl=cl, accM=accM: e.reciprocal(out=cl[:, 0:1], in_=accM[:, 256:257]),
                         reads=[baM], writes=[bcl])
                    P.op("dve", lambda e, omem=omem, accM=accM, cl=cl, smz=smz, hd=hd: e.scalar_tensor_tensor(
                        out=omem[:, hd * 256:(hd + 1) * 256], in0=accM[:, 0:256], scalar=cl[:, 0:1],
                        in1=smz[:, hd * 256:(hd + 1) * 256], op0=ALU.mult, op1=ALU.mult),
                        reads=[baM, bcl, bsmz, bomem], writes=[bomem])
                    yield
                P.dma("sp", om_d[r0:r0 + 128, :], omem[:], reads=[bomem], writes=[bom_d], semname="st_" + bomem.name)
                pab, bpab = C.get("ps2")

                def mmab(e, pab=pab, tsl=tsl):
                    for kc in range(8):
                        ins = e.matmul(pab[:, 0:8], lhsT=hT[:, kc, tsl], rhs=wg[:, kc, 1536:1544],
                                       start=(kc == 0), stop=(kc == 7))
                    return ins
                P.op("pe", mmab, reads=[bwg, bhT], writes=[bpab])
                pz, bpz = C.get("ps2")

                def mmz(e, pz=pz, tsl=tsl):
                    for kc in range(8):
                        ins = e.matmul(pz[:, :], lhsT=hT[:, kc, tsl], rhs=wg[:, kc, 1544:2056],
                                       start=(kc == 0), stop=(kc == 7))
                    return ins
                P.op("pe", mmz, reads=[bwg, bhT], writes=[bpz])
                sz, bsz = C.get("sz")
                P.op("act", lambda e, sz=sz, pz=pz: e.activation(out=sz[:], in_=pz[:, :], func=AF.Silu),
                     reads=[bpz], writes=[bsz])
                yield
                sm, bsm = C.get("sm")

                def small1(e, sm=sm, pab=pab):
                    e.tensor_tensor(out=sm[:, 0:4], in0=pab[:, 0:4], in1=dtb[0][:], op=ALU.add)
                    return e.tensor_copy(out=sm[:, 4:8], in_=pab[:, 4:8])
                P.op("dve", small1, reads=[bpab, dtb[1]], writes=[bsm])
                yield

                def small2(e, sm=sm):
                    e.activation(out=sm[:, 0:4], in_=sm[:, 0:4], func=AF.Exp)
                    return e.activation(out=sm[:, 4:8], in_=sm[:, 4:8], func=AF.Exp, scale=-1.0)
                P.op("act", small2, reads=[bsm], writes=[bsm])
                P.op("act", lambda e, sm=sm: e.activation(out=sm[:, 0:4], in_=sm[:, 0:4], func=AF.Ln,
                                                          bias=ones[:, 0:1], scale=1.0), reads=[bsm, bon],
                     writes=[bsm])
                yield

                def small3(e, sm=sm):
                    e.tensor_tensor(out=sm[:, 0:4], in0=sm[:, 0:4], in1=negA[:], op=ALU.mult)
                    return e.tensor_scalar(out=sm[:, 4:8], in0=sm[:, 4:8], scalar1=1.0, scalar2=None, op0=ALU.add)
                P.op("dve", small3, reads=[bsm, bnA], writes=[bsm])
                P.op("dve", lambda e, sm=sm: e.reciprocal(out=sm[:, 4:8], in_=sm[:, 4:8]), reads=[bsm], writes=[bsm])
                yield
                P.op("act", lambda e, sm=sm: e.activation(out=sm[:, 8:12], in_=sm[:, 4:8], func=AF.Ln),
                     reads=[bsm], writes=[bsm])
                pg, bpg = C.get("ps2")

                def mmg(e, pg=pg, sm=sm):
                    e.matmul(pg[:, 0:4], lhsT=triu[:], rhs=sm[:, 0:4], start=True, stop=True)
                    return e.matmul(pg[:, 4:8], lhsT=ones[:], rhs=sm[:, 0:4], start=True, stop=True)
                P.op("pe", mmg, reads=[bsm, btr, bon], writes=[bpg])
                yield

                def small4(e, sm=sm, pg=pg):
                    e.tensor_copy(out=sm[:, 12:16], in_=pg[:, 0:4])
                    return e.tensor_scalar(out=sm[:, 16:20], in0=pg[:, 0:4], scalar1=-1.0, scalar2=None, op0=ALU.mult)
                P.op("dve", small4, reads=[bpg], writes=[bsm])
                P.op("dve", lambda e, sm=sm, pg=pg: e.tensor_tensor(out=sm[:, 28:32], in0=pg[:, 4:8], in1=sm[:, 12:16],
                                                                    op=ALU.subtract), reads=[bpg, bsm], writes=[bsm])

                def small5(e, sm=sm, pg=pg):
                    e.activation(out=sm[:, 20:24], in_=pg[:, 4:8], func=AF.Exp)
                    e.activation(out=sm[:, 24:28], in_=sm[:, 12:16], func=AF.Exp)
                    return e.activation(out=sm[:, 28:32], in_=sm[:, 28:32], func=AF.Exp)
                P.op("act", small5, reads=[bsm, bpg], writes=[bsm])
                yield
                P.op("dve", lambda e, sm=sm: e.tensor_tensor(out=sm[:, 24:28], in0=sm[:, 24:28], in1=sm[:, 4:8],
                                                             op=ALU.mult), reads=[bsm], writes=[bsm])
                ocat, boc = C.get("ocat")
                tile_res[t] = dict(sm=sm, bsm=bsm, sz=sz, bsz=bsz, ocat=ocat, boc=boc, tsl=tsl, r0=r0)

            for _ in pro_gen(0):
                pass
            for c in range(8 if stage >= 2 else 0):
                bqkv = bqkvs[c]
                sq, bsq = C.get("cacc")
                P.op("pool", lambda e, sq=sq, c=c: e.tensor_tensor(out=sq[:], in0=qkv[:, c, :], in1=qkv[:, c, :],
                                                                   op=ALU.mult), reads=[bqkv], writes=[bsq])
                pp, bpp = C.get("ps")
                P.op("pe", lambda e, pp=pp, sq=sq: e.matmul(pp[:, :], lhsT=ones[:], rhs=sq[:], start=True, stop=True),
                     reads=[bsq, bon], writes=[bpp])
                rn, brn = C.get("cacc")
                P.op("act", lambda e, rn=rn, pp=pp: e.activation(out=rn[:], in_=pp[:, :], func=AF.Sqrt,
                                                                  bias=epst[:, 0:1], scale=1.0),
                     reads=[bpp, be], writes=[brn])
                P.op("dve", lambda e, rn=rn: e.reciprocal(out=rn[:], in_=rn[:]), reads=[brn], writes=[brn])
                sc = (128.0 ** -0.5) if c < 4 else 1.0
                P.op("dve", lambda e, rn=rn, c=c, sc=sc: e.scalar_tensor_tensor(
                    out=_r(qkv[:, c, :]), in0=qkv[:, c, :], scalar=sc, in1=rn[:], op0=ALU.mult, op1=ALU.mult),
                    reads=[bqkv, brn], writes=[bqkv])
            for t in range(4):
                tr_ = tile_res[t]
                sm, bsm, sz, bsz = tr_['sm'], tr_['bsm'], tr_['sz'], tr_['bsz']
                ocat, boc, tsl, r0 = tr_['ocat'], tr_['boc'], tr_['tsl'], tr_['r0']

                def head_gen(h, sm=sm, bsm=bsm, sz=sz, bsz=bsz, ocat=ocat, boc=boc, tsl=tsl):
                    qT = qkv[:, h, tsl]
                    kT = qkv[:, 4 + h, tsl]
                    vT = qkv[:, 8 + h, tsl]
                    St, bS = S_t[h]
                    bqkv_h = [bqkvs[h], bqkvs[4 + h], bqkvs[8 + h]]
                    M = hm[h]
                    (gtri, bgt), (gtri2, bgt2), (E3, bE3), (E1, bE1), (E2, bE2) = M[0], M[1], M[2], M[3], M[4]
                    (Bm, bB), (attT, bat), (kb, bkb), (kd, bkd), (vb, bvb), (qd, bqd) = M[5], M[6], M[7], M[8], M[9], M[10]
                    P.op("pool", lambda e: e.tensor_scalar(
                        out=gtri[:], in0=triu[:], scalar1=sm[:, h:h + 1], scalar2=None, op0=ALU.mult),
                        reads=[bsm, btr], writes=[bgt])
                    P.op("dve", lambda e: e.scalar_tensor_tensor(
                        out=gtri2[:], in0=ident[:], scalar=sm[:, 8 + h:9 + h], in1=gtri[:], op0=ALU.mult, op1=ALU.add),
                        reads=[bsm, bi, bgt], writes=[bgt2])
                    yield
                    pX, bpX = C.get("ps")

                    def mmx(e):
                        e.matmul(pX[:, 0:128], lhsT=ones[:], rhs=gtri[:], start=True, stop=True)
                        e.matmul(pX[:, 128:256], lhsT=ones[:], rhs=gtri[:], start=True, stop=False)
                        e.matmul(pX[:, 128:256], lhsT=ident[:], rhs=mincl[:], start=False, stop=True)
                        e.matmul(pX[:, 256:384], lhsT=ones[:], rhs=gtri2[:], start=True, stop=False)
                        return e.matmul(pX[:, 256:384], lhsT=ident[:], rhs=mstr[:], start=False, stop=True)
                    P.op("pe", mmx, reads=[bgt, bgt2, bon, bi, bmi, bms], writes=[bpX])

                    def exps(e):
                        e.activation(out=_r(E3[:]), in_=pX[:, 0:128], func=AF.Exp)
                        e.activation(out=_r(E1[:]), in_=pX[:, 128:256], func=AF.Exp, bias=sm[:, 16 + h:17 + h], scale=1.0)
                        return e.activation(out=_r(E2[:]), in_=pX[:, 256:384], func=AF.Exp, bias=sm[:, 16 + h:17 + h],
                                            scale=1.0)
                    P.op("act", exps, reads=[bpX, bsm], writes=[bE3, bE1, bE2])
                    yield
                    pK, bpK = C.get("ps")

                    def mmk(e):
                        e.matmul(pK[:, 0:128], lhsT=_r(kT), rhs=_r(kT), start=True, stop=True)
                        e.matmul(pK[:, 128:256], lhsT=_r(kT), rhs=_r(qT), start=True, stop=True)
                        e.transpose(pK[:, 256:384], kT, ident[:])
                        return e.transpose(pK[:, 384:512], vT, ident[:])
                    P.op("pe", mmk, reads=bqkv_h + [bi], writes=[bpK])

                    def ev1(e):
                        e.tensor_tensor(out=_r(Bm[:]), in0=pK[:, 0:128], in1=E2[:], op=ALU.mult)
                        e.tensor_tensor(out=_r(attT[:]), in0=pK[:, 128:256], in1=E1[:], op=ALU.mult)
                        e.tensor_scalar(out=_r(kb[:]), in0=pK[:, 256:384], scalar1=sm[:, 24 + h:25 + h], scalar2=None,
                                        op0=ALU.mult)
                        e.tensor_scalar(out=_r(kd[:]), in0=pK[:, 256:384], scalar1=sm[:, 28 + h:29 + h], scalar2=None,
                                        op0=ALU.mult)
                        return e.tensor_scalar(out=_r(vb[:]), in0=pK[:, 384:512], scalar1=sm[:, 4 + h:5 + h], scalar2=None,
                                               op0=ALU.mult)
                    P.op("dve", ev1, reads=[bpK, bE1, bE2, bsm], writes=[bB, bat, bkb, bkd, bvb])
                    P.op("pool", lambda e: e.tensor_tensor(out=_r(qd[:]), in0=qT, in1=E3[:], op=ALU.mult),
                         reads=[bqkvs[h], bE3], writes=[bqd])
                    yield
                    pA, bpA = C.get("ps")
                    P.op("pe", lambda e: e.transpose(pA[:, 0:128], Bm[:], ident[:]), reads=[bB, bi], writes=[bpA])
                    PT, bPT = M[11]
                    P.op("act", lambda e: e.copy(out=_r(PT[:]), in_=pA[:, 0:128]), reads=[bpA], writes=[bPT])
                    PB, bPB = hpb[h][0]
                    P.op("pool", lambda e: e.tensor_tensor(out=_r(PB[:, 128:256]), in0=ident[:], in1=Bm[:], op=ALU.subtract),
                         reads=[bB, bi], writes=[bPB])
                    yield
                    pN, bpN = C.get("ps")

                    def n0(e, pN=pN, PT=PT):
                        e.matmul(pN[:, 0:128], lhsT=_r(PT[:]), rhs=_r(Bm[:]), start=True, stop=True)
                        return e.matmul(pN[:, 256:384], lhsT=_r(Bm[:]), rhs=_r(PT[:]), start=True, stop=True)
                    P.op("pe", n0, reads=[bPT, bB], writes=[bpN])
                    PT2, bPT2 = M[12]
                    P.op("act", lambda e, pN=pN: e.copy(out=_r(PT2[:]), in_=pN[:, 256:384]), reads=[bpN], writes=[bPT2])
                    P.op("dve", lambda e, pN=pN, PB=PB: e.tensor_copy(out=_r(PB[:, 0:128]), in_=pN[:, 0:128]), reads=[bpN],
                         writes=[bPB])
                    PT, bPT = PT2, bPT2
                    cur = 1
                    yield
                    for j in range(1, 7):
                        last = (j == 6)
                        pN, bpN = C.get("ps")

                        def nj(e, pN=pN, PT=PT, PB=PB, last=last):
                            if last:
                                return e.matmul(pN[:, 128:256], lhsT=_r(PT[:]), rhs=_r(PB[:, 128:256]), start=True, stop=True)
                            e.matmul(pN[:, 0:256], lhsT=_r(PT[:]), rhs=_r(PB[:, 0:256]), start=True, stop=True)
                            return e.matmul(pN[:, 256:384], lhsT=_r(PB[:, 0:128]), rhs=_r(PT[:]), start=True, stop=True)
                        P.op("pe", nj, reads=[bPT, bPB], writes=[bpN])
                        PBn, bPBn = hpb[h][j % 2]
                        if not last:
                            PTn, bPTn = M[11 + (1 - cur)]
                            P.op("act", lambda e, PTn=PTn, pN=pN, PBn=PBn: (
                                e.copy(out=_r(PTn[:]), in_=pN[:, 256:384]),
                                e.copy(out=_r(PBn[:, 0:128]), in_=pN[:, 0:128]))[1], reads=[bpN], writes=[bPTn, bPBn])
                        P.op("dve", lambda e, PBn=PBn, PB=PB, pN=pN: e.tensor_tensor(
                            out=_r(PBn[:, 128:256]), in0=PB[:, 128:256], in1=pN[:, 128:256], op=ALU.add),
                            reads=[bpN, bPB], writes=[bPBn])
                        PB, bPB = PBn, bPBn
                        if not last:
                            PT, bPT = PTn, bPTn
                            cur = 1 - cur
                        yield
                    TT = PB[:, 128:256]
                    pW, bpW = C.get("ps")
                    P.op("pe", lambda e: e.matmul(pW[:, 0:128], lhsT=_r(kb[:]), rhs=_r(TT), start=True, stop=True),
                         reads=[bkb, bPB], writes=[bpW])
                    nwT, bnw = M[2]
                    P.op("act", lambda e: e.activation(out=_r(nwT[:]), in_=pW[:, 0:128], func=AF.Copy, scale=-1.0),
                         reads=[bpW], writes=[bnw])
                    yield
                    pV, bpV = C.get("ps")

                    def mv(e):
                        e.matmul(pV[:, 0:128], lhsT=_r(TT), rhs=_r(vb[:]), start=True, stop=False)
                        return e.matmul(pV[:, 0:128], lhsT=_r(nwT[:]), rhs=_r(St[:]), start=False, stop=True)
                    P.op("pe", mv, reads=[bPB, bvb, bnw, bS], writes=[bpV])
                    vn, bvn = M[3]
                    P.op("act", lambda e: e.copy(out=_r(vn[:]), in_=pV[:, 0:128]), reads=[bpV], writes=[bvn])
                    yield
                    pO, bpO = C.get("ps")

                    def mo(e):
                        e.matmul(pO[:, 0:128], lhsT=_r(qd[:]), rhs=_r(St[:]), start=True, stop=False)
                        e.matmul(pO[:, 0:128], lhsT=_r(attT[:]), rhs=_r(vn[:]), start=False, stop=True)
                        return e.matmul(pO[:, 128:256], lhsT=_r(kd[:]), rhs=_r(vn[:]), start=True, stop=True)
                    P.op("pe", mo, reads=[bqd, bS, bat, bvn, bkd], writes=[bpO])
                    P.op("dve", lambda e: e.scalar_tensor_tensor(
                        out=_r(St[:]), in0=St[:], scalar=sm[:, 20 + h:21 + h], in1=pO[:, 128:256], op0=ALU.mult, op1=ALU.add),
                        reads=[bpO, bsm, bS], writes=[bS])
                    jk, bjk = M[4]
                    ss, bss = C.get("col")
                    P.op("act", lambda e: e.activation(out=_r(jk[:]), in_=pO[:, 0:128], func=AF.Square, accum_out=ss[:, 0:1]),
                         reads=[bpO], writes=[bjk, bss])
                    P.op("act", lambda e: e.activation(out=ss[:, 1:2], in_=ss[:, 0:1], func=AF.Sqrt, bias=epst[:, 0:1],
                                                       scale=1.0 / 128.0), reads=[bss, be], writes=[bss])
                    P.op("dve", lambda e: e.reciprocal(out=ss[:, 2:3], in_=ss[:, 1:2]), reads=[bss], writes=[bss])
                    P.op("dve", lambda e: e.scalar_tensor_tensor(
                        out=_r(jk[:]), in0=pO[:, 0:128], scalar=ss[:, 2:3], in1=gnw[0][:], op0=ALU.mult, op1=ALU.mult),
                        reads=[bpO, bss, gnw[1], bjk], writes=[bjk])
                    P.op("dve", lambda e: e.tensor_tensor(
                        out=ocat[:, h * 128:(h + 1) * 128], in0=jk[:], in1=sz[:, h * 128:(h + 1) * 128], op=ALU.mult),
                        reads=[bjk, bsz, boc], writes=[boc])

                gens = [head_gen(h) for h in range(4)]
                if t + 1 < 4:
                    gens.append(pro_gen(t + 1))
                while gens:
                    for g in list(gens):
                        try:
                            next(g)
                        except StopIteration:
                            gens.remove(g)
                P.dma("sp", o_d[r0:r0 + 128, :], ocat[:], reads=[boc], writes=[bo_d], semname="st_" + boc.name)
        P.final_waits("sp", [bo_d, bom_d])
    if standalone:
        P.finish()
    return nc


def build_diff(S, C=None, io=None, tag="a2"):
    standalone = C is None
    if standalone:
        nc = bass.Bass("TRN2", target_bir_lowering=False)
        C = Ctx(nc)
        io = dict(x=C.dram_in("x", [S, D_MODEL]), wd=C.dram_in("wd", [D_MODEL, 2048]),
                  prew=C.dram_in("prew", [D_MODEL]), lamv=C.dram_in("lamv", [256]), dnw=C.dram_in("dnw", [128]),
                  qx=C.dram_in("qx", [8, 512]), alb=C.dram_in("alb", [128, 128]), li=C.dram_in("li", [2]),
                  o_diff=C.dram_out("o_diff", [S, 512]))
    nc = C.nc
    P = C.P
    NT = S // 128
    NST = S // 512
    x_d, wd_d, prew_d, lamv_d, dnw_d = io["x"], io["wd"], io["prew"], io["lamv"], io["dnw"]
    qx_d, alb_d, li_d, o_d = io["qx"], io["alb"], io["li"], io["o_diff"]
    bo_d = Buf("o_d", multi=True)
    with C.phase(tag):
        K = _consts(C)
        ident, bi = K["ident"]; ones, bon = K["ones"]; mincl, bmi = K["mincl"]; epst, be = K["eps"]
        identb, bib = K["identb"]
        minclb, bmb = C.sb([128, 128], BF16, "minclb")
        P.op("pool", lambda e: e.tensor_copy(out=minclb[:], in_=mincl[:]), reads=[bmi], writes=[bmb])
        prew = _bcast_load(C, prew_d, D_MODEL, "prew")
        dnw = _bcast_load(C, dnw_d, 128, "dnw")
        lamv = _bcast_load(C, lamv_d, 256, "lamv")
        li = _bcast_load(C, li_d, 2, "li")
        alb, balb = C.sb([128, 128], F32, "alb_sb")
        P.dma("sp", alb[:], alb_d[:, :], writes=[balb], semname="ld_alb")
        lm, blm = C.sb([128, 8], F32, "lm")
        lp, blp = C.sb([128, 128], F32, "lp")

        def lam1(e):
            e.tensor_tensor(out=lp[:, 0:64], in0=lamv[0][:, 0:64], in1=lamv[0][:, 64:128], op=ALU.mult)
            return e.tensor_tensor(out=lp[:, 64:128], in0=lamv[0][:, 128:192], in1=lamv[0][:, 192:256], op=ALU.mult)
        P.op("dve", lam1, reads=[lamv[1]], writes=[blp])

        def lam2(e):
            e.reduce_sum(out=lm[:, 0:1], in_=lp[:, 0:64], axis=mybir.AxisListType.X)
            return e.reduce_sum(out=lm[:, 1:2], in_=lp[:, 64:128], axis=mybir.AxisListType.X)
        P.op("dve", lam2, reads=[blp], writes=[blm])
        P.op("act", lambda e: e.activation(out=lm[:, 2:4], in_=lm[:, 0:2], func=AF.Exp), reads=[blm], writes=[blm])
        P.op("dve", lambda e: e.tensor_tensor(out=lm[:, 4:5], in0=lm[:, 3:4], in1=lm[:, 2:3], op=ALU.subtract),
             reads=[blm], writes=[blm])
        P.op("dve", lambda e: e.tensor_scalar(out=lm[:, 5:6], in0=lm[:, 4:5], scalar1=li[0][:, 0:1], scalar2=None,
                                              op0=ALU.add), reads=[blm, li[1]], writes=[blm])
        P.op("dve", lambda e: e.tensor_scalar(out=dnw[0][:], in0=dnw[0][:], scalar1=li[0][:, 1:2], scalar2=None,
                                              op0=ALU.mult), reads=[dnw[1], li[1]], writes=[dnw[1]])
        wd, bwd = C.sb([128, 8, 2048], BF16, "wd_sb")
        _load_w(C, wd, bwd, wd_d, 2048, "ld_wd")
        hT, bhT = C.sb([128, 8, 512], BF16, "hT")
        ka = [[C.sb([66, S], BF16, f"ka{h}{m}") for m in range(2)] for h in range(4)]
        qa = [[C.sb([66, 512], BF16, f"qa{h}{m}") for m in range(2)] for h in range(4)]
        vaug, _ = C.sb([128, NT, 4, 129], BF16, "vaug")
        bv = [Buf(f"v{t}") for t in range(NT)]
        bka = [[[Buf(f"ka{h}{m}_{s}") for s in range(NST)] for m in range(2)] for h in range(4)]
        for h in range(4):
            for m in range(2):
                P.op("pool", lambda e, h=h, m=m: e.memset(ka[h][m][0][64:66, :], 1.0), writes=bka[h][m])
                P.dma("pool", qa[h][m][0][64:66, :], qx_d[2 * h:2 * h + 2, :], writes=[qa[h][m][1]], semname=f"ld_qx{h}{m}")
        P.op("pool", lambda e: e.memset(vaug[:, :, :, 128:129], 1.0), writes=bv)
        C.pool("xt", 2, [128, 1024], F32)
        C.pool("junk", 1, [128, 1024], F32)
        C.pool("col", 12, [128, 8], F32)
        C.pool("hb", 4, [128, 1024], BF16)
        C.pool("pst", 1, [128, 1024], BF16, psum=True)
        C.pool("ps", 3, [128, 512], F32, psum=True)
        C.pool("acc", 4, [128, 512], F32, psum=True)
        C.pool("pT", 4, [128, 512], BF16)
        C.pool("szd", 1, [128, 4, 512], F32)
        C.pool("o0", 2, [128, 4, 128], F32)
        C.pool("ob", 8, [128, 128], F32)
        C.pool("ocat", 1, [128, 4, 512], F32)

        hbs_next = _norm_part(C, K, x_d, 0, prew)
        for I in range(NST):
            _tr_part(C, K, hbs_next, hT, bhT)
            for h in range(4):
                for which, col0 in (("q", h * 128), ("k", 512 + h * 128)):
                    pp, bpp = C.get("ps")

                    def mm(e, pp=pp, col0=col0):
                        for kc in range(8):
                            ins = e.matmul(pp[:, :], lhsT=wd[:, kc, col0:col0 + 128], rhs=hT[:, kc, :],
                                           start=(kc == 0), stop=(kc == 7))
                        return ins
                    P.op("pe", mm, reads=[bwd, bhT], writes=[bpp])
                    for m in range(2):
                        if which == "q":
                            dst, bd = qa[h][m][0][0:64, :], qa[h][m][1]
                        else:
                            dst, bd = ka[h][m][0][0:64, I * 512:(I + 1) * 512], bka[h][m][I]
                        eng = "act" if m == 0 else "dve"
                        if eng == "act":
                            P.op("act", lambda e, dst=dst, pp=pp, m=m: e.copy(out=dst, in_=pp[64 * m:64 * m + 64, :]),
                                 reads=[bpp], writes=[bd])
                        else:
                            P.op("dve", lambda e, dst=dst, pp=pp, m=m: e.tensor_copy(out=dst, in_=pp[64 * m:64 * m + 64, :]),
                                 reads=[bpp], writes=[bd])
            szd, bszd = C.get("szd")
            for t in range(4):
                tsl = slice(t * 128, (t + 1) * 128)
                tile = I * 4 + t
                pv_, bpv = C.get("ps")

                def mmv(e, pv_=pv_, tsl=tsl):
                    for kc in range(8):
                        ins = e.matmul(pv_[:, :], lhsT=hT[:, kc, tsl], rhs=wd[:, kc, 1024:1536], start=(kc == 0),
                                       stop=(kc == 7))
                    return ins
                P.op("pe", mmv, reads=[bwd, bhT], writes=[bpv])
                P.op("dve", lambda e, pv_=pv_, tile=tile: e.tensor_copy(
                    out=vaug[:, tile, :, 0:128], in_=pv_[:, :].rearrange("p (h e) -> p h e", h=4)),
                    reads=[bpv], writes=[bv[tile]])
                pz, bpz = C.get("ps")

                def mmz(e, pz=pz, tsl=tsl):
                    for kc in range(8):
                        ins = e.matmul(pz[:, :], lhsT=hT[:, kc, tsl], rhs=wd[:, kc, 1536:2048], start=(kc == 0),
                                       stop=(kc == 7))
                    return ins
                P.op("pe", mmz, reads=[bwd, bhT], writes=[bpz])
                P.op("act", lambda e, szd=szd, pz=pz, t=t: e.activation(out=szd[:, t, :], in_=pz[:, :], func=AF.Silu),
                     reads=[bpz], writes=[bszd])
            if I + 1 < NST:
                hbs_next = _norm_part(C, K, x_d, I + 1, prew)
            ocat, boc = C.get("ocat")
            nj = 4 * I + 4
            items = []
            for h in range(4):
                o0, bo0 = C.get("o0")
                for m in range(2):
                    accA, baA = C.get("acc")
                    accB, baB = C.get("acc")
                    for j in range(nj):
                        items.append(dict(h=h, m=m, j=j, o0=o0, bo0=bo0, accA=accA, baA=baA, accB=accB, baB=baB))

            def issue_sc(it):
                h, m, j = it["h"], it["m"], it["j"]
                jj = j - 4 * I
                q0 = 128 * jj if jj > 0 else 0
                sc, bsc = C.get("ps")

                def mms(e):
                    ins = e.matmul(sc[:, q0:512], lhsT=ka[h][m][0][0:66, j * 128:(j + 1) * 128],
                                   rhs=qa[h][m][0][0:66, q0:512], start=True, stop=(jj < 0))
                    if jj >= 0:
                        ins = e.matmul(sc[:, q0:q0 + 128], lhsT=identb[:], rhs=minclb[:], start=False, stop=True,
                                       skip_group_check=True)
                    return ins
                P.op("pe", mms, reads=[bka[h][m][j // 4], qa[h][m][1], bib, bmb], writes=[bsc])
                pT, bpT = C.get("pT")
                bcol = h * 32 + (jj + 28)
                P.op("act", lambda e: e.activation(out=pT[:, q0:512], in_=sc[:, q0:512], func=AF.Exp,
                                                   bias=alb[:, bcol:bcol + 1], scale=0.125),
                     reads=[bsc, balb], writes=[bpT])
                it["pT"], it["bpT"], it["jj"] = pT, bpT, jj

            def acc_of(it, ss):
                return (it["accA"], ss * 129, it["baA"]) if ss < 3 else (it["accB"], 0, it["baB"])

            def issue_pv(it):
                h, j, jj, pT = it["h"], it["j"], it["jj"], it["pT"]

                def mmpv(e):
                    ins = None
                    for ss in range(max(jj, 0), 4):
                        a, off, _ = acc_of(it, ss)
                        first = (j == 0) and (ss == 0 or ss == 3)
                        ins = e.matmul(a[:, off:off + 129], lhsT=pT[:, ss * 128:(ss + 1) * 128], rhs=vaug[:, j, h, :],
                                       start=first, stop=(j == 4 * I + ss), skip_group_check=True)
                    return ins
                P.op("pe", mmpv, reads=[it["bpT"], bv[j]], writes=[it["baA"], it["baB"]])

            def finalize(it):
                h, m, o0, bo0 = it["h"], it["m"], it["o0"], it["bo0"]
                for ss in range(4):
                    a, off, ba = acc_of(it, ss)
                    cl, bcl = C.get("col")
                    P.op("dve", lambda e, cl=cl, a=a, off=off: e.reciprocal(out=cl[:, 0:1], in_=a[:, off + 128:off + 129]),
                         reads=[ba], writes=[bcl])
                    if m == 0:
                        P.op("dve", lambda e, ss=ss, a=a, off=off, cl=cl: e.tensor_scalar(
                            out=o0[:, ss, :], in0=a[:, off:off + 128], scalar1=cl[:, 0:1], scalar2=None, op0=ALU.mult),
                            reads=[ba, bcl], writes=[bo0])
                        continue
                    ob, bob = C.get("ob")
                    P.op("dve", lambda e, ob=ob, a=a, off=off, cl=cl: e.tensor_scalar(
                        out=ob[:], in0=a[:, off:off + 128], scalar1=cl[:, 0:1], scalar2=None, op0=ALU.mult),
                        reads=[ba, bcl], writes=[bob])
                    P.op("dve", lambda e, ob=ob, ss=ss: e.scalar_tensor_tensor(
                        out=ob[:], in0=ob[:], scalar=lm[:, 5:6], in1=o0[:, ss, :], op0=ALU.mult, op1=ALU.add),
                        reads=[bob, bo0, blm], writes=[bob])
                    jk, bjk = C.get("ob")
                    P.op("act", lambda e, jk=jk, ob=ob, cl=cl: e.activation(out=jk[:], in_=ob[:], func=AF.Square,
                                                                             accum_out=cl[:, 1:2]),
                         reads=[bob, bcl], writes=[bjk, bcl])
                    P.op("act", lambda e, cl=cl: e.activation(out=cl[:, 2:3], in_=cl[:, 1:2], func=AF.Sqrt,
                                                              bias=epst[:, 0:1], scale=1.0 / 128.0),
                         reads=[bcl, be], writes=[bcl])
                    P.op("dve", lambda e, cl=cl: e.reciprocal(out=cl[:, 3:4], in_=cl[:, 2:3]), reads=[bcl], writes=[bcl])
                    P.op("dve", lambda e, ob=ob, cl=cl: e.scalar_tensor_tensor(
                        out=ob[:], in0=ob[:], scalar=cl[:, 3:4], in1=dnw[0][:], op0=ALU.mult, op1=ALU.mult),
                        reads=[bob, bcl, dnw[1]], writes=[bob])
                    P.op("dve", lambda e, ob=ob, ss=ss: e.tensor_tensor(
                        out=ocat[:, ss, h * 128:(h + 1) * 128], in0=ob[:], in1=szd[:, ss, h * 128:(h + 1) * 128],
                        op=ALU.mult), reads=[bob, bszd, boc], writes=[boc])

            pending = []
            for it in items:
                issue_sc(it)
                pending.append(it)
                if len(pending) > 2:
                    p_ = pending.pop(0)
                    issue_pv(p_)
                    if p_["j"] == nj - 1:
                        finalize(p_)
            for p_ in pending:
                issue_pv(p_)
                if p_["j"] == nj - 1:
                    finalize(p_)
            for ss in range(4):
                r0 = (I * 4 + ss) * 128
                P.dma("sp", o_d[r0:r0 + 128, :], ocat[:, ss, :], reads=[boc], writes=[bo_d], semname="st_" + boc.name)
        P.final_waits("sp", [bo_d])
    if standalone:
        P.finish()
    return nc


def diff_consts(hh):
    slopes = [2.0 ** (-(4 * hh + h + 1)) for h in range(4)]
    q = np.arange(512)
    qx = np.zeros((8, 512), np.float32)
    alb = np.zeros((128, 128), np.float32)
    k = np.arange(128, dtype=np.float64)
    for h in range(4):
        qx[2 * h] = -8.0 * slopes[h] * (q % 128)
        qx[2 * h + 1] = -8.0 * slopes[h] * 128.0 * (q // 128)
        for d in range(32):
            alb[:, h * 32 + d] = slopes[h] * (k + 128.0 * (d - 28))
    return qx, alb


def build_merge(TB):
    nc = bass.Bass("TRN2", target_bir_lowering=False)
    C = Ctx(nc)
    P = C.P
    NT = TB // 128
    x_d = C.dram_in("x", [TB, D_MODEL])
    oc_d = C.dram_in("oc", [TB, 3072])
    wgt_d = C.dram_in("wgate", [D_MODEL, 3072])
    wbr_d = C.dram_in("wbr", [3072, D_MODEL])
    wout_d = C.dram_in("wout", [D_MODEL, D_MODEL])
    prew_d = C.dram_in("prew", [D_MODEL])
    postw_d = C.dram_in("postw", [D_MODEL])
    y_d = C.dram_out("xo", [TB, D_MODEL])
    by_d = Buf("y_d", multi=True)
    with C.phase("b"):
        K = _consts(C)
        identb, bib = K["identb"]; epst, be = K["eps"]
        prew = _bcast_load(C, prew_d, D_MODEL, "prew")
        postw = _bcast_load(C, postw_d, D_MODEL, "postw")
        wgt, bwgt = C.sb([128, 8, 3072], BF16, "wgt_sb")
        _load_w(C, wgt, bwgt, wgt_d, 3072, "ld_wgt")
        wbr, bwbr = C.sb([128, 24, 1024], BF16, "wbr_sb")
        src = wbr_d.rearrange("(kc p) n -> p kc n", p=128)
        for kc in range(24):
            P.dma("pool", wbr[:, kc, :], src[:, kc, :], writes=[bwbr] if kc == 0 else [], semname="ld_wbr")
        bwbr.w = {("d", "ld_wbr"): P.dma_sems["ld_wbr"]}
        wout, bwout = C.sb([128, 8, 1024], BF16, "wout_sb")
        _load_w(C, wout, bwout, wout_d, 1024, "ld_wout")
        C.pool("xt", 2, [128, 1024], F32)
        C.pool("junk", 1, [128, 1024], F32)
        C.pool("col", 4, [128, 8], F32)
        C.pool("hb", 2, [128, 1024], BF16)
        C.pool("pst", 2, [128, 1024], BF16, psum=True)
        C.pool("ps", 5, [128, 512], F32, psum=True)
        C.pool("hT", 2, [128, 8, 128], BF16)
        C.pool("ob", 2, [128, 3072], BF16)
        C.pool("oT", 2, [128, 24, 128], BF16)
        C.pool("sig", 1, [128, 3, 1024], F32)
        C.pool("y", 2, [128, 1024], F32)
        C.pool("yb", 2, [128, 1024], BF16)
        C.pool("yT", 2, [128, 8, 128], BF16)
        C.pool("tmp", 2, [128, 512], F32)
        C.pool("xo", 2, [128, 1024], F32)
        for t in range(NT):
            r0 = t * 128
            hT, bhT = C.get("hT")
            xt, bx = C.get("xt")
            P.dma("sp", xt[:], x_d[r0:r0 + 128, :], writes=[bx], semname="ld_" + bx.name)
            sq, bsq = C.get("junk")
            ss, bss = C.get("col")
            P.op("act", lambda e, xt=xt, sq=sq, ss=ss: e.activation(out=sq[:], in_=xt[:], func=AF.Square,
                                                                     accum_out=ss[:, 0:1]), reads=[bx], writes=[bsq, bss])
            P.op("act", lambda e, ss=ss: e.activation(out=ss[:, 1:2], in_=ss[:, 0:1], func=AF.Sqrt, bias=epst[:, 0:1],
                                                      scale=1.0 / D_MODEL), reads=[bss, be], writes=[bss])
            P.op("dve", lambda e, ss=ss: e.reciprocal(out=ss[:, 2:3], in_=ss[:, 1:2]), reads=[bss], writes=[bss])
            hb, bhb = C.get("hb")
            P.op("dve", lambda e, hb=hb, xt=xt, ss=ss: e.scalar_tensor_tensor(
                out=hb[:], in0=xt[:], scalar=ss[:, 2:3], in1=prew[0][:], op0=ALU.mult, op1=ALU.mult),
                reads=[bx, bss, prew[1]], writes=[bhb])
            pt, bpt = C.get("pst")

            def tr(e, hb=hb, pt=pt):
                for kc in range(8):
                    ins = e.transpose(pt[:, kc * 128:(kc + 1) * 128], hb[:, kc * 128:(kc + 1) * 128], identb[:])
                return ins
            P.op("pe", tr, reads=[bhb, bib], writes=[bpt])
            P.op("act", lambda e, pt=pt, hT=hT: e.copy(out=hT[:, :, :], in_=pt[:, :].rearrange("p (k n) -> p k n", k=8)),
                 reads=[bpt], writes=[bhT])
            ob, bob = C.get("ob")
            P.dma("pool", ob[:], oc_d[r0:r0 + 128, :], writes=[bob], semname="ld_" + bob.name)
            oT, boT = C.get("oT")
            for g in range(3):
                pt, bpt = C.get("pst")

                def tr2(e, ob=ob, pt=pt, g=g):
                    for kc in range(8):
                        c0 = (g * 8 + kc) * 128
                        ins = e.transpose(pt[:, kc * 128:(kc + 1) * 128], ob[:, c0:c0 + 128], identb[:])
                    return ins
                P.op("pe", tr2, reads=[bob, bib], writes=[bpt])
                P.op("dve" if g == 1 else "act", (lambda e, pt=pt, oT=oT, g=g: e.tensor_copy(
                    out=oT[:, g * 8:(g + 1) * 8, :], in_=pt[:, :].rearrange("p (k n) -> p k n", k=8))) if g == 1 else
                    (lambda e, pt=pt, oT=oT, g=g: e.copy(out=oT[:, g * 8:(g + 1) * 8, :],
                                                        in_=pt[:, :].rearrange("p (k n) -> p k n", k=8))),
                    reads=[bpt], writes=[boT])
            sig, bsig = C.get("sig")
            for br in range(3):
                for half in range(2):
                    pg, bpg = C.get("ps")
                    c0 = br * 1024 + half * 512

                    def mmg(e, pg=pg, c0=c0, hT=hT):
                        for kc in range(8):
                            ins = e.matmul(pg[:, :], lhsT=hT[:, kc, :], rhs=wgt[:, kc, c0:c0 + 512], start=(kc == 0),
                                           stop=(kc == 7))
                        return ins
                    P.op("pe", mmg, reads=[bhT, bwgt], writes=[bpg])
                    P.op("act", lambda e, pg=pg, sig=sig, br=br, half=half: e.activation(
                        out=sig[:, br, half * 512:(half + 1) * 512], in_=pg[:, :], func=AF.Sigmoid),
                        reads=[bpg], writes=[bsig])
            y, by = C.get("y")
            for half in range(2):
                hs = slice(half * 512, (half + 1) * 512)
                for br in range(3):
                    pb, bpb = C.get("ps")

                    def mmb(e, pb=pb, br=br, half=half, oT=oT):
                        for kc in range(8):
                            ins = e.matmul(pb[:, :], lhsT=oT[:, br * 8 + kc, :],
                                           rhs=wbr[:, br * 8 + kc, half * 512:(half + 1) * 512], start=(kc == 0),
                                           stop=(kc == 7))
                        return ins
                    P.op("pe", mmb, reads=[boT, bwbr], writes=[bpb])
                    if br == 0:
                        P.op("dve", lambda e, y=y, pb=pb, sig=sig, hs=hs: e.tensor_tensor(
                            out=y[:, hs], in0=pb[:, :], in1=sig[:, 0, hs], op=ALU.mult), reads=[bpb, bsig, by], writes=[by])
                    else:
                        tmp, btmp = C.get("tmp")
                        P.op("dve", lambda e, tmp=tmp, pb=pb, sig=sig, hs=hs, br=br: e.tensor_tensor(
                            out=tmp[:], in0=pb[:, :], in1=sig[:, br, hs], op=ALU.mult), reads=[bpb, bsig], writes=[btmp])
                        P.op("pool", lambda e, y=y, tmp=tmp, hs=hs: e.tensor_tensor(
                            out=y[:, hs], in0=y[:, hs], in1=tmp[:], op=ALU.add), reads=[btmp, by], writes=[by])
            yb, byb = C.get("yb")
            P.op("act", lambda e, yb=yb, y=y: e.copy(out=yb[:], in_=y[:]), reads=[by], writes=[byb])
            pt, bpt = C.get("pst")

            def tr3(e, yb=yb, pt=pt):
                for kc in range(8):
                    ins = e.transpose(pt[:, kc * 128:(kc + 1) * 128], yb[:, kc * 128:(kc + 1) * 128], identb[:])
                return ins
            P.op("pe", tr3, reads=[byb, bib], writes=[bpt])
            yT, byT = C.get("yT")
            P.op("act", lambda e, pt=pt, yT=yT: e.copy(out=yT[:, :, :], in_=pt[:, :].rearrange("p (k n) -> p k n", k=8)),
                 reads=[bpt], writes=[byT])
            xo, bxo = C.get("xo")
            cl, bcl = C.get("col")
            for half in range(2):
                po, bpo = C.get("ps")

                def mmo(e, po=po, half=half, yT=yT):
                    for kc in range(8):
                        ins = e.matmul(po[:, :], lhsT=yT[:, kc, :], rhs=wout[:, kc, half * 512:(half + 1) * 512],
                                       start=(kc == 0), stop=(kc == 7))
                    return ins
                P.op("pe", mmo, reads=[byT, bwout], writes=[bpo])
                P.op("act", lambda e, po=po, xo=xo, half=half: e.copy(out=xo[:, half * 512:(half + 1) * 512], in_=po[:, :]),
                     reads=[bpo], writes=[bxo])
            sq, bsq = C.get("junk")
            P.op("act", lambda e, sq=sq, xo=xo, cl=cl: e.activation(out=sq[:], in_=xo[:], func=AF.Square,
                                                                    accum_out=cl[:, 0:1]), reads=[bxo], writes=[bsq, bcl])
            P.op("act", lambda e, cl=cl: e.activation(out=cl[:, 1:2], in_=cl[:, 0:1], func=AF.Sqrt, bias=epst[:, 0:1],
                                                      scale=1.0 / D_MODEL), reads=[bcl, be], writes=[bcl])
            P.op("dve", lambda e, cl=cl: e.reciprocal(out=cl[:, 2:3], in_=cl[:, 1:2]), reads=[bcl], writes=[bcl])
            P.op("dve", lambda e, xo=xo, cl=cl: e.scalar_tensor_tensor(
                out=xo[:], in0=xo[:], scalar=cl[:, 2:3], in1=postw[0][:], op0=ALU.mult, op1=ALU.mult),
                reads=[bxo, bcl, postw[1]], writes=[bxo])
            P.op("pool", lambda e, xo=xo, xt=xt: e.tensor_tensor(out=xo[:], in0=xo[:], in1=xt[:], op=ALU.add),
                 reads=[bxo, bx], writes=[bxo])
            P.dma("sp", y_d[r0:r0 + 128, :], xo[:], reads=[bxo], writes=[by_d], semname="st_" + bxo.name)
        P.final_waits("sp", [by_d])
    P.finish()
    return nc


def phase_b1(C, S, io, tag):
    P = C.P
    NT = S // 128
    x_d, oc_d, wgt_d, wbr_d, prew_d, yp_d = io["x"], io["oc"], io["wgate"], io["wbrm"], io["prew"], io["yp"]
    byp = Buf("yp_d", multi=True)
    with C.phase(tag):
        K = _consts(C)
        identb, bib = K["identb"]; epst, be = K["eps"]
        prew = _bcast_load(C, prew_d, D_MODEL, "prew")
        wgt, bwgt = C.sb([128, 8, 3072], BF16, "wgt_sb")
        _load_w(C, wgt, bwgt, wgt_d, 3072, "ld_wgt")
        wbr, bwbr = C.sb([128, 12, 1024], BF16, "wbr_sb")
        src = wbr_d.rearrange("(kc p) n -> p kc n", p=128)
        for kc in range(12):
            P.dma("pool", wbr[:, kc, :], src[:, kc, :], writes=[bwbr] if kc == 0 else [], semname="ld_wbr")
        bwbr.w = {("d", "ld_wbr"): P.dma_sems["ld_wbr"]}
        C.pool("xt", 3, [128, 1024], F32)
        C.pool("junk", 1, [128, 1024], F32)
        C.pool("col", 6, [128, 8], F32)
        C.pool("hb", 3, [128, 1024], BF16)
        C.pool("pst", 2, [128, 1024], BF16, psum=True)
        C.pool("ps", 5, [128, 512], F32, psum=True)
        C.pool("hT1", 3, [128, 8, 128], BF16)
        C.pool("ob", 3, [128, 1536], BF16)
        C.pool("oT", 3, [128, 12, 128], BF16)
        C.pool("sig", 2, [128, 3, 1024], F32)
        C.pool("y", 2, [128, 1024], F32)
        C.pool("ybf", 2, [128, 1024], BF16)
        C.pool("tmp", 4, [128, 512], F32)

        def front_a(t):
            r0 = t * 128
            xt, bx = C.get("xt")
            P.dma("sp", xt[:], x_d[r0:r0 + 128, :], writes=[bx], semname="ld_" + bx.name)
            sq, bsq = C.get("junk")
            ss, bss = C.get("col")
            P.op("act", lambda e: e.activation(out=sq[:, 0:1024], in_=xt[:], func=AF.Square, accum_out=ss[:, 0:1]),
                 reads=[bx], writes=[bsq, bss])
            P.op("act", lambda e: e.activation(out=ss[:, 1:2], in_=ss[:, 0:1], func=AF.Sqrt, bias=epst[:, 0:1],
                                               scale=1.0 / D_MODEL), reads=[bss, be], writes=[bss])
            P.op("dve", lambda e: e.reciprocal(out=ss[:, 2:3], in_=ss[:, 1:2]), reads=[bss], writes=[bss])
            hb, bhb = C.get("hb")
            P.op("dve", lambda e: e.scalar_tensor_tensor(out=hb[:], in0=xt[:], scalar=ss[:, 2:3], in1=prew[0][:],
                                                         op0=ALU.mult, op1=ALU.mult),
                 reads=[bx, bss, prew[1]], writes=[bhb])
            ob, bob = C.get("ob")
            P.dma("pool", ob[:], oc_d[r0:r0 + 128, :], writes=[bob], semname="ld_" + bob.name)
            return dict(r0=r0, hb=hb, bhb=bhb, ob=ob, bob=bob)

        def front(d0):
            r0, hb, bhb, ob, bob = d0["r0"], d0["hb"], d0["bhb"], d0["ob"], d0["bob"]
            hT, bhT = C.get("hT1")
            pt, bpt = C.get("pst")

            def tr(e):
                for kc in range(8):
                    ins = e.transpose(pt[:, kc * 128:(kc + 1) * 128], hb[:, kc * 128:(kc + 1) * 128], identb[:])
                return ins
            P.op("pe", tr, reads=[bhb, bib], writes=[bpt])
            P.op("act", lambda e: e.copy(out=hT[:, :, :], in_=pt[:, :].rearrange("p (k n) -> p k n", k=8)),
                 reads=[bpt], writes=[bhT])
            oT, boT = C.get("oT")
            for g in range(2):
                pt, bpt = C.get("pst")
                nk = 8 if g == 0 else 4

                def tr2(e, ob=ob, pt=pt, g=g, nk=nk):
                    for kc in range(nk):
                        c0 = (g * 8 + kc) * 128
                        ins = e.transpose(pt[:, kc * 128:(kc + 1) * 128], ob[:, c0:c0 + 128], identb[:])
                    return ins
                P.op("pe", tr2, reads=[bob, bib], writes=[bpt])
                if g == 0:
                    P.op("act", lambda e, pt=pt, oT=oT: e.copy(
                        out=oT[:, 0:8, :], in_=pt[:, :].rearrange("p (k n) -> p k n", k=8)), reads=[bpt], writes=[boT])
                else:
                    P.op("dve", lambda e, pt=pt, oT=oT: e.tensor_copy(
                        out=oT[:, 8:12, :], in_=pt[:, 0:512].rearrange("p (k n) -> p k n", k=4)), reads=[bpt],
                        writes=[boT])
            return dict(hT=hT, bhT=bhT, oT=oT, boT=boT, r0=r0)

        def back(d):
            hT, bhT, oT, boT, r0 = d["hT"], d["bhT"], d["oT"], d["boT"], d["r0"]
            sig, bsig = C.get("sig")
            for br in range(3):
                for half in range(2):
                    pg, bpg = C.get("ps")
                    c0 = br * 1024 + half * 512

                    def mmg(e, pg=pg, c0=c0, hT=hT):
                        for kc in range(8):
                            ins = e.matmul(pg[:, :], lhsT=hT[:, kc, :], rhs=wgt[:, kc, c0:c0 + 512], start=(kc == 0),
                                           stop=(kc == 7))
                        return ins
                    P.op("pe", mmg, reads=[bhT, bwgt], writes=[bpg])
                    P.op("act", lambda e, pg=pg, sig=sig, br=br, half=half: e.activation(
                        out=sig[:, br, half * 512:(half + 1) * 512], in_=pg[:, :], func=AF.Sigmoid),
                        reads=[bpg], writes=[bsig])
            y, by = C.get("y")
            for half in range(2):
                hs = slice(half * 512, (half + 1) * 512)
                for br in range(3):
                    pb, bpb = C.get("ps")

                    def mmb(e, pb=pb, br=br, half=half, oT=oT):
                        for kc in range(4):
                            ins = e.matmul(pb[:, :], lhsT=oT[:, br * 4 + kc, :],
                                           rhs=wbr[:, br * 4 + kc, half * 512:(half + 1) * 512], start=(kc == 0),
                                           stop=(kc == 3))
                        return ins
                    P.op("pe", mmb, reads=[boT, bwbr], writes=[bpb])
                    if br == 0:
                        P.op("dve", lambda e, y=y, pb=pb, sig=sig, hs=hs: e.tensor_tensor(
                            out=y[:, hs], in0=pb[:, :], in1=sig[:, 0, hs], op=ALU.mult), reads=[bpb, bsig, by], writes=[by])
                    else:
                        tmp, btmp = C.get("tmp")
                        P.op("dve", lambda e, tmp=tmp, pb=pb, sig=sig, hs=hs, br=br: e.tensor_tensor(
                            out=tmp[:], in0=pb[:, :], in1=sig[:, br, hs], op=ALU.mult), reads=[bpb, bsig], writes=[btmp])
                        P.op("pool", lambda e, y=y, tmp=tmp, hs=hs: e.tensor_tensor(
                            out=y[:, hs], in0=y[:, hs], in1=tmp[:], op=ALU.add), reads=[btmp, by], writes=[by])
            yb, byb = C.get("ybf")
            P.op("act", lambda e: e.copy(out=yb[:], in_=y[:]), reads=[by], writes=[byb])
            P.dma("sp", yp_d[r0:r0 + 128, :], yb[:], reads=[byb], writes=[byp], semname="st_" + byb.name)

        fa = front_a(0)
        nxt = front(fa)
        for t in range(NT):
            cur_ = nxt
            if t + 1 < NT:
                fa = front_a(t + 1)
            back(cur_)
            if t + 1 < NT:
                nxt = front(fa)
        P.final_waits("sp", [byp])


def phase_b2(C, TB, io, tag):
    P = C.P
    NT = TB // 128
    ys_d, xr_d, wout_d, postw_d, xo_d = io["ysum"], io["xres"], io["wout"], io["postw"], io["xout"]
    bxo_d = Buf("xo_d", multi=True)
    with C.phase(tag):
        K = _consts(C)
        identb, bib = K["identb"]; epst, be = K["eps"]
        postw = _bcast_load(C, postw_d, D_MODEL, "postw")
        wout, bwout = C.sb([128, 8, 1024], BF16, "wout_sb")
        _load_w(C, wout, bwout, wout_d, 1024, "ld_wout")
        C.pool("xt", 2, [128, 1024], F32)
        C.pool("junk", 1, [128, 1024], F32)
        C.pool("col", 4, [128, 8], F32)
        C.pool("pst", 2, [128, 1024], BF16, psum=True)
        C.pool("ps", 4, [128, 512], F32, psum=True)
        C.pool("yb", 2, [128, 1024], BF16)
        C.pool("yT", 2, [128, 8, 128], BF16)
        C.pool("xo", 2, [128, 1024], F32)
        for t in range(NT):
            r0 = t * 128
            xt, bx = C.get("xt")
            P.dma("sp", xt[:], xr_d[r0:r0 + 128, :], writes=[bx], semname="ld_" + bx.name)
            yb, byb = C.get("yb")
            P.dma("sp", yb[:], ys_d[r0:r0 + 128, :], writes=[byb], semname="ld_" + byb.name)
            pt, bpt = C.get("pst")

            def tr3(e, yb=yb, pt=pt):
                for kc in range(8):
                    ins = e.transpose(pt[:, kc * 128:(kc + 1) * 128], yb[:, kc * 128:(kc + 1) * 128], identb[:])
                return ins
            P.op("pe", tr3, reads=[byb, bib], writes=[bpt])
            yT, byT = C.get("yT")
            P.op("act", lambda e, pt=pt, yT=yT: e.copy(out=yT[:, :, :], in_=pt[:, :].rearrange("p (k n) -> p k n", k=8)),
                 reads=[bpt], writes=[byT])
            xo, bxo = C.get("xo")
            cl, bcl = C.get("col")
            for half in range(2):
                po, bpo = C.get("ps")

                def mmo(e, po=po, half=half, yT=yT):
                    for kc in range(8):
                        ins = e.matmul(po[:, :], lhsT=yT[:, kc, :], rhs=wout[:, kc, half * 512:(half + 1) * 512],
                                       start=(kc == 0), stop=(kc == 7))
                    return ins
                P.op("pe", mmo, reads=[byT, bwout], writes=[bpo])
                P.op("act", lambda e, po=po, xo=xo, half=half: e.copy(out=xo[:, half * 512:(half + 1) * 512], in_=po[:, :]),
                     reads=[bpo], writes=[bxo])
            sq, bsq = C.get("junk")
            P.op("act", lambda e, sq=sq, xo=xo, cl=cl: e.activation(out=sq[:], in_=xo[:], func=AF.Square,
                                                                    accum_out=cl[:, 0:1]), reads=[bxo], writes=[bsq, bcl])
            P.op("act", lambda e, cl=cl: e.activation(out=cl[:, 1:2], in_=cl[:, 0:1], func=AF.Sqrt, bias=epst[:, 0:1],
                                                      scale=1.0 / D_MODEL), reads=[bcl, be], writes=[bcl])
            P.op("dve", lambda e, cl=cl: e.reciprocal(out=cl[:, 2:3], in_=cl[:, 1:2]), reads=[bcl], writes=[bcl])
            P.op("dve", lambda e, xo=xo, cl=cl: e.scalar_tensor_tensor(
                out=xo[:], in0=xo[:], scalar=cl[:, 2:3], in1=postw[0][:], op0=ALU.mult, op1=ALU.mult),
                reads=[bxo, bcl, postw[1]], writes=[bxo])
            P.op("pool", lambda e, xo=xo, xt=xt: e.tensor_tensor(out=xo[:], in0=xo[:], in1=xt[:], op=ALU.add),
                 reads=[bxo, bx], writes=[bxo])
            P.dma("pool", xo_d[r0:r0 + 128, :], xo[:], reads=[bxo], writes=[bxo_d], semname="st_" + bxo.name)
        P.final_waits("sp", [bxo_d])


PAIRS = [[0, 1], [2, 3], [4, 5], [6, 7]]


def build_fused(S, L):
    nc = bass.Bass("TRN2", target_bir_lowering=False, num_devices=8)
    C = Ctx(nc)
    P = C.P
    TB = S // 2
    di = C.dram_in
    x_d = di("x", [S, D_MODEL]); xh_d = di("xhalf", [TB, D_MODEL]); mem_d = di("mem", [MEM_LEN, D_MODEL])
    wg_d = di("wg", [L, D_MODEL, 2056]); cw_d = di("convw", [L, 1536, 4]); prew_d = di("prew", [L, D_MODEL])
    alog_d = di("alog", [L, 4]); dtb_d = di("dtb", [L, 4]); gnw_d = di("gnw", [L, 128]); mnw_d = di("mnw", [L, D_MODEL])
    wkv_d = di("wkv", [L, D_MODEL, 1024]); wm_d = di("wm", [L, D_MODEL, 1024]); wd_d = di("wd", [L, D_MODEL, 2048])
    lamv_d = di("lamv", [L, 256]); dnw_d = di("dnw", [L, 128]); qx_d = di("qx", [8, 512]); alb_d = di("alb", [128, 128])
    li_d = di("li", [L, 2]); wgt_d = di("wgate", [L, D_MODEL, 3072]); wbr_d = di("wbrm", [L, 1536, D_MODEL])
    wout_d = di("wout", [L, D_MODEL, D_MODEL]); postw_d = di("postw", [L, D_MODEL])
    xo_d = C.dram_out("xo", [TB, D_MODEL])
    it = lambda name, shape: nc.dram_tensor(name, list(shape), F32, addr_space="Local", kind="Internal").ap()
    oc_i = it("oc_i", [S, 1536])
    itb = lambda name, shape: nc.dram_tensor(name, list(shape), BF16, addr_space="Local", kind="Internal").ap()
    yp_i = itb("yp_i", [S, D_MODEL])
    ys_i = itb("ys_i", [TB, D_MODEL])
    xh_i = it("xh_i", [TB, D_MODEL])
    xf_i = it("xf_i", [S, D_MODEL])
    C.gst = contextlib.ExitStack()
    C.st = C.gst
    C.pfx = "g_"
    C.K = _consts(C)
    for l in range(L):
        xs = x_d if l == 0 else xf_i
        build_gdn(S, 99, C, dict(x=xs, wg=wg_d[l], convw=cw_d[l], prew=prew_d[l], alog=alog_d[l], dtb=dtb_d[l],
                                 gnw=gnw_d[l], o_gdn=oc_i[:, 0:512], mem=mem_d, mnw=mnw_d[l], wkv=wkv_d[l], wm=wm_d[l],
                                 o_mem=oc_i[:, 1024:1536]), tag=f"L{l}a1")
        build_diff(S, C, dict(x=xs, wd=wd_d[l], prew=prew_d[l], lamv=lamv_d[l], dnw=dnw_d[l], qx=qx_d, alb=alb_d,
                              li=li_d[l], o_diff=oc_i[:, 512:1024]), tag=f"L{l}a2")
        phase_b1(C, S, dict(x=xs, oc=oc_i, wgate=wgt_d[l], wbrm=wbr_d[l], prew=prew_d[l], yp=yp_i), tag=f"L{l}b1")
        with C.phase(f"L{l}rs"):
            P.coll(lambda e: e.collective_compute("ReduceScatter", ALU.add, replica_groups=PAIRS, ins=[yp_i],
                                                  outs=[ys_i]), semname="cc_rs")
        last = (l == L - 1)
        phase_b2(C, TB, dict(ysum=ys_i, xres=(xh_d if l == 0 else xh_i), wout=wout_d[l], postw=postw_d[l],
                             xout=(xo_d if last else xh_i)), tag=f"L{l}b2")
        if not last:
            with C.phase(f"L{l}ag"):
                P.coll(lambda e: e.collective_compute("AllGather", ALU.bypass, replica_groups=PAIRS, ins=[xh_i],
                                                      outs=[xf_i]), semname="cc_ag")
    P.finish()
    C.gst.close()
    return nc


_PROGS = {}


def _c(a):
    return np.ascontiguousarray(a, dtype=np.float32)


def _core_inputs(r, L, w_in, gdn_conv_w, gdn_a_log, gdn_dt_bias, w_mem_kv, w_br_gdn, w_br_diff, w_br_mem):
    sl = lambda base: slice(base + r * 512, base + r * 512 + 512)
    wg, wd, wm, cw, wkv, wbrm, wgate = [], [], [], [], [], [], []
    for l in range(L):
        wl = np.asarray(w_in[l], np.float32)
        wg.append(np.concatenate([wl[:, sl(0)], wl[:, sl(1024)], wl[:, sl(2048)], wl[:, 3072 + r * 4:3076 + r * 4],
                                  wl[:, 3080 + r * 4:3084 + r * 4], wl[:, sl(3088)]], axis=1))
        wd.append(np.concatenate([wl[:, sl(4112)], wl[:, sl(5136)], wl[:, sl(6160)], wl[:, sl(7184)]], axis=1))
        wm.append(np.concatenate([wl[:, sl(8208)], wl[:, sl(9232)]], axis=1))
        wgate.append(wl[:, 10256:13328])
        cwl = np.asarray(gdn_conv_w[l], np.float32)
        cw.append(np.concatenate([cwl[:, sl(0)], cwl[:, sl(1024)], cwl[:, sl(2048)]], axis=1).T)
        kvl = np.asarray(w_mem_kv[l], np.float32)
        wkv.append(np.concatenate([kvl[:, sl(0)], kvl[:, sl(1024)]], axis=1))
        wbrm.append(np.concatenate([np.asarray(w_br_gdn[l])[sl(0)], np.asarray(w_br_diff[l])[sl(0)],
                                    np.asarray(w_br_mem[l])[sl(0)]], axis=0))
    st = lambda xs: _c(np.stack(xs))
    return dict(wg=st(wg), wd=st(wd), wm=st(wm), convw=st(cw), wkv=st(wkv), wbrm=st(wbrm), wgate=st(wgate),
                alog=_c(np.asarray(gdn_a_log)[:L, r * 4:r * 4 + 4]), dtb=_c(np.asarray(gdn_dt_bias)[:L, r * 4:r * 4 + 4]))


LAYERS_PER_LAUNCH = 1


def kernel(x, mem, pre_norm_w, post_norm_w, w_in, gdn_conv_w, gdn_a_log, gdn_dt_bias, gdn_norm_w, diff_lambda,
           diff_norm_w, mem_norm_w, w_mem_kv, w_br_gdn, w_br_diff, w_br_mem, w_out):
    x = np.asarray(x, np.float32)
    B, S, D = x.shape
    L = np.asarray(w_in).shape[0]
    TB = S // 2
    G = LAYERS_PER_LAUNCH
    key = (S, G)
    if key not in _PROGS:
        _PROGS[key] = build_fused(S, G)
    nc = _PROGS[key]
    li_all = np.array([[-(0.8 - 0.6 * math.exp(-0.3 * l)), 1.0 - (0.8 - 0.6 * math.exp(-0.3 * l))] for l in range(L)],
                      np.float32)
    consts = [diff_consts(r) for r in range(2)]
    for l0 in range(0, L, G):
        ls = slice(l0, l0 + G)
        shared = dict(prew=_c(np.asarray(pre_norm_w)[ls]), postw=_c(np.asarray(post_norm_w)[ls]),
                      gnw=_c(np.asarray(gdn_norm_w)[ls]), mnw=_c(np.asarray(mem_norm_w)[ls]),
                      lamv=_c(np.asarray(diff_lambda)[ls].reshape(G, 256)), dnw=_c(np.asarray(diff_norm_w)[ls]),
                      wout=_c(np.asarray(w_out)[ls]), li=_c(li_all[ls]))
        per_r = []
        for r in range(2):
            d = _core_inputs(r, G, np.asarray(w_in)[ls], np.asarray(gdn_conv_w)[ls], np.asarray(gdn_a_log)[ls],
                             np.asarray(gdn_dt_bias)[ls], np.asarray(w_mem_kv)[ls], np.asarray(w_br_gdn)[ls],
                             np.asarray(w_br_diff)[ls], np.asarray(w_br_mem)[ls])
            d.update(qx=consts[r][0], alb=consts[r][1])
            d.update(shared)
            per_r.append(d)
        in_maps = []
        for c in range(8):
            b, r = c // 2, c % 2
            m = dict(per_r[r])
            m.update(x=_c(x[b]), xhalf=_c(x[b, r * TB:(r + 1) * TB]), mem=_c(np.asarray(mem)[b]))
            in_maps.append(m)
        res = run_bass_kernel_spmd(nc, in_maps, core_ids=list(range(8))).results
        xn = np.empty((B, S, D), np.float32)
        for c in range(8):
            b, r = c // 2, c % 2
            xn[b, r * TB:(r + 1) * TB] = res[c]["xo"]
        x = xn
    return x
```

```python
import contextlib
import math
import numpy as np
import concourse.bass as bass
import concourse.mybir as mybir
from concourse.bass_utils import run_bass_kernel_spmd

F32 = mybir.dt.float32
F32R = mybir.dt.float32r
BF16 = mybir.dt.bfloat16
ALU = mybir.AluOpType
AF = mybir.ActivationFunctionType

D_MODEL = 1024
BATCH = 4
SEQ = 4096
DEPTH = 4
MEM_LEN = 256
EPS = 1e-6
IN_COLS = 13328
NEG = -1.0e30

ENGS = ("pe", "act", "dve", "pool", "sp")


class Buf:
    __slots__ = ("name", "w", "r", "excl", "multi")

    def __init__(self, name="", excl=False, multi=False):
        self.name = name
        self.w = {}
        self.r = {}
        self.multi = multi
        self.excl = excl


class Prog:
    def __init__(self, nc, same_engine_sync=True):
        self.nc = nc
        self.q = {e: [] for e in ENGS}
        self.cnt = {e: 0 for e in ENGS}
        self.seen = {e: {} for e in ENGS}
        self.dma_sems = {}
        self.same = same_engine_sync
        self.sem_handles = {}
        self.gst = None

    def _need(self, eng, reads, writes):
        need = {}

        def add(d):
            for k, v in d.items():
                if need.get(k, 0) < v:
                    need[k] = v
        for b in reads:
            add(b.w)
            if b.excl:
                add({k: v for k, v in b.r.items() if k != ("e", eng)})
        for b in writes:
            if b.multi:
                continue
            add(b.w)
            add(b.r)
        out = []
        seen = self.seen[eng]
        for k, v in need.items():
            if not self.same and k == ("e", eng):
                continue
            if seen.get(k, 0) >= v:
                continue
            seen[k] = v
            out.append((k, v))
        return out

    def op(self, eng, fn, reads=(), writes=()):
        waits = self._need(eng, reads, writes)
        self.cnt[eng] += 1
        c = self.cnt[eng]
        key = ("e", eng)
        self.q[eng].append((waits, fn, key, 1))
        for b in writes:
            b.w = {key: c}
            b.r = {}
        for b in reads:
            if b.r.get(key, 0) < c:
                b.r[key] = c

    def dma(self, eng, out_ap, in_ap, reads=(), writes=(), semname=None, **kw):
        waits = self._need(eng, reads, writes)
        key = ("d", semname)
        self.dma_sems[semname] = self.dma_sems.get(semname, 0) + 16
        c = self.dma_sems[semname]

        def fn(e, out_ap=out_ap, in_ap=in_ap, kw=kw):
            return e.dma_start(out=out_ap, in_=in_ap, **kw)
        self.q[eng].append((waits, fn, key, 16))
        for b in writes:
            if b.multi:
                b.w[key] = c
                continue
            b.w = {key: c}
            b.r = {}
        for b in reads:
            if b.r.get(key, 0) < c:
                b.r[key] = c

    def coll(self, fn, reads=(), writes=(), semname=None):
        waits = self._need("pool", reads, writes)
        key = ("d", semname)
        self.dma_sems[semname] = self.dma_sems.get(semname, 0) + 1
        c = self.dma_sems[semname]
        self.q["pool"].append((waits, fn, key, 1))
        for b in writes:
            b.w = {key: c}
            b.r = {}
        for b in reads:
            if b.r.get(key, 0) < c:
                b.r[key] = c

    def final_waits(self, eng, bufs):
        waits = self._need(eng, bufs, ())
        self.q[eng].append((waits, None, None, 0))

    def barrier(self):
        allk = [(("e", e), c) for e, c in self.cnt.items() if c > 0]
        allk += [(("d", n), c) for n, c in self.dma_sems.items()]
        for eng in ENGS:
            seen = self.seen[eng]
            waits = []
            for k, v in allk:
                if seen.get(k, 0) >= v:
                    continue
                seen[k] = v
                waits.append((k, v))
            self.q[eng].append((waits, None, None, 0))

    def _sem(self, key):
        if key not in self.sem_handles:
            if self.gst is None:
                self.gst = contextlib.ExitStack()
            nm = ("se_" if key[0] == "e" else "sd_") + key[1]
            self.sem_handles[key] = self.gst.enter_context(self.nc.semaphore(nm))
        return self.sem_handles[key]

    def flush(self):
        nc = self.nc
        for e in ENGS:
            self._sem(("e", e))
        for lst in self.q.values():
            for waits, fn, key, inc in lst:
                for k, v in waits:
                    self._sem(k)
                if key is not None:
                    self._sem(key)
        H = self.sem_handles
        q = self.q
        self.q = {e: [] for e in ENGS}
        with nc.Block() as block:
            def run(engobj, lst):
                for waits, fn, key, inc in lst:
                    for k, v in waits:
                        engobj.wait_ge(H[k], v)
                    if fn is not None:
                        fn(engobj).then_inc(H[key], inc)

            @block.tensor
            def _(e):
                run(e, q["pe"])

            @block.scalar
            def _(e):
                run(e, q["act"])

            @block.vector
            def _(e):
                run(e, q["dve"])

            @block.gpsimd
            def _(e):
                run(e, q["pool"])

            @block.sync
            def _(e):
                run(e, q["sp"])

    def emit(self):
        self.flush()

    def finish(self):
        if self.gst is not None:
            self.gst.close()
            self.gst = None


class Ctx:
    def __init__(self, nc):
        self.nc = nc
        self.P = Prog(nc)
        self.st = contextlib.ExitStack()
        self.n = 0
        self.rot = {}
        self.pfx = ""
        self.K = None
        self.gst = None

    @contextlib.contextmanager
    def phase(self, name):
        self.pfx = name + "_"
        self.rot = {}
        self.st = contextlib.ExitStack()
        with self.st:
            yield
            self.P.barrier()
            self.P.flush()

    def sb(self, shape, dt=F32, name=None):
        self.n += 1
        nm = name or f"sb{self.n}"
        t = self.st.enter_context(self.nc.sbuf_tensor(self.pfx + nm, list(shape), dt))
        return t, Buf(nm)

    def ps(self, shape, dt=F32, name=None):
        self.n += 1
        nm = name or f"ps{self.n}"
        t = self.st.enter_context(self.nc.psum_tensor(self.pfx + nm, list(shape), dt))
        return t, Buf(nm, excl=True)

    def pool(self, tag, n, shape, dt=F32, psum=False):
        self.rot[tag] = [[(self.ps if psum else self.sb)(shape, dt, f"{tag}{i}") for i in range(n)], 0]

    def get(self, tag):
        r = self.rot[tag]
        t = r[0][r[1] % len(r[0])]
        r[1] += 1
        return t

    def dram_in(self, name, shape, dt=F32):
        return self.nc.dram_tensor(name, list(shape), dt, kind="ExternalInput").ap()

    def dram_out(self, name, shape, dt=F32):
        return self.nc.dram_tensor(name, list(shape), dt, kind="ExternalOutput").ap()


def _r(ap):
    return ap


def _consts(C):
    if C.K is not None:
        return C.K
    P = C.P
    K = {}
    ident, bi = C.sb([128, 128], F32, "ident")
    ones, bo = C.sb([128, 128], F32, "ones")
    triu, bt = C.sb([128, 128], F32, "triu")
    mincl, bm1 = C.sb([128, 128], F32, "mincl")
    mstr, bm2 = C.sb([128, 128], F32, "mstr")
    identb, bib = C.sb([128, 128], BF16, "identb")
    epst, be = C.sb([128, 1], F32, "epst")

    def mk0(e):
        e.memset(ident[:], 0.0)
        e.memset(ones[:], 1.0)
        e.memset(triu[:], 1.0)
        e.memset(mincl[:], 0.0)
        e.memset(mstr[:], 0.0)
        return e.memset(epst[:], EPS)
    P.op("pool", mk0, writes=[bi, bo, bt, bm1, bm2, be])

    def mk(e):
        e.affine_select(out=ident[:], in_=ident[:], pattern=[[-1, 128]], compare_op=ALU.not_equal, fill=1.0,
                        base=0, channel_multiplier=1)
        e.affine_select(out=triu[:], in_=triu[:], pattern=[[1, 128]], compare_op=ALU.is_ge, fill=0.0,
                        base=0, channel_multiplier=-1)
        e.affine_select(out=mincl[:], in_=mincl[:], pattern=[[1, 128]], compare_op=ALU.is_ge, fill=NEG,
                        base=0, channel_multiplier=-1)
        return e.affine_select(out=mstr[:], in_=mstr[:], pattern=[[1, 128]], compare_op=ALU.is_gt, fill=NEG,
                               base=0, channel_multiplier=-1)
    P.op("pool", mk, reads=[bi, bt, bm1, bm2], writes=[bi, bt, bm1, bm2])
    P.op("pool", lambda e: e.tensor_copy(out=identb[:], in_=ident[:]), reads=[bi], writes=[bib])
    K.update(ident=(ident, bi), ones=(ones, bo), triu=(triu, bt), mincl=(mincl, bm1), mstr=(mstr, bm2),
             identb=(identb, bib), eps=(epst, be))
    return K


def _bcast_load(C, dram_vec, n, name):
    t, b = C.sb([128, n], F32, name + "_bc")
    src = dram_vec.partition_broadcast(128)
    C.P.dma("sp", t[:], src, writes=[b], semname="ld_" + name)
    return t, b


def _load_w(C, wt, wb, wdram, ncols, semname):
    src = wdram.rearrange("(kc p) n -> p kc n", p=128)
    for kc in range(8):
        C.P.dma("pool", wt[:, kc, :], src[:, kc, :], writes=[wb] if kc == 0 else [], reads=[], semname=semname)
    wb.w = {("d", semname): C.P.dma_sems[semname]}


def _norm_part(C, K, x_dram, st, prew, n_tiles=4):
    P = C.P
    epst, be = K["eps"]
    hbs = []
    for t in range(n_tiles):
        xt, bx = C.get("xt")
        r0 = (st * n_tiles + t) * 128
        P.dma("sp", xt[:], x_dram[r0:r0 + 128, :], writes=[bx], semname="ld_" + bx.name)
        sq, bsq = C.get("junk")
        ss, bss = C.get("col")
        P.op("act", lambda e, xt=xt, sq=sq, ss=ss: e.activation(out=sq[:, 0:1024], in_=xt[:], func=AF.Square,
                                                                 accum_out=ss[:, 0:1]),
             reads=[bx], writes=[bsq, bss])
        P.op("act", lambda e, ss=ss: e.activation(out=ss[:, 1:2], in_=ss[:, 0:1], func=AF.Sqrt, bias=epst[:, 0:1],
                                                  scale=1.0 / D_MODEL), reads=[bss, be], writes=[bss])
        P.op("dve", lambda e, ss=ss: e.reciprocal(out=ss[:, 2:3], in_=ss[:, 1:2]), reads=[bss], writes=[bss])
        hb, bhb = C.get("hb")
        P.op("dve", lambda e, hb=hb, xt=xt, ss=ss: e.scalar_tensor_tensor(
            out=hb[:], in0=xt[:], scalar=ss[:, 2:3], in1=prew[0][:], op0=ALU.mult, op1=ALU.mult),
            reads=[bx, bss, prew[1]], writes=[bhb])
        hbs.append((hb, bhb))
    return hbs


def _tr_part(C, K, hbs, hT, hTb):
    P = C.P
    identb, bib = K["identb"]
    for t, (hb, bhb) in enumerate(hbs):
        pt, bpt = C.get("pst")

        def tr(e, hb=hb, pt=pt):
            for kc in range(8):
                ins = e.transpose(pt[:, kc * 128:(kc + 1) * 128], hb[:, kc * 128:(kc + 1) * 128], identb[:])
            return ins
        P.op("pe", tr, reads=[bhb, bib], writes=[bpt])
        P.op("act", lambda e, pt=pt, t=t: e.copy(out=hT[:, :, t * 128:(t + 1) * 128],
                                                 in_=pt[:, :].rearrange("p (k n) -> p k n", k=8)),
             reads=[bpt], writes=[hTb])


def _make_hT(C, K, x_dram, st, prew, hT, hTb, n_tiles=4):
    _tr_part(C, K, _norm_part(C, K, x_dram, st, prew, n_tiles), hT, hTb)


def build_gdn(S, stage=99, C=None, io=None, tag="a1"):
    standalone = C is None
    if standalone:
        nc = bass.Bass("TRN2", target_bir_lowering=False)
        C = Ctx(nc)
        io = dict(x=C.dram_in("x", [S, D_MODEL]), wg=C.dram_in("wg", [D_MODEL, 2056]),
                  convw=C.dram_in("convw", [1536, 4]), prew=C.dram_in("prew", [D_MODEL]),
                  alog=C.dram_in("alog", [4]), dtb=C.dram_in("dtb", [4]), gnw=C.dram_in("gnw", [128]),
                  o_gdn=C.dram_out("o_gdn", [S, 512]), mem=C.dram_in("mem", [MEM_LEN, D_MODEL]),
                  mnw=C.dram_in("mnw", [D_MODEL]), wkv=C.dram_in("wkv", [D_MODEL, 1024]),
                  wm=C.dram_in("wm", [D_MODEL, 1024]), o_mem=C.dram_out("o_mem", [S, 512]))
    nc = C.nc
    P = C.P
    NT = S // 128
    NST = S // 512
    x_d, wg_d, convw_d, prew_d = io["x"], io["wg"], io["convw"], io["prew"]
    alog_d, dtb_d, gnw_d, o_d = io["alog"], io["dtb"], io["gnw"], io["o_gdn"]
    mem_d, mnw_d, wkv_d, wm_d, om_d = io["mem"], io["mnw"], io["wkv"], io["wm"], io["o_mem"]
    bo_d = Buf("o_d", multi=True)
    bom_d = Buf("om_d", multi=True)
    with C.phase(tag):
        K = _consts(C)
        ident, bi = K["ident"]; ones, bon = K["ones"]; triu, btr = K["triu"]
        mincl, bmi = K["mincl"]; mstr, bms = K["mstr"]; epst, be = K["eps"]
        prew = _bcast_load(C, prew_d, D_MODEL, "prew")
        gnw = _bcast_load(C, gnw_d, 128, "gnw")
        alog = _bcast_load(C, alog_d, 4, "alog")
        dtb = _bcast_load(C, dtb_d, 4, "dtb")
        cw, bcw = C.sb([128, 12, 4], F32, "cw")
        P.dma("sp", cw[:], convw_d.rearrange("(c p) j -> p c j", p=128), writes=[bcw], semname="ld_cw")
        negA, bnA = C.sb([128, 4], F32, "negA")
        P.op("act", lambda e: e.activation(out=negA[:], in_=alog[0][:], func=AF.Exp), reads=[alog[1]], writes=[bnA])
        P.op("dve", lambda e: e.tensor_scalar(out=negA[:], in0=negA[:], scalar1=-1.0, scalar2=None, op0=ALU.mult),
             reads=[bnA], writes=[bnA])
        wg, bwg = C.sb([128, 8, 2056], BF16, "wg_sb")
        _load_w(C, wg, bwg, wg_d, 2056, "ld_wg")
        hT, bhT = C.sb([128, 8, 512], BF16, "hT")
        cin, _ = C.sb([128, 12, 515], F32, "cin")
        qkv, _ = C.sb([128, 12, 512], F32, "qkvT")
        bcins = [Buf(f"cin{c}") for c in range(12)]
        bqkvs = [Buf(f"qkv{c}") for c in range(12)]
        S_t = [C.sb([128, 128], F32, f"S{h}") for h in range(4)]
        C.pool("xt", 2, [128, 1024], F32)
        C.pool("junk", 1, [128, 1024], F32)
        C.pool("col", 10, [128, 8], F32)
        C.pool("hb", 4, [128, 1024], BF16)
        C.pool("pst", 1, [128, 1024], BF16, psum=True)
        C.pool("ps", 4, [128, 512], F32, psum=True)
        C.pool("ps2", 3, [128, 512], F32, psum=True)
        C.pool("cacc", 2, [128, 512], F32)
        C.pool("sz", 2, [128, 512], F32)
        C.pool("sm", 4, [128, 32], F32)
        hm = [[C.sb([128, 128], F32, f"hm{h}_{i}") for i in range(13)] for h in range(4)]
        hpb = [[C.sb([128, 256], F32, f"hpb{h}_{i}") for i in range(2)] for h in range(4)]
        C.pool("ocat", 2, [128, 512], F32)
        P.op("pool", lambda e: e.memset(cin[:, :, 0:3], 0.0), writes=bcins)
        mnw = _bcast_load(C, mnw_d, D_MODEL, "mnw")
        wm, bwm = C.sb([128, 8, 1024], BF16, "wm_sb")
        wkv, bwkv = wm, bwm
        _load_w(C, wkv, bwkv, wkv_d, 1024, "ld_wkv")
        mT, bmT = C.sb([128, 8, 256], BF16, "mT")
        mkT, bmk = C.sb([128, 4, 256], BF16, "mkT")
        mva, bmv = C.sb([128, 2, 2, 257], BF16, "mva")
        mq, bmq = C.sb([128, 4, 512], BF16, "mq")
        C.pool("pTm", 2, [128, 128], BF16)
        C.pool("smz", 2, [128, 512], F32)
        C.pool("omem", 2, [128, 512], F32)
        _make_hT(C, K, mem_d, 0, mnw, mT, bmT, n_tiles=2)
        P.op("pool", lambda e: e.memset(mva[:, :, :, 256:257], 1.0), writes=[bmv])
        for c in range(4):
            pp, bpp = C.get("ps")

            def mmk_(e, pp=pp, c=c):
                for kc in range(8):
                    ins = e.matmul(pp[:, 0:256], lhsT=wkv[:, kc, c * 128:(c + 1) * 128], rhs=mT[:, kc, :],
                                   start=(kc == 0), stop=(kc == 7))
                return ins
            P.op("pe", mmk_, reads=[bwkv, bmT], writes=[bpp])
            P.op("act", lambda e, pp=pp, c=c: e.copy(out=mkT[:, c, :], in_=pp[:, 0:256]), reads=[bpp], writes=[bmk])
        for mt in range(2):
            pp, bpp = C.get("ps")

            def mmv_(e, pp=pp, mt=mt):
                for kc in range(8):
                    ins = e.matmul(pp[:, :], lhsT=mT[:, kc, mt * 128:(mt + 1) * 128], rhs=wkv[:, kc, 512:1024],
                                   start=(kc == 0), stop=(kc == 7))
                return ins
            P.op("pe", mmv_, reads=[bwkv, bmT], writes=[bpp])
            P.op("act", lambda e, pp=pp, mt=mt: e.copy(out=mva[:, mt, :, 0:256],
                                                       in_=pp[:, :].rearrange("p (h e) -> p h e", h=2)),
                 reads=[bpp], writes=[bmv])
        _load_w(C, wm, bwm, wm_d, 1024, "ld_wm")
        for h in range(4):
            P.op("pool", lambda e, h=h: e.tensor_tensor(out=_r(S_t[h][0][:]), in0=ident[:], in1=ident[:], op=ALU.subtract),
                 reads=[bi], writes=[S_t[h][1]])

        hbs_next = _norm_part(C, K, x_d, 0, prew)
        for st in range(NST):
            _tr_part(C, K, hbs_next, hT, bhT)
            for c in range(12):
                pp, bpp = C.get("ps")
                bcin = bcins[c]
                bqkv = bqkvs[c]

                def mm(e, pp=pp, c=c):
                    for kc in range(8):
                        ins = e.matmul(pp[:, :], lhsT=wg[:, kc, c * 128:(c + 1) * 128], rhs=hT[:, kc, :],
                                       start=(kc == 0), stop=(kc == 7))
                    return ins
                P.op("pe", mm, reads=[bwg, bhT], writes=[bpp])
                P.op("act", lambda e, pp=pp, c=c: e.copy(out=cin[:, c, 3:515], in_=pp[:, :]), reads=[bpp], writes=[bcin])
                acc, bacc = C.get("cacc")

                P.op("dve", lambda e, acc=acc, c=c: e.tensor_scalar(
                    out=acc[:], in0=cin[:, c, 0:512], scalar1=cw[:, c, 0:1], scalar2=None, op0=ALU.mult),
                    reads=[bcin, bcw], writes=[bacc])
                for j in range(1, 4):
                    P.op("dve", lambda e, acc=acc, c=c, j=j: e.scalar_tensor_tensor(
                        out=acc[:], in0=cin[:, c, j:j + 512], scalar=cw[:, c, j:j + 1], in1=acc[:], op0=ALU.mult,
                        op1=ALU.add), reads=[bcin, bcw, bacc], writes=[bacc])
                P.op("pool", lambda e, c=c: e.tensor_copy(out=cin[:, c, 0:3], in_=cin[:, c, 512:515]),
                     reads=[bcin, bacc], writes=[bcin])
                P.op("act", lambda e, acc=acc, c=c: e.activation(out=_r(qkv[:, c, :]), in_=acc[:], func=AF.Silu),
                     reads=[bacc], writes=[bqkv])
            for c in range(4):
                pp, bpp = C.get("ps")

                def mmq_(e, pp=pp, c=c):
                    for kc in range(8):
                        ins = e.matmul(pp[:, :], lhsT=wm[:, kc, c * 128:(c + 1) * 128], rhs=hT[:, kc, :],
                                       start=(kc == 0), stop=(kc == 7))
                    return ins
                P.op("pe", mmq_, reads=[bwm, bhT], writes=[bpp])
                P.op("act", lambda e, pp=pp, c=c: e.copy(out=mq[:, c, :], in_=pp[:, :]), reads=[bpp], writes=[bmq])
            if st + 1 < NST:
                hbs_next = _norm_part(C, K, x_d, st + 1, prew)
            tile_res = {}

            def pro_gen(t):
                tsl = slice(t * 128, (t + 1) * 128)
                r0 = (st * 4 + t) * 128
                pmz, bpmz = C.get("ps2")

                def mmmz(e, pmz=pmz, tsl=tsl):
                    for kc in range(8):
                        ins = e.matmul(pmz[:, :], lhsT=hT[:, kc, tsl], rhs=wm[:, kc, 512:1024], start=(kc == 0),
                                       stop=(kc == 7))
                    return ins
                P.op("pe", mmmz, reads=[bwm, bhT], writes=[bpmz])
                smz, bsmz = C.get("smz")
                P.op("act", lambda e, smz=smz, pmz=pmz: e.activation(out=smz[:], in_=pmz[:, :], func=AF.Silu),
                     reads=[bpmz], writes=[bsmz])
                yield
                omem, bomem = C.get("omem")
                for hd in range(2):
                    accM, baM = C.get("ps2")
                    for mt in range(2):
                        scm, bscm = C.get("ps2")

                        def mmsc(e, scm=scm, hd=hd, mt=mt, tsl=tsl):
                            for dc in range(2):
                                ins = e.matmul(scm[:, 0:128], lhsT=mkT[:, hd * 2 + dc, mt * 128:(mt + 1) * 128],
                                               rhs=mq[:, hd * 2 + dc, tsl], start=(dc == 0), stop=(dc == 1))
                            return ins
                        P.op("pe", mmsc, reads=[bmk, bmq], writes=[bscm])
                        pTm, bpTm = C.get("pTm")
                        P.op("act", lambda e, pTm=pTm, scm=scm: e.activation(out=pTm[:], in_=scm[:, 0:128], func=AF.Exp,
                                                                             scale=1.0 / 16.0), reads=[bscm], writes=[bpTm])
                        P.op("pe", lambda e, accM=accM, pTm=pTm, mt=mt, hd=hd: e.matmul(
                            accM[:, 0:257], lhsT=pTm[:], rhs=mva[:, mt, hd, :], start=(mt == 0), stop=(mt == 1)),
                            reads=[bpTm, bmv], writes=[baM])
                        yield
                    cl, bcl = C.get("col")
                    P.op("dve", lambda e, cl=cl, accM=accM: e.reciprocal(out=cl[:, 0:1], in_=accM[:, 256:257]),
                         reads=[baM], writes=[bcl])
                    P.op("dve", lambda e, omem=omem, accM=accM, cl=cl, smz=smz, hd=hd: e.scalar_tensor_tensor(
                        out=omem[:, hd * 256:(hd + 1) * 256], in0=accM[:, 0:256], scalar=cl[:, 0:1],
                        in1=smz[:, hd * 256:(hd + 1) * 256], op0=ALU.mult, op1=ALU.mult),
                        reads=[baM, bcl, bsmz, bomem], writes=[bomem])
                    yield
                P.dma("sp", om_d[r0:r0 + 128, :], omem[:], reads=[bomem], writes=[bom_d], semname="st_" + bomem.name)
                pab, bpab = C.get("ps2")

                def mmab(e, pab=pab, tsl=tsl):
                    for kc in range(8):
                        ins = e.matmul(pab[:, 0:8], lhsT=hT[:, kc, tsl], rhs=wg[:, kc, 1536:1544],
                                       start=(kc == 0), stop=(kc == 7))
                    return ins
                P.op("pe", mmab, reads=[bwg, bhT], writes=[bpab])
                pz, bpz = C.get("ps2")

                def mmz(e, pz=pz, tsl=tsl):
                    for kc in range(8):
                        ins = e.matmul(pz[:, :], lhsT=hT[:, kc, tsl], rhs=wg[:, kc, 1544:2056],
                                       start=(kc == 0), stop=(kc == 7))
                    return ins
                P.op("pe", mmz, reads=[bwg, bhT], writes=[bpz])
                sz, bsz = C.get("sz")
                P.op("act", lambda e, sz=sz, pz=pz: e.activation(out=sz[:], in_=pz[:, :], func=AF.Silu),
                     reads=[bpz], writes=[bsz])
                yield
                sm, bsm = C.get("sm")

                def small1(e, sm=sm, pab=pab):
                    e.tensor_tensor(out=sm[:, 0:4], in0=pab[:, 0:4], in1=dtb[0][:], op=ALU.add)
                    return e.tensor_copy(out=sm[:, 4:8], in_=pab[:, 4:8])
                P.op("dve", small1, reads=[bpab, dtb[1]], writes=[bsm])
                yield

                def small2(e, sm=sm):
                    e.activation(out=sm[:, 0:4], in_=sm[:, 0:4], func=AF.Exp)
                    return e.activation(out=sm[:, 4:8], in_=sm[:, 4:8], func=AF.Exp, scale=-1.0)
                P.op("act", small2, reads=[bsm], writes=[bsm])
                P.op("act", lambda e, sm=sm: e.activation(out=sm[:, 0:4], in_=sm[:, 0:4], func=AF.Ln,
                                                          bias=ones[:, 0:1], scale=1.0), reads=[bsm, bon],
                     writes=[bsm])
                yield

                def small3(e, sm=sm):
                    e.tensor_tensor(out=sm[:, 0:4], in0=sm[:, 0:4], in1=negA[:], op=ALU.mult)
                    return e.tensor_scalar(out=sm[:, 4:8], in0=sm[:, 4:8], scalar1=1.0, scalar2=None, op0=ALU.add)
                P.op("dve", small3, reads=[bsm, bnA], writes=[bsm])
                P.op("dve", lambda e, sm=sm: e.reciprocal(out=sm[:, 4:8], in_=sm[:, 4:8]), reads=[bsm], writes=[bsm])
                yield
                P.op("act", lambda e, sm=sm: e.activation(out=sm[:, 8:12], in_=sm[:, 4:8], func=AF.Ln),
                     reads=[bsm], writes=[bsm])
                pg, bpg = C.get("ps2")

                def mmg(e, pg=pg, sm=sm):
                    e.matmul(pg[:, 0:4], lhsT=triu[:], rhs=sm[:, 0:4], start=True, stop=True)
                    return e.matmul(pg[:, 4:8], lhsT=ones[:], rhs=sm[:, 0:4], start=True, stop=True)
                P.op("pe", mmg, reads=[bsm, btr, bon], writes=[bpg])
                yield

                def small4(e, sm=sm, pg=pg):
                    e.tensor_copy(out=sm[:, 12:16], in_=pg[:, 0:4])
                    return e.tensor_scalar(out=sm[:, 16:20], in0=pg[:, 0:4], scalar1=-1.0, scalar2=None, op0=ALU.mult)
                P.op("dve", small4, reads=[bpg], writes=[bsm])
                P.op("dve", lambda e, sm=sm, pg=pg: e.tensor_tensor(out=sm[:, 28:32], in0=pg[:, 4:8], in1=sm[:, 12:16],
                                                                    op=ALU.subtract), reads=[bpg, bsm], writes=[bsm])

                def small5(e, sm=sm, pg=pg):
                    e.activation(out=sm[:, 20:24], in_=pg[:, 4:8], func=AF.Exp)
                    e.activation(out=sm[:, 24:28], in_=sm[:, 12:16], func=AF.Exp)
                    return e.activation(out=sm[:, 28:32], in_=sm[:, 28:32], func=AF.Exp)
                P.op("act", small5, reads=[bsm, bpg], writes=[bsm])
                yield
                P.op("dve", lambda e, sm=sm: e.tensor_tensor(out=sm[:, 24:28], in0=sm[:, 24:28], in1=sm[:, 4:8],
                                                             op=ALU.mult), reads=[bsm], writes=[bsm])
                ocat, boc = C.get("ocat")
                tile_res[t] = dict(sm=sm, bsm=bsm, sz=sz, bsz=bsz, ocat=ocat, boc=boc, tsl=tsl, r0=r0)

            for _ in pro_gen(0):
                pass
            for c in range(8 if stage >= 2 else 0):
                bqkv = bqkvs[c]
                sq, bsq = C.get("cacc")
                P.op("pool", lambda e, sq=sq, c=c: e.tensor_tensor(out=sq[:], in0=qkv[:, c, :], in1=qkv[:, c, :],
                                                                   op=ALU.mult), reads=[bqkv], writes=[bsq])
                pp, bpp = C.get("ps")
                P.op("pe", lambda e, pp=pp, sq=sq: e.matmul(pp[:, :], lhsT=ones[:], rhs=sq[:], start=True, stop=True),
                     reads=[bsq, bon], writes=[bpp])
                rn, brn = C.get("cacc")
                P.op("act", lambda e, rn=rn, pp=pp: e.activation(out=rn[:], in_=pp[:, :], func=AF.Sqrt,
                                                                  bias=epst[:, 0:1], scale=1.0),
                     reads=[bpp, be], writes=[brn])
                P.op("dve", lambda e, rn=rn: e.reciprocal(out=rn[:], in_=rn[:]), reads=[brn], writes=[brn])
                sc = (128.0 ** -0.5) if c < 4 else 1.0
                P.op("dve", lambda e, rn=rn, c=c, sc=sc: e.scalar_tensor_tensor(
                    out=_r(qkv[:, c, :]), in0=qkv[:, c, :], scalar=sc, in1=rn[:], op0=ALU.mult, op1=ALU.mult),
                    reads=[bqkv, brn], writes=[bqkv])
            for t in range(4):
                tr_ = tile_res[t]
                sm, bsm, sz, bsz = tr_['sm'], tr_['bsm'], tr_['sz'], tr_['bsz']
                ocat, boc, tsl, r0 = tr_['ocat'], tr_['boc'], tr_['tsl'], tr_['r0']

                def head_gen(h, sm=sm, bsm=bsm, sz=sz, bsz=bsz, ocat=ocat, boc=boc, tsl=tsl):
                    qT = qkv[:, h, tsl]
                    kT = qkv[:, 4 + h, tsl]
                    vT = qkv[:, 8 + h, tsl]
                    St, bS = S_t[h]
                    bqkv_h = [bqkvs[h], bqkvs[4 + h], bqkvs[8 + h]]
                    M = hm[h]
                    (gtri, bgt), (gtri2, bgt2), (E3, bE3), (E1, bE1), (E2, bE2) = M[0], M[1], M[2], M[3], M[4]
                    (Bm, bB), (attT, bat), (kb, bkb), (kd, bkd), (vb, bvb), (qd, bqd) = M[5], M[6], M[7], M[8], M[9], M[10]
                    P.op("pool", lambda e: e.tensor_scalar(
                        out=gtri[:], in0=triu[:], scalar1=sm[:, h:h + 1], scalar2=None, op0=ALU.mult),
                        reads=[bsm, btr], writes=[bgt])
                    P.op("dve", lambda e: e.scalar_tensor_tensor(
                        out=gtri2[:], in0=ident[:], scalar=sm[:, 8 + h:9 + h], in1=gtri[:], op0=ALU.mult, op1=ALU.add),
                        reads=[bsm, bi, bgt], writes=[bgt2])
                    yield
                    pX, bpX = C.get("ps")

                    def mmx(e):
                        e.matmul(pX[:, 0:128], lhsT=ones[:], rhs=gtri[:], start=True, stop=True)
                        e.matmul(pX[:, 128:256], lhsT=ones[:], rhs=gtri[:], start=True, stop=False)
                        e.matmul(pX[:, 128:256], lhsT=ident[:], rhs=mincl[:], start=False, stop=True)
                        e.matmul(pX[:, 256:384], lhsT=ones[:], rhs=gtri2[:], start=True, stop=False)
                        return e.matmul(pX[:, 256:384], lhsT=ident[:], rhs=mstr[:], start=False, stop=True)
                    P.op("pe", mmx, reads=[bgt, bgt2, bon, bi, bmi, bms], writes=[bpX])

                    def exps(e):
                        e.activation(out=_r(E3[:]), in_=pX[:, 0:128], func=AF.Exp)
                        e.activation(out=_r(E1[:]), in_=pX[:, 128:256], func=AF.Exp, bias=sm[:, 16 + h:17 + h], scale=1.0)
                        return e.activation(out=_r(E2[:]), in_=pX[:, 256:384], func=AF.Exp, bias=sm[:, 16 + h:17 + h],
                                            scale=1.0)
                    P.op("act", exps, reads=[bpX, bsm], writes=[bE3, bE1, bE2])
                    yield
                    pK, bpK = C.get("ps")

                    def mmk(e):
                        e.matmul(pK[:, 0:128], lhsT=_r(kT), rhs=_r(kT), start=True, stop=True)
                        e.matmul(pK[:, 128:256], lhsT=_r(kT), rhs=_r(qT), start=True, stop=True)
                        e.transpose(pK[:, 256:384], kT, ident[:])
                        return e.transpose(pK[:, 384:512], vT, ident[:])
                    P.op("pe", mmk, reads=bqkv_h + [bi], writes=[bpK])

                    def ev1(e):
                        e.tensor_tensor(out=_r(Bm[:]), in0=pK[:, 0:128], in1=E2[:], op=ALU.mult)
                        e.tensor_tensor(out=_r(attT[:]), in0=pK[:, 128:256], in1=E1[:], op=ALU.mult)
                        e.tensor_scalar(out=_r(kb[:]), in0=pK[:, 256:384], scalar1=sm[:, 24 + h:25 + h], scalar2=None,
                                        op0=ALU.mult)
                        e.tensor_scalar(out=_r(kd[:]), in0=pK[:, 256:384], scalar1=sm[:, 28 + h:29 + h], scalar2=None,
                                        op0=ALU.mult)
                        return e.tensor_scalar(out=_r(vb[:]), in0=pK[:, 384:512], scalar1=sm[:, 4 + h:5 + h], scalar2=None,
                                               op0=ALU.mult)
                    P.op("dve", ev1, reads=[bpK, bE1, bE2, bsm], writes=[bB, bat, bkb, bkd, bvb])
                    P.op("pool", lambda e: e.tensor_tensor(out=_r(qd[:]), in0=qT, in1=E3[:], op=ALU.mult),
                         reads=[bqkvs[h], bE3], writes=[bqd])
                    yield
                    pA, bpA = C.get("ps")
                    P.op("pe", lambda e: e.transpose(pA[:, 0:128], Bm[:], ident[:]), reads=[bB, bi], writes=[bpA])
                    PT, bPT = M[11]
                    P.op("act", lambda e: e.copy(out=_r(PT[:]), in_=pA[:, 0:128]), reads=[bpA], writes=[bPT])
                    PB, bPB = hpb[h][0]
                    P.op("pool", lambda e: e.tensor_tensor(out=_r(PB[:, 128:256]), in0=ident[:], in1=Bm[:], op=ALU.subtract),
                         reads=[bB, bi], writes=[bPB])
                    yield
                    pN, bpN = C.get("ps")

                    def n0(e, pN=pN, PT=PT):
                        e.matmul(pN[:, 0:128], lhsT=_r(PT[:]), rhs=_r(Bm[:]), start=True, stop=True)
                        return e.matmul(pN[:, 256:384], lhsT=_r(Bm[:]), rhs=_r(PT[:]), start=True, stop=True)
                    P.op("pe", n0, reads=[bPT, bB], writes=[bpN])
                    PT2, bPT2 = M[12]
                    P.op("act", lambda e, pN=pN: e.copy(out=_r(PT2[:]), in_=pN[:, 256:384]), reads=[bpN], writes=[bPT2])
                    P.op("dve", lambda e, pN=pN, PB=PB: e.tensor_copy(out=_r(PB[:, 0:128]), in_=pN[:, 0:128]), reads=[bpN],
                         writes=[bPB])
                    PT, bPT = PT2, bPT2
                    cur = 1
                    yield
                    for j in range(1, 7):
                        last = (j == 6)
                        pN, bpN = C.get("ps")

                        def nj(e, pN=pN, PT=PT, PB=PB, last=last):
                            if last:
                                return e.matmul(pN[:, 128:256], lhsT=_r(PT[:]), rhs=_r(PB[:, 128:256]), start=True, stop=True)
                            e.matmul(pN[:, 0:256], lhsT=_r(PT[:]), rhs=_r(PB[:, 0:256]), start=True, stop=True)
                            return e.matmul(pN[:, 256:384], lhsT=_r(PB[:, 0:128]), rhs=_r(PT[:]), start=True, stop=True)
                        P.op("pe", nj, reads=[bPT, bPB], writes=[bpN])
                        PBn, bPBn = hpb[h][j % 2]
                        if not last:
                            PTn, bPTn = M[11 + (1 - cur)]
                            P.op("act", lambda e, PTn=PTn, pN=pN, PBn=PBn: (
                                e.copy(out=_r(PTn[:]), in_=pN[:, 256:384]),
                                e.copy(out=_r(PBn[:, 0:128]), in_=pN[:, 0:128]))[1], reads=[bpN], writes=[bPTn, bPBn])
                        P.op("dve", lambda e, PBn=PBn, PB=PB, pN=pN: e.tensor_tensor(
                            out=_r(PBn[:, 128:256]), in0=PB[:, 128:256], in1=pN[:, 128:256], op=ALU.add),
                            reads=[bpN, bPB], writes=[bPBn])
                        PB, bPB = PBn, bPBn
                        if not last:
                            PT, bPT = PTn, bPTn
                            cur = 1 - cur
                        yield
                    TT = PB[:, 128:256]
                    pW, bpW = C.get("ps")
                    P.op("pe", lambda e: e.matmul(pW[:, 0:128], lhsT=_r(kb[:]), rhs=_r(TT), start=True, stop=True),
                         reads=[bkb, bPB], writes=[bpW])
                    nwT, bnw = M[2]
                    P.op("act", lambda e: e.activation(out=_r(nwT[:]), in_=pW[:, 0:128], func=AF.Copy, scale=-1.0),
                         reads=[bpW], writes=[bnw])
                    yield
                    pV, bpV = C.get("ps")

                    def mv(e):
                        e.matmul(pV[:, 0:128], lhsT=_r(TT), rhs=_r(vb[:]), start=True, stop=False)
                        return e.matmul(pV[:, 0:128], lhsT=_r(nwT[:]), rhs=_r(St[:]), start=False, stop=True)
                    P.op("pe", mv, reads=[bPB, bvb, bnw, bS], writes=[bpV])
                    vn, bvn = M[3]
                    P.op("act", lambda e: e.copy(out=_r(vn[:]), in_=pV[:, 0:128]), reads=[bpV], writes=[bvn])
                    yield
                    pO, bpO = C.get("ps")

                    def mo(e):
                        e.matmul(pO[:, 0:128], lhsT=_r(qd[:]), rhs=_r(St[:]), start=True, stop=False)
                        e.matmul(pO[:, 0:128], lhsT=_r(attT[:]), rhs=_r(vn[:]), start=False, stop=True)
                        return e.matmul(pO[:, 128:256], lhsT=_r(kd[:]), rhs=_r(vn[:]), start=True, stop=True)
                    P.op("pe", mo, reads=[bqd, bS, bat, bvn, bkd], writes=[bpO])
                    P.op("dve", lambda e: e.scalar_tensor_tensor(
                        out=_r(St[:]), in0=St[:], scalar=sm[:, 20 + h:21 + h], in1=pO[:, 128:256], op0=ALU.mult, op1=ALU.add),
                        reads=[bpO, bsm, bS], writes=[bS])
                    jk, bjk = M[4]
                    ss, bss = C.get("col")
                    P.op("act", lambda e: e.activation(out=_r(jk[:]), in_=pO[:, 0:128], func=AF.Square, accum_out=ss[:, 0:1]),
                         reads=[bpO], writes=[bjk, bss])
                    P.op("act", lambda e: e.activation(out=ss[:, 1:2], in_=ss[:, 0:1], func=AF.Sqrt, bias=epst[:, 0:1],
                                                       scale=1.0 / 128.0), reads=[bss, be], writes=[bss])
                    P.op("dve", lambda e: e.reciprocal(out=ss[:, 2:3], in_=ss[:, 1:2]), reads=[bss], writes=[bss])
                    P.op("dve", lambda e: e.scalar_tensor_tensor(
                        out=_r(jk[:]), in0=pO[:, 0:128], scalar=ss[:, 2:3], in1=gnw[0][:], op0=ALU.mult, op1=ALU.mult),
                        reads=[bpO, bss, gnw[1], bjk], writes=[bjk])
                    P.op("dve", lambda e: e.tensor_tensor(
                        out=ocat[:, h * 128:(h + 1) * 128], in0=jk[:], in1=sz[:, h * 128:(h + 1) * 128], op=ALU.mult),
                        reads=[bjk, bsz, boc], writes=[boc])

                gens = [head_gen(h) for h in range(4)]
                if t + 1 < 4:
                    gens.append(pro_gen(t + 1))
                while gens:
                    for g in list(gens):
                        try:
                            next(g)
                        except StopIteration:
                            gens.remove(g)
                P.dma("sp", o_d[r0:r0 + 128, :], ocat[:], reads=[boc], writes=[bo_d], semname="st_" + boc.name)
        P.final_waits("sp", [bo_d, bom_d])
    if standalone:
        P.finish()
    return nc


def build_diff(S, C=None, io=None, tag="a2"):
    standalone = C is None
    if standalone:
        nc = bass.Bass("TRN2", target_bir_lowering=False)
        C = Ctx(nc)
        io = dict(x=C.dram_in("x", [S, D_MODEL]), wd=C.dram_in("wd", [D_MODEL, 2048]),
                  prew=C.dram_in("prew", [D_MODEL]), lamv=C.dram_in("lamv", [256]), dnw=C.dram_in("dnw", [128]),
                  qx=C.dram_in("qx", [8, 512]), alb=C.dram_in("alb", [128, 128]), li=C.dram_in("li", [2]),
                  o_diff=C.dram_out("o_diff", [S, 512]))
    nc = C.nc
    P = C.P
    NT = S // 128
    NST = S // 512
    x_d, wd_d, prew_d, lamv_d, dnw_d = io["x"], io["wd"], io["prew"], io["lamv"], io["dnw"]
    qx_d, alb_d, li_d, o_d = io["qx"], io["alb"], io["li"], io["o_diff"]
    bo_d = Buf("o_d", multi=True)
    with C.phase(tag):
        K = _consts(C)
        ident, bi = K["ident"]; ones, bon = K["ones"]; mincl, bmi = K["mincl"]; epst, be = K["eps"]
        identb, bib = K["identb"]
        minclb, bmb = C.sb([128, 128], BF16, "minclb")
        P.op("pool", lambda e: e.tensor_copy(out=minclb[:], in_=mincl[:]), reads=[bmi], writes=[bmb])
        prew = _bcast_load(C, prew_d, D_MODEL, "prew")
        dnw = _bcast_load(C, dnw_d, 128, "dnw")
        lamv = _bcast_load(C, lamv_d, 256, "lamv")
        li = _bcast_load(C, li_d, 2, "li")
        alb, balb = C.sb([128, 128], F32, "alb_sb")
        P.dma("sp", alb[:], alb_d[:, :], writes=[balb], semname="ld_alb")
        lm, blm = C.sb([128, 8], F32, "lm")
        lp, blp = C.sb([128, 128], F32, "lp")

        def lam1(e):
            e.tensor_tensor(out=lp[:, 0:64], in0=lamv[0][:, 0:64], in1=lamv[0][:, 64:128], op=ALU.mult)
            return e.tensor_tensor(out=lp[:, 64:128], in0=lamv[0][:, 128:192], in1=lamv[0][:, 192:256], op=ALU.mult)
        P.op("dve", lam1, reads=[lamv[1]], writes=[blp])

        def lam2(e):
            e.reduce_sum(out=lm[:, 0:1], in_=lp[:, 0:64], axis=mybir.AxisListType.X)
            return e.reduce_sum(out=lm[:, 1:2], in_=lp[:, 64:128], axis=mybir.AxisListType.X)
        P.op("dve", lam2, reads=[blp], writes=[blm])
        P.op("act", lambda e: e.activation(out=lm[:, 2:4], in_=lm[:, 0:2], func=AF.Exp), reads=[blm], writes=[blm])
        P.op("dve", lambda e: e.tensor_tensor(out=lm[:, 4:5], in0=lm[:, 3:4], in1=lm[:, 2:3], op=ALU.subtract),
             reads=[blm], writes=[blm])
        P.op("dve", lambda e: e.tensor_scalar(out=lm[:, 5:6], in0=lm[:, 4:5], scalar1=li[0][:, 0:1], scalar2=None,
                                              op0=ALU.add), reads=[blm, li[1]], writes=[blm])
        P.op("dve", lambda e: e.tensor_scalar(out=dnw[0][:], in0=dnw[0][:], scalar1=li[0][:, 1:2], scalar2=None,
                                              op0=ALU.mult), reads=[dnw[1], li[1]], writes=[dnw[1]])
        wd, bwd = C.sb([128, 8, 2048], BF16, "wd_sb")
        _load_w(C, wd, bwd, wd_d, 2048, "ld_wd")
        hT, bhT = C.sb([128, 8, 512], BF16, "hT")
        ka = [[C.sb([66, S], BF16, f"ka{h}{m}") for m in range(2)] for h in range(4)]
        qa = [[C.sb([66, 512], BF16, f"qa{h}{m}") for m in range(2)] for h in range(4)]
        vaug, _ = C.sb([128, NT, 4, 129], BF16, "vaug")
        bv = [Buf(f"v{t}") for t in range(NT)]
        bka = [[[Buf(f"ka{h}{m}_{s}") for s in range(NST)] for m in range(2)] for h in range(4)]
        for h in range(4):
            for m in range(2):
                P.op("pool", lambda e, h=h, m=m: e.memset(ka[h][m][0][64:66, :], 1.0), writes=bka[h][m])
                P.dma("pool", qa[h][m][0][64:66, :], qx_d[2 * h:2 * h + 2, :], writes=[qa[h][m][1]], semname=f"ld_qx{h}{m}")
        P.op("pool", lambda e: e.memset(vaug[:, :, :, 128:129], 1.0), writes=bv)
        C.pool("xt", 2, [128, 1024], F32)
        C.pool("junk", 1, [128, 1024], F32)
        C.pool("col", 12, [128, 8], F32)
        C.pool("hb", 4, [128, 1024], BF16)
        C.pool("pst", 1, [128, 1024], BF16, psum=True)
        C.pool("ps", 3, [128, 512], F32, psum=True)
        C.pool("acc", 4, [128, 512], F32, psum=True)
        C.pool("pT", 4, [128, 512], BF16)
        C.pool("szd", 1, [128, 4, 512], F32)
        C.pool("o0", 2, [128, 4, 128], F32)
        C.pool("ob", 8, [128, 128], F32)
        C.pool("ocat", 1, [128, 4, 512], F32)

        hbs_next = _norm_part(C, K, x_d, 0, prew)
        for I in range(NST):
            _tr_part(C, K, hbs_next, hT, bhT)
            for h in range(4):
                for which, col0 in (("q", h * 128), ("k", 512 + h * 128)):
                    pp, bpp = C.get("ps")

                    def mm(e, pp=pp, col0=col0):
                        for kc in range(8):
                            ins = e.matmul(pp[:, :], lhsT=wd[:, kc, col0:col0 + 128], rhs=hT[:, kc, :],
                                           start=(kc == 0), stop=(kc == 7))
                        return ins
                    P.op("pe", mm, reads=[bwd, bhT], writes=[bpp])
                    for m in range(2):
                        if which == "q":
                            dst, bd = qa[h][m][0][0:64, :], qa[h][m][1]
                        else:
                            dst, bd = ka[h][m][0][0:64, I * 512:(I + 1) * 512], bka[h][m][I]
                        eng = "act" if m == 0 else "dve"
                        if eng == "act":
                            P.op("act", lambda e, dst=dst, pp=pp, m=m: e.copy(out=dst, in_=pp[64 * m:64 * m + 64, :]),
                                 reads=[bpp], writes=[bd])
                        else:
                            P.op("dve", lambda e, dst=dst, pp=pp, m=m: e.tensor_copy(out=dst, in_=pp[64 * m:64 * m + 64, :]),
                                 reads=[bpp], writes=[bd])
            szd, bszd = C.get("szd")
            for t in range(4):
                tsl = slice(t * 128, (t + 1) * 128)
                tile = I * 4 + t
                pv_, bpv = C.get("ps")

                def mmv(e, pv_=pv_, tsl=tsl):
                    for kc in range(8):
                        ins = e.matmul(pv_[:, :], lhsT=hT[:, kc, tsl], rhs=wd[:, kc, 1024:1536], start=(kc == 0),
                                       stop=(kc == 7))
                    return ins
                P.op("pe", mmv, reads=[bwd, bhT], writes=[bpv])
                P.op("dve", lambda e, pv_=pv_, tile=tile: e.tensor_copy(
                    out=vaug[:, tile, :, 0:128], in_=pv_[:, :].rearrange("p (h e) -> p h e", h=4)),
                    reads=[bpv], writes=[bv[tile]])
                pz, bpz = C.get("ps")

                def mmz(e, pz=pz, tsl=tsl):
                    for kc in range(8):
                        ins = e.matmul(pz[:, :], lhsT=hT[:, kc, tsl], rhs=wd[:, kc, 1536:2048], start=(kc == 0),
                                       stop=(kc == 7))
                    return ins
                P.op("pe", mmz, reads=[bwd, bhT], writes=[bpz])
                P.op("act", lambda e, szd=szd, pz=pz, t=t: e.activation(out=szd[:, t, :], in_=pz[:, :], func=AF.Silu),
                     reads=[bpz], writes=[bszd])
            if I + 1 < NST:
                hbs_next = _norm_part(C, K, x_d, I + 1, prew)
            ocat, boc = C.get("ocat")
            nj = 4 * I + 4
            items = []
            for h in range(4):
                o0, bo0 = C.get("o0")
                for m in range(2):
                    accA, baA = C.get("acc")
                    accB, baB = C.get("acc")
                    for j in range(nj):
                        items.append(dict(h=h, m=m, j=j, o0=o0, bo0=bo0, accA=accA, baA=baA, accB=accB, baB=baB))

            def issue_sc(it):
                h, m, j = it["h"], it["m"], it["j"]
                jj = j - 4 * I
                q0 = 128 * jj if jj > 0 else 0
                sc, bsc = C.get("ps")

                def mms(e):
                    ins = e.matmul(sc[:, q0:512], lhsT=ka[h][m][0][0:66, j * 128:(j + 1) * 128],
                                   rhs=qa[h][m][0][0:66, q0:512], start=True, stop=(jj < 0))
                    if jj >= 0:
                        ins = e.matmul(sc[:, q0:q0 + 128], lhsT=identb[:], rhs=minclb[:], start=False, stop=True,
                                       skip_group_check=True)
                    return ins
                P.op("pe", mms, reads=[bka[h][m][j // 4], qa[h][m][1], bib, bmb], writes=[bsc])
                pT, bpT = C.get("pT")
                bcol = h * 32 + (jj + 28)
                P.op("act", lambda e: e.activation(out=pT[:, q0:512], in_=sc[:, q0:512], func=AF.Exp,
                                                   bias=alb[:, bcol:bcol + 1], scale=0.125),
                     reads=[bsc, balb], writes=[bpT])
                it["pT"], it["bpT"], it["jj"] = pT, bpT, jj

            def acc_of(it, ss):
                return (it["accA"], ss * 129, it["baA"]) if ss < 3 else (it["accB"], 0, it["baB"])

            def issue_pv(it):
                h, j, jj, pT = it["h"], it["j"], it["jj"], it["pT"]

                def mmpv(e):
                    ins = None
                    for ss in range(max(jj, 0), 4):
                        a, off, _ = acc_of(it, ss)
                        first = (j == 0) and (ss == 0 or ss == 3)
                        ins = e.matmul(a[:, off:off + 129], lhsT=pT[:, ss * 128:(ss + 1) * 128], rhs=vaug[:, j, h, :],
                                       start=first, stop=(j == 4 * I + ss), skip_group_check=True)
                    return ins
                P.op("pe", mmpv, reads=[it["bpT"], bv[j]], writes=[it["baA"], it["baB"]])

            def finalize(it):
                h, m, o0, bo0 = it["h"], it["m"], it["o0"], it["bo0"]
                for ss in range(4):
                    a, off, ba = acc_of(it, ss)
                    cl, bcl = C.get("col")
                    P.op("dve", lambda e, cl=cl, a=a, off=off: e.reciprocal(out=cl[:, 0:1], in_=a[:, off + 128:off + 129]),
                         reads=[ba], writes=[bcl])
                    if m == 0:
                        P.op("dve", lambda e, ss=ss, a=a, off=off, cl=cl: e.tensor_scalar(
                            out=o0[:, ss, :], in0=a[:, off:off + 128], scalar1=cl[:, 0:1], scalar2=None, op0=ALU.mult),
                            reads=[ba, bcl], writes=[bo0])
                        continue
                    ob, bob = C.get("ob")
                    P.op("dve", lambda e, ob=ob, a=a, off=off, cl=cl: e.tensor_scalar(
                        out=ob[:], in0=a[:, off:off + 128], scalar1=cl[:, 0:1], scalar2=None, op0=ALU.mult),
                        reads=[ba, bcl], writes=[bob])
                    P.op("dve", lambda e, ob=ob, ss=ss: e.scalar_tensor_tensor(
                        out=ob[:], in0=ob[:], scalar=lm[:, 5:6], in1=o0[:, ss, :], op0=ALU.mult, op1=ALU.add),
                        reads=[bob, bo0, blm], writes=[bob])
                    jk, bjk = C.get("ob")
                    P.op("act", lambda e, jk=jk, ob=ob, cl=cl: e.activation(out=jk[:], in_=ob[:], func=AF.Square,
                                                                             accum_out=cl[:, 1:2]),
                         reads=[bob, bcl], writes=[bjk, bcl])
                    P.op("act", lambda e, cl=cl: e.activation(out=cl[:, 2:3], in_=cl[:, 1:2], func=AF.Sqrt,
                                                              bias=epst[:, 0:1], scale=1.0 / 128.0),
                         reads=[bcl, be], writes=[bcl])
                    P.op("dve", lambda e, cl=cl: e.reciprocal(out=cl[:, 3:4], in_=cl[:, 2:3]), reads=[bcl], writes=[bcl])
                    P.op("dve", lambda e, ob=ob, cl=cl: e.scalar_tensor_tensor(
                        out=ob[:], in0=ob[:], scalar=cl[:, 3:4], in1=dnw[0][:], op0=ALU.mult, op1=ALU.mult),
                        reads=[bob, bcl, dnw[1]], writes=[bob])
                    P.op("dve", lambda e, ob=ob, ss=ss: e.tensor_tensor(
                        out=ocat[:, ss, h * 128:(h + 1) * 128], in0=ob[:], in1=szd[:, ss, h * 128:(h + 1) * 128],
                        op=ALU.mult), reads=[bob, bszd, boc], writes=[boc])

            pending = []
            for it in items:
                issue_sc(it)
                pending.append(it)
                if len(pending) > 2:
                    p_ = pending.pop(0)
                    issue_pv(p_)
                    if p_["j"] == nj - 1:
                        finalize(p_)
            for p_ in pending:
                issue_pv(p_)
                if p_["j"] == nj - 1:
                    finalize(p_)
            for ss in range(4):
                r0 = (I * 4 + ss) * 128
                P.dma("sp", o_d[r0:r0 + 128, :], ocat[:, ss, :], reads=[boc], writes=[bo_d], semname="st_" + boc.name)
        P.final_waits("sp", [bo_d])
    if standalone:
        P.finish()
    return nc


def diff_consts(hh):
    slopes = [2.0 ** (-(4 * hh + h + 1)) for h in range(4)]
    q = np.arange(512)
    qx = np.zeros((8, 512), np.float32)
    alb = np.zeros((128, 128), np.float32)
    k = np.arange(128, dtype=np.float64)
    for h in range(4):
        qx[2 * h] = -8.0 * slopes[h] * (q % 128)
        qx[2 * h + 1] = -8.0 * slopes[h] * 128.0 * (q // 128)
        for d in range(32):
            alb[:, h * 32 + d] = slopes[h] * (k + 128.0 * (d - 28))
    return qx, alb


def build_merge(TB):
    nc = bass.Bass("TRN2", target_bir_lowering=False)
    C = Ctx(nc)
    P = C.P
    NT = TB // 128
    x_d = C.dram_in("x", [TB, D_MODEL])
    oc_d = C.dram_in("oc", [TB, 3072])
    wgt_d = C.dram_in("wgate", [D_MODEL, 3072])
    wbr_d = C.dram_in("wbr", [3072, D_MODEL])
    wout_d = C.dram_in("wout", [D_MODEL, D_MODEL])
    prew_d = C.dram_in("prew", [D_MODEL])
    postw_d = C.dram_in("postw", [D_MODEL])
    y_d = C.dram_out("xo", [TB, D_MODEL])
    by_d = Buf("y_d", multi=True)
    with C.phase("b"):
        K = _consts(C)
        identb, bib = K["identb"]; epst, be = K["eps"]
        prew = _bcast_load(C, prew_d, D_MODEL, "prew")
        postw = _bcast_load(C, postw_d, D_MODEL, "postw")
        wgt, bwgt = C.sb([128, 8, 3072], BF16, "wgt_sb")
        _load_w(C, wgt, bwgt, wgt_d, 3072, "ld_wgt")
        wbr, bwbr = C.sb([128, 24, 1024], BF16, "wbr_sb")
        src = wbr_d.rearrange("(kc p) n -> p kc n", p=128)
        for kc in range(24):
            P.dma("pool", wbr[:, kc, :], src[:, kc, :], writes=[bwbr] if kc == 0 else [], semname="ld_wbr")
        bwbr.w = {("d", "ld_wbr"): P.dma_sems["ld_wbr"]}
        wout, bwout = C.sb([128, 8, 1024], BF16, "wout_sb")
        _load_w(C, wout, bwout, wout_d, 1024, "ld_wout")
        C.pool("xt", 2, [128, 1024], F32)
        C.pool("junk", 1, [128, 1024], F32)
        C.pool("col", 4, [128, 8], F32)
        C.pool("hb", 2, [128, 1024], BF16)
        C.pool("pst", 2, [128, 1024], BF16, psum=True)
        C.pool("ps", 5, [128, 512], F32, psum=True)
        C.pool("hT", 2, [128, 8, 128], BF16)
        C.pool("ob", 2, [128, 3072], BF16)
        C.pool("oT", 2, [128, 24, 128], BF16)
        C.pool("sig", 1, [128, 3, 1024], F32)
        C.pool("y", 2, [128, 1024], F32)
        C.pool("yb", 2, [128, 1024], BF16)
        C.pool("yT", 2, [128, 8, 128], BF16)
        C.pool("tmp", 2, [128, 512], F32)
        C.pool("xo", 2, [128, 1024], F32)
        for t in range(NT):
            r0 = t * 128
            hT, bhT = C.get("hT")
            xt, bx = C.get("xt")
            P.dma("sp", xt[:], x_d[r0:r0 + 128, :], writes=[bx], semname="ld_" + bx.name)
            sq, bsq = C.get("junk")
            ss, bss = C.get("col")
            P.op("act", lambda e, xt=xt, sq=sq, ss=ss: e.activation(out=sq[:], in_=xt[:], func=AF.Square,
                                                                     accum_out=ss[:, 0:1]), reads=[bx], writes=[bsq, bss])
            P.op("act", lambda e, ss=ss: e.activation(out=ss[:, 1:2], in_=ss[:, 0:1], func=AF.Sqrt, bias=epst[:, 0:1],
                                                      scale=1.0 / D_MODEL), reads=[bss, be], writes=[bss])
            P.op("dve", lambda e, ss=ss: e.reciprocal(out=ss[:, 2:3], in_=ss[:, 1:2]), reads=[bss], writes=[bss])
            hb, bhb = C.get("hb")
            P.op("dve", lambda e, hb=hb, xt=xt, ss=ss: e.scalar_tensor_tensor(
                out=hb[:], in0=xt[:], scalar=ss[:, 2:3], in1=prew[0][:], op0=ALU.mult, op1=ALU.mult),
                reads=[bx, bss, prew[1]], writes=[bhb])
            pt, bpt = C.get("pst")

            def tr(e, hb=hb, pt=pt):
                for kc in range(8):
                    ins = e.transpose(pt[:, kc * 128:(kc + 1) * 128], hb[:, kc * 128:(kc + 1) * 128], identb[:])
                return ins
            P.op("pe", tr, reads=[bhb, bib], writes=[bpt])
            P.op("act", lambda e, pt=pt, hT=hT: e.copy(out=hT[:, :, :], in_=pt[:, :].rearrange("p (k n) -> p k n", k=8)),
                 reads=[bpt], writes=[bhT])
            ob, bob = C.get("ob")
            P.dma("pool", ob[:], oc_d[r0:r0 + 128, :], writes=[bob], semname="ld_" + bob.name)
            oT, boT = C.get("oT")
            for g in range(3):
                pt, bpt = C.get("pst")

                def tr2(e, ob=ob, pt=pt, g=g):
                    for kc in range(8):
                        c0 = (g * 8 + kc) * 128
                        ins = e.transpose(pt[:, kc * 128:(kc + 1) * 128], ob[:, c0:c0 + 128], identb[:])
                    return ins
                P.op("pe", tr2, reads=[bob, bib], writes=[bpt])
                P.op("dve" if g == 1 else "act", (lambda e, pt=pt, oT=oT, g=g: e.tensor_copy(
                    out=oT[:, g * 8:(g + 1) * 8, :], in_=pt[:, :].rearrange("p (k n) -> p k n", k=8))) if g == 1 else
                    (lambda e, pt=pt, oT=oT, g=g: e.copy(out=oT[:, g * 8:(g + 1) * 8, :],
                                                        in_=pt[:, :].rearrange("p (k n) -> p k n", k=8))),
                    reads=[bpt], writes=[boT])
            sig, bsig = C.get("sig")
            for br in range(3):
                for half in range(2):
                    pg, bpg = C.get("ps")
                    c0 = br * 1024 + half * 512

                    def mmg(e, pg=pg, c0=c0, hT=hT):
                        for kc in range(8):
                            ins = e.matmul(pg[:, :], lhsT=hT[:, kc, :], rhs=wgt[:, kc, c0:c0 + 512], start=(kc == 0),
                                           stop=(kc == 7))
                        return ins
                    P.op("pe", mmg, reads=[bhT, bwgt], writes=[bpg])
                    P.op("act", lambda e, pg=pg, sig=sig, br=br, half=half: e.activation(
                        out=sig[:, br, half * 512:(half + 1) * 512], in_=pg[:, :], func=AF.Sigmoid),
                        reads=[bpg], writes=[bsig])
            y, by = C.get("y")
            for half in range(2):
                hs = slice(half * 512, (half + 1) * 512)
                for br in range(3):
                    pb, bpb = C.get("ps")

                    def mmb(e, pb=pb, br=br, half=half, oT=oT):
                        for kc in range(8):
                            ins = e.matmul(pb[:, :], lhsT=oT[:, br * 8 + kc, :],
                                           rhs=wbr[:, br * 8 + kc, half * 512:(half + 1) * 512], start=(kc == 0),
                                           stop=(kc == 7))
                        return ins
                    P.op("pe", mmb, reads=[boT, bwbr], writes=[bpb])
                    if br == 0:
                        P.op("dve", lambda e, y=y, pb=pb, sig=sig, hs=hs: e.tensor_tensor(
                            out=y[:, hs], in0=pb[:, :], in1=sig[:, 0, hs], op=ALU.mult), reads=[bpb, bsig, by], writes=[by])
                    else:
                        tmp, btmp = C.get("tmp")
                        P.op("dve", lambda e, tmp=tmp, pb=pb, sig=sig, hs=hs, br=br: e.tensor_tensor(
                            out=tmp[:], in0=pb[:, :], in1=sig[:, br, hs], op=ALU.mult), reads=[bpb, bsig], writes=[btmp])
                        P.op("pool", lambda e, y=y, tmp=tmp, hs=hs: e.tensor_tensor(
                            out=y[:, hs], in0=y[:, hs], in1=tmp[:], op=ALU.add), reads=[btmp, by], writes=[by])
            yb, byb = C.get("yb")
            P.op("act", lambda e, yb=yb, y=y: e.copy(out=yb[:], in_=y[:]), reads=[by], writes=[byb])
            pt, bpt = C.get("pst")

            def tr3(e, yb=yb, pt=pt):
                for kc in range(8):
                    ins = e.transpose(pt[:, kc * 128:(kc + 1) * 128], yb[:, kc * 128:(kc + 1) * 128], identb[:])
                return ins
            P.op("pe", tr3, reads=[byb, bib], writes=[bpt])
            yT, byT = C.get("yT")
            P.op("act", lambda e, pt=pt, yT=yT: e.copy(out=yT[:, :, :], in_=pt[:, :].rearrange("p (k n) -> p k n", k=8)),
                 reads=[bpt], writes=[byT])
            xo, bxo = C.get("xo")
            cl, bcl = C.get("col")
            for half in range(2):
                po, bpo = C.get("ps")

                def mmo(e, po=po, half=half, yT=yT):
                    for kc in range(8):
                        ins = e.matmul(po[:, :], lhsT=yT[:, kc, :], rhs=wout[:, kc, half * 512:(half + 1) * 512],
                                       start=(kc == 0), stop=(kc == 7))
                    return ins
                P.op("pe", mmo, reads=[byT, bwout], writes=[bpo])
                P.op("act", lambda e, po=po, xo=xo, half=half: e.copy(out=xo[:, half * 512:(half + 1) * 512], in_=po[:, :]),
                     reads=[bpo], writes=[bxo])
            sq, bsq = C.get("junk")
            P.op("act", lambda e, sq=sq, xo=xo, cl=cl: e.activation(out=sq[:], in_=xo[:], func=AF.Square,
                                                                    accum_out=cl[:, 0:1]), reads=[bxo], writes=[bsq, bcl])
            P.op("act", lambda e, cl=cl: e.activation(out=cl[:, 1:2], in_=cl[:, 0:1], func=AF.Sqrt, bias=epst[:, 0:1],
                                                      scale=1.0 / D_MODEL), reads=[bcl, be], writes=[bcl])
            P.op("dve", lambda e, cl=cl: e.reciprocal(out=cl[:, 2:3], in_=cl[:, 1:2]), reads=[bcl], writes=[bcl])
            P.op("dve", lambda e, xo=xo, cl=cl: e.scalar_tensor_tensor(
                out=xo[:], in0=xo[:], scalar=cl[:, 2:3], in1=postw[0][:], op0=ALU.mult, op1=ALU.mult),
                reads=[bxo, bcl, postw[1]], writes=[bxo])
            P.op("pool", lambda e, xo=xo, xt=xt: e.tensor_tensor(out=xo[:], in0=xo[:], in1=xt[:], op=ALU.add),
                 reads=[bxo, bx], writes=[bxo])
            P.dma("sp", y_d[r0:r0 + 128, :], xo[:], reads=[bxo], writes=[by_d], semname="st_" + bxo.name)
        P.final_waits("sp", [by_d])
    P.finish()
    return nc


def phase_b1(C, S, io, tag):
    P = C.P
    NT = S // 128
    x_d, oc_d, wgt_d, wbr_d, prew_d, yp_d = io["x"], io["oc"], io["wgate"], io["wbrm"], io["prew"], io["yp"]
    byp = Buf("yp_d", multi=True)
    with C.phase(tag):
        K = _consts(C)
        identb, bib = K["identb"]; epst, be = K["eps"]
        prew = _bcast_load(C, prew_d, D_MODEL, "prew")
        wgt, bwgt = C.sb([128, 8, 3072], BF16, "wgt_sb")
        _load_w(C, wgt, bwgt, wgt_d, 3072, "ld_wgt")
        wbr, bwbr = C.sb([128, 12, 1024], BF16, "wbr_sb")
        src = wbr_d.rearrange("(kc p) n -> p kc n", p=128)
        for kc in range(12):
            P.dma("pool", wbr[:, kc, :], src[:, kc, :], writes=[bwbr] if kc == 0 else [], semname="ld_wbr")
        bwbr.w = {("d", "ld_wbr"): P.dma_sems["ld_wbr"]}
        C.pool("xt", 3, [128, 1024], F32)
        C.pool("junk", 1, [128, 1024], F32)
        C.pool("col", 6, [128, 8], F32)
        C.pool("hb", 3, [128, 1024], BF16)
        C.pool("pst", 2, [128, 1024], BF16, psum=True)
        C.pool("ps", 5, [128, 512], F32, psum=True)
        C.pool("hT1", 3, [128, 8, 128], BF16)
        C.pool("ob", 3, [128, 1536], BF16)
        C.pool("oT", 3, [128, 12, 128], BF16)
        C.pool("sig", 2, [128, 3, 1024], F32)
        C.pool("y", 2, [128, 1024], F32)
        C.pool("ybf", 2, [128, 1024], BF16)
        C.pool("tmp", 4, [128, 512], F32)

        def front_a(t):
            r0 = t * 128
            xt, bx = C.get("xt")
            P.dma("sp", xt[:], x_d[r0:r0 + 128, :], writes=[bx], semname="ld_" + bx.name)
            sq, bsq = C.get("junk")
            ss, bss = C.get("col")
            P.op("act", lambda e: e.activation(out=sq[:, 0:1024], in_=xt[:], func=AF.Square, accum_out=ss[:, 0:1]),
                 reads=[bx], writes=[bsq, bss])
            P.op("act", lambda e: e.activation(out=ss[:, 1:2], in_=ss[:, 0:1], func=AF.Sqrt, bias=epst[:, 0:1],
                                               scale=1.0 / D_MODEL), reads=[bss, be], writes=[bss])
            P.op("dve", lambda e: e.reciprocal(out=ss[:, 2:3], in_=ss[:, 1:2]), reads=[bss], writes=[bss])
            hb, bhb = C.get("hb")
            P.op("dve", lambda e: e.scalar_tensor_tensor(out=hb[:], in0=xt[:], scalar=ss[:, 2:3], in1=prew[0][:],
                                                         op0=ALU.mult, op1=ALU.mult),
                 reads=[bx, bss, prew[1]], writes=[bhb])
            ob, bob = C.get("ob")
            P.dma("pool", ob[:], oc_d[r0:r0 + 128, :], writes=[bob], semname="ld_" + bob.name)
            return dict(r0=r0, hb=hb, bhb=bhb, ob=ob, bob=bob)

        def front(d0):
            r0, hb, bhb, ob, bob = d0["r0"], d0["hb"], d0["bhb"], d0["ob"], d0["bob"]
            hT, bhT = C.get("hT1")
            pt, bpt = C.get("pst")

            def tr(e):
                for kc in range(8):
                    ins = e.transpose(pt[:, kc * 128:(kc + 1) * 128], hb[:, kc * 128:(kc + 1) * 128], identb[:])
                return ins
            P.op("pe", tr, reads=[bhb, bib], writes=[bpt])
            P.op("act", lambda e: e.copy(out=hT[:, :, :], in_=pt[:, :].rearrange("p (k n) -> p k n", k=8)),
                 reads=[bpt], writes=[bhT])
            oT, boT = C.get("oT")
            for g in range(2):
                pt, bpt = C.get("pst")
                nk = 8 if g == 0 else 4

                def tr2(e, ob=ob, pt=pt, g=g, nk=nk):
                    for kc in range(nk):
                        c0 = (g * 8 + kc) * 128
                        ins = e.transpose(pt[:, kc * 128:(kc + 1) * 128], ob[:, c0:c0 + 128], identb[:])
                    return ins
                P.op("pe", tr2, reads=[bob, bib], writes=[bpt])
                if g == 0:
                    P.op("act", lambda e, pt=pt, oT=oT: e.copy(
                        out=oT[:, 0:8, :], in_=pt[:, :].rearrange("p (k n) -> p k n", k=8)), reads=[bpt], writes=[boT])
                else:
                    P.op("dve", lambda e, pt=pt, oT=oT: e.tensor_copy(
                        out=oT[:, 8:12, :], in_=pt[:, 0:512].rearrange("p (k n) -> p k n", k=4)), reads=[bpt],
                        writes=[boT])
            return dict(hT=hT, bhT=bhT, oT=oT, boT=boT, r0=r0)

        def back(d):
            hT, bhT, oT, boT, r0 = d["hT"], d["bhT"], d["oT"], d["boT"], d["r0"]
            sig, bsig = C.get("sig")
            for br in range(3):
                for half in range(2):
                    pg, bpg = C.get("ps")
                    c0 = br * 1024 + half * 512

                    def mmg(e, pg=pg, c0=c0, hT=hT):
                        for kc in range(8):
                            ins = e.matmul(pg[:, :], lhsT=hT[:, kc, :], rhs=wgt[:, kc, c0:c0 + 512], start=(kc == 0),
                                           stop=(kc == 7))
                        return ins
                    P.op("pe", mmg, reads=[bhT, bwgt], writes=[bpg])
                    P.op("act", lambda e, pg=pg, sig=sig, br=br, half=half: e.activation(
                        out=sig[:, br, half * 512:(half + 1) * 512], in_=pg[:, :], func=AF.Sigmoid),
                        reads=[bpg], writes=[bsig])
            y, by = C.get("y")
            for half in range(2):
                hs = slice(half * 512, (half + 1) * 512)
                for br in range(3):
                    pb, bpb = C.get("ps")

                    def mmb(e, pb=pb, br=br, half=half, oT=oT):
                        for kc in range(4):
                            ins = e.matmul(pb[:, :], lhsT=oT[:, br * 4 + kc, :],
                                           rhs=wbr[:, br * 4 + kc, half * 512:(half + 1) * 512], start=(kc == 0),
                                           stop=(kc == 3))
                        return ins
                    P.op("pe", mmb, reads=[boT, bwbr], writes=[bpb])
                    if br == 0:
                        P.op("dve", lambda e, y=y, pb=pb, sig=sig, hs=hs: e.tensor_tensor(
                            out=y[:, hs], in0=pb[:, :], in1=sig[:, 0, hs], op=ALU.mult), reads=[bpb, bsig, by], writes=[by])
                    else:
                        tmp, btmp = C.get("tmp")
                        P.op("dve", lambda e, tmp=tmp, pb=pb, sig=sig, hs=hs, br=br: e.tensor_tensor(
                            out=tmp[:], in0=pb[:, :], in1=sig[:, br, hs], op=ALU.mult), reads=[bpb, bsig], writes=[btmp])
                        P.op("dve", lambda e, y=y, tmp=tmp, hs=hs: e.tensor_tensor(
                            out=y[:, hs], in0=y[:, hs], in1=tmp[:], op=ALU.add), reads=[btmp, by], writes=[by])
            yb, byb = C.get("ybf")
            P.op("act", lambda e: e.copy(out=yb[:], in_=y[:]), reads=[by], writes=[byb])
            P.dma("sp", yp_d[r0:r0 + 128, :], yb[:], reads=[byb], writes=[byp], semname="st_" + byb.name)

        fa = front_a(0)
        nxt = front(fa)
        for t in range(NT):
            cur_ = nxt
            if t + 1 < NT:
                fa = front_a(t + 1)
            back(cur_)
            if t + 1 < NT:
                nxt = front(fa)
        P.final_waits("sp", [byp])


def phase_b2(C, TB, io, tag):
    P = C.P
    NT = TB // 128
    ys_d, xr_d, wout_d, postw_d, xo_d = io["ysum"], io["xres"], io["wout"], io["postw"], io["xout"]
    bxo_d = Buf("xo_d", multi=True)
    with C.phase(tag):
        K = _consts(C)
        identb, bib = K["identb"]; epst, be = K["eps"]
        postw = _bcast_load(C, postw_d, D_MODEL, "postw")
        wout, bwout = C.sb([128, 8, 1024], BF16, "wout_sb")
        _load_w(C, wout, bwout, wout_d, 1024, "ld_wout")
        C.pool("xt", 2, [128, 1024], F32)
        C.pool("junk", 1, [128, 1024], F32)
        C.pool("col", 4, [128, 8], F32)
        C.pool("pst", 2, [128, 1024], BF16, psum=True)
        C.pool("ps", 4, [128, 512], F32, psum=True)
        C.pool("yb", 2, [128, 1024], BF16)
        C.pool("yT", 2, [128, 8, 128], BF16)
        C.pool("xo", 2, [128, 1024], F32)
        for t in range(NT):
            r0 = t * 128
            xt, bx = C.get("xt")
            P.dma("sp", xt[:], xr_d[r0:r0 + 128, :], writes=[bx], semname="ld_" + bx.name)
            yb, byb = C.get("yb")
            P.dma("sp", yb[:], ys_d[r0:r0 + 128, :], writes=[byb], semname="ld_" + byb.name)
            pt, bpt = C.get("pst")

            def tr3(e, yb=yb, pt=pt):
                for kc in range(8):
                    ins = e.transpose(pt[:, kc * 128:(kc + 1) * 128], yb[:, kc * 128:(kc + 1) * 128], identb[:])
                return ins
            P.op("pe", tr3, reads=[byb, bib], writes=[bpt])
            yT, byT = C.get("yT")
            P.op("act", lambda e, pt=pt, yT=yT: e.copy(out=yT[:, :, :], in_=pt[:, :].rearrange("p (k n) -> p k n", k=8)),
                 reads=[bpt], writes=[byT])
            xo, bxo = C.get("xo")
            cl, bcl = C.get("col")
            for half in range(2):
                po, bpo = C.get("ps")

                def mmo(e, po=po, half=half, yT=yT):
                    for kc in range(8):
                        ins = e.matmul(po[:, :], lhsT=yT[:, kc, :], rhs=wout[:, kc, half * 512:(half + 1) * 512],
                                       start=(kc == 0), stop=(kc == 7))
                    return ins
                P.op("pe", mmo, reads=[byT, bwout], writes=[bpo])
                P.op("act", lambda e, po=po, xo=xo, half=half: e.copy(out=xo[:, half * 512:(half + 1) * 512], in_=po[:, :]),
                     reads=[bpo], writes=[bxo])
            sq, bsq = C.get("junk")
            P.op("act", lambda e, sq=sq, xo=xo, cl=cl: e.activation(out=sq[:], in_=xo[:], func=AF.Square,
                                                                    accum_out=cl[:, 0:1]), reads=[bxo], writes=[bsq, bcl])
            P.op("act", lambda e, cl=cl: e.activation(out=cl[:, 1:2], in_=cl[:, 0:1], func=AF.Sqrt, bias=epst[:, 0:1],
                                                      scale=1.0 / D_MODEL), reads=[bcl, be], writes=[bcl])
            P.op("dve", lambda e, cl=cl: e.reciprocal(out=cl[:, 2:3], in_=cl[:, 1:2]), reads=[bcl], writes=[bcl])
            P.op("dve", lambda e, xo=xo, cl=cl: e.scalar_tensor_tensor(
                out=xo[:], in0=xo[:], scalar=cl[:, 2:3], in1=postw[0][:], op0=ALU.mult, op1=ALU.mult),
                reads=[bxo, bcl, postw[1]], writes=[bxo])
            P.op("pool", lambda e, xo=xo, xt=xt: e.tensor_tensor(out=xo[:], in0=xo[:], in1=xt[:], op=ALU.add),
                 reads=[bxo, bx], writes=[bxo])
            P.dma("pool", xo_d[r0:r0 + 128, :], xo[:], reads=[bxo], writes=[bxo_d], semname="st_" + bxo.name)
        P.final_waits("sp", [bxo_d])


PAIRS = [[0, 1], [2, 3], [4, 5], [6, 7]]


def build_fused(S, L):
    nc = bass.Bass("TRN2", target_bir_lowering=False, num_devices=8)
    C = Ctx(nc)
    P = C.P
    TB = S // 2
    di = C.dram_in
    x_d = di("x", [S, D_MODEL]); xh_d = di("xhalf", [TB, D_MODEL]); mem_d = di("mem", [MEM_LEN, D_MODEL])
    wg_d = di("wg", [L, D_MODEL, 2056]); cw_d = di("convw", [L, 1536, 4]); prew_d = di("prew", [L, D_MODEL])
    alog_d = di("alog", [L, 4]); dtb_d = di("dtb", [L, 4]); gnw_d = di("gnw", [L, 128]); mnw_d = di("mnw", [L, D_MODEL])
    wkv_d = di("wkv", [L, D_MODEL, 1024]); wm_d = di("wm", [L, D_MODEL, 1024]); wd_d = di("wd", [L, D_MODEL, 2048])
    lamv_d = di("lamv", [L, 256]); dnw_d = di("dnw", [L, 128]); qx_d = di("qx", [8, 512]); alb_d = di("alb", [128, 128])
    li_d = di("li", [L, 2]); wgt_d = di("wgate", [L, D_MODEL, 3072]); wbr_d = di("wbrm", [L, 1536, D_MODEL])
    wout_d = di("wout", [L, D_MODEL, D_MODEL]); postw_d = di("postw", [L, D_MODEL])
    xo_d = C.dram_out("xo", [TB, D_MODEL])
    it = lambda name, shape: nc.dram_tensor(name, list(shape), F32, addr_space="Local", kind="Internal").ap()
    oc_i = it("oc_i", [S, 1536])
    itb = lambda name, shape: nc.dram_tensor(name, list(shape), BF16, addr_space="Local", kind="Internal").ap()
    yp_i = itb("yp_i", [S, D_MODEL])
    ys_i = itb("ys_i", [TB, D_MODEL])
    xh_i = it("xh_i", [TB, D_MODEL])
    xf_i = it("xf_i", [S, D_MODEL])
    C.gst = contextlib.ExitStack()
    C.st = C.gst
    C.pfx = "g_"
    C.K = _consts(C)
    for l in range(L):
        xs = x_d if l == 0 else xf_i
        build_gdn(S, 99, C, dict(x=xs, wg=wg_d[l], convw=cw_d[l], prew=prew_d[l], alog=alog_d[l], dtb=dtb_d[l],
                                 gnw=gnw_d[l], o_gdn=oc_i[:, 0:512], mem=mem_d, mnw=mnw_d[l], wkv=wkv_d[l], wm=wm_d[l],
                                 o_mem=oc_i[:, 1024:1536]), tag=f"L{l}a1")
        build_diff(S, C, dict(x=xs, wd=wd_d[l], prew=prew_d[l], lamv=lamv_d[l], dnw=dnw_d[l], qx=qx_d, alb=alb_d,
                              li=li_d[l], o_diff=oc_i[:, 512:1024]), tag=f"L{l}a2")
        phase_b1(C, S, dict(x=xs, oc=oc_i, wgate=wgt_d[l], wbrm=wbr_d[l], prew=prew_d[l], yp=yp_i), tag=f"L{l}b1")
        with C.phase(f"L{l}rs"):
            P.coll(lambda e: e.collective_compute("ReduceScatter", ALU.add, replica_groups=PAIRS, ins=[yp_i],
                                                  outs=[ys_i]), semname="cc_rs")
        last = (l == L - 1)
        phase_b2(C, TB, dict(ysum=ys_i, xres=(xh_d if l == 0 else xh_i), wout=wout_d[l], postw=postw_d[l],
                             xout=(xo_d if last else xh_i)), tag=f"L{l}b2")
        if not last:
            with C.phase(f"L{l}ag"):
                P.coll(lambda e: e.collective_compute("AllGather", ALU.bypass, replica_groups=PAIRS, ins=[xh_i],
                                                      outs=[xf_i]), semname="cc_ag")
    P.finish()
    C.gst.close()
    return nc


_PROGS = {}


def _c(a):
    return np.ascontiguousarray(a, dtype=np.float32)


def _core_inputs(r, L, w_in, gdn_conv_w, gdn_a_log, gdn_dt_bias, w_mem_kv, w_br_gdn, w_br_diff, w_br_mem):
    sl = lambda base: slice(base + r * 512, base + r * 512 + 512)
    wg, wd, wm, cw, wkv, wbrm, wgate = [], [], [], [], [], [], []
    for l in range(L):
        wl = np.asarray(w_in[l], np.float32)
        wg.append(np.concatenate([wl[:, sl(0)], wl[:, sl(1024)], wl[:, sl(2048)], wl[:, 3072 + r * 4:3076 + r * 4],
                                  wl[:, 3080 + r * 4:3084 + r * 4], wl[:, sl(3088)]], axis=1))
        wd.append(np.concatenate([wl[:, sl(4112)], wl[:, sl(5136)], wl[:, sl(6160)], wl[:, sl(7184)]], axis=1))
        wm.append(np.concatenate([wl[:, sl(8208)], wl[:, sl(9232)]], axis=1))
        wgate.append(wl[:, 10256:13328])
        cwl = np.asarray(gdn_conv_w[l], np.float32)
        cw.append(np.concatenate([cwl[:, sl(0)], cwl[:, sl(1024)], cwl[:, sl(2048)]], axis=1).T)
        kvl = np.asarray(w_mem_kv[l], np.float32)
        wkv.append(np.concatenate([kvl[:, sl(0)], kvl[:, sl(1024)]], axis=1))
        wbrm.append(np.concatenate([np.asarray(w_br_gdn[l])[sl(0)], np.asarray(w_br_diff[l])[sl(0)],
                                    np.asarray(w_br_mem[l])[sl(0)]], axis=0))
    st = lambda xs: _c(np.stack(xs))
    return dict(wg=st(wg), wd=st(wd), wm=st(wm), convw=st(cw), wkv=st(wkv), wbrm=st(wbrm), wgate=st(wgate),
                alog=_c(np.asarray(gdn_a_log)[:L, r * 4:r * 4 + 4]), dtb=_c(np.asarray(gdn_dt_bias)[:L, r * 4:r * 4 + 4]))


LAYERS_PER_LAUNCH = 1


def kernel(x, mem, pre_norm_w, post_norm_w, w_in, gdn_conv_w, gdn_a_log, gdn_dt_bias, gdn_norm_w, diff_lambda,
           diff_norm_w, mem_norm_w, w_mem_kv, w_br_gdn, w_br_diff, w_br_mem, w_out):
    x = np.asarray(x, np.float32)
    B, S, D = x.shape
    L = np.asarray(w_in).shape[0]
    TB = S // 2
    G = LAYERS_PER_LAUNCH
    key = (S, G)
    if key not in _PROGS:
        _PROGS[key] = build_fused(S, G)
    nc = _PROGS[key]
    li_all = np.array([[-(0.8 - 0.6 * math.exp(-0.3 * l)), 1.0 - (0.8 - 0.6 * math.exp(-0.3 * l))] for l in range(L)],
                      np.float32)
    consts = [diff_consts(r) for r in range(2)]
    for l0 in range(0, L, G):
        ls = slice(l0, l0 + G)
        shared = dict(prew=_c(np.asarray(pre_norm_w)[ls]), postw=_c(np.asarray(post_norm_w)[ls]),
                      gnw=_c(np.asarray(gdn_norm_w)[ls]), mnw=_c(np.asarray(mem_norm_w)[ls]),
                      lamv=_c(np.asarray(diff_lambda)[ls].reshape(G, 256)), dnw=_c(np.asarray(diff_norm_w)[ls]),
                      wout=_c(np.asarray(w_out)[ls]), li=_c(li_all[ls]))
        per_r = []
        for r in range(2):
            d = _core_inputs(r, G, np.asarray(w_in)[ls], np.asarray(gdn_conv_w)[ls], np.asarray(gdn_a_log)[ls],
                             np.asarray(gdn_dt_bias)[ls], np.asarray(w_mem_kv)[ls], np.asarray(w_br_gdn)[ls],
                             np.asarray(w_br_diff)[ls], np.asarray(w_br_mem)[ls])
            d.update(qx=consts[r][0], alb=consts[r][1])
            d.update(shared)
            per_r.append(d)
        in_maps = []
        for c in range(8):
            b, r = c // 2, c % 2
            m = dict(per_r[r])
            m.update(x=_c(x[b]), xhalf=_c(x[b, r * TB:(r + 1) * TB]), mem=_c(np.asarray(mem)[b]))
            in_maps.append(m)
        res = run_bass_kernel_spmd(nc, in_maps, core_ids=list(range(8))).results
        xn = np.empty((B, S, D), np.float32)
        for c in range(8):
            b, r = c // 2, c % 2
            xn[b, r * TB:(r + 1) * TB] = res[c]["xo"]
        x = xn
    return x
```
